# Optimizing a Trainium2 kernel written in Bass

```python
import math
import jax, jax.numpy as jnp
from jax import lax
import numpy as np

D_MODEL = 1024
BATCH = 16
SEQ = 2048
DEPTH = 4

N_EVEN = (DEPTH + 1) // 2
N_ODD = DEPTH // 2
EPS = 1e-6
N_MOD = 9

HEAD_DIM = 64
MIX_A_WIDTH = D_MODEL // 2
MIX_B_WIDTH = D_MODEL - MIX_A_WIDTH
A_HEADS = MIX_A_WIDTH // (2 * HEAD_DIM)
A_VDIM = 2 * HEAD_DIM
B_Q_HEADS = MIX_B_WIDTH // HEAD_DIM
B_KV_HEADS = 2
B_GROUP = B_Q_HEADS // B_KV_HEADS
WINDOW = 128
BLOCK = 128
NUM_BUCKETS = 32
MAX_DISTANCE = 128
N_ATTN_HEADS = A_HEADS + B_Q_HEADS

C_HEADS = 8
C_KDIM = D_MODEL // C_HEADS
C_VDIM = D_MODEL // C_HEADS
CHUNK = 32

D_FF = 2816

A_QK = A_HEADS * 2 * HEAD_DIM
A_V = A_HEADS * A_VDIM
B_Q = B_Q_HEADS * HEAD_DIM
B_KV = B_KV_HEADS * HEAD_DIM
EVEN_SPLITS = (A_QK, 2 * A_QK, 2 * A_QK + A_V, 2 * A_QK + A_V + B_Q, 2 * A_QK + A_V + B_Q + B_KV)
EVEN_IN = 2 * A_QK + A_V + B_Q + 2 * B_KV
EVEN_OUT = A_V + B_Q
C_HK = C_HEADS * C_KDIM
C_HV = C_HEADS * C_VDIM
ODD_SPLITS = (C_HK, 2 * C_HK, 3 * C_HK, 3 * C_HK + C_HV)
ODD_IN = 3 * C_HK + 2 * C_HV

kernel_name = "hybrid_diffattn_swa_hgrn2_macaron_encoder"


def rms_norm(x, g):
    xf = x.astype(jnp.float32)
    y = xf * lax.rsqrt(jnp.mean(xf * xf, axis=-1, keepdims=True) + EPS)
    return (y * g.astype(jnp.float32)).astype(x.dtype)


def modulate(h, shift, scale):
    return h * (1 + scale[:, None, :]) + shift[:, None, :]


def swiglu(h, w_up, w_down):
    a, b = jnp.split(h @ w_up, 2, axis=-1)
    return (jax.nn.silu(a) * b) @ w_down


def t5_bucket(rel):
    half = NUM_BUCKETS // 2
    max_exact = half // 2
    n = jnp.abs(rel)
    nf = jnp.maximum(n, 1).astype(jnp.float32)
    large = max_exact + (jnp.log(nf / max_exact) / math.log(MAX_DISTANCE / max_exact)
                         * (half - max_exact)).astype(jnp.int32)
    large = jnp.minimum(large, half - 1)
    return jnp.where(rel > 0, half, 0) + jnp.where(n < max_exact, n, large)


def diff_attention(q1, q2, k1, k2, v, lam, lam_init, table_a, subln_g):
    bsz, s = q1.shape[:2]
    nb = s // BLOCK
    scale = HEAD_DIM ** -0.5
    kpos = jnp.arange(s)

    def to_blocks(t):
        return jnp.moveaxis(t.reshape(bsz, nb, BLOCK, A_HEADS, HEAD_DIM), 1, 0)

    def block_fn(args):
        qb1, qb2, idx = args
        qpos = idx * BLOCK + jnp.arange(BLOCK)
        bias = jnp.transpose(table_a[t5_bucket(kpos[None, :] - qpos[:, None])], (2, 0, 1))
        bias = bias.astype(jnp.float32)
        s1 = jnp.einsum('bqhd,bkhd->bhqk', qb1, k1).astype(jnp.float32) * scale + bias
        s2 = jnp.einsum('bqhd,bkhd->bhqk', qb2, k2).astype(jnp.float32) * scale + bias
        p = jax.nn.softmax(s1, axis=-1) - lam * jax.nn.softmax(s2, axis=-1)
        return jnp.einsum('bhqk,bkhv->bqhv', p.astype(v.dtype), v)

    o = lax.map(block_fn, (to_blocks(q1), to_blocks(q2), jnp.arange(nb)))
    o = jnp.moveaxis(o, 0, 1).reshape(bsz, s, A_HEADS, A_VDIM)
    o = rms_norm(o, subln_g) * (1 - lam_init)
    return o.reshape(bsz, s, A_HEADS * A_VDIM)


def window_attention(q, k, v, sink, table_b):
    bsz, s = q.shape[:2]
    nb = s // BLOCK
    scale = HEAD_DIM ** -0.5
    qb = q.reshape(bsz, nb, BLOCK, B_KV_HEADS, B_GROUP, HEAD_DIM)

    def neighbours(t):
        tp = jnp.pad(t, ((0, 0), (BLOCK, BLOCK), (0, 0), (0, 0)))
        tp = tp.reshape(bsz, nb + 2, BLOCK, B_KV_HEADS, HEAD_DIM)
        return jnp.concatenate([tp[:, :-2], tp[:, 1:-1], tp[:, 2:]], axis=2)

    kn, vn = neighbours(k), neighbours(v)
    rel = jnp.arange(3 * BLOCK)[None, :] - BLOCK - jnp.arange(BLOCK)[:, None]
    kpos = (jnp.arange(nb)[:, None] - 1) * BLOCK + jnp.arange(3 * BLOCK)[None, :]
    valid = (jnp.abs(rel) <= WINDOW)[None] & ((kpos >= 0) & (kpos < s))[:, None, :]
    bias = jnp.transpose(table_b[t5_bucket(rel)], (2, 0, 1))
    bias = bias.reshape(B_KV_HEADS, B_GROUP, BLOCK, 3 * BLOCK).astype(jnp.float32)
    sc = jnp.einsum('bnqhgd,bnkhd->bnhgqk', qb, kn).astype(jnp.float32) * scale + bias
    sc = jnp.where(valid[None, :, None, None], sc, -jnp.inf)
    sk = sink.astype(jnp.float32).reshape(1, 1, B_KV_HEADS, B_GROUP, 1, 1)
    m = jnp.maximum(jnp.max(sc, axis=-1, keepdims=True), sk)
    e = jnp.exp(sc - m)
    p = e / (jnp.sum(e, axis=-1, keepdims=True) + jnp.exp(sk - m))
    o = jnp.einsum('bnhgqk,bnkhd->bnqhgd', p.astype(v.dtype), vn)
    return o.reshape(bsz, s, B_Q_HEADS * HEAD_DIM)


def even_mixer(h, w_in, w_out, qk_g, lam_p, subln_g, sink, rel_bias, layer_idx):
    bsz, s, _ = h.shape
    aq, ak, av, bq, bk, bv = jnp.split(h @ w_in, EVEN_SPLITS, axis=-1)
    aq = rms_norm(aq.reshape(bsz, s, A_HEADS, 2, HEAD_DIM), qk_g[0])
    ak = rms_norm(ak.reshape(bsz, s, A_HEADS, 2, HEAD_DIM), qk_g[1])
    av = av.reshape(bsz, s, A_HEADS, A_VDIM)
    lam_init = 0.8 - 0.6 * math.exp(-0.3 * layer_idx)
    lp = lam_p.astype(jnp.float32)
    lam = jnp.exp(jnp.sum(lp[0] * lp[1])) - jnp.exp(jnp.sum(lp[2] * lp[3])) + lam_init
    ya = diff_attention(aq[..., 0, :], aq[..., 1, :], ak[..., 0, :], ak[..., 1, :], av,
                        lam, lam_init, rel_bias[:, :A_HEADS], subln_g)
    bq = rms_norm(bq.reshape(bsz, s, B_Q_HEADS, HEAD_DIM), qk_g[2])
    bk = rms_norm(bk.reshape(bsz, s, B_KV_HEADS, HEAD_DIM), qk_g[3])
    bv = bv.reshape(bsz, s, B_KV_HEADS, HEAD_DIM)
    yb = window_attention(bq, bk, bv, sink, rel_bias[:, A_HEADS:])
    return jnp.concatenate([ya, yb], axis=-1) @ w_out


def chunk_gla(q, k, v, log_f):
    n, s, h, dk = q.shape
    dv = v.shape[-1]
    nc = s // CHUNK

    def chunks(t):
        return jnp.moveaxis(t.reshape(n, nc, CHUNK, h, t.shape[-1]), 1, 0)

    g_cum = jnp.cumsum(chunks(log_f), axis=2)
    causal = jnp.tril(jnp.ones((CHUNK, CHUNK), dtype=bool))[None, :, :, None, None]

    def step(state, inp):
        qc, kc, vc, gc = inp
        decay = jnp.exp(jnp.where(causal, gc[:, :, None] - gc[:, None], -jnp.inf))
        attn = jnp.einsum('ntshd,nshd->nhts', qc[:, :, None] * decay, kc)
        o = jnp.einsum('nhts,nshv->nthv', attn, vc) + jnp.einsum('nthd,nhdv->nthv', qc * jnp.exp(gc), state)
        g_last = gc[:, -1]
        state = jnp.exp(g_last)[..., None] * state + jnp.einsum(
            'nshd,nshv->nhdv', kc * jnp.exp(g_last[:, None] - gc), vc)
        return state, o

    s0 = jnp.zeros((n, h, dk, dv), jnp.float32)
    _, o = lax.scan(step, s0, (chunks(q), chunks(k), chunks(v), g_cum))
    return jnp.moveaxis(o, 0, 1).reshape(n, s, h, dv)


def log_forget(f_raw, lb):
    lb = lb.reshape(C_HEADS, C_KDIM)
    return jnp.logaddexp(jnp.log(lb), jnp.log1p(-lb) + jax.nn.log_sigmoid(f_raw.astype(jnp.float32)))


def hgrn2_bidirectional(q, i_in, log_f_fwd, log_f_bwd):
    bsz = q.shape[0]
    flip = lambda t: jnp.flip(t, axis=1)
    qq = jnp.concatenate([q, flip(q)], axis=0).astype(jnp.float32)
    vv = jnp.concatenate([i_in, flip(i_in)], axis=0).astype(jnp.float32)
    lf = jnp.concatenate([log_f_fwd, flip(log_f_bwd)], axis=0)
    kk = -jnp.expm1(lf)
    o = chunk_gla(qq, kk, vv, lf)
    return o[:bsz] + flip(o[bsz:])


def odd_mixer(h, w_in, w_out, lb, out_g):
    bsz, s, _ = h.shape
    q, ff, fb, iv, g = jnp.split(h @ w_in, ODD_SPLITS, axis=-1)
    heads_k = lambda t: t.reshape(bsz, s, C_HEADS, C_KDIM)
    heads_v = lambda t: t.reshape(bsz, s, C_HEADS, C_VDIM)
    o = hgrn2_bidirectional(jax.nn.silu(heads_k(q)), heads_v(iv),
                            log_forget(heads_k(ff), lb[0]), log_forget(heads_k(fb), lb[1]))
    o = rms_norm(o.astype(h.dtype), out_g) * jax.nn.silu(heads_v(g))
    return o.reshape(bsz, s, C_HV) @ w_out


def setup_inputs(seed: int = 0) -> dict:
    key = jax.random.key(seed)
    ks = jax.random.split(key, 18)
    nrm = lambda k, shape, sc: jax.random.normal(k, shape, jnp.float32) * sc
    return {
        "x": nrm(ks[0], (BATCH, SEQ, D_MODEL), 1.0),
        "c": nrm(ks[1], (BATCH, D_MODEL), 1.0),
        "ada_w": nrm(ks[2], (DEPTH, D_MODEL, N_MOD * D_MODEL), 0.5 * D_MODEL ** -0.5),
        "ada_b": nrm(ks[3], (DEPTH, N_MOD * D_MODEL), 0.02),
        "norm_g": 1.0 + nrm(ks[4], (DEPTH, 3, D_MODEL), 0.02),
        "ffn_up": nrm(ks[5], (DEPTH, 2, D_MODEL, 2 * D_FF), D_MODEL ** -0.5),
        "ffn_down": nrm(ks[6], (DEPTH, 2, D_FF, D_MODEL), D_FF ** -0.5),
        "even_w_in": nrm(ks[7], (N_EVEN, D_MODEL, EVEN_IN), D_MODEL ** -0.5),
        "even_w_out": nrm(ks[8], (N_EVEN, EVEN_OUT, D_MODEL), EVEN_OUT ** -0.5),
        "qk_norm_g": 1.0 + nrm(ks[9], (N_EVEN, 4, HEAD_DIM), 0.02),
        "diff_lambda": nrm(ks[10], (N_EVEN, 4, HEAD_DIM), 0.1),
        "diff_subln_g": 1.0 + nrm(ks[11], (N_EVEN, A_VDIM), 0.02),
        "sink_logit": nrm(ks[12], (N_EVEN, B_Q_HEADS), 0.5),
        "rel_bias": nrm(ks[13], (NUM_BUCKETS, N_ATTN_HEADS), 0.1),
        "odd_w_in": nrm(ks[14], (N_ODD, D_MODEL, ODD_IN), D_MODEL ** -0.5),
        "odd_w_out": nrm(ks[15], (N_ODD, C_HV, D_MODEL), C_HV ** -0.5),
        "c_lower_bound": nrm(ks[16], (2, DEPTH, C_HK), 0.1),
        "c_out_norm_g": 1.0 + nrm(ks[17], (N_ODD, C_VDIM), 0.02),
    }


def reference(x, c, ada_w, ada_b, norm_g, ffn_up, ffn_down, even_w_in, even_w_out, qk_norm_g,
              diff_lambda, diff_subln_g, sink_logit, rel_bias, odd_w_in, odd_w_out,
              c_lower_bound, c_out_norm_g):
    mod_all = jnp.einsum('bd,ldm->lbm', jax.nn.silu(c), ada_w) + ada_b[:, None, :]
    sm = jax.nn.softmax(c_lower_bound.astype(jnp.float32), axis=1)
    lb_all = jnp.cumsum(sm, axis=1) - sm[:, :1]
    for l in range(DEPTH):
        sh1, sc1, g1, sh2, sc2, g2, sh3, sc3, g3 = jnp.split(mod_all[l], N_MOD, axis=-1)
        h = modulate(rms_norm(x, norm_g[l, 0]), sh1, sc1)
        x = x + 0.5 * g1[:, None, :] * swiglu(h, ffn_up[l, 0], ffn_down[l, 0])
        h = modulate(rms_norm(x, norm_g[l, 1]), sh2, sc2)
        if l % 2 == 0:
            e = l // 2
            y = even_mixer(h, even_w_in[e], even_w_out[e], qk_norm_g[e], diff_lambda[e],
                           diff_subln_g[e], sink_logit[e], rel_bias, l)
        else:
            o = l // 2
            y = odd_mixer(h, odd_w_in[o], odd_w_out[o], lb_all[:, l], c_out_norm_g[o])
        x = x + g2[:, None, :] * y
        h = modulate(rms_norm(x, norm_g[l, 2]), sh3, sc3)
        x = x + 0.5 * g3[:, None, :] * swiglu(h, ffn_up[l, 1], ffn_down[l, 1])
    return x
```

```python
import math
from contextlib import ExitStack
import numpy as np
import concourse.bass as bass
import concourse.mybir as mybir
from concourse.bass_utils import run_bass_kernel_spmd

F32 = mybir.dt.float32
BF16 = mybir.dt.bfloat16
AF = mybir.ActivationFunctionType
ALU = mybir.AluOpType

D = 1024
S = 2048
DEPTH = 4
NCORES = 8
NSEQ = 2
DFF = 2816
NF = 22
EPS = 1e-6
SLOT = 3072
NSLOT = 7
FGROUPS = [(0, 4), (4, 8), (8, 12), (12, 16), (16, 19), (19, 22)]
N_ADA_PIECES = 24
SCALE = 0.125
NEG = -30000.0
ARENA = 60416


def _kin(w):
    return w.reshape(8, 128, -1).transpose(1, 0, 2)


def _pad(a):
    a = np.ascontiguousarray(a, dtype=np.float32).reshape(128, -1)
    out = np.zeros((128, SLOT), np.float32)
    out[:, : a.shape[1]] = a
    return out


def t5_bucket_np(rel):
    half = 16
    max_exact = 8
    n = np.abs(rel)
    nf = np.maximum(n, 1).astype(np.float32)
    large = max_exact + (np.log(nf / max_exact) / math.log(128 / max_exact) * (half - max_exact)).astype(np.int32)
    large = np.minimum(large, half - 1)
    return np.where(rel > 0, half, 0) + np.where(n < max_exact, n, large)


def ada_pieces(ada_w, l):
    w = _kin(ada_w[l])
    return [_pad(w[:, :, j * 384:(j + 1) * 384]) for j in range(N_ADA_PIECES)]


def ffn_pieces(ffn_up, ffn_down, l, j):
    up = _kin(ffn_up[l, j]).reshape(128, 8, 2, NF, 128)
    out = []
    for i in range(NF):
        u = up[:, :, :, i, :].reshape(128, 2048)
        dn = ffn_down[l, j, i * 128:(i + 1) * 128, :]
        out.append(_pad(np.concatenate([u, dn], axis=1)))
    return out


def wout_piece(w_out, chunk):
    return _pad(w_out[chunk * 128:(chunk + 1) * 128, :])


EVEN_COLS = [SLOT, 1024] * 4 + [1024] + [SLOT, 1024, 1024] * 2
ODD_COLS = [SLOT, 2048, 1024] * 8


def even_pieces(even_w_in, even_w_out, e):
    w = _kin(even_w_in[e])
    out = []
    for h in range(4):
        q = w[:, :, h * 128:(h + 1) * 128]
        k = w[:, :, 512 + h * 128:512 + (h + 1) * 128]
        v = w[:, :, 1024 + h * 128:1024 + (h + 1) * 128]
        out.append(_pad(np.concatenate([q, k, v], axis=2)))
        out.append(wout_piece(even_w_out[e], h))
    out.append(_pad(w[:, :, 2176:2304]))
    for g in range(2):
        q = w[:, :, 1536 + g * 256:1536 + (g + 1) * 256]
        k = w[:, :, 2048 + g * 64:2048 + (g + 1) * 64]
        out.append(_pad(np.concatenate([q, k, k], axis=2)))
        out.append(wout_piece(even_w_out[e], 4 + 2 * g))
        out.append(wout_piece(even_w_out[e], 5 + 2 * g))
    return out


def odd_pieces(odd_w_in, odd_w_out, o):
    w = _kin(odd_w_in[o])
    out = []
    for h in range(8):
        sl = slice(h * 128, (h + 1) * 128)
        q, ff, fb, iv, g = (w[:, :, k * 1024:(k + 1) * 1024][:, :, sl] for k in range(5))
        out.append(_pad(np.concatenate([q, ff, fb], axis=2)))
        out.append(_pad(np.concatenate([iv, g], axis=2)))
        out.append(wout_piece(odd_w_out[o], h))
    return out


SM_QKG, SM_DLAM, SM_SUBLN, SM_SINK, SM_CLOHI, SM_CLB, SM_OUTG, SM_W = 0, 8, 520, 776, 792, 816, 880, 882


def small_inputs(inputs):
    sm = np.zeros((128, SM_W), np.float32)
    p = np.arange(128)
    sm[:, SM_QKG:SM_QKG + 8] = inputs["qk_norm_g"][:, :, p % 64].transpose(2, 0, 1).reshape(128, 8)
    sm[:, SM_DLAM:SM_DLAM + 512] = inputs["diff_lambda"].reshape(1, 512)
    sm[:, SM_SUBLN:SM_SUBLN + 256] = inputs["diff_subln_g"].reshape(1, 256)
    sm[:, SM_SINK:SM_SINK + 16] = inputs["sink_logit"].reshape(1, 16)
    sm[:, SM_CLOHI:SM_CLOHI + 24] = inputs["rel_bias"][[15, 31], :].T.reshape(1, 24)
    sm[:, SM_CLB:SM_CLB + 64] = inputs["c_lower_bound"].reshape(2, 4, 8, 128).transpose(3, 0, 1, 2).reshape(128, 64)
    sm[:, SM_OUTG:SM_OUTG + 2] = inputs["c_out_norm_g"].T
    return sm


def strips_input(inputs):
    k = np.arange(128)[:, None]
    q = np.arange(128)[None, :]
    out = np.zeros((13, 128, 384), np.float32)
    for j, d in enumerate((1, 0, -1)):
        idx = t5_bucket_np(k - q + 128 * d)
        out[:12, :, j * 128:(j + 1) * 128] = inputs["rel_bias"][idx].transpose(2, 0, 1)
    out[12, :, 0:128] = np.where(k <= q, 0.0, NEG)
    out[12, :, 256:384] = np.where(k >= q, 0.0, NEG)
    return out


def consts_input():
    c = np.zeros((128, 4, 128), np.float32)
    c[:, 0, :] = np.eye(128)
    pp = np.arange(128)
    c[:, 1, :] = (pp[:, None] // 64 == pp[None, :] // 64)
    same = (pp[:, None] // 64 == pp[None, :] // 64)
    c[:, 2, :] = same & (pp[:, None] <= pp[None, :])
    c[:, 3, :] = same & (pp[:, None] >= pp[None, :])
    return c


class T:
    __slots__ = ("name", "w", "r")

    def __init__(self, name):
        self.name = name
        self.w = None
        self.r = {}


class Sched:
    ENG = ("pe", "act", "dve", "pool", "sp")

    def __init__(self):
        self.q = {e: [] for e in self.ENG}
        self.val = {}
        self.seen = {e: {} for e in self.ENG}

    def _deps(self, eng, reads, writes):
        need = {}

        def add(k, v):
            if v > need.get(k, 0):
                need[k] = v

        for t in reads:
            if t.w is not None:
                add(*t.w)
        for t in writes:
            if t.w is not None:
                add(*t.w)
            for k, v in t.r.items():
                add(k, v)
        waits = []
        seen = self.seen[eng]
        for k, v in need.items():
            if eng == "pe" and k == "c_pe":
                continue
            if seen.get(k, 0) < v:
                waits.append((k, v))
                seen[k] = v
        return waits

    def _mark(self, ev, reads, writes):
        k, v = ev
        for t in reads:
            if t.r.get(k, 0) < v:
                t.r[k] = v
        for t in writes:
            t.w = ev
            t.r = {}

    def op(self, eng, fn, reads=(), writes=()):
        waits = self._deps(eng, reads, writes)
        k = "c_" + eng
        v = self.val.get(k, 0) + 1
        self.val[k] = v
        self.q[eng].append((waits, fn, (k, 1)))
        self._mark((k, v), reads, writes)

    def dma(self, queue, fns, semkey, reads=(), writes=()):
        waits = self._deps(queue, reads, writes)
        for i, fn in enumerate(fns):
            self.val[semkey] = self.val.get(semkey, 0) + 16
            self.q[queue].append((waits if i == 0 else [], fn, (semkey, 16)))
        self._mark((semkey, self.val[semkey]), reads, writes)

    def final_wait(self, eng, semkeys):
        waits = [(k, self.val[k]) for k in semkeys if self.val.get(k, 0) > 0]
        self.q[eng].append((waits, None, None))


class Buf:
    def __init__(self, ap, name):
        self.ap = ap
        self.T = T(name)


class Ring:
    def __init__(self, bufs):
        self.bufs = bufs
        self.i = 0

    def next(self):
        b = self.bufs[self.i % len(self.bufs)]
        self.i += 1
        return b


class Builder:
    def __init__(self, layers, nseq=NSEQ, do_mixer=True, do_ffn=True, debug=None):
        self.debug = debug or set()
        self.layers = list(layers)
        self.nseq = nseq
        self.do_mixer = do_mixer
        self.do_ffn = do_ffn
        self.s = Sched()
        self.sems = {}
        self.nc = bass.Bass("TRN2", target_bir_lowering=False)
        self.stack = ExitStack()

    def sb(self, name, shape, dt=F32):
        t = self.stack.enter_context(self.nc.sbuf_tensor(name, list(shape), dt))
        return t

    def view(self, name, off, shape, dt=F32):
        n = int(np.prod(shape[1:]))
        nbytes = n * (4 if dt == F32 else 2)
        assert off % 4 == 0 and off + nbytes <= ARENA, (name, off, nbytes)
        ap = self.arena[:, off // 4:(off + (nbytes + 3) // 4 * 4) // 4]
        if dt != F32:
            ap = ap.bitcast(dt)
            ap = ap[:, 0:n]
        if len(shape) == 3:
            ap = ap.rearrange("p (a b) -> p a b", b=shape[2])
        elif len(shape) == 4:
            ap = ap.rearrange("p (a b c) -> p a b c", b=shape[2], c=shape[3])
        return Buf(ap, name)

    def barrier(self):
        keys = [k for k in self.s.val if k in ("c_pe", "c_act", "c_dve") or k.startswith("ld_a") or (self.debug and k == "st_x")]
        for eng in ("pe", "act", "dve", "sp"):
            waits = []
            for k in keys:
                v = self.s.val[k]
                if eng == "pe" and k == "c_pe":
                    continue
                if self.s.seen[eng].get(k, 0) < v:
                    waits.append((k, v))
                    self.s.seen[eng][k] = v
            if waits:
                self.s.q[eng].append((waits, None, None))

    def buf(self, name, shape, dt=F32):
        t = self.sb(name, shape, dt)
        return Buf(t, name)

    def act(self, out, in_, func, reads, writes, **kw):
        self.s.op("act", lambda e: e.activation(out=out, in_=in_, func=func, **kw), reads, writes)

    def tt(self, out, in0, in1, op, reads, writes, eng="dve"):
        self.s.op(eng, lambda e: e.tensor_tensor(out=out, in0=in0, in1=in1, op=op), reads, writes)

    def ts(self, out, in0, s1, s2, op0, op1, reads, writes, eng="dve"):
        if op1 is None:
            self.s.op(eng, lambda e: e.tensor_scalar(out=out, in0=in0, scalar1=s1, scalar2=None, op0=op0), reads, writes)
        else:
            self.s.op(eng, lambda e: e.tensor_scalar(out=out, in0=in0, scalar1=s1, scalar2=s2, op0=op0, op1=op1), reads, writes)

    def stt(self, out, in0, scalar, in1, op0, op1, reads, writes):
        self.s.op("dve", lambda e: e.scalar_tensor_tensor(out=out, in0=in0, scalar=scalar, in1=in1, op0=op0, op1=op1), reads, writes)

    def copy(self, out, in_, reads, writes, eng="dve"):
        if eng == "act":
            self.s.op("act", lambda e: e.activation(out=out, in_=in_, func=AF.Copy), reads, writes)
        else:
            self.s.op(eng, lambda e: e.tensor_copy(out=out, in_=in_), reads, writes)

    def recip(self, out, in_, reads, writes):
        self.s.op("dve", lambda e: e.reciprocal(out=out, in_=in_), reads, writes)

    def memset(self, ap, val, writes, eng="dve"):
        self.s.op(eng, lambda e: e.memset(ap, val), (), writes)

    def mm(self, items, reads, writes):
        items = list(items)

        def fn(e):
            ins = None
            for (o, l, r, st, sp) in items:
                ins = e.matmul(o, lhsT=l, rhs=r, start=st, stop=sp)
            return ins

        self.s.op("pe", fn, reads, writes)

    def mmacc(self, out, pairs, reads, writes):
        n = len(pairs)
        self.mm([(out, l, r, i == 0, i == n - 1) for i, (l, r) in enumerate(pairs)], reads, writes)

    def transposes(self, items, reads, writes):
        items = list(items)

        def fn(e):
            ins = None
            for (o, i_, ident) in items:
                ins = e.transpose(o, i_, ident)
            return ins

        self.s.op("pe", fn, reads, writes)

    def dump(self, name, ap, reads, dt=F32):
        shape = list(ap.shape)
        d = self.nc.dram_tensor("dbg_" + name, shape, dt, kind="ExternalOutput").ap()
        self.s.dma("sp", [lambda e: e.dma_start(out=d, in_=ap)], "st_x", reads, ())

    def load(self, out, in_, semkey, writes, queue="sp"):
        self.s.dma(queue, [lambda e: e.dma_start(out=out, in_=in_)], semkey, (), writes)

    def gbank(self):
        b = self.gb[self.gbi % 4]
        self.gbi += 1
        return b

    def abank(self):
        b = self.ab[self.abi % 4]
        self.abi += 1
        return b

    def w_prefetch(self):
        while self.w_free > 0 and self.w_next_load < len(self.w_plan):
            k = self.w_next_load
            idx, ncols = self.w_plan[k]
            slot = k % NSLOT
            out = self.ring[:, slot, 0:ncols]
            in_ = self.wstream[idx, :, 0:ncols]
            self.s.dma("pool", [lambda e, o=out, i=in_: e.dma_start(out=o, in_=i)], "w%d" % slot, (), [self.slotT[slot]])
            self.w_next_load += 1
            self.w_free -= 1

    def w_acquire(self, expect_idx=None):
        k = self.w_next_use
        assert k < self.w_next_load, "weight stream underflow"
        if expect_idx is not None:
            assert self.w_plan[k][0] == expect_idx, (k, self.w_plan[k], expect_idx)
        self.w_next_use += 1
        slot = k % NSLOT
        return self.ring[:, slot, :], self.slotT[slot]

    def w_release(self, n=1):
        self.w_free += n
        self.w_prefetch()

    def build(self, piece_index, n_pieces):
        nc = self.nc
        st = self.stack
        L = self.layers
        NL = len(L)
        self.wstream = nc.dram_tensor("wstream", [n_pieces, 128, SLOT], F32, kind="ExternalInput").ap()
        self.x_in = nc.dram_tensor("x_in", [self.nseq, 128, 8, S], F32, kind="ExternalInput").ap()
        self.x_out = nc.dram_tensor("x_out", [self.nseq, 128, 8, S], F32, kind="ExternalOutput").ap()
        self.c_in = nc.dram_tensor("c_in", [128, 8, 2], F32, kind="ExternalInput").ap()
        self.adab_in = nc.dram_tensor("adab_in", [128, DEPTH, 72], F32, kind="ExternalInput").ap()
        self.normg_in = nc.dram_tensor("normg_in", [128, DEPTH, 3, 8], F32, kind="ExternalInput").ap()
        self.small_in = nc.dram_tensor("small_in", [128, SM_W], F32, kind="ExternalInput").ap()
        self.strips_in = nc.dram_tensor("strips_in", [13, 128, 384], F32, kind="ExternalInput").ap()
        self.consts_in = nc.dram_tensor("consts_in", [128, 4, 128], F32, kind="ExternalInput").ap()

        self.xT = self.sb("xT", [128, 8, S], F32)
        self.xT_T = [[T("x%d_%d" % (c, b)) for b in range(4)] for c in range(8)]
        self.hT = self.sb("hT", [128, 8, S], BF16)
        self.hT_T = [[T("h%d_%d" % (c, b)) for b in range(4)] for c in range(8)]
        self.ring = self.sb("ring", [128, NSLOT, SLOT], BF16)
        self.slotT = [T("slot%d" % i) for i in range(NSLOT)]
        self.arena = self.sb("arena", [128, ARENA // 4], F32)
        self.arenaT = T("arena")
        self.hid = [self.view("hid%d" % i, 8192 + 4096 * i, [128, 4, 512], BF16) for i in range(2)]
        self.hid_i = 0
        self.sq = Ring([self.view("sq0", 0, [128, 8, 512], BF16)])
        self.f32a = Ring([self.view("f32a%d" % i, 16384 + 2048 * i, [128, 512], F32) for i in range(4)])
        self.rstd = Ring([self.view("rstd%d" % i, 24576 + 2048 * i, [128, 512], F32) for i in range(2)])
        self.modT = self.buf("modT", [128, DEPTH, 9, 8, 2], F32)
        self.gmod = self.buf("gmod", [128, DEPTH, 3, 8, 2], F32)
        self.gate = self.buf("gate", [128, DEPTH, 3, 8, 2], F32)
        self.cT = self.buf("cT", [128, 8, 2], F32)
        self.scT = self.buf("scT", [128, 8, 2], BF16)
        self.adab = self.buf("adab", [128, DEPTH, 72], F32)
        self.normg = self.buf("normg", [128, DEPTH, 3, 8], F32)
        self.small = self.buf("small", [128, SM_W], F32)
        self.consts = self.buf("consts", [128, 4, 128], BF16)
        self.derived = self.buf("derived", [128, 128], F32)
        self.ones_bf = self.buf("ones_bf", [128, 128], BF16)
        self.epsT = self.buf("epsT", [128, 1], F32)

        banks = [st.enter_context(nc.psum_tensor("bank%d" % i, [128, 512], F32)) for i in range(8)]
        self.gb = [Buf(banks[i], "gb%d" % i) for i in range(4)]
        self.ab = [Buf(banks[4 + i], "ab%d" % i) for i in range(4)]
        self.gbi = 0
        self.abi = 0

        plan = []
        for l in L:
            first, cols = piece_index["ada%d" % l]
            plan += [(first + i, c) for i, c in enumerate(cols)]
        per_seq = []
        for l in L:
            for nm in ("ffn%d_0" % l, "mix%d" % l, "ffn%d_1" % l):
                if nm.startswith("mix") and not self.do_mixer:
                    continue
                if nm.startswith("ffn") and not self.do_ffn:
                    continue
                first, cols = piece_index[nm]
                per_seq += [(first + i, c) for i, c in enumerate(cols)]
        for _ in range(self.nseq):
            plan += per_seq
        self.w_plan = plan
        self.w_next_load = 0
        self.w_next_use = 0
        self.w_free = NSLOT
        self.piece_index = piece_index

        self.load(self.cT.ap[:], self.c_in[:, :, :], "ld_small", [self.cT.T])
        self.load(self.adab.ap[:], self.adab_in[:, :, :], "ld_small", [self.adab.T])
        self.load(self.normg.ap[:], self.normg_in[:, :, :, :], "ld_small", [self.normg.T])
        self.load(self.small.ap[:], self.small_in[:, :], "ld_small", [self.small.T])
        self.memset(self.ones_bf.ap[:], 1.0, [self.ones_bf.T])
        self.memset(self.epsT.ap[:], EPS, [self.epsT.T])
        self.w_prefetch()
        self.setup_extra()
        self.ada_phase()

        for s in range(self.nseq):
            for c in range(8):
                self.s.dma("sp", [lambda e, c=c, s=s: e.dma_start(out=self.xT[:, c, :], in_=self.x_in[s, :, c, :])],
                           "ld_x", (), self.xT_T[c])
            for l in L:
                if self.do_ffn:
                    self.barrier()
                    self.norm(l, 0, s)
                    if "h0" in self.debug and s == 0 and l == L[0]:
                        self.dump("h0", self.hT[:, :, :], [t for r in self.hT_T for t in r], BF16)
                        self.dump("modT", self.modT.ap[:], [self.modT.T])
                        self.dump("gmod", self.gmod.ap[:], [self.gmod.T])
                        self.dump("gate", self.gate.ap[:], [self.gate.T])
                    self.ffn(l, 0, s)
                    if "x1" in self.debug and s == 0 and l == L[0]:
                        self.dump("x1", self.xT[:, :, :], [t for r in self.xT_T for t in r])
                if self.do_mixer:
                    self.barrier()
                    self.norm(l, 1, s)
                    self.barrier()
                    if l % 2 == 0:
                        self.even_mixer(l, s)
                    else:
                        self.odd_mixer(l, s)
                if self.do_ffn:
                    self.barrier()
                    self.norm(l, 2, s)
                    self.ffn(l, 1, s)
            for c in range(8):
                self.s.dma("sp", [lambda e, c=c, s=s: e.dma_start(out=self.x_out[s, :, c, :], in_=self.xT[:, c, :])],
                           "st_x", self.xT_T[c], ())
        self.s.final_wait("sp", ["st_x"])
        assert self.w_next_use == len(self.w_plan), (self.w_next_use, len(self.w_plan))
        self.emit()
        return nc

    def setup_extra(self):
        AX = mybir.AxisListType.X
        ctmp = self.view("ctmp", 0, [128, 4, 128], F32)
        self.load(ctmp.ap[:], self.consts_in[:, :, :], "ld_a", [ctmp.T])
        self.copy(self.consts.ap[:], ctmp.ap[:], [ctmp.T], [self.consts.T])
        self.ident = self.consts.ap[:, 0, :]
        self.blk64 = self.consts.ap[:, 1, :]
        self.mask_f = self.consts.ap[:, 2, :]
        self.mask_b = self.consts.ap[:, 3, :]
        sm = self.small.ap
        smT = self.small.T
        dv = self.derived.ap
        dT = self.derived.T
        t = self.view("setup_t", 4096, [128, 512], F32)
        for e in range(2):
            lam_init = 0.8 - 0.6 * math.exp(-0.3 * (2 * e))
            base = SM_DLAM + e * 256
            self.tt(t.ap[:, 0:64], sm[:, base:base + 64], sm[:, base + 64:base + 128], ALU.mult, [smT], [t.T])
            self.tt(t.ap[:, 64:128], sm[:, base + 128:base + 192], sm[:, base + 192:base + 256], ALU.mult, [smT], [t.T])
            self.s.op("dve", lambda e_: e_.tensor_reduce(out=t.ap[:, 128:130],
                                                         in_=t.ap[:, 0:128].rearrange("p (a b) -> p a b", b=64),
                                                         axis=AX, op=ALU.add), [t.T], [t.T])
            self.act(t.ap[:, 130:132], t.ap[:, 128:130], AF.Exp, [t.T], [t.T])
            self.tt(t.ap[:, 132:133], t.ap[:, 131:132], t.ap[:, 130:131], ALU.subtract, [t.T], [t.T])
            self.ts(dv[:, e:e + 1], t.ap[:, 132:133], -lam_init, None, ALU.add, None, [t.T], [dT])
            sb_ = SM_SUBLN + e * 128
            self.ts(sm[:, sb_:sb_ + 128], sm[:, sb_:sb_ + 128], 1.0 - lam_init, None, ALU.mult, None, [smT], [smT])
        self.act(dv[:, 2:18], sm[:, SM_SINK:SM_SINK + 16], AF.Exp, [smT], [dT])
        ex = t.ap[:, 256:320].rearrange("p (d l h) -> p d l h", d=2, l=4)
        self.act(t.ap[:, 256:320], sm[:, SM_CLB:SM_CLB + 64], AF.Exp, [smT], [t.T])
        ssum = t.ap[:, 320:336].rearrange("p (d h) -> p d h", d=2)
        self.s.op("dve", lambda e_: e_.tensor_reduce(out=ssum, in_=t.ap[:, 256:320].rearrange("p (d l h) -> p d h l", d=2, l=4),
                                                     axis=AX, op=ALU.add), [t.T], [t.T])
        rs = t.ap[:, 336:352].rearrange("p (d h) -> p d h", d=2)
        self.recip(t.ap[:, 336:352], t.ap[:, 320:336], [t.T], [t.T])
        e23 = t.ap[:, 352:368].rearrange("p (d h) -> p d h", d=2)
        self.tt(e23, ex[:, :, 2, :], ex[:, :, 3, :], ALU.add, [t.T], [t.T])
        self.tt(e23, e23, ex[:, :, 1, :], ALU.add, [t.T], [t.T])
        lbv = dv[:, 18:50].rearrange("p (d o h) -> p d o h", d=2, o=2)
        self.tt(lbv[:, :, 0, :], ex[:, :, 1, :], rs, ALU.mult, [t.T], [dT])
        self.tt(lbv[:, :, 1, :], e23, rs, ALU.mult, [t.T], [dT])
        self.ts(dv[:, 50:82], dv[:, 18:50], -1.0, 1.0, ALU.mult, ALU.add, [dT], [dT])
        self.ts(dv[:, 82:114], dv[:, 18:50], -1.0, None, ALU.add, None, [dT], [dT])
        self.barrier()

    def out_proj_chunk(self, l, s, ytq, wslot, wT):
        for tb in range(4):
            tsl = slice(tb * 512, (tb + 1) * 512)
            for c in range(8):
                by = self.abank()
                self.mm([(by.ap[:, :], wslot[:, c * 128:(c + 1) * 128], ytq.ap[:, tsl], True, True)], [wT, ytq.T], [by.T])
                self.stt(self.xT[:, c, tsl], by.ap[:], self.gate.ap[:, l, 1, c, s:s + 1], self.xT[:, c, tsl],
                         ALU.mult, ALU.add, [by.T, self.gate.T, self.xT_T[c][tb]], [self.xT_T[c][tb]])

    def qk_proj_norm(self, slot, sT, col0, dst, dst_cols, gcol, tmp, sqr):
        for tb in range(4):
            tsl = slice(tb * 512, (tb + 1) * 512)
            hTs = [self.hT_T[c][tb] for c in range(8)]
            bank = self.gbank()
            self.mmacc(bank.ap[:, :], [(slot[:, c * 384 + col0:c * 384 + col0 + 128], self.hT[:, c, tsl]) for c in range(8)],
                       [sT] + hTs, [bank.T])
            raw = tmp.next()
            self.act(raw.ap[:], bank.ap[:], AF.Copy, [bank.T], [raw.T])
            sqb = sqr.next()
            self.act(sqb.ap[:], bank.ap[:], AF.Square, [bank.T], [sqb.T])
            b2 = self.gbank()
            self.mm([(b2.ap[:, :], self.blk64, sqb.ap[:], True, True)], [sqb.T, self.consts.T], [b2.T])
            t0 = tmp.next()
            self.act(t0.ap[:], b2.ap[:], AF.Sqrt, [b2.T, self.epsT.T], [t0.T], scale=1.0 / 64, bias=self.epsT.ap[:, 0:1])
            rs = tmp.next()
            self.recip(rs.ap[:], t0.ap[:], [t0.T], [rs.T])
            self.stt(dst.ap[:, dst_cols(tsl)] if callable(dst_cols) else dst.ap[:, tsl], raw.ap[:], gcol, rs.ap[:],
                     ALU.mult, ALU.mult, [raw.T, rs.T, self.small.T], [dst.T])

    def even_mixer(self, l, s):
        AX = mybir.AxisListType.X
        e = l // 2
        first = self.piece_index["mix%d" % l][0]
        sm = self.small.ap
        dv = self.derived.ap
        QT = self.view("QT", 0, [128, 2048], BF16)
        KT = self.view("KT", 4096, [128, 2048], BF16)
        Va = self.view("Va", 8192, [128, 16, 130], BF16)
        strips = self.view("stripsA", 13312, [128, 4, 384], F32)
        PT = Ring([self.view("PT%d" % i, 19456 + 1024 * i, [128, 512], BF16) for i in range(3)])
        o1 = self.view("o1", 22528, [128, 4, 128], F32)
        o2 = self.view("o2", 24576, [128, 4, 128], F32)
        tmp = Ring([self.view("tmpA%d" % i, 26624 + 2048 * i, [128, 512], F32) for i in range(4)])
        ystage = self.view("ystage", 34816, [128, 16, 128], BF16)
        ytq = self.view("ytq", 38912, [128, 2048], BF16)
        stat = self.view("stat", 43008, [128, 32], F32)
        self.s.dma("sp", [lambda e_: e_.dma_start(out=strips.ap[:, :, :], in_=self.strips_in[0:4, :, :].rearrange("h p c -> p h c"))],
                   "ld_a", (), [strips.T])
        self.memset(Va.ap[:, :, 128:130], 1.0, [Va.T])
        for h in range(4):
            slot, sT = self.w_acquire(first + 2 * h)
            self.qk_proj_norm(slot, sT, 0, QT, None, sm[:, SM_QKG + e * 4 + 0:SM_QKG + e * 4 + 1], tmp, PT)
            self.qk_proj_norm(slot, sT, 128, KT, None, sm[:, SM_QKG + e * 4 + 1:SM_QKG + e * 4 + 2], tmp, PT)
            for tg in range(4):
                bank = self.gbank()
                items = []
                for tt_ in range(4):
                    tok = slice((tg * 4 + tt_) * 128, (tg * 4 + tt_ + 1) * 128)
                    for c in range(8):
                        items.append((bank.ap[:, tt_ * 128:(tt_ + 1) * 128], self.hT[:, c, tok],
                                      slot[:, c * 384 + 256:c * 384 + 384], c == 0, c == 7))
                self.mm(items, [sT] + [self.hT_T[c][tg] for c in range(8)], [bank.T])
                self.copy(Va.ap[:, tg * 4:(tg + 1) * 4, 0:128], bank.ap[:, :].rearrange("p (a b) -> p a b", b=128),
                          [bank.T], [Va.T], eng="act")
            self.w_release(1)
            clo = sm[:, SM_CLOHI + 2 * h:SM_CLOHI + 2 * h + 1]
            chi = sm[:, SM_CLOHI + 2 * h + 1:SM_CLOHI + 2 * h + 2]
            for qb in range(4):
                qsl = slice(qb * 512, (qb + 1) * 512)
                for sidx in range(2):
                    ph = slice(64 * sidx, 64 * sidx + 64)
                    osb = o1 if sidx == 0 else o2
                    Ob = [self.abank() for _ in range(4)]

                    def s_mm(kt):
                        b = self.gbank()
                        self.mm([(b.ap[:, :], KT.ap[ph, kt * 128:(kt + 1) * 128], QT.ap[ph, qsl], True, True)],
                                [KT.T, QT.T], [b.T])
                        return b

                    nxt = s_mm(0)
                    for kt in range(16):
                        sbk = nxt
                        if kt < 15:
                            nxt = s_mm(kt + 1)
                        pt = PT.next()
                        ee = kt - 4 * qb
                        ds = [ee - j for j in range(4)]
                        near = [j for j in range(4) if abs(ds[j]) <= 1]
                        if not near:
                            bias = clo if ds[0] < 0 else chi
                            self.act(pt.ap[:], sbk.ap[:], AF.Exp, [sbk.T, self.small.T], [pt.T], scale=SCALE, bias=bias)
                        else:
                            j0, j1 = near[0], near[-1]
                            c0 = (1 - ds[j0]) * 128
                            w_ = (j1 - j0 + 1) * 128
                            self.stt(sbk.ap[:, j0 * 128:j0 * 128 + w_], sbk.ap[:, j0 * 128:j0 * 128 + w_], SCALE,
                                     strips.ap[:, h, c0:c0 + w_], ALU.mult, ALU.add, [sbk.T, strips.T], [sbk.T])
                            self.act(pt.ap[:, j0 * 128:j0 * 128 + w_], sbk.ap[:, j0 * 128:j0 * 128 + w_], AF.Exp,
                                     [sbk.T], [pt.T])
                            if j0 > 0:
                                self.act(pt.ap[:, 0:j0 * 128], sbk.ap[:, 0:j0 * 128], AF.Exp, [sbk.T, self.small.T], [pt.T],
                                         scale=SCALE, bias=chi)
                            if j1 < 3:
                                self.act(pt.ap[:, (j1 + 1) * 128:512], sbk.ap[:, (j1 + 1) * 128:512], AF.Exp,
                                         [sbk.T, self.small.T], [pt.T], scale=SCALE, bias=clo)
                        self.mm([(Ob[j].ap[:, 0:129], pt.ap[:, j * 128:(j + 1) * 128], Va.ap[:, kt, 0:129], kt == 0, kt == 15)
                                 for j in range(4)], [pt.T, Va.T], [b.T for b in Ob])
                    for j in range(4):
                        self.recip(stat.ap[:, j:j + 1], Ob[j].ap[:, 128:129], [Ob[j].T], [stat.T])
                        self.ts(osb.ap[:, j, :], Ob[j].ap[:, 0:128], stat.ap[:, j:j + 1], None, ALU.mult, None,
                                [Ob[j].T, stat.T], [osb.T])
                o1f = o1.ap[:, :, :].rearrange("p a b -> p (a b)")
                o2f = o2.ap[:, :, :].rearrange("p a b -> p (a b)")
                self.stt(o1f, o2f, dv[:, e:e + 1], o1f, ALU.mult, ALU.add, [o1.T, o2.T, self.derived.T], [o1.T])
                self.tt(o2f, o1f, o1f, ALU.mult, [o1.T], [o2.T])
                self.s.op("dve", lambda e_: e_.tensor_reduce(out=stat.ap[:, 8:12], in_=o2.ap[:, :, :], axis=AX, op=ALU.add),
                          [o2.T], [stat.T])
                self.act(stat.ap[:, 12:16], stat.ap[:, 8:12], AF.Sqrt, [stat.T, self.epsT.T], [stat.T],
                         scale=1.0 / 128, bias=self.epsT.ap[:, 0:1])
                self.recip(stat.ap[:, 16:20], stat.ap[:, 12:16], [stat.T], [stat.T])
                for j in range(4):
                    self.stt(ystage.ap[:, qb * 4 + j, :], o1.ap[:, j, :], stat.ap[:, 16 + j:17 + j],
                             sm[:, SM_SUBLN + e * 128:SM_SUBLN + (e + 1) * 128], ALU.mult, ALU.mult,
                             [o1.T, stat.T, self.small.T], [ystage.T])
                tb_ = self.gbank()
                tbf = tb_.ap[:, :].bitcast(BF16)
                self.transposes([(tbf[:, j * 128:(j + 1) * 128], ystage.ap[:, qb * 4 + j, :], self.ident) for j in range(4)],
                                [ystage.T, self.consts.T], [tb_.T])
                self.copy(ytq.ap[:, qsl], tbf[:, 0:512], [tb_.T], [ytq.T], eng="act")
            wslot, wT = self.w_acquire(first + 2 * h + 1)
            self.out_proj_chunk(l, s, ytq, wslot, wT)
            self.w_release(1)
        self.barrier()
        self.even_mixer_b(l, s)
        self.barrier()

    def even_mixer_b(self, l, s):
        e = l // 2
        first = self.piece_index["mix%d" % l][0] + 8
        sm = self.small.ap
        dv = self.derived.ap
        QTb = self.view("QTb", 0, [128, 2, 2048], BF16)
        KTb = self.view("KTb", 8192, [128, 2048], BF16)
        Vb = self.view("Vb", 12288, [128, 16, 2, 66], BF16)
        strips = self.view("stripsB", 16896, [128, 8, 384], F32)
        maskB = self.view("maskB", 29184, [128, 384], F32)
        PT = Ring([self.view("PTb%d" % i, 30720 + 1024 * i, [128, 512], BF16) for i in range(3)])
        tmp = Ring([self.view("tmpB%d" % i, 33792 + 2048 * i, [128, 512], F32) for i in range(4)])
        ystage = self.view("ystageB", 41984, [128, 16, 128], BF16)
        ytq = self.view("ytqB", 46080, [128, 2048], BF16)
        stat = self.view("statB", 50176, [128, 32], F32)
        self.s.dma("sp", [lambda e_: e_.dma_start(out=strips.ap[:, :, :], in_=self.strips_in[4:12, :, :].rearrange("h p c -> p h c")),
                          lambda e_: e_.dma_start(out=maskB.ap[:, :], in_=self.strips_in[12, :, :])],
                   "ld_a", (), [strips.T, maskB.T])
        self.tt(strips.ap[:, :, :], strips.ap[:, :, :], maskB.ap[:, :].unsqueeze(1).to_broadcast([128, 8, 384]), ALU.add,
                [strips.T, maskB.T], [strips.T])
        self.memset(Vb.ap[:, :, :, 64:66], 1.0, [Vb.T])
        slot, sT = self.w_acquire(first)
        for tg in range(4):
            bank = self.gbank()
            items = []
            for tt_ in range(4):
                tok = slice((tg * 4 + tt_) * 128, (tg * 4 + tt_ + 1) * 128)
                for c in range(8):
                    items.append((bank.ap[:, tt_ * 128:(tt_ + 1) * 128], self.hT[:, c, tok],
                                  slot[:, c * 128:(c + 1) * 128], c == 0, c == 7))
            self.mm(items, [sT] + [self.hT_T[c][tg] for c in range(8)], [bank.T])
            self.copy(Vb.ap[:, tg * 4:(tg + 1) * 4, :, 0:64],
                      bank.ap[:, :].rearrange("p (a g d) -> p a g d", g=2, d=64), [bank.T], [Vb.T], eng="act")
        self.w_release(1)
        for g in range(2):
            slot, sT = self.w_acquire(first + 1 + 3 * g)
            for cb in range(2):
                qv = Buf(QTb.ap[:, cb, :], "QTb%d" % cb)
                qv.T = QTb.T
                self.qk_proj_norm(slot, sT, cb * 128, qv, None, sm[:, SM_QKG + e * 4 + 2:SM_QKG + e * 4 + 3], tmp, PT)
            self.qk_proj_norm(slot, sT, 256, KTb, None, sm[:, SM_QKG + e * 4 + 3:SM_QKG + e * 4 + 4], tmp, PT)
            self.w_release(1)
            for cb in range(2):
                for hh in range(2):
                    hq = g * 4 + cb * 2 + hh
                    ph = slice(64 * hh, 64 * hh + 64)
                    Ob = {}
                    for kt in range(16):
                        qt0 = max(kt - 1, 0)
                        qt1 = min(kt + 1, 15)
                        W = (qt1 - qt0 + 1) * 128
                        off = 128 if kt == 0 else 0
                        sbk = self.gbank()
                        self.mm([(sbk.ap[:, 0:W], KTb.ap[ph, kt * 128:(kt + 1) * 128], QTb.ap[ph, cb, qt0 * 128:qt0 * 128 + W],
                                  True, True)], [KTb.T, QTb.T], [sbk.T])
                        self.stt(sbk.ap[:, 0:W], sbk.ap[:, 0:W], SCALE, strips.ap[:, hq, off:off + W], ALU.mult, ALU.add,
                                 [sbk.T, strips.T], [sbk.T])
                        pt = PT.next()
                        self.act(pt.ap[:, 0:W], sbk.ap[:, 0:W], AF.Exp, [sbk.T], [pt.T])
                        items = []
                        for qt in range(qt0, qt1 + 1):
                            if qt not in Ob:
                                Ob[qt] = self.abank()
                            items.append((Ob[qt].ap[:, 0:65], pt.ap[:, (qt - qt0) * 128:(qt - qt0 + 1) * 128],
                                          Vb.ap[:, kt, g, 0:65], kt == max(qt - 1, 0), kt == min(qt + 1, 15)))
                        self.mm(items, [pt.T, Vb.T], [Ob[qt].T for qt in range(qt0, qt1 + 1)])
                        done = [qt for qt in range(qt0, qt1 + 1) if kt == min(qt + 1, 15)]
                        for qt in done:
                            ob = Ob.pop(qt)
                            self.ts(stat.ap[:, 0:1], ob.ap[:, 64:65], dv[:, 2 + e * 8 + hq:3 + e * 8 + hq], None, ALU.add, None,
                                    [ob.T, self.derived.T], [stat.T])
                            self.recip(stat.ap[:, 1:2], stat.ap[:, 0:1], [stat.T], [stat.T])
                            self.ts(ystage.ap[:, qt, hh * 64:(hh + 1) * 64], ob.ap[:, 0:64], stat.ap[:, 1:2], None,
                                    ALU.mult, None, [ob.T, stat.T], [ystage.T])
                for qb in range(4):
                    tb_ = self.gbank()
                    tbf = tb_.ap[:, :].bitcast(BF16)
                    self.transposes([(tbf[:, j * 128:(j + 1) * 128], ystage.ap[:, qb * 4 + j, :], self.ident) for j in range(4)],
                                    [ystage.T, self.consts.T], [tb_.T])
                    self.copy(ytq.ap[:, qb * 512:(qb + 1) * 512], tbf[:, 0:512], [tb_.T], [ytq.T], eng="act")
                wslot, wT = self.w_acquire(first + 2 + 3 * g + cb)
                self.out_proj_chunk(l, s, ytq, wslot, wT)
                self.w_release(1)

    def ada_phase(self):
        self.act(self.scT.ap[:], self.cT.ap[:], AF.Silu, [self.cT.T], [self.scT.T])
        for l in self.layers:
            bank = self.abank()
            for pj in range(N_ADA_PIECES):
                slot, sT = self.w_acquire(self.piece_index["ada%d" % l][0] + pj)
                items = []
                for mcc in range(3):
                    mc = pj * 3 + mcc
                    for dc in range(8):
                        items.append((bank.ap[:, mc * 2:mc * 2 + 2],
                                      slot[:, dc * 384 + mcc * 128: dc * 384 + (mcc + 1) * 128],
                                      self.scT.ap[:, dc, :], dc == 0, dc == 7))
                self.mm(items, [sT, self.scT.T], [bank.T])
                self.w_release(1)
            self.tt(self.modT.ap[:, l, :, :, :].rearrange("p m c s -> p (m c) s"),
                    bank.ap[:, 0:144].rearrange("p (m s) -> p m s", s=2),
                    self.adab.ap[:, l, :].unsqueeze(2).to_broadcast([128, 72, 2]),
                    ALU.add, [bank.T, self.adab.T], [self.modT.T])
            for j in range(3):
                self.stt(self.gmod.ap[:, l, j, :, :], self.modT.ap[:, l, 3 * j + 1, :, :], 1.0,
                         self.normg.ap[:, l, j, :].unsqueeze(2).to_broadcast([128, 8, 2]),
                         ALU.add, ALU.mult, [self.modT.T, self.normg.T], [self.gmod.T])
                self.ts(self.gate.ap[:, l, j, :, :], self.modT.ap[:, l, 3 * j + 2, :, :],
                        (1.0 if j == 1 else 0.5), None, ALU.mult, None, [self.modT.T], [self.gate.T])

    def norm(self, l, j, s):
        for tb in range(4):
            tsl = slice(tb * 512, (tb + 1) * 512)
            xTs = [self.xT_T[c][tb] for c in range(8)]
            sq = self.sq.next()
            self.act(sq.ap[:], self.xT[:, :, tsl], AF.Square, xTs, [sq.T])
            bank = self.gbank()
            self.mmacc(bank.ap[:, :], [(self.ones_bf.ap[:], sq.ap[:, c, :]) for c in range(8)],
                       [sq.T, self.ones_bf.T], [bank.T])
            t0 = self.f32a.next()
            self.act(t0.ap[:], bank.ap[:], AF.Sqrt, [bank.T, self.epsT.T], [t0.T], scale=1.0 / D, bias=self.epsT.ap[:, 0:1])
            rstd = self.rstd.next()
            self.recip(rstd.ap[:], t0.ap[:], [t0.T], [rstd.T])
            for c in range(8):
                t1 = self.f32a.next()
                self.stt(t1.ap[:], self.xT[:, c, tsl], self.gmod.ap[:, l, j, c, s:s + 1], rstd.ap[:],
                         ALU.mult, ALU.mult, [self.xT_T[c][tb], self.gmod.T, rstd.T], [t1.T])
                self.act(self.hT[:, c, tsl], t1.ap[:], AF.Identity, [t1.T, self.modT.T], [self.hT_T[c][tb]],
                         bias=self.modT.ap[:, l, 3 * j, c, s:s + 1], scale=1.0)

    def ffn(self, l, j, s):
        first = self.piece_index["ffn%d_%d" % (l, j)][0]
        jm = 0 if j == 0 else 2
        for (f0, f1) in FGROUPS:
            slots = [self.w_acquire(first + f) for f in range(f0, f1)]
            ng = f1 - f0
            for tb in range(4):
                tsl = slice(tb * 512, (tb + 1) * 512)
                hTs = [self.hT_T[c][tb] for c in range(8)]
                hid = self.hid[self.hid_i % 2]
                self.hid_i += 1
                for fi in range(ng):
                    slot, sT = slots[fi]
                    ba = self.gbank()
                    bb = self.gbank()
                    self.mmacc(ba.ap[:, :], [(slot[:, c * 256:c * 256 + 128], self.hT[:, c, tsl]) for c in range(8)],
                               [sT] + hTs, [ba.T])
                    self.mmacc(bb.ap[:, :], [(slot[:, c * 256 + 128:c * 256 + 256], self.hT[:, c, tsl]) for c in range(8)],
                               [sT] + hTs, [bb.T])
                    sa = self.f32a.next()
                    self.act(sa.ap[:], ba.ap[:], AF.Silu, [ba.T], [sa.T])
                    self.tt(hid.ap[:, fi, :], sa.ap[:], bb.ap[:], ALU.mult, [sa.T, bb.T], [hid.T])
                for c in range(8):
                    by = self.abank()
                    self.mmacc(by.ap[:, :], [(slots[fi][0][:, 2048 + c * 128:2048 + (c + 1) * 128], hid.ap[:, fi, :])
                                             for fi in range(ng)],
                               [hid.T] + [sl[1] for sl in slots], [by.T])
                    self.stt(self.xT[:, c, tsl], by.ap[:], self.gate.ap[:, l, jm, c, s:s + 1], self.xT[:, c, tsl],
                             ALU.mult, ALU.add, [by.T, self.gate.T, self.xT_T[c][tb]], [self.xT_T[c][tb]])
            self.w_release(ng)

    def odd_mixer(self, l, s):
        o = l // 2
        first = self.piece_index["mix%d" % l][0]
        sm = self.small.ap
        dv = self.derived.ap
        qs = self.view("qs", 0, [128, 2048], F32)
        V = self.view("Vo", 8192, [128, 16, 128], BF16)
        gs = self.view("gs", 12288, [128, 2048], BF16)
        qtl = [self.view("qtl%d" % d, 16384 + 4096 * d, [128, 2048], BF16) for d in range(2)]
        ktl = [self.view("ktl%d" % d, 24576 + 4096 * d, [128, 2048], BF16) for d in range(2)]
        khat = [self.view("khat%d" % d, 32768 + 4096 * d, [128, 16, 128], BF16) for d in range(2)]
        lf = self.view("lf", 40960, [128, 512], F32)
        kk = self.view("kk", 43008, [128, 512], F32)
        G = self.view("G", 45056, [128, 512], F32)
        tm = self.view("tm", 47104, [128, 512], F32)
        tm2 = self.view("tm2", 49152, [128, 512], F32)
        khT = self.view("khT", 51200, [128, 512], BF16)
        sig = self.view("sig", 52224, [128, 512], F32)
        scanmask = self.view("scanmask", 54272, [128, 512], F32)
        Shat = [self.view("Shat%d" % d, 40960 + 8192 * d, [128, 32, 128], BF16) for d in range(2)]
        cs = self.view("cs", 57344, [128, 2, 2, 32], F32)
        yq = Ring([self.view("yq%d" % i, 58368 + 1024 * i, [128, 512], BF16) for i in range(2)])
        Am = Ring([self.view("Am%d" % i, 256 * i, [128, 128], BF16) for i in range(4)])
        Sst = [self.view("Sst%d" % d, 1024 + 512 * d, [128, 128], F32) for d in range(2)]
        Ost = self.view("Ost", 2048, [128, 512], F32)
        sqO = self.view("sqO", 4096, [128, 512], BF16)
        rsO = self.view("rsO", 5120, [128, 512], F32)


        def c3(buf):
            return buf.ap[:, :].rearrange("p (c k) -> p c k", k=64)

        for h in range(8):
            p1, p1T = self.w_acquire(first + 3 * h)
            p2, p2T = self.w_acquire(first + 3 * h + 1)
            for tb in range(4):
                tsl = slice(tb * 512, (tb + 1) * 512)
                hTs = [self.hT_T[c][tb] for c in range(8)]
                bq = self.gbank()
                self.mmacc(bq.ap[:, :], [(p1[:, c * 384:c * 384 + 128], self.hT[:, c, tsl]) for c in range(8)], [p1T] + hTs, [bq.T])
                self.act(qs.ap[:, tsl], bq.ap[:], AF.Silu, [bq.T], [qs.T])
                bg = self.gbank()
                self.mmacc(bg.ap[:, :], [(p2[:, c * 256 + 128:c * 256 + 256], self.hT[:, c, tsl]) for c in range(8)], [p2T] + hTs, [bg.T])
                self.act(gs.ap[:, tsl], bg.ap[:], AF.Silu, [bg.T], [gs.T])
            for tg in range(4):
                bank = self.gbank()
                items = []
                for tt_ in range(4):
                    tok = slice((tg * 4 + tt_) * 128, (tg * 4 + tt_ + 1) * 128)
                    for c in range(8):
                        items.append((bank.ap[:, tt_ * 128:(tt_ + 1) * 128], self.hT[:, c, tok],
                                      p2[:, c * 256:c * 256 + 128], c == 0, c == 7))
                self.mm(items, [p2T] + [self.hT_T[c][tg] for c in range(8)], [bank.T])
                self.copy(V.ap[:, tg * 4:(tg + 1) * 4, :], bank.ap[:, :].rearrange("p (a b) -> p a b", b=128),
                          [bank.T], [V.T], eng="act")
            self.memset(scanmask.ap[:, :], 1.0, [scanmask.T])
            self.memset(scanmask.ap[:, :].rearrange("p (c k) -> p c k", k=64)[:, :, 0:1], 0.0, [scanmask.T])
            for d in range(2):
                ci = (d * 2 + o) * 8 + h
                lbc = dv[:, 18 + ci:19 + ci]
                omlc = dv[:, 50 + ci:51 + ci]
                nomlc = dv[:, 82 + ci:83 + ci]
                for tb in range(4):
                    tsl = slice(tb * 512, (tb + 1) * 512)
                    hTs = [self.hT_T[c][tb] for c in range(8)]
                    bf_ = self.gbank()
                    self.mmacc(bf_.ap[:, :], [(p1[:, c * 384 + 128 * (1 + d):c * 384 + 128 * (2 + d)], self.hT[:, c, tsl])
                                              for c in range(8)], [p1T] + hTs, [bf_.T])
                    self.act(sig.ap[:], bf_.ap[:], AF.Sigmoid, [bf_.T], [sig.T])
                    self.act(lf.ap[:], sig.ap[:], AF.Ln, [sig.T, self.derived.T], [lf.T], scale=omlc, bias=lbc)
                    self.ts(kk.ap[:], sig.ap[:], nomlc, omlc, ALU.mult, ALU.add, [sig.T, self.derived.T], [kk.T])
                    self.s.op("dve", lambda e_: e_.tensor_tensor_scan(out=G.ap[:], data0=scanmask.ap[:], data1=lf.ap[:],
                                                                      initial=0.0, op0=ALU.mult, op1=ALU.add),
                              [scanmask.T, lf.T], [G.T])
                    csl = slice(tb * 8, (tb + 1) * 8)
                    if d == 0:
                        R = c3(G)[:, :, 31:32]
                        GL = c3(G)[:, :, 63:64]
                        self.tt(c3(tm), c3(G), R.to_broadcast([128, 8, 64]), ALU.subtract, [G.T], [tm.T])
                        self.tt(c3(tm2), GL.to_broadcast([128, 8, 64]), c3(G), ALU.subtract, [G.T], [tm2.T])
                        self.act(cs.ap[:, 0, 0, csl], c3(G)[:, :, 31], AF.Exp, [G.T], [cs.T])
                        self.act(cs.ap[:, 0, 1, csl], c3(G)[:, :, 63], AF.Exp, [G.T], [cs.T])
                    else:
                        self.act(cs.ap[:, 1, 1, csl], c3(G)[:, :, 63], AF.Exp, [G.T], [cs.T])
                        self.copy(sig.ap[:, 0:8], c3(G)[:, :, 63], [G.T], [sig.T])
                        self.tt(G.ap[:], G.ap[:], lf.ap[:], ALU.subtract, [G.T, lf.T], [G.T])
                        R = c3(G)[:, :, 32:33]
                        self.tt(sig.ap[:, 0:8], sig.ap[:, 0:8], c3(G)[:, :, 32], ALU.subtract, [sig.T, G.T], [sig.T])
                        self.act(cs.ap[:, 1, 0, csl], sig.ap[:, 0:8], AF.Exp, [sig.T], [cs.T])
                        self.tt(c3(tm), R.to_broadcast([128, 8, 64]), c3(G), ALU.subtract, [G.T], [tm.T])
                        self.copy(tm2.ap[:], G.ap[:], [G.T], [tm2.T], eng="pool")
                    self.act(tm.ap[:], tm.ap[:], AF.Exp, [tm.T], [tm.T])
                    self.tt(qtl[d].ap[:, tsl], qs.ap[:, tsl], tm.ap[:], ALU.mult, [qs.T, tm.T], [qtl[d].T])
                    self.recip(tm.ap[:], tm.ap[:], [tm.T], [tm.T])
                    self.tt(ktl[d].ap[:, tsl], kk.ap[:], tm.ap[:], ALU.mult, [kk.T, tm.T], [ktl[d].T])
                    self.act(tm2.ap[:], tm2.ap[:], AF.Exp, [tm2.T], [tm2.T])
                    self.tt(khT.ap[:], kk.ap[:], tm2.ap[:], ALU.mult, [kk.T, tm2.T], [khT.T])
                    tb_ = self.gbank()
                    tbf = tb_.ap[:, :].bitcast(BF16)
                    self.transposes([(tbf[:, j * 128:(j + 1) * 128], khT.ap[:, j * 128:(j + 1) * 128], self.ident) for j in range(4)],
                                    [khT.T, self.consts.T], [tb_.T])
                    self.copy(khat[d].ap[:, tb * 4:(tb + 1) * 4, :], tbf[:, 0:512].rearrange("p (a b) -> p a b", b=128),
                              [tb_.T], [khat[d].T], eng="act")
            self.w_release(2)
            wslot, wT = self.w_acquire(first + 3 * h + 2)
            self.barrier()
            if "odd" in self.debug and h == 0 and s == 0:
                self.dump("derived", self.derived.ap[:], [self.derived.T])
                self.dump("cs", cs.ap[:], [cs.T])
                self.dump("qs", qs.ap[:], [qs.T])
                self.dump("gs", gs.ap[:], [gs.T], BF16)
                self.dump("V", V.ap[:], [V.T], BF16)
                for d in range(2):
                    self.dump("qtl%d" % d, qtl[d].ap[:], [qtl[d].T], BF16)
                    self.dump("ktl%d" % d, ktl[d].ap[:], [ktl[d].T], BF16)
                    self.dump("khat%d" % d, khat[d].ap[:], [khat[d].T], BF16)
                self.dump("lf", lf.ap[:], [lf.T])
                self.dump("kk", kk.ap[:], [kk.T])
                self.dump("G", G.ap[:], [G.T])
                self.barrier()
            for i in range(32):
                for d in range(2):
                    c = i if d == 0 else 31 - i
                    St = Sst[d]
                    rows = slice((c % 2) * 64, (c % 2) * 64 + 64)
                    if i == 0:
                        self.memset(Shat[d].ap[:, c, :], 0.0, [Shat[d].T])
                    else:
                        self.ts(Shat[d].ap[:, c, :], St.ap[:], cs.ap[:, d, 0, c:c + 1], None, ALU.mult, None,
                                [St.T, cs.T], [Shat[d].T])
                    bs = self.gbank()
                    self.mm([(bs.ap[:, 0:128], khat[d].ap[rows, c // 2, :], V.ap[rows, c // 2, :], True, True)],
                            [khat[d].T, V.T], [bs.T])
                    if i == 0:
                        self.copy(St.ap[:], bs.ap[:, 0:128], [bs.T], [St.T])
                    else:
                        self.stt(St.ap[:], St.ap[:], cs.ap[:, d, 1, c:c + 1], bs.ap[:, 0:128], ALU.mult, ALU.add,
                                 [St.T, cs.T, bs.T], [St.T])
            if "odd" in self.debug and h == 0 and s == 0:
                self.barrier()
                for d in range(2):
                    self.dump("Shat%d" % d, Shat[d].ap[:], [Shat[d].T], BF16)
                self.barrier()
            for qb in range(4):
                qsl = slice(qb * 512, (qb + 1) * 512)
                bO = self.abank()
                for blk in range(4):
                    nb = qb * 4 + blk
                    tok = slice(nb * 128, (nb + 1) * 128)
                    ams = []
                    for d in range(2):
                        ba = self.gbank()
                        self.mm([(ba.ap[:, 0:128], ktl[d].ap[:, tok], qtl[d].ap[:, tok], True, True)], [ktl[d].T, qtl[d].T], [ba.T])
                        am = Am.next()
                        self.tt(am.ap[:], ba.ap[:, 0:128], self.mask_f if d == 0 else self.mask_b, ALU.mult,
                                [ba.T, self.consts.T], [am.T])
                        ams.append(am)
                    oc = bO.ap[:, blk * 128:(blk + 1) * 128]
                    items = [(oc, V.ap[:, nb, :], ams[0].ap[:], True, False),
                             (oc, V.ap[:, nb, :], ams[1].ap[:], False, False)]
                    for d in range(2):
                        for hf in range(2):
                            c = 2 * nb + hf
                            items.append((bO.ap[:, blk * 128 + hf * 64:blk * 128 + (hf + 1) * 64], Shat[d].ap[:, c, :],
                                          qtl[d].ap[:, nb * 128 + hf * 64:nb * 128 + (hf + 1) * 64], False, d == 1 and hf == 1))
                    self.mm(items, [V.T, ams[0].T, ams[1].T, Shat[0].T, Shat[1].T, qtl[0].T, qtl[1].T], [bO.T])
                self.act(Ost.ap[:], bO.ap[:], AF.Copy, [bO.T], [Ost.T])
                if "odd" in self.debug and h == 0 and s == 0 and qb == 0:
                    self.dump("Ost", Ost.ap[:], [Ost.T])
                    self.dump("Am", ams[0].ap[:], [ams[0].T], BF16)
                    self.dump("Amb", ams[1].ap[:], [ams[1].T], BF16)
                    self.barrier()
                self.act(sqO.ap[:], bO.ap[:], AF.Square, [bO.T], [sqO.T])
                b2 = self.gbank()
                self.mm([(b2.ap[:, :], self.ones_bf.ap[:], sqO.ap[:], True, True)], [sqO.T, self.ones_bf.T], [b2.T])
                self.act(rsO.ap[:], b2.ap[:], AF.Sqrt, [b2.T, self.epsT.T], [rsO.T], scale=1.0 / 128, bias=self.epsT.ap[:, 0:1])
                self.recip(rsO.ap[:], rsO.ap[:], [rsO.T], [rsO.T])
                self.tt(Ost.ap[:], Ost.ap[:], rsO.ap[:], ALU.mult, [Ost.T, rsO.T], [Ost.T])
                y = yq.next()
                self.stt(y.ap[:], Ost.ap[:], sm[:, SM_OUTG + o:SM_OUTG + o + 1], gs.ap[:, qsl], ALU.mult, ALU.mult,
                         [Ost.T, self.small.T, gs.T], [y.T])
                if "odd" in self.debug and s == 0 and qb in (0, 3):
                    self.dump("y_h%d_q%d" % (h, qb), y.ap[:], [y.T], BF16)
                    self.dump("Ost_h%d_q%d" % (h, qb), Ost.ap[:], [Ost.T])
                    self.barrier()
                for c in range(8):
                    by = self.abank()
                    self.mm([(by.ap[:, :], wslot[:, c * 128:(c + 1) * 128], y.ap[:], True, True)], [wT, y.T], [by.T])
                    self.stt(self.xT[:, c, qsl], by.ap[:], self.gate.ap[:, l, 1, c, s:s + 1], self.xT[:, c, qsl],
                             ALU.mult, ALU.add, [by.T, self.gate.T, self.xT_T[c][qb]], [self.xT_T[c][qb]])
            self.w_release(1)
            self.barrier()
            if "odd" in self.debug and s == 0 and h in (0, 1):
                self.dump("x_after_h%d" % h, self.xT[:, :, :], [t for r_ in self.xT_T for t in r_])
                self.barrier()

    def emit(self):
        nc = self.nc
        keys = sorted(self.s.val.keys())
        for k in keys:
            self.sems[k] = self.stack.enter_context(nc.semaphore(k))
        sems = self.sems
        q = self.s.q

        def run(e, ops):
            for waits, fn, inc in ops:
                for k, v in waits:
                    e.wait_ge(sems[k], v)
                if fn is None:
                    continue
                ins = fn(e)
                ins.then_inc(sems[inc[0]], inc[1])

        with nc.Block() as block:
            @block.tensor
            def _(e):
                run(e, q["pe"])

            @block.scalar
            def _(e):
                run(e, q["act"])

            @block.vector
            def _(e):
                run(e, q["dve"])

            @block.gpsimd
            def _(e):
                run(e, q["pool"])

            @block.sync
            def _(e):
                run(e, q["sp"])
        self.stack.close()


def make_stream(inputs, layers, do_mixer=True, do_ffn=True):
    pieces = []
    index = {}

    def add(name, plist, cols):
        index[name] = (len(pieces), cols)
        pieces.extend(plist)

    for l in layers:
        add("ada%d" % l, ada_pieces(inputs["ada_w"], l), [SLOT] * N_ADA_PIECES)
    for l in layers:
        if do_ffn:
            add("ffn%d_0" % l, ffn_pieces(inputs["ffn_up"], inputs["ffn_down"], l, 0), [SLOT] * NF)
        if do_mixer:
            if l % 2 == 0:
                add("mix%d" % l, even_pieces(inputs["even_w_in"], inputs["even_w_out"], l // 2),
                    EVEN_COLS)
            else:
                add("mix%d" % l, odd_pieces(inputs["odd_w_in"], inputs["odd_w_out"], l // 2),
                    ODD_COLS)
        if do_ffn:
            add("ffn%d_1" % l, ffn_pieces(inputs["ffn_up"], inputs["ffn_down"], l, 1), [SLOT] * NF)
    return np.stack(pieces, axis=0), index


def lay_x(xb):
    return np.ascontiguousarray(xb.reshape(S, 8, 128).transpose(2, 1, 0))


def unlay_x(xt):
    return np.ascontiguousarray(xt.transpose(2, 1, 0).reshape(S, D))


def common_maps(inputs, x_cur, batch_ids_per_core, wstream):
    adab = np.ascontiguousarray(inputs["ada_b"].reshape(DEPTH, 72, 128).transpose(2, 0, 1))
    normg = np.ascontiguousarray(inputs["norm_g"].reshape(DEPTH, 3, 8, 128).transpose(3, 0, 1, 2))
    small = small_inputs(inputs)
    strips = strips_input(inputs)
    consts = consts_input()
    maps = []
    for bids in batch_ids_per_core:
        xin = np.stack([lay_x(x_cur[b]) for b in bids], axis=0)
        cT = np.stack([inputs["c"][b].reshape(8, 128).T for b in bids], axis=2)
        if len(bids) == 1:
            cT = np.concatenate([cT, cT], axis=2)
        maps.append({"wstream": wstream, "x_in": xin, "c_in": np.ascontiguousarray(cT, dtype=np.float32),
                     "adab_in": adab, "normg_in": normg, "small_in": small, "strips_in": strips, "consts_in": consts})
    return maps


def run_layers(inputs, x_cur, layers, batch_ids_per_core, do_mixer=True, do_ffn=True, trace=False, debug=None):
    inputs = {k: np.asarray(v, dtype=np.float32) for k, v in inputs.items()}
    nseq = len(batch_ids_per_core[0])
    wstream, index = make_stream(inputs, layers, do_mixer, do_ffn)
    b = Builder(layers, nseq=nseq, do_mixer=do_mixer, do_ffn=do_ffn, debug=debug)
    nc = b.build(index, wstream.shape[0])
    maps = common_maps(inputs, x_cur, batch_ids_per_core, wstream)
    res = run_bass_kernel_spmd(nc, maps, core_ids=list(range(len(maps))), **({"trace": True} if trace else {}))
    out = np.array(x_cur, dtype=np.float32, copy=True)
    for ci, bids in enumerate(batch_ids_per_core):
        xo = res.results[ci]["x_out"]
        for si, bb in enumerate(bids):
            out[bb] = unlay_x(xo[si])
    return out, res


FUSED = False


def kernel(**inputs):
    inputs = {k: np.asarray(v, dtype=np.float32) for k, v in inputs.items()}
    x = inputs["x"]
    bids = [[2 * i, 2 * i + 1] for i in range(NCORES)]
    if FUSED:
        out, _ = run_layers(inputs, x, list(range(DEPTH)), bids)
    else:
        out = x
        for l in range(DEPTH):
            out, _ = run_layers(inputs, out, [l], bids)
    return out.astype(np.float32)
```

```python
import math
from contextlib import ExitStack
import numpy as np
import concourse.bass as bass
import concourse.mybir as mybir
from concourse.bass_utils import run_bass_kernel_spmd

F32 = mybir.dt.float32
BF16 = mybir.dt.bfloat16
AF = mybir.ActivationFunctionType
ALU = mybir.AluOpType

D = 1024
S = 2048
DEPTH = 4
NCORES = 8
NSEQ = 2
DFF = 2816
NF = 22
EPS = 1e-6
SLOT = 3072
NSLOT = 7
FGROUPS = [(0, 4), (4, 8), (8, 12), (12, 16), (16, 19), (19, 22)]
N_ADA_PIECES = 24
SCALE = 0.125
NEG = -30000.0
ARENA = 60416


def _kin(w):
    return w.reshape(8, 128, -1).transpose(1, 0, 2)


def _pad(a):
    a = np.ascontiguousarray(a, dtype=np.float32).reshape(128, -1)
    out = np.zeros((128, SLOT), np.float32)
    out[:, : a.shape[1]] = a
    return out


def t5_bucket_np(rel):
    half = 16
    max_exact = 8
    n = np.abs(rel)
    nf = np.maximum(n, 1).astype(np.float32)
    large = max_exact + (np.log(nf / max_exact) / math.log(128 / max_exact) * (half - max_exact)).astype(np.int32)
    large = np.minimum(large, half - 1)
    return np.where(rel > 0, half, 0) + np.where(n < max_exact, n, large)


def ada_pieces(ada_w, l):
    w = _kin(ada_w[l])
    return [_pad(w[:, :, j * 384:(j + 1) * 384]) for j in range(N_ADA_PIECES)]


def ffn_pieces(ffn_up, ffn_down, l, j):
    up = _kin(ffn_up[l, j]).reshape(128, 8, 2, NF, 128)
    out = []
    for i in range(NF):
        u = up[:, :, :, i, :].reshape(128, 2048)
        dn = ffn_down[l, j, i * 128:(i + 1) * 128, :]
        out.append(_pad(np.concatenate([u, dn], axis=1)))
    return out


def wout_piece(w_out, chunk):
    return _pad(w_out[chunk * 128:(chunk + 1) * 128, :])


EVEN_COLS = [SLOT, 1024] * 4 + [1024] + [SLOT, 1024, 1024] * 2
ODD_COLS = [SLOT, 2048, 1024] * 8


def even_pieces(even_w_in, even_w_out, e):
    w = _kin(even_w_in[e])
    out = []
    for h in range(4):
        q = w[:, :, h * 128:(h + 1) * 128]
        k = w[:, :, 512 + h * 128:512 + (h + 1) * 128]
        v = w[:, :, 1024 + h * 128:1024 + (h + 1) * 128]
        out.append(_pad(np.concatenate([q, k, v], axis=2)))
        out.append(wout_piece(even_w_out[e], h))
    out.append(_pad(w[:, :, 2176:2304]))
    for g in range(2):
        q = w[:, :, 1536 + g * 256:1536 + (g + 1) * 256]
        k = w[:, :, 2048 + g * 64:2048 + (g + 1) * 64]
        out.append(_pad(np.concatenate([q, k, k], axis=2)))
        out.append(wout_piece(even_w_out[e], 4 + 2 * g))
        out.append(wout_piece(even_w_out[e], 5 + 2 * g))
    return out


def odd_pieces(odd_w_in, odd_w_out, o):
    w = _kin(odd_w_in[o])
    out = []
    for h in range(8):
        sl = slice(h * 128, (h + 1) * 128)
        q, ff, fb, iv, g = (w[:, :, k * 1024:(k + 1) * 1024][:, :, sl] for k in range(5))
        out.append(_pad(np.concatenate([q, ff, fb], axis=2)))
        out.append(_pad(np.concatenate([iv, g], axis=2)))
        out.append(wout_piece(odd_w_out[o], h))
    return out


SM_QKG, SM_DLAM, SM_SUBLN, SM_SINK, SM_CLOHI, SM_CLB, SM_OUTG, SM_W = 0, 8, 520, 776, 792, 816, 880, 882


def small_inputs(inputs):
    sm = np.zeros((128, SM_W), np.float32)
    p = np.arange(128)
    sm[:, SM_QKG:SM_QKG + 8] = inputs["qk_norm_g"][:, :, p % 64].transpose(2, 0, 1).reshape(128, 8)
    sm[:, SM_DLAM:SM_DLAM + 512] = inputs["diff_lambda"].reshape(1, 512)
    sm[:, SM_SUBLN:SM_SUBLN + 256] = inputs["diff_subln_g"].reshape(1, 256)
    sm[:, SM_SINK:SM_SINK + 16] = inputs["sink_logit"].reshape(1, 16)
    sm[:, SM_CLOHI:SM_CLOHI + 24] = inputs["rel_bias"][[15, 31], :].T.reshape(1, 24)
    sm[:, SM_CLB:SM_CLB + 64] = inputs["c_lower_bound"].reshape(2, 4, 8, 128).transpose(3, 0, 1, 2).reshape(128, 64)
    sm[:, SM_OUTG:SM_OUTG + 2] = inputs["c_out_norm_g"].T
    return sm


def strips_input(inputs):
    k = np.arange(128)[:, None]
    q = np.arange(128)[None, :]
    out = np.zeros((13, 128, 384), np.float32)
    for j, d in enumerate((1, 0, -1)):
        idx = t5_bucket_np(k - q + 128 * d)
        out[:12, :, j * 128:(j + 1) * 128] = inputs["rel_bias"][idx].transpose(2, 0, 1)
    out[12, :, 0:128] = np.where(k <= q, 0.0, NEG)
    out[12, :, 256:384] = np.where(k >= q, 0.0, NEG)
    return out


def consts_input():
    c = np.zeros((128, 4, 128), np.float32)
    c[:, 0, :] = np.eye(128)
    pp = np.arange(128)
    c[:, 1, :] = (pp[:, None] // 64 == pp[None, :] // 64)
    same = (pp[:, None] // 64 == pp[None, :] // 64)
    c[:, 2, :] = same & (pp[:, None] <= pp[None, :])
    c[:, 3, :] = same & (pp[:, None] >= pp[None, :])
    return c


class T:
    __slots__ = ("name", "w", "r")

    def __init__(self, name):
        self.name = name
        self.w = None
        self.r = {}


class Sched:
    ENG = ("pe", "act", "dve", "pool", "sp")

    def __init__(self):
        self.q = {e: [] for e in self.ENG}
        self.val = {}
        self.seen = {e: {} for e in self.ENG}

    def _deps(self, eng, reads, writes):
        need = {}

        def add(k, v):
            if v > need.get(k, 0):
                need[k] = v

        for t in reads:
            if t.w is not None:
                add(*t.w)
        for t in writes:
            if t.w is not None:
                add(*t.w)
            for k, v in t.r.items():
                add(k, v)
        waits = []
        seen = self.seen[eng]
        for k, v in need.items():
            if eng == "pe" and k == "c_pe":
                continue
            if seen.get(k, 0) < v:
                waits.append((k, v))
                seen[k] = v
        return waits

    def _mark(self, ev, reads, writes):
        k, v = ev
        for t in reads:
            if t.r.get(k, 0) < v:
                t.r[k] = v
        for t in writes:
            t.w = ev
            t.r = {}

    def op(self, eng, fn, reads=(), writes=()):
        waits = self._deps(eng, reads, writes)
        k = "c_" + eng
        v = self.val.get(k, 0) + 1
        self.val[k] = v
        self.q[eng].append((waits, fn, (k, 1)))
        self._mark((k, v), reads, writes)

    def dma(self, queue, fns, semkey, reads=(), writes=()):
        waits = self._deps(queue, reads, writes)
        for i, fn in enumerate(fns):
            self.val[semkey] = self.val.get(semkey, 0) + 16
            self.q[queue].append((waits if i == 0 else [], fn, (semkey, 16)))
        self._mark((semkey, self.val[semkey]), reads, writes)

    def final_wait(self, eng, semkeys):
        waits = [(k, self.val[k]) for k in semkeys if self.val.get(k, 0) > 0]
        self.q[eng].append((waits, None, None))


class Buf:
    def __init__(self, ap, name):
        self.ap = ap
        self.T = T(name)


class Ring:
    def __init__(self, bufs):
        self.bufs = bufs
        self.i = 0

    def next(self):
        b = self.bufs[self.i % len(self.bufs)]
        self.i += 1
        return b


class Builder:
    def __init__(self, layers, nseq=NSEQ, do_mixer=True, do_ffn=True, debug=None):
        self.debug = debug or set()
        self.layers = list(layers)
        self.nseq = nseq
        self.do_mixer = do_mixer
        self.do_ffn = do_ffn
        self.s = Sched()
        self.sems = {}
        self.nc = bass.Bass("TRN2", target_bir_lowering=False)
        self.stack = ExitStack()

    def sb(self, name, shape, dt=F32):
        t = self.stack.enter_context(self.nc.sbuf_tensor(name, list(shape), dt))
        return t

    def view(self, name, off, shape, dt=F32):
        n = int(np.prod(shape[1:]))
        nbytes = n * (4 if dt == F32 else 2)
        assert off % 4 == 0 and off + nbytes <= ARENA, (name, off, nbytes)
        ap = self.arena[:, off // 4:(off + (nbytes + 3) // 4 * 4) // 4]
        if dt != F32:
            ap = ap.bitcast(dt)
            ap = ap[:, 0:n]
        if len(shape) == 3:
            ap = ap.rearrange("p (a b) -> p a b", b=shape[2])
        elif len(shape) == 4:
            ap = ap.rearrange("p (a b c) -> p a b c", b=shape[2], c=shape[3])
        return Buf(ap, name)

    def barrier(self):
        keys = [k for k in self.s.val if k in ("c_pe", "c_act", "c_dve") or k.startswith("ld_a") or (self.debug and k == "st_x")]
        for eng in ("pe", "act", "dve", "sp"):
            waits = []
            for k in keys:
                v = self.s.val[k]
                if eng == "pe" and k == "c_pe":
                    continue
                if self.s.seen[eng].get(k, 0) < v:
                    waits.append((k, v))
                    self.s.seen[eng][k] = v
            if waits:
                self.s.q[eng].append((waits, None, None))

    def buf(self, name, shape, dt=F32):
        t = self.sb(name, shape, dt)
        return Buf(t, name)

    def act(self, out, in_, func, reads, writes, **kw):
        self.s.op("act", lambda e: e.activation(out=out, in_=in_, func=func, **kw), reads, writes)

    def tt(self, out, in0, in1, op, reads, writes, eng="dve"):
        self.s.op(eng, lambda e: e.tensor_tensor(out=out, in0=in0, in1=in1, op=op), reads, writes)

    def ts(self, out, in0, s1, s2, op0, op1, reads, writes, eng="dve"):
        if op1 is None:
            self.s.op(eng, lambda e: e.tensor_scalar(out=out, in0=in0, scalar1=s1, scalar2=None, op0=op0), reads, writes)
        else:
            self.s.op(eng, lambda e: e.tensor_scalar(out=out, in0=in0, scalar1=s1, scalar2=s2, op0=op0, op1=op1), reads, writes)

    def stt(self, out, in0, scalar, in1, op0, op1, reads, writes):
        self.s.op("dve", lambda e: e.scalar_tensor_tensor(out=out, in0=in0, scalar=scalar, in1=in1, op0=op0, op1=op1), reads, writes)

    def copy(self, out, in_, reads, writes, eng="dve"):
        if eng == "act":
            self.s.op("act", lambda e: e.activation(out=out, in_=in_, func=AF.Copy), reads, writes)
        else:
            self.s.op(eng, lambda e: e.tensor_copy(out=out, in_=in_), reads, writes)

    def recip(self, out, in_, reads, writes):
        self.s.op("dve", lambda e: e.reciprocal(out=out, in_=in_), reads, writes)

    def memset(self, ap, val, writes, eng="dve"):
        self.s.op(eng, lambda e: e.memset(ap, val), (), writes)

    def mm(self, items, reads, writes):
        items = list(items)

        def fn(e):
            ins = None
            for (o, l, r, st, sp) in items:
                ins = e.matmul(o, lhsT=l, rhs=r, start=st, stop=sp)
            return ins

        self.s.op("pe", fn, reads, writes)

    def mmacc(self, out, pairs, reads, writes):
        n = len(pairs)
        self.mm([(out, l, r, i == 0, i == n - 1) for i, (l, r) in enumerate(pairs)], reads, writes)

    def transposes(self, items, reads, writes):
        items = list(items)

        def fn(e):
            ins = None
            for (o, i_, ident) in items:
                ins = e.transpose(o, i_, ident)
            return ins

        self.s.op("pe", fn, reads, writes)

    def dump(self, name, ap, reads, dt=F32):
        shape = list(ap.shape)
        d = self.nc.dram_tensor("dbg_" + name, shape, dt, kind="ExternalOutput").ap()
        self.s.dma("sp", [lambda e: e.dma_start(out=d, in_=ap)], "st_x", reads, ())

    def load(self, out, in_, semkey, writes, queue="sp"):
        self.s.dma(queue, [lambda e: e.dma_start(out=out, in_=in_)], semkey, (), writes)

    def gbank(self):
        b = self.gb[self.gbi % 4]
        self.gbi += 1
        return b

    def abank(self):
        b = self.ab[self.abi % 4]
        self.abi += 1
        return b

    def w_prefetch(self):
        while self.w_free > 0 and self.w_next_load < len(self.w_plan):
            k = self.w_next_load
            idx, ncols = self.w_plan[k]
            slot = k % NSLOT
            out = self.ring[:, slot, 0:ncols]
            in_ = self.wstream[idx, :, 0:ncols]
            self.s.dma("pool", [lambda e, o=out, i=in_: e.dma_start(out=o, in_=i)], "w%d" % slot, (), [self.slotT[slot]])
            self.w_next_load += 1
            self.w_free -= 1

    def w_acquire(self, expect_idx=None):
        k = self.w_next_use
        assert k < self.w_next_load, "weight stream underflow"
        if expect_idx is not None:
            assert self.w_plan[k][0] == expect_idx, (k, self.w_plan[k], expect_idx)
        self.w_next_use += 1
        slot = k % NSLOT
        return self.ring[:, slot, :], self.slotT[slot]

    def w_release(self, n=1):
        self.w_free += n
        self.w_prefetch()

    def build(self, piece_index, n_pieces):
        nc = self.nc
        st = self.stack
        L = self.layers
        NL = len(L)
        self.wstream = nc.dram_tensor("wstream", [n_pieces, 128, SLOT], F32, kind="ExternalInput").ap()
        self.x_in = nc.dram_tensor("x_in", [self.nseq, 128, 8, S], F32, kind="ExternalInput").ap()
        self.x_out = nc.dram_tensor("x_out", [self.nseq, 128, 8, S], F32, kind="ExternalOutput").ap()
        self.c_in = nc.dram_tensor("c_in", [128, 8, 2], F32, kind="ExternalInput").ap()
        self.adab_in = nc.dram_tensor("adab_in", [128, DEPTH, 72], F32, kind="ExternalInput").ap()
        self.normg_in = nc.dram_tensor("normg_in", [128, DEPTH, 3, 8], F32, kind="ExternalInput").ap()
        self.small_in = nc.dram_tensor("small_in", [128, SM_W], F32, kind="ExternalInput").ap()
        self.strips_in = nc.dram_tensor("strips_in", [13, 128, 384], F32, kind="ExternalInput").ap()
        self.consts_in = nc.dram_tensor("consts_in", [128, 4, 128], F32, kind="ExternalInput").ap()

        self.xT = self.sb("xT", [128, 8, S], F32)
        self.xT_T = [[T("x%d_%d" % (c, b)) for b in range(4)] for c in range(8)]
        self.hT = self.sb("hT", [128, 8, S], BF16)
        self.hT_T = [[T("h%d_%d" % (c, b)) for b in range(4)] for c in range(8)]
        self.ring = self.sb("ring", [128, NSLOT, SLOT], BF16)
        self.slotT = [T("slot%d" % i) for i in range(NSLOT)]
        self.arena = self.sb("arena", [128, ARENA // 4], F32)
        self.arenaT = T("arena")
        self.hid = [self.view("hid%d" % i, 8192 + 4096 * i, [128, 4, 512], BF16) for i in range(2)]
        self.hid_i = 0
        self.sq = Ring([self.view("sq0", 0, [128, 8, 512], BF16)])
        self.f32a = Ring([self.view("f32a%d" % i, 16384 + 2048 * i, [128, 512], F32) for i in range(4)])
        self.rstd = Ring([self.view("rstd%d" % i, 24576 + 2048 * i, [128, 512], F32) for i in range(2)])
        self.modT = self.buf("modT", [128, DEPTH, 9, 8, 2], F32)
        self.gmod = self.buf("gmod", [128, DEPTH, 3, 8, 2], F32)
        self.gate = self.buf("gate", [128, DEPTH, 3, 8, 2], F32)
        self.cT = self.buf("cT", [128, 8, 2], F32)
        self.scT = self.buf("scT", [128, 8, 2], BF16)
        self.adab = self.buf("adab", [128, DEPTH, 72], F32)
        self.normg = self.buf("normg", [128, DEPTH, 3, 8], F32)
        self.small = self.buf("small", [128, SM_W], F32)
        self.consts = self.buf("consts", [128, 4, 128], BF16)
        self.derived = self.buf("derived", [128, 128], F32)
        self.ones_bf = self.buf("ones_bf", [128, 128], BF16)
        self.epsT = self.buf("epsT", [128, 1], F32)

        banks = [st.enter_context(nc.psum_tensor("bank%d" % i, [128, 512], F32)) for i in range(8)]
        self.gb = [Buf(banks[i], "gb%d" % i) for i in range(4)]
        self.ab = [Buf(banks[4 + i], "ab%d" % i) for i in range(4)]
        self.gbi = 0
        self.abi = 0

        plan = []
        for l in L:
            first, cols = piece_index["ada%d" % l]
            plan += [(first + i, c) for i, c in enumerate(cols)]
        per_seq = []
        for l in L:
            for nm in ("ffn%d_0" % l, "mix%d" % l, "ffn%d_1" % l):
                if nm.startswith("mix") and not self.do_mixer:
                    continue
                if nm.startswith("ffn") and not self.do_ffn:
                    continue
                first, cols = piece_index[nm]
                per_seq += [(first + i, c) for i, c in enumerate(cols)]
        for _ in range(self.nseq):
            plan += per_seq
        self.w_plan = plan
        self.w_next_load = 0
        self.w_next_use = 0
        self.w_free = NSLOT
        self.piece_index = piece_index

        self.load(self.cT.ap[:], self.c_in[:, :, :], "ld_small", [self.cT.T])
        self.load(self.adab.ap[:], self.adab_in[:, :, :], "ld_small", [self.adab.T])
        self.load(self.normg.ap[:], self.normg_in[:, :, :, :], "ld_small", [self.normg.T])
        self.load(self.small.ap[:], self.small_in[:, :], "ld_small", [self.small.T])
        self.memset(self.ones_bf.ap[:], 1.0, [self.ones_bf.T])
        self.memset(self.epsT.ap[:], EPS, [self.epsT.T])
        self.w_prefetch()
        self.setup_extra()
        self.ada_phase()

        for s in range(self.nseq):
            for c in range(8):
                self.s.dma("sp", [lambda e, c=c, s=s: e.dma_start(out=self.xT[:, c, :], in_=self.x_in[s, :, c, :])],
                           "ld_x", (), self.xT_T[c])
            for l in L:
                if self.do_ffn:
                    self.barrier()
                    self.norm(l, 0, s)
                    if "h0" in self.debug and s == 0 and l == L[0]:
                        self.dump("h0", self.hT[:, :, :], [t for r in self.hT_T for t in r], BF16)
                        self.dump("modT", self.modT.ap[:], [self.modT.T])
                        self.dump("gmod", self.gmod.ap[:], [self.gmod.T])
                        self.dump("gate", self.gate.ap[:], [self.gate.T])
                    self.ffn(l, 0, s)
                    if "x1" in self.debug and s == 0 and l == L[0]:
                        self.dump("x1", self.xT[:, :, :], [t for r in self.xT_T for t in r])
                if self.do_mixer:
                    self.barrier()
                    self.norm(l, 1, s)
                    self.barrier()
                    if l % 2 == 0:
                        self.even_mixer(l, s)
                    else:
                        self.odd_mixer(l, s)
                if self.do_ffn:
                    self.barrier()
                    self.norm(l, 2, s)
                    self.ffn(l, 1, s)
            for c in range(8):
                self.s.dma("sp", [lambda e, c=c, s=s: e.dma_start(out=self.x_out[s, :, c, :], in_=self.xT[:, c, :])],
                           "st_x", self.xT_T[c], ())
        self.s.final_wait("sp", ["st_x"])
        assert self.w_next_use == len(self.w_plan), (self.w_next_use, len(self.w_plan))
        self.emit()
        return nc

    def setup_extra(self):
        AX = mybir.AxisListType.X
        ctmp = self.view("ctmp", 0, [128, 4, 128], F32)
        self.load(ctmp.ap[:], self.consts_in[:, :, :], "ld_a", [ctmp.T])
        self.copy(self.consts.ap[:], ctmp.ap[:], [ctmp.T], [self.consts.T])
        self.ident = self.consts.ap[:, 0, :]
        self.blk64 = self.consts.ap[:, 1, :]
        self.mask_f = self.consts.ap[:, 2, :]
        self.mask_b = self.consts.ap[:, 3, :]
        sm = self.small.ap
        smT = self.small.T
        dv = self.derived.ap
        dT = self.derived.T
        t = self.view("setup_t", 4096, [128, 512], F32)
        for e in range(2):
            lam_init = 0.8 - 0.6 * math.exp(-0.3 * (2 * e))
            base = SM_DLAM + e * 256
            self.tt(t.ap[:, 0:64], sm[:, base:base + 64], sm[:, base + 64:base + 128], ALU.mult, [smT], [t.T])
            self.tt(t.ap[:, 64:128], sm[:, base + 128:base + 192], sm[:, base + 192:base + 256], ALU.mult, [smT], [t.T])
            self.s.op("dve", lambda e_: e_.tensor_reduce(out=t.ap[:, 128:130],
                                                         in_=t.ap[:, 0:128].rearrange("p (a b) -> p a b", b=64),
                                                         axis=AX, op=ALU.add), [t.T], [t.T])
            self.act(t.ap[:, 130:132], t.ap[:, 128:130], AF.Exp, [t.T], [t.T])
            self.tt(t.ap[:, 132:133], t.ap[:, 131:132], t.ap[:, 130:131], ALU.subtract, [t.T], [t.T])
            self.ts(dv[:, e:e + 1], t.ap[:, 132:133], -lam_init, None, ALU.add, None, [t.T], [dT])
            sb_ = SM_SUBLN + e * 128
            self.ts(sm[:, sb_:sb_ + 128], sm[:, sb_:sb_ + 128], 1.0 - lam_init, None, ALU.mult, None, [smT], [smT])
        self.act(dv[:, 2:18], sm[:, SM_SINK:SM_SINK + 16], AF.Exp, [smT], [dT])
        ex = t.ap[:, 256:320].rearrange("p (d l h) -> p d l h", d=2, l=4)
        self.act(t.ap[:, 256:320], sm[:, SM_CLB:SM_CLB + 64], AF.Exp, [smT], [t.T])
        ssum = t.ap[:, 320:336].rearrange("p (d h) -> p d h", d=2)
        self.s.op("dve", lambda e_: e_.tensor_reduce(out=ssum, in_=t.ap[:, 256:320].rearrange("p (d l h) -> p d h l", d=2, l=4),
                                                     axis=AX, op=ALU.add), [t.T], [t.T])
        rs = t.ap[:, 336:352].rearrange("p (d h) -> p d h", d=2)
        self.recip(t.ap[:, 336:352], t.ap[:, 320:336], [t.T], [t.T])
        e23 = t.ap[:, 352:368].rearrange("p (d h) -> p d h", d=2)
        self.tt(e23, ex[:, :, 2, :], ex[:, :, 3, :], ALU.add, [t.T], [t.T])
        self.tt(e23, e23, ex[:, :, 1, :], ALU.add, [t.T], [t.T])
        lbv = dv[:, 18:50].rearrange("p (d o h) -> p d o h", d=2, o=2)
        self.tt(lbv[:, :, 0, :], ex[:, :, 1, :], rs, ALU.mult, [t.T], [dT])
        self.tt(lbv[:, :, 1, :], e23, rs, ALU.mult, [t.T], [dT])
        self.ts(dv[:, 50:82], dv[:, 18:50], -1.0, 1.0, ALU.mult, ALU.add, [dT], [dT])
        self.ts(dv[:, 82:114], dv[:, 18:50], -1.0, None, ALU.add, None, [dT], [dT])
        self.barrier()

    def out_proj_chunk(self, l, s, ytq, wslot, wT):
        for tb in range(4):
            tsl = slice(tb * 512, (tb + 1) * 512)
            for c in range(8):
                by = self.abank()
                self.mm([(by.ap[:, :], wslot[:, c * 128:(c + 1) * 128], ytq.ap[:, tsl], True, True)], [wT, ytq.T], [by.T])
                self.stt(self.xT[:, c, tsl], by.ap[:], self.gate.ap[:, l, 1, c, s:s + 1], self.xT[:, c, tsl],
                         ALU.mult, ALU.add, [by.T, self.gate.T, self.xT_T[c][tb]], [self.xT_T[c][tb]])

    def qk_proj_norm(self, slot, sT, col0, dst, dst_cols, gcol, tmp, sqr):
        for tb in range(4):
            tsl = slice(tb * 512, (tb + 1) * 512)
            hTs = [self.hT_T[c][tb] for c in range(8)]
            bank = self.gbank()
            self.mmacc(bank.ap[:, :], [(slot[:, c * 384 + col0:c * 384 + col0 + 128], self.hT[:, c, tsl]) for c in range(8)],
                       [sT] + hTs, [bank.T])
            raw = tmp.next()
            self.act(raw.ap[:], bank.ap[:], AF.Copy, [bank.T], [raw.T])
            sqb = sqr.next()
            self.act(sqb.ap[:], bank.ap[:], AF.Square, [bank.T], [sqb.T])
            b2 = self.gbank()
            self.mm([(b2.ap[:, :], self.blk64, sqb.ap[:], True, True)], [sqb.T, self.consts.T], [b2.T])
            t0 = tmp.next()
            self.act(t0.ap[:], b2.ap[:], AF.Sqrt, [b2.T, self.epsT.T], [t0.T], scale=1.0 / 64, bias=self.epsT.ap[:, 0:1])
            rs = tmp.next()
            self.recip(rs.ap[:], t0.ap[:], [t0.T], [rs.T])
            self.stt(dst.ap[:, dst_cols(tsl)] if callable(dst_cols) else dst.ap[:, tsl], raw.ap[:], gcol, rs.ap[:],
                     ALU.mult, ALU.mult, [raw.T, rs.T, self.small.T], [dst.T])

    def even_mixer(self, l, s):
        AX = mybir.AxisListType.X
        e = l // 2
        first = self.piece_index["mix%d" % l][0]
        sm = self.small.ap
        dv = self.derived.ap
        QT = self.view("QT", 0, [128, 2048], BF16)
        KT = self.view("KT", 4096, [128, 2048], BF16)
        Va = self.view("Va", 8192, [128, 16, 130], BF16)
        strips = self.view("stripsA", 13312, [128, 4, 384], F32)
        PT = Ring([self.view("PT%d" % i, 19456 + 1024 * i, [128, 512], BF16) for i in range(3)])
        o1 = self.view("o1", 22528, [128, 4, 128], F32)
        o2 = self.view("o2", 24576, [128, 4, 128], F32)
        tmp = Ring([self.view("tmpA%d" % i, 26624 + 2048 * i, [128, 512], F32) for i in range(4)])
        ystage = self.view("ystage", 34816, [128, 16, 128], BF16)
        ytq = self.view("ytq", 38912, [128, 2048], BF16)
        stat = self.view("stat", 43008, [128, 32], F32)
        self.s.dma("sp", [lambda e_: e_.dma_start(out=strips.ap[:, :, :], in_=self.strips_in[0:4, :, :].rearrange("h p c -> p h c"))],
                   "ld_a", (), [strips.T])
        self.memset(Va.ap[:, :, 128:130], 1.0, [Va.T])
        for h in range(4):
            slot, sT = self.w_acquire(first + 2 * h)
            self.qk_proj_norm(slot, sT, 0, QT, None, sm[:, SM_QKG + e * 4 + 0:SM_QKG + e * 4 + 1], tmp, PT)
            self.qk_proj_norm(slot, sT, 128, KT, None, sm[:, SM_QKG + e * 4 + 1:SM_QKG + e * 4 + 2], tmp, PT)
            for tg in range(4):
                bank = self.gbank()
                items = []
                for tt_ in range(4):
                    tok = slice((tg * 4 + tt_) * 128, (tg * 4 + tt_ + 1) * 128)
                    for c in range(8):
                        items.append((bank.ap[:, tt_ * 128:(tt_ + 1) * 128], self.hT[:, c, tok],
                                      slot[:, c * 384 + 256:c * 384 + 384], c == 0, c == 7))
                self.mm(items, [sT] + [self.hT_T[c][tg] for c in range(8)], [bank.T])
                self.copy(Va.ap[:, tg * 4:(tg + 1) * 4, 0:128], bank.ap[:, :].rearrange("p (a b) -> p a b", b=128),
                          [bank.T], [Va.T], eng="act")
            self.w_release(1)
            clo = sm[:, SM_CLOHI + 2 * h:SM_CLOHI + 2 * h + 1]
            chi = sm[:, SM_CLOHI + 2 * h + 1:SM_CLOHI + 2 * h + 2]
            for qb in range(4):
                qsl = slice(qb * 512, (qb + 1) * 512)
                for sidx in range(2):
                    ph = slice(64 * sidx, 64 * sidx + 64)
                    osb = o1 if sidx == 0 else o2
                    Ob = [self.abank() for _ in range(4)]

                    def s_mm(kt):
                        b = self.gbank()
                        self.mm([(b.ap[:, :], KT.ap[ph, kt * 128:(kt + 1) * 128], QT.ap[ph, qsl], True, True)],
                                [KT.T, QT.T], [b.T])
                        return b

                    nxt = s_mm(0)
                    for kt in range(16):
                        sbk = nxt
                        if kt < 15:
                            nxt = s_mm(kt + 1)
                        pt = PT.next()
                        ee = kt - 4 * qb
                        ds = [ee - j for j in range(4)]
                        near = [j for j in range(4) if abs(ds[j]) <= 1]
                        if not near:
                            bias = clo if ds[0] < 0 else chi
                            self.act(pt.ap[:], sbk.ap[:], AF.Exp, [sbk.T, self.small.T], [pt.T], scale=SCALE, bias=bias)
                        else:
                            j0, j1 = near[0], near[-1]
                            c0 = (1 - ds[j0]) * 128
                            w_ = (j1 - j0 + 1) * 128
                            self.stt(sbk.ap[:, j0 * 128:j0 * 128 + w_], sbk.ap[:, j0 * 128:j0 * 128 + w_], SCALE,
                                     strips.ap[:, h, c0:c0 + w_], ALU.mult, ALU.add, [sbk.T, strips.T], [sbk.T])
                            self.act(pt.ap[:, j0 * 128:j0 * 128 + w_], sbk.ap[:, j0 * 128:j0 * 128 + w_], AF.Exp,
                                     [sbk.T], [pt.T])
                            if j0 > 0:
                                self.act(pt.ap[:, 0:j0 * 128], sbk.ap[:, 0:j0 * 128], AF.Exp, [sbk.T, self.small.T], [pt.T],
                                         scale=SCALE, bias=chi)
                            if j1 < 3:
                                self.act(pt.ap[:, (j1 + 1) * 128:512], sbk.ap[:, (j1 + 1) * 128:512], AF.Exp,
                                         [sbk.T, self.small.T], [pt.T], scale=SCALE, bias=clo)
                        self.mm([(Ob[j].ap[:, 0:129], pt.ap[:, j * 128:(j + 1) * 128], Va.ap[:, kt, 0:129], kt == 0, kt == 15)
                                 for j in range(4)], [pt.T, Va.T], [b.T for b in Ob])
                    for j in range(4):
                        self.recip(stat.ap[:, j:j + 1], Ob[j].ap[:, 128:129], [Ob[j].T], [stat.T])
                        self.ts(osb.ap[:, j, :], Ob[j].ap[:, 0:128], stat.ap[:, j:j + 1], None, ALU.mult, None,
                                [Ob[j].T, stat.T], [osb.T])
                o1f = o1.ap[:, :, :].rearrange("p a b -> p (a b)")
                o2f = o2.ap[:, :, :].rearrange("p a b -> p (a b)")
                self.stt(o1f, o2f, dv[:, e:e + 1], o1f, ALU.mult, ALU.add, [o1.T, o2.T, self.derived.T], [o1.T])
                self.tt(o2f, o1f, o1f, ALU.mult, [o1.T], [o2.T])
                self.s.op("dve", lambda e_: e_.tensor_reduce(out=stat.ap[:, 8:12], in_=o2.ap[:, :, :], axis=AX, op=ALU.add),
                          [o2.T], [stat.T])
                self.act(stat.ap[:, 12:16], stat.ap[:, 8:12], AF.Sqrt, [stat.T, self.epsT.T], [stat.T],
                         scale=1.0 / 128, bias=self.epsT.ap[:, 0:1])
                self.recip(stat.ap[:, 16:20], stat.ap[:, 12:16], [stat.T], [stat.T])
                for j in range(4):
                    self.stt(ystage.ap[:, qb * 4 + j, :], o1.ap[:, j, :], stat.ap[:, 16 + j:17 + j],
                             sm[:, SM_SUBLN + e * 128:SM_SUBLN + (e + 1) * 128], ALU.mult, ALU.mult,
                             [o1.T, stat.T, self.small.T], [ystage.T])
                tb_ = self.gbank()
                tbf = tb_.ap[:, :].bitcast(BF16)
                self.transposes([(tbf[:, j * 128:(j + 1) * 128], ystage.ap[:, qb * 4 + j, :], self.ident) for j in range(4)],
                                [ystage.T, self.consts.T], [tb_.T])
                self.copy(ytq.ap[:, qsl], tbf[:, 0:512], [tb_.T], [ytq.T], eng="act")
            wslot, wT = self.w_acquire(first + 2 * h + 1)
            self.out_proj_chunk(l, s, ytq, wslot, wT)
            self.w_release(1)
        self.barrier()
        self.even_mixer_b(l, s)
        self.barrier()

    def even_mixer_b(self, l, s):
        e = l // 2
        first = self.piece_index["mix%d" % l][0] + 8
        sm = self.small.ap
        dv = self.derived.ap
        QTb = self.view("QTb", 0, [128, 2, 2048], BF16)
        KTb = self.view("KTb", 8192, [128, 2048], BF16)
        Vb = self.view("Vb", 12288, [128, 16, 2, 66], BF16)
        strips = self.view("stripsB", 16896, [128, 8, 384], F32)
        maskB = self.view("maskB", 29184, [128, 384], F32)
        PT = Ring([self.view("PTb%d" % i, 30720 + 1024 * i, [128, 512], BF16) for i in range(3)])
        tmp = Ring([self.view("tmpB%d" % i, 33792 + 2048 * i, [128, 512], F32) for i in range(4)])
        ystage = self.view("ystageB", 41984, [128, 16, 128], BF16)
        ytq = self.view("ytqB", 46080, [128, 2048], BF16)
        stat = self.view("statB", 50176, [128, 32], F32)
        self.s.dma("sp", [lambda e_: e_.dma_start(out=strips.ap[:, :, :], in_=self.strips_in[4:12, :, :].rearrange("h p c -> p h c")),
                          lambda e_: e_.dma_start(out=maskB.ap[:, :], in_=self.strips_in[12, :, :])],
                   "ld_a", (), [strips.T, maskB.T])
        self.tt(strips.ap[:, :, :], strips.ap[:, :, :], maskB.ap[:, :].unsqueeze(1).to_broadcast([128, 8, 384]), ALU.add,
                [strips.T, maskB.T], [strips.T])
        self.memset(Vb.ap[:, :, :, 64:66], 1.0, [Vb.T])
        slot, sT = self.w_acquire(first)
        for tg in range(4):
            bank = self.gbank()
            items = []
            for tt_ in range(4):
                tok = slice((tg * 4 + tt_) * 128, (tg * 4 + tt_ + 1) * 128)
                for c in range(8):
                    items.append((bank.ap[:, tt_ * 128:(tt_ + 1) * 128], self.hT[:, c, tok],
                                  slot[:, c * 128:(c + 1) * 128], c == 0, c == 7))
            self.mm(items, [sT] + [self.hT_T[c][tg] for c in range(8)], [bank.T])
            self.copy(Vb.ap[:, tg * 4:(tg + 1) * 4, :, 0:64],
                      bank.ap[:, :].rearrange("p (a g d) -> p a g d", g=2, d=64), [bank.T], [Vb.T], eng="act")
        self.w_release(1)
        for g in range(2):
            slot, sT = self.w_acquire(first + 1 + 3 * g)
            for cb in range(2):
                qv = Buf(QTb.ap[:, cb, :], "QTb%d" % cb)
                qv.T = QTb.T
                self.qk_proj_norm(slot, sT, cb * 128, qv, None, sm[:, SM_QKG + e * 4 + 2:SM_QKG + e * 4 + 3], tmp, PT)
            self.qk_proj_norm(slot, sT, 256, KTb, None, sm[:, SM_QKG + e * 4 + 3:SM_QKG + e * 4 + 4], tmp, PT)
            self.w_release(1)
            for cb in range(2):
                for hh in range(2):
                    hq = g * 4 + cb * 2 + hh
                    ph = slice(64 * hh, 64 * hh + 64)
                    Ob = {}
                    for kt in range(16):
                        qt0 = max(kt - 1, 0)
                        qt1 = min(kt + 1, 15)
                        W = (qt1 - qt0 + 1) * 128
                        off = 128 if kt == 0 else 0
                        sbk = self.gbank()
                        self.mm([(sbk.ap[:, 0:W], KTb.ap[ph, kt * 128:(kt + 1) * 128], QTb.ap[ph, cb, qt0 * 128:qt0 * 128 + W],
                                  True, True)], [KTb.T, QTb.T], [sbk.T])
                        self.stt(sbk.ap[:, 0:W], sbk.ap[:, 0:W], SCALE, strips.ap[:, hq, off:off + W], ALU.mult, ALU.add,
                                 [sbk.T, strips.T], [sbk.T])
                        pt = PT.next()
                        self.act(pt.ap[:, 0:W], sbk.ap[:, 0:W], AF.Exp, [sbk.T], [pt.T])
                        items = []
                        for qt in range(qt0, qt1 + 1):
                            if qt not in Ob:
                                Ob[qt] = self.abank()
                            items.append((Ob[qt].ap[:, 0:65], pt.ap[:, (qt - qt0) * 128:(qt - qt0 + 1) * 128],
                                          Vb.ap[:, kt, g, 0:65], kt == max(qt - 1, 0), kt == min(qt + 1, 15)))
                        self.mm(items, [pt.T, Vb.T], [Ob[qt].T for qt in range(qt0, qt1 + 1)])
                        done = [qt for qt in range(qt0, qt1 + 1) if kt == min(qt + 1, 15)]
                        for qt in done:
                            ob = Ob.pop(qt)
                            self.ts(stat.ap[:, 0:1], ob.ap[:, 64:65], dv[:, 2 + e * 8 + hq:3 + e * 8 + hq], None, ALU.add, None,
                                    [ob.T, self.derived.T], [stat.T])
                            self.recip(stat.ap[:, 1:2], stat.ap[:, 0:1], [stat.T], [stat.T])
                            self.ts(ystage.ap[:, qt, hh * 64:(hh + 1) * 64], ob.ap[:, 0:64], stat.ap[:, 1:2], None,
                                    ALU.mult, None, [ob.T, stat.T], [ystage.T])
                for qb in range(4):
                    tb_ = self.gbank()
                    tbf = tb_.ap[:, :].bitcast(BF16)
                    self.transposes([(tbf[:, j * 128:(j + 1) * 128], ystage.ap[:, qb * 4 + j, :], self.ident) for j in range(4)],
                                    [ystage.T, self.consts.T], [tb_.T])
                    self.copy(ytq.ap[:, qb * 512:(qb + 1) * 512], tbf[:, 0:512], [tb_.T], [ytq.T], eng="act")
                wslot, wT = self.w_acquire(first + 2 + 3 * g + cb)
                self.out_proj_chunk(l, s, ytq, wslot, wT)
                self.w_release(1)

    def ada_phase(self):
        self.act(self.scT.ap[:], self.cT.ap[:], AF.Silu, [self.cT.T], [self.scT.T])
        for l in self.layers:
            bank = self.abank()
            for pj in range(N_ADA_PIECES):
                slot, sT = self.w_acquire(self.piece_index["ada%d" % l][0] + pj)
                items = []
                for mcc in range(3):
                    mc = pj * 3 + mcc
                    for dc in range(8):
                        items.append((bank.ap[:, mc * 2:mc * 2 + 2],
                                      slot[:, dc * 384 + mcc * 128: dc * 384 + (mcc + 1) * 128],
                                      self.scT.ap[:, dc, :], dc == 0, dc == 7))
                self.mm(items, [sT, self.scT.T], [bank.T])
                self.w_release(1)
            self.tt(self.modT.ap[:, l, :, :, :].rearrange("p m c s -> p (m c) s"),
                    bank.ap[:, 0:144].rearrange("p (m s) -> p m s", s=2),
                    self.adab.ap[:, l, :].unsqueeze(2).to_broadcast([128, 72, 2]),
                    ALU.add, [bank.T, self.adab.T], [self.modT.T])
            for j in range(3):
                self.stt(self.gmod.ap[:, l, j, :, :], self.modT.ap[:, l, 3 * j + 1, :, :], 1.0,
                         self.normg.ap[:, l, j, :].unsqueeze(2).to_broadcast([128, 8, 2]),
                         ALU.add, ALU.mult, [self.modT.T, self.normg.T], [self.gmod.T])
                self.ts(self.gate.ap[:, l, j, :, :], self.modT.ap[:, l, 3 * j + 2, :, :],
                        (1.0 if j == 1 else 0.5), None, ALU.mult, None, [self.modT.T], [self.gate.T])

    def norm(self, l, j, s):
        for tb in range(4):
            tsl = slice(tb * 512, (tb + 1) * 512)
            xTs = [self.xT_T[c][tb] for c in range(8)]
            sq = self.sq.next()
            self.act(sq.ap[:], self.xT[:, :, tsl], AF.Square, xTs, [sq.T])
            bank = self.gbank()
            self.mmacc(bank.ap[:, :], [(self.ones_bf.ap[:], sq.ap[:, c, :]) for c in range(8)],
                       [sq.T, self.ones_bf.T], [bank.T])
            t0 = self.f32a.next()
            self.act(t0.ap[:], bank.ap[:], AF.Sqrt, [bank.T, self.epsT.T], [t0.T], scale=1.0 / D, bias=self.epsT.ap[:, 0:1])
            rstd = self.rstd.next()
            self.recip(rstd.ap[:], t0.ap[:], [t0.T], [rstd.T])
            for c in range(8):
                t1 = self.f32a.next()
                self.stt(t1.ap[:], self.xT[:, c, tsl], self.gmod.ap[:, l, j, c, s:s + 1], rstd.ap[:],
                         ALU.mult, ALU.mult, [self.xT_T[c][tb], self.gmod.T, rstd.T], [t1.T])
                self.act(self.hT[:, c, tsl], t1.ap[:], AF.Identity, [t1.T, self.modT.T], [self.hT_T[c][tb]],
                         bias=self.modT.ap[:, l, 3 * j, c, s:s + 1], scale=1.0)

    def ffn(self, l, j, s):
        first = self.piece_index["ffn%d_%d" % (l, j)][0]
        jm = 0 if j == 0 else 2
        for (f0, f1) in FGROUPS:
            slots = [self.w_acquire(first + f) for f in range(f0, f1)]
            ng = f1 - f0
            for tb in range(4):
                tsl = slice(tb * 512, (tb + 1) * 512)
                hTs = [self.hT_T[c][tb] for c in range(8)]
                hid = self.hid[self.hid_i % 2]
                self.hid_i += 1
                for fi in range(ng):
                    slot, sT = slots[fi]
                    ba = self.gbank()
                    bb = self.gbank()
                    self.mmacc(ba.ap[:, :], [(slot[:, c * 256:c * 256 + 128], self.hT[:, c, tsl]) for c in range(8)],
                               [sT] + hTs, [ba.T])
                    self.mmacc(bb.ap[:, :], [(slot[:, c * 256 + 128:c * 256 + 256], self.hT[:, c, tsl]) for c in range(8)],
                               [sT] + hTs, [bb.T])
                    sa = self.f32a.next()
                    self.act(sa.ap[:], ba.ap[:], AF.Silu, [ba.T], [sa.T])
                    self.tt(hid.ap[:, fi, :], sa.ap[:], bb.ap[:], ALU.mult, [sa.T, bb.T], [hid.T])
                for c in range(8):
                    by = self.abank()
                    self.mmacc(by.ap[:, :], [(slots[fi][0][:, 2048 + c * 128:2048 + (c + 1) * 128], hid.ap[:, fi, :])
                                             for fi in range(ng)],
                               [hid.T] + [sl[1] for sl in slots], [by.T])
                    self.stt(self.xT[:, c, tsl], by.ap[:], self.gate.ap[:, l, jm, c, s:s + 1], self.xT[:, c, tsl],
                             ALU.mult, ALU.add, [by.T, self.gate.T, self.xT_T[c][tb]], [self.xT_T[c][tb]])
            self.w_release(ng)

    def odd_mixer(self, l, s):
        o = l // 2
        first = self.piece_index["mix%d" % l][0]
        sm = self.small.ap
        dv = self.derived.ap
        qs = self.view("qs", 0, [128, 2048], F32)
        V = self.view("Vo", 8192, [128, 16, 128], BF16)
        gs = self.view("gs", 12288, [128, 2048], BF16)
        qtl = [self.view("qtl%d" % d, 16384 + 4096 * d, [128, 2048], BF16) for d in range(2)]
        ktl = [self.view("ktl%d" % d, 24576 + 4096 * d, [128, 2048], BF16) for d in range(2)]
        khat = [self.view("khat%d" % d, 32768 + 4096 * d, [128, 16, 128], BF16) for d in range(2)]
        lf = self.view("lf", 40960, [128, 512], F32)
        kk = self.view("kk", 43008, [128, 512], F32)
        G = self.view("G", 45056, [128, 512], F32)
        tm = self.view("tm", 47104, [128, 512], F32)
        tm2 = self.view("tm2", 49152, [128, 512], F32)
        khT = self.view("khT", 51200, [128, 512], BF16)
        sig = self.view("sig", 52224, [128, 512], F32)
        scanmask = self.view("scanmask", 54272, [128, 512], F32)
        Shat = [self.view("Shat%d" % d, 40960 + 8192 * d, [128, 32, 128], BF16) for d in range(2)]
        cs = self.view("cs", 57344, [128, 2, 2, 32], F32)
        yq = Ring([self.view("yq%d" % i, 58368 + 1024 * i, [128, 512], BF16) for i in range(2)])
        Am = Ring([self.view("Am%d" % i, 256 * i, [128, 128], BF16) for i in range(4)])
        Sst = [self.view("Sst%d" % d, 1024 + 512 * d, [128, 128], F32) for d in range(2)]
        Ost = self.view("Ost", 2048, [128, 512], F32)
        sqO = self.view("sqO", 4096, [128, 512], BF16)
        rsO = self.view("rsO", 5120, [128, 512], F32)


        def c3(buf):
            return buf.ap[:, :].rearrange("p (c k) -> p c k", k=64)

        for h in range(8):
            p1, p1T = self.w_acquire(first + 3 * h)
            p2, p2T = self.w_acquire(first + 3 * h + 1)
            for tb in range(4):
                tsl = slice(tb * 512, (tb + 1) * 512)
                hTs = [self.hT_T[c][tb] for c in range(8)]
                bq = self.gbank()
                self.mmacc(bq.ap[:, :], [(p1[:, c * 384:c * 384 + 128], self.hT[:, c, tsl]) for c in range(8)], [p1T] + hTs, [bq.T])
                self.act(qs.ap[:, tsl], bq.ap[:], AF.Silu, [bq.T], [qs.T])
                bg = self.gbank()
                self.mmacc(bg.ap[:, :], [(p2[:, c * 256 + 128:c * 256 + 256], self.hT[:, c, tsl]) for c in range(8)], [p2T] + hTs, [bg.T])
                self.act(gs.ap[:, tsl], bg.ap[:], AF.Silu, [bg.T], [gs.T])
            for tg in range(4):
                bank = self.gbank()
                items = []
                for tt_ in range(4):
                    tok = slice((tg * 4 + tt_) * 128, (tg * 4 + tt_ + 1) * 128)
                    for c in range(8):
                        items.append((bank.ap[:, tt_ * 128:(tt_ + 1) * 128], self.hT[:, c, tok],
                                      p2[:, c * 256:c * 256 + 128], c == 0, c == 7))
                self.mm(items, [p2T] + [self.hT_T[c][tg] for c in range(8)], [bank.T])
                self.copy(V.ap[:, tg * 4:(tg + 1) * 4, :], bank.ap[:, :].rearrange("p (a b) -> p a b", b=128),
                          [bank.T], [V.T], eng="act")
            self.memset(scanmask.ap[:, :], 1.0, [scanmask.T])
            self.memset(scanmask.ap[:, :].rearrange("p (c k) -> p c k", k=64)[:, :, 0:1], 0.0, [scanmask.T])
            for d in range(2):
                ci = (d * 2 + o) * 8 + h
                lbc = dv[:, 18 + ci:19 + ci]
                omlc = dv[:, 50 + ci:51 + ci]
                nomlc = dv[:, 82 + ci:83 + ci]
                for tb in range(4):
                    tsl = slice(tb * 512, (tb + 1) * 512)
                    hTs = [self.hT_T[c][tb] for c in range(8)]
                    bf_ = self.gbank()
                    self.mmacc(bf_.ap[:, :], [(p1[:, c * 384 + 128 * (1 + d):c * 384 + 128 * (2 + d)], self.hT[:, c, tsl])
                                              for c in range(8)], [p1T] + hTs, [bf_.T])
                    self.act(sig.ap[:], bf_.ap[:], AF.Sigmoid, [bf_.T], [sig.T])
                    self.act(lf.ap[:], sig.ap[:], AF.Ln, [sig.T, self.derived.T], [lf.T], scale=omlc, bias=lbc)
                    self.ts(kk.ap[:], sig.ap[:], nomlc, omlc, ALU.mult, ALU.add, [sig.T, self.derived.T], [kk.T])
                    self.s.op("dve", lambda e_: e_.tensor_tensor_scan(out=G.ap[:], data0=scanmask.ap[:], data1=lf.ap[:],
                                                                      initial=0.0, op0=ALU.mult, op1=ALU.add),
                              [scanmask.T, lf.T], [G.T])
                    csl = slice(tb * 8, (tb + 1) * 8)
                    if d == 0:
                        R = c3(G)[:, :, 31:32]
                        GL = c3(G)[:, :, 63:64]
                        self.tt(c3(tm), c3(G), R.to_broadcast([128, 8, 64]), ALU.subtract, [G.T], [tm.T])
                        self.tt(c3(tm2), GL.to_broadcast([128, 8, 64]), c3(G), ALU.subtract, [G.T], [tm2.T])
                        self.act(cs.ap[:, 0, 0, csl], c3(G)[:, :, 31], AF.Exp, [G.T], [cs.T])
                        self.act(cs.ap[:, 0, 1, csl], c3(G)[:, :, 63], AF.Exp, [G.T], [cs.T])
                    else:
                        self.act(cs.ap[:, 1, 1, csl], c3(G)[:, :, 63], AF.Exp, [G.T], [cs.T])
                        self.copy(sig.ap[:, 0:8], c3(G)[:, :, 63], [G.T], [sig.T])
                        self.tt(G.ap[:], G.ap[:], lf.ap[:], ALU.subtract, [G.T, lf.T], [G.T])
                        R = c3(G)[:, :, 32:33]
                        self.tt(sig.ap[:, 0:8], sig.ap[:, 0:8], c3(G)[:, :, 32], ALU.subtract, [sig.T, G.T], [sig.T])
                        self.act(cs.ap[:, 1, 0, csl], sig.ap[:, 0:8], AF.Exp, [sig.T], [cs.T])
                        self.tt(c3(tm), R.to_broadcast([128, 8, 64]), c3(G), ALU.subtract, [G.T], [tm.T])
                        self.copy(tm2.ap[:], G.ap[:], [G.T], [tm2.T], eng="pool")
                    self.act(tm.ap[:], tm.ap[:], AF.Exp, [tm.T], [tm.T])
                    self.tt(qtl[d].ap[:, tsl], qs.ap[:, tsl], tm.ap[:], ALU.mult, [qs.T, tm.T], [qtl[d].T])
                    self.recip(tm.ap[:], tm.ap[:], [tm.T], [tm.T])
                    self.tt(ktl[d].ap[:, tsl], kk.ap[:], tm.ap[:], ALU.mult, [kk.T, tm.T], [ktl[d].T])
                    self.act(tm2.ap[:], tm2.ap[:], AF.Exp, [tm2.T], [tm2.T])
                    self.tt(khT.ap[:], kk.ap[:], tm2.ap[:], ALU.mult, [kk.T, tm2.T], [khT.T])
                    tb_ = self.gbank()
                    tbf = tb_.ap[:, :].bitcast(BF16)
                    self.transposes([(tbf[:, j * 128:(j + 1) * 128], khT.ap[:, j * 128:(j + 1) * 128], self.ident) for j in range(4)],
                                    [khT.T, self.consts.T], [tb_.T])
                    self.copy(khat[d].ap[:, tb * 4:(tb + 1) * 4, :], tbf[:, 0:512].rearrange("p (a b) -> p a b", b=128),
                              [tb_.T], [khat[d].T], eng="act")
            self.w_release(2)
            wslot, wT = self.w_acquire(first + 3 * h + 2)
            self.barrier()
            if "odd" in self.debug and h == 0 and s == 0:
                self.dump("derived", self.derived.ap[:], [self.derived.T])
                self.dump("cs", cs.ap[:], [cs.T])
                self.dump("qs", qs.ap[:], [qs.T])
                self.dump("gs", gs.ap[:], [gs.T], BF16)
                self.dump("V", V.ap[:], [V.T], BF16)
                for d in range(2):
                    self.dump("qtl%d" % d, qtl[d].ap[:], [qtl[d].T], BF16)
                    self.dump("ktl%d" % d, ktl[d].ap[:], [ktl[d].T], BF16)
                    self.dump("khat%d" % d, khat[d].ap[:], [khat[d].T], BF16)
                self.dump("lf", lf.ap[:], [lf.T])
                self.dump("kk", kk.ap[:], [kk.T])
                self.dump("G", G.ap[:], [G.T])
                self.barrier()
            for i in range(32):
                for d in range(2):
                    c = i if d == 0 else 31 - i
                    St = Sst[d]
                    rows = slice((c % 2) * 64, (c % 2) * 64 + 64)
                    if i == 0:
                        self.memset(Shat[d].ap[:, c, :], 0.0, [Shat[d].T])
                    else:
                        self.ts(Shat[d].ap[:, c, :], St.ap[:], cs.ap[:, d, 0, c:c + 1], None, ALU.mult, None,
                                [St.T, cs.T], [Shat[d].T])
                    bs = self.gbank()
                    self.mm([(bs.ap[:, 0:128], khat[d].ap[rows, c // 2, :], V.ap[rows, c // 2, :], True, True)],
                            [khat[d].T, V.T], [bs.T])
                    if i == 0:
                        self.copy(St.ap[:], bs.ap[:, 0:128], [bs.T], [St.T])
                    else:
                        self.stt(St.ap[:], St.ap[:], cs.ap[:, d, 1, c:c + 1], bs.ap[:, 0:128], ALU.mult, ALU.add,
                                 [St.T, cs.T, bs.T], [St.T])
            if "odd" in self.debug and h == 0 and s == 0:
                self.barrier()
                for d in range(2):
                    self.dump("Shat%d" % d, Shat[d].ap[:], [Shat[d].T], BF16)
                self.barrier()
            for qb in range(4):
                qsl = slice(qb * 512, (qb + 1) * 512)
                bO = self.abank()
                for blk in range(4):
                    nb = qb * 4 + blk
                    tok = slice(nb * 128, (nb + 1) * 128)
                    ams = []
                    for d in range(2):
                        ba = self.gbank()
                        self.mm([(ba.ap[:, 0:128], ktl[d].ap[:, tok], qtl[d].ap[:, tok], True, True)], [ktl[d].T, qtl[d].T], [ba.T])
                        am = Am.next()
                        self.tt(am.ap[:], ba.ap[:, 0:128], self.mask_f if d == 0 else self.mask_b, ALU.mult,
                                [ba.T, self.consts.T], [am.T])
                        ams.append(am)
                    oc = bO.ap[:, blk * 128:(blk + 1) * 128]
                    items = [(oc, V.ap[:, nb, :], ams[0].ap[:], True, False),
                             (oc, V.ap[:, nb, :], ams[1].ap[:], False, False)]
                    for d in range(2):
                        for hf in range(2):
                            c = 2 * nb + hf
                            items.append((bO.ap[:, blk * 128 + hf * 64:blk * 128 + (hf + 1) * 64], Shat[d].ap[:, c, :],
                                          qtl[d].ap[:, nb * 128 + hf * 64:nb * 128 + (hf + 1) * 64], False, d == 1 and hf == 1))
                    self.mm(items, [V.T, ams[0].T, ams[1].T, Shat[0].T, Shat[1].T, qtl[0].T, qtl[1].T], [bO.T])
                self.act(Ost.ap[:], bO.ap[:], AF.Copy, [bO.T], [Ost.T])
                if "odd" in self.debug and h == 0 and s == 0 and qb == 0:
                    self.dump("Ost", Ost.ap[:], [Ost.T])
                    self.dump("Am", ams[0].ap[:], [ams[0].T], BF16)
                    self.dump("Amb", ams[1].ap[:], [ams[1].T], BF16)
                    self.barrier()
                self.act(sqO.ap[:], bO.ap[:], AF.Square, [bO.T], [sqO.T])
                b2 = self.gbank()
                self.mm([(b2.ap[:, :], self.ones_bf.ap[:], sqO.ap[:], True, True)], [sqO.T, self.ones_bf.T], [b2.T])
                self.act(rsO.ap[:], b2.ap[:], AF.Sqrt, [b2.T, self.epsT.T], [rsO.T], scale=1.0 / 128, bias=self.epsT.ap[:, 0:1])
                self.recip(rsO.ap[:], rsO.ap[:], [rsO.T], [rsO.T])
                self.tt(Ost.ap[:], Ost.ap[:], rsO.ap[:], ALU.mult, [Ost.T, rsO.T], [Ost.T])
                y = yq.next()
                self.stt(y.ap[:], Ost.ap[:], sm[:, SM_OUTG + o:SM_OUTG + o + 1], gs.ap[:, qsl], ALU.mult, ALU.mult,
                         [Ost.T, self.small.T, gs.T], [y.T])
                if "odd" in self.debug and s == 0 and qb in (0, 3):
                    self.dump("y_h%d_q%d" % (h, qb), y.ap[:], [y.T], BF16)
                    self.dump("Ost_h%d_q%d" % (h, qb), Ost.ap[:], [Ost.T])
                    self.barrier()
                for c in range(8):
                    by = self.abank()
                    self.mm([(by.ap[:, :], wslot[:, c * 128:(c + 1) * 128], y.ap[:], True, True)], [wT, y.T], [by.T])
                    self.stt(self.xT[:, c, qsl], by.ap[:], self.gate.ap[:, l, 1, c, s:s + 1], self.xT[:, c, qsl],
                             ALU.mult, ALU.add, [by.T, self.gate.T, self.xT_T[c][qb]], [self.xT_T[c][qb]])
            self.w_release(1)
            self.barrier()
            if "odd" in self.debug and s == 0 and h in (0, 1):
                self.dump("x_after_h%d" % h, self.xT[:, :, :], [t for r_ in self.xT_T for t in r_])
                self.barrier()

    def emit(self):
        nc = self.nc
        keys = sorted(self.s.val.keys())
        for k in keys:
            self.sems[k] = self.stack.enter_context(nc.semaphore(k))
        sems = self.sems
        q = self.s.q

        def run(e, ops):
            for waits, fn, inc in ops:
                for k, v in waits:
                    e.wait_ge(sems[k], v)
                if fn is None:
                    continue
                ins = fn(e)
                ins.then_inc(sems[inc[0]], inc[1])

        with nc.Block() as block:
            @block.tensor
            def _(e):
                run(e, q["pe"])

            @block.scalar
            def _(e):
                run(e, q["act"])

            @block.vector
            def _(e):
                run(e, q["dve"])

            @block.gpsimd
            def _(e):
                run(e, q["pool"])

            @block.sync
            def _(e):
                run(e, q["sp"])
        self.stack.close()


def make_stream(inputs, layers, do_mixer=True, do_ffn=True):
    pieces = []
    index = {}

    def add(name, plist, cols):
        index[name] = (len(pieces), cols)
        pieces.extend(plist)

    for l in layers:
        add("ada%d" % l, ada_pieces(inputs["ada_w"], l), [SLOT] * N_ADA_PIECES)
    for l in layers:
        if do_ffn:
            add("ffn%d_0" % l, ffn_pieces(inputs["ffn_up"], inputs["ffn_down"], l, 0), [SLOT] * NF)
        if do_mixer:
            if l % 2 == 0:
                add("mix%d" % l, even_pieces(inputs["even_w_in"], inputs["even_w_out"], l // 2),
                    EVEN_COLS)
            else:
                add("mix%d" % l, odd_pieces(inputs["odd_w_in"], inputs["odd_w_out"], l // 2),
                    ODD_COLS)
        if do_ffn:
            add("ffn%d_1" % l, ffn_pieces(inputs["ffn_up"], inputs["ffn_down"], l, 1), [SLOT] * NF)
    return np.stack(pieces, axis=0), index


def lay_x(xb):
    return np.ascontiguousarray(xb.reshape(S, 8, 128).transpose(2, 1, 0))


def unlay_x(xt):
    return np.ascontiguousarray(xt.transpose(2, 1, 0).reshape(S, D))


def common_maps(inputs, x_cur, batch_ids_per_core, wstream):
    adab = np.ascontiguousarray(inputs["ada_b"].reshape(DEPTH, 72, 128).transpose(2, 0, 1))
    normg = np.ascontiguousarray(inputs["norm_g"].reshape(DEPTH, 3, 8, 128).transpose(3, 0, 1, 2))
    small = small_inputs(inputs)
    strips = strips_input(inputs)
    consts = consts_input()
    maps = []
    for bids in batch_ids_per_core:
        xin = np.stack([lay_x(x_cur[b]) for b in bids], axis=0)
        cT = np.stack([inputs["c"][b].reshape(8, 128).T for b in bids], axis=2)
        if len(bids) == 1:
            cT = np.concatenate([cT, cT], axis=2)
        maps.append({"wstream": wstream, "x_in": xin, "c_in": np.ascontiguousarray(cT, dtype=np.float32),
                     "adab_in": adab, "normg_in": normg, "small_in": small, "strips_in": strips, "consts_in": consts})
    return maps


def run_layers(inputs, x_cur, layers, batch_ids_per_core, do_mixer=True, do_ffn=True, trace=False, debug=None):
    inputs = {k: np.asarray(v, dtype=np.float32) for k, v in inputs.items()}
    nseq = len(batch_ids_per_core[0])
    wstream, index = make_stream(inputs, layers, do_mixer, do_ffn)
    b = Builder(layers, nseq=nseq, do_mixer=do_mixer, do_ffn=do_ffn, debug=debug)
    nc = b.build(index, wstream.shape[0])
    maps = common_maps(inputs, x_cur, batch_ids_per_core, wstream)
    res = run_bass_kernel_spmd(nc, maps, core_ids=list(range(len(maps))), **({"trace": True} if trace else {}))
    out = np.array(x_cur, dtype=np.float32, copy=True)
    for ci, bids in enumerate(batch_ids_per_core):
        xo = res.results[ci]["x_out"]
        for si, bb in enumerate(bids):
            out[bb] = unlay_x(xo[si])
    return out, res


FUSED = True


def kernel(**inputs):
    inputs = {k: np.asarray(v, dtype=np.float32) for k, v in inputs.items()}
    x = inputs["x"]
    bids = [[2 * i, 2 * i + 1] for i in range(NCORES)]
    if FUSED:
        out, _ = run_layers(inputs, x, list(range(DEPTH)), bids)
    else:
        out = x
        for l in range(DEPTH):
            out, _ = run_layers(inputs, out, [l], bids)
    return out.astype(np.float32)
```

```python
import math
from contextlib import ExitStack
import numpy as np
import concourse.bass as bass
import concourse.mybir as mybir
from concourse.bass_utils import run_bass_kernel_spmd

F32 = mybir.dt.float32
BF16 = mybir.dt.bfloat16
AF = mybir.ActivationFunctionType
ALU = mybir.AluOpType

D = 1024
S = 2048
DEPTH = 4
NCORES = 8
NSEQ = 2
DFF = 2816
NF = 22
EPS = 1e-6
SLOT = 3072
NSLOT = 7
FGROUPS = [(0, 4), (4, 8), (8, 12), (12, 16), (16, 19), (19, 22)]
N_ADA_PIECES = 24
SCALE = 0.125
NEG = -30000.0
ARENA = 60416


def _kin(w):
    return w.reshape(8, 128, -1).transpose(1, 0, 2)


def _pad(a):
    a = np.ascontiguousarray(a, dtype=np.float32).reshape(128, -1)
    out = np.zeros((128, SLOT), np.float32)
    out[:, : a.shape[1]] = a
    return out


def t5_bucket_np(rel):
    half = 16
    max_exact = 8
    n = np.abs(rel)
    nf = np.maximum(n, 1).astype(np.float32)
    large = max_exact + (np.log(nf / max_exact) / math.log(128 / max_exact) * (half - max_exact)).astype(np.int32)
    large = np.minimum(large, half - 1)
    return np.where(rel > 0, half, 0) + np.where(n < max_exact, n, large)


def ada_pieces(ada_w, l):
    w = _kin(ada_w[l])
    return [_pad(w[:, :, j * 384:(j + 1) * 384]) for j in range(N_ADA_PIECES)]


def ffn_pieces(ffn_up, ffn_down, l, j):
    up = _kin(ffn_up[l, j]).reshape(128, 8, 2, NF, 128)
    out = []
    for i in range(NF):
        u = up[:, :, :, i, :].reshape(128, 2048)
        dn = ffn_down[l, j, i * 128:(i + 1) * 128, :]
        out.append(_pad(np.concatenate([u, dn], axis=1)))
    return out


def wout_piece(w_out, chunk):
    return _pad(w_out[chunk * 128:(chunk + 1) * 128, :])


EVEN_COLS = [SLOT, 1024] * 4 + [1024] + [SLOT, 1024, 1024] * 2
ODD_COLS = [SLOT, 2048, 1024] * 8


def even_pieces(even_w_in, even_w_out, e):
    w = _kin(even_w_in[e])
    out = []
    for h in range(4):
        q = w[:, :, h * 128:(h + 1) * 128]
        k = w[:, :, 512 + h * 128:512 + (h + 1) * 128]
        v = w[:, :, 1024 + h * 128:1024 + (h + 1) * 128]
        out.append(_pad(np.concatenate([q, k, v], axis=2)))
        out.append(wout_piece(even_w_out[e], h))
    out.append(_pad(w[:, :, 2176:2304]))
    for g in range(2):
        q = w[:, :, 1536 + g * 256:1536 + (g + 1) * 256]
        k = w[:, :, 2048 + g * 64:2048 + (g + 1) * 64]
        out.append(_pad(np.concatenate([q, k, k], axis=2)))
        out.append(wout_piece(even_w_out[e], 4 + 2 * g))
        out.append(wout_piece(even_w_out[e], 5 + 2 * g))
    return out


def odd_pieces(odd_w_in, odd_w_out, o):
    w = _kin(odd_w_in[o])
    out = []
    for h in range(8):
        sl = slice(h * 128, (h + 1) * 128)
        q, ff, fb, iv, g = (w[:, :, k * 1024:(k + 1) * 1024][:, :, sl] for k in range(5))
        out.append(_pad(np.concatenate([q, ff, fb], axis=2)))
        out.append(_pad(np.concatenate([iv, g], axis=2)))
        out.append(wout_piece(odd_w_out[o], h))
    return out


SM_QKG, SM_DLAM, SM_SUBLN, SM_SINK, SM_CLOHI, SM_CLB, SM_OUTG, SM_W = 0, 8, 520, 776, 792, 816, 880, 882


def small_inputs(inputs):
    sm = np.zeros((128, SM_W), np.float32)
    p = np.arange(128)
    sm[:, SM_QKG:SM_QKG + 8] = inputs["qk_norm_g"][:, :, p % 64].transpose(2, 0, 1).reshape(128, 8)
    sm[:, SM_DLAM:SM_DLAM + 512] = inputs["diff_lambda"].reshape(1, 512)
    sm[:, SM_SUBLN:SM_SUBLN + 256] = inputs["diff_subln_g"].reshape(1, 256)
    sm[:, SM_SINK:SM_SINK + 16] = inputs["sink_logit"].reshape(1, 16)
    sm[:, SM_CLOHI:SM_CLOHI + 24] = inputs["rel_bias"][[15, 31], :].T.reshape(1, 24)
    sm[:, SM_CLB:SM_CLB + 64] = inputs["c_lower_bound"].reshape(2, 4, 8, 128).transpose(3, 0, 1, 2).reshape(128, 64)
    sm[:, SM_OUTG:SM_OUTG + 2] = inputs["c_out_norm_g"].T
    return sm


def strips_input(inputs):
    k = np.arange(128)[:, None]
    q = np.arange(128)[None, :]
    out = np.zeros((13, 128, 384), np.float32)
    for j, d in enumerate((1, 0, -1)):
        idx = t5_bucket_np(k - q + 128 * d)
        out[:12, :, j * 128:(j + 1) * 128] = inputs["rel_bias"][idx].transpose(2, 0, 1)
    out[12, :, 0:128] = np.where(k <= q, 0.0, NEG)
    out[12, :, 256:384] = np.where(k >= q, 0.0, NEG)
    return out


def consts_input():
    c = np.zeros((128, 4, 128), np.float32)
    c[:, 0, :] = np.eye(128)
    pp = np.arange(128)
    c[:, 1, :] = (pp[:, None] // 64 == pp[None, :] // 64)
    same = (pp[:, None] // 64 == pp[None, :] // 64)
    c[:, 2, :] = same & (pp[:, None] <= pp[None, :])
    c[:, 3, :] = same & (pp[:, None] >= pp[None, :])
    return c


class T:
    __slots__ = ("name", "w", "r")

    def __init__(self, name):
        self.name = name
        self.w = None
        self.r = {}


class Sched:
    ENG = ("pe", "act", "dve", "pool", "sp")

    def __init__(self):
        self.q = {e: [] for e in self.ENG}
        self.val = {}
        self.seen = {e: {} for e in self.ENG}

    def _deps(self, eng, reads, writes):
        need = {}

        def add(k, v):
            if v > need.get(k, 0):
                need[k] = v

        for t in reads:
            if t.w is not None:
                add(*t.w)
        for t in writes:
            if t.w is not None:
                add(*t.w)
            for k, v in t.r.items():
                add(k, v)
        waits = []
        seen = self.seen[eng]
        for k, v in need.items():
            if eng == "pe" and k == "c_pe":
                continue
            if seen.get(k, 0) < v:
                waits.append((k, v))
                seen[k] = v
        return waits

    def _mark(self, ev, reads, writes):
        k, v = ev
        for t in reads:
            if t.r.get(k, 0) < v:
                t.r[k] = v
        for t in writes:
            t.w = ev
            t.r = {}

    def op(self, eng, fn, reads=(), writes=()):
        waits = self._deps(eng, reads, writes)
        k = "c_" + eng
        v = self.val.get(k, 0) + 1
        self.val[k] = v
        self.q[eng].append((waits, fn, (k, 1)))
        self._mark((k, v), reads, writes)

    def dma(self, queue, fns, semkey, reads=(), writes=()):
        waits = self._deps(queue, reads, writes)
        for i, fn in enumerate(fns):
            self.val[semkey] = self.val.get(semkey, 0) + 16
            self.q[queue].append((waits if i == 0 else [], fn, (semkey, 16)))
        self._mark((semkey, self.val[semkey]), reads, writes)

    def final_wait(self, eng, semkeys):
        waits = [(k, self.val[k]) for k in semkeys if self.val.get(k, 0) > 0]
        self.q[eng].append((waits, None, None))


class Buf:
    def __init__(self, ap, name):
        self.ap = ap
        self.T = T(name)


class Ring:
    def __init__(self, bufs):
        self.bufs = bufs
        self.i = 0

    def next(self):
        b = self.bufs[self.i % len(self.bufs)]
        self.i += 1
        return b


class Builder:
    def __init__(self, layers, nseq=NSEQ, do_mixer=True, do_ffn=True, debug=None):
        self.debug = debug or set()
        self.layers = list(layers)
        self.nseq = nseq
        self.do_mixer = do_mixer
        self.do_ffn = do_ffn
        self.s = Sched()
        self.sems = {}
        self.nc = bass.Bass("TRN2", target_bir_lowering=False)
        self.stack = ExitStack()

    def sb(self, name, shape, dt=F32):
        t = self.stack.enter_context(self.nc.sbuf_tensor(name, list(shape), dt))
        return t

    def view(self, name, off, shape, dt=F32):
        n = int(np.prod(shape[1:]))
        nbytes = n * (4 if dt == F32 else 2)
        assert off % 4 == 0 and off + nbytes <= ARENA, (name, off, nbytes)
        ap = self.arena[:, off // 4:(off + (nbytes + 3) // 4 * 4) // 4]
        if dt != F32:
            ap = ap.bitcast(dt)
            ap = ap[:, 0:n]
        if len(shape) == 3:
            ap = ap.rearrange("p (a b) -> p a b", b=shape[2])
        elif len(shape) == 4:
            ap = ap.rearrange("p (a b c) -> p a b c", b=shape[2], c=shape[3])
        return Buf(ap, name)

    def barrier(self):
        keys = [k for k in self.s.val if k in ("c_pe", "c_act", "c_dve") or k.startswith("ld_a") or (self.debug and k == "st_x")]
        for eng in ("pe", "act", "dve", "sp"):
            waits = []
            for k in keys:
                v = self.s.val[k]
                if eng == "pe" and k == "c_pe":
                    continue
                if self.s.seen[eng].get(k, 0) < v:
                    waits.append((k, v))
                    self.s.seen[eng][k] = v
            if waits:
                self.s.q[eng].append((waits, None, None))

    def buf(self, name, shape, dt=F32):
        t = self.sb(name, shape, dt)
        return Buf(t, name)

    def act(self, out, in_, func, reads, writes, **kw):
        self.s.op("act", lambda e: e.activation(out=out, in_=in_, func=func, **kw), reads, writes)

    def tt(self, out, in0, in1, op, reads, writes, eng="dve"):
        self.s.op(eng, lambda e: e.tensor_tensor(out=out, in0=in0, in1=in1, op=op), reads, writes)

    def ts(self, out, in0, s1, s2, op0, op1, reads, writes, eng="dve"):
        if op1 is None:
            self.s.op(eng, lambda e: e.tensor_scalar(out=out, in0=in0, scalar1=s1, scalar2=None, op0=op0), reads, writes)
        else:
            self.s.op(eng, lambda e: e.tensor_scalar(out=out, in0=in0, scalar1=s1, scalar2=s2, op0=op0, op1=op1), reads, writes)

    def stt(self, out, in0, scalar, in1, op0, op1, reads, writes):
        self.s.op("dve", lambda e: e.scalar_tensor_tensor(out=out, in0=in0, scalar=scalar, in1=in1, op0=op0, op1=op1), reads, writes)

    def copy(self, out, in_, reads, writes, eng="dve"):
        if eng == "act":
            self.s.op("act", lambda e: e.activation(out=out, in_=in_, func=AF.Copy), reads, writes)
        else:
            self.s.op(eng, lambda e: e.tensor_copy(out=out, in_=in_), reads, writes)

    def recip(self, out, in_, reads, writes):
        self.s.op("dve", lambda e: e.reciprocal(out=out, in_=in_), reads, writes)

    def memset(self, ap, val, writes, eng="dve"):
        self.s.op(eng, lambda e: e.memset(ap, val), (), writes)

    def mm(self, items, reads, writes):
        items = list(items)

        def fn(e):
            ins = None
            for (o, l, r, st, sp) in items:
                ins = e.matmul(o, lhsT=l, rhs=r, start=st, stop=sp)
            return ins

        self.s.op("pe", fn, reads, writes)

    def mmacc(self, out, pairs, reads, writes):
        n = len(pairs)
        self.mm([(out, l, r, i == 0, i == n - 1) for i, (l, r) in enumerate(pairs)], reads, writes)

    def transposes(self, items, reads, writes):
        items = list(items)

        def fn(e):
            ins = None
            for (o, i_, ident) in items:
                ins = e.transpose(o, i_, ident)
            return ins

        self.s.op("pe", fn, reads, writes)

    def dump(self, name, ap, reads, dt=F32):
        shape = list(ap.shape)
        d = self.nc.dram_tensor("dbg_" + name, shape, dt, kind="ExternalOutput").ap()
        self.s.dma("sp", [lambda e: e.dma_start(out=d, in_=ap)], "st_x", reads, ())

    def load(self, out, in_, semkey, writes, queue="sp"):
        self.s.dma(queue, [lambda e: e.dma_start(out=out, in_=in_)], semkey, (), writes)

    def gbank(self):
        b = self.gb[self.gbi % 4]
        self.gbi += 1
        return b

    def abank(self):
        b = self.ab[self.abi % self.ab_n]
        self.abi += 1
        return b

    def w_prefetch(self):
        while self.w_free > 0 and self.w_next_load < len(self.w_plan):
            k = self.w_next_load
            idx, ncols = self.w_plan[k]
            slot = k % NSLOT
            out = self.ring[:, slot, 0:ncols]
            in_ = self.wstream[idx, :, 0:ncols]
            self.s.dma("pool", [lambda e, o=out, i=in_: e.dma_start(out=o, in_=i)], "w%d" % slot, (), [self.slotT[slot]])
            self.w_next_load += 1
            self.w_free -= 1

    def w_acquire(self, expect_idx=None):
        k = self.w_next_use
        assert k < self.w_next_load, "weight stream underflow"
        if expect_idx is not None:
            assert self.w_plan[k][0] == expect_idx, (k, self.w_plan[k], expect_idx)
        self.w_next_use += 1
        slot = k % NSLOT
        return self.ring[:, slot, :], self.slotT[slot]

    def w_release(self, n=1):
        self.w_free += n
        self.w_prefetch()

    def build(self, piece_index, n_pieces):
        nc = self.nc
        st = self.stack
        L = self.layers
        NL = len(L)
        self.wstream = nc.dram_tensor("wstream", [n_pieces, 128, SLOT], F32, kind="ExternalInput").ap()
        self.x_in = nc.dram_tensor("x_in", [self.nseq, 128, 8, S], F32, kind="ExternalInput").ap()
        self.x_out = nc.dram_tensor("x_out", [self.nseq, 128, 8, S], F32, kind="ExternalOutput").ap()
        self.c_in = nc.dram_tensor("c_in", [128, 8, 2], F32, kind="ExternalInput").ap()
        self.adab_in = nc.dram_tensor("adab_in", [128, DEPTH, 72], F32, kind="ExternalInput").ap()
        self.normg_in = nc.dram_tensor("normg_in", [128, DEPTH, 3, 8], F32, kind="ExternalInput").ap()
        self.small_in = nc.dram_tensor("small_in", [128, SM_W], F32, kind="ExternalInput").ap()
        self.strips_in = nc.dram_tensor("strips_in", [13, 128, 384], F32, kind="ExternalInput").ap()
        self.consts_in = nc.dram_tensor("consts_in", [128, 4, 128], F32, kind="ExternalInput").ap()

        self.xT = self.sb("xT", [128, 8, S], F32)
        self.xT_T = [[T("x%d_%d" % (c, b)) for b in range(4)] for c in range(8)]
        self.hT = self.sb("hT", [128, 8, S], BF16)
        self.hT_T = [[T("h%d_%d" % (c, b)) for b in range(4)] for c in range(8)]
        self.ring = self.sb("ring", [128, NSLOT, SLOT], BF16)
        self.slotT = [T("slot%d" % i) for i in range(NSLOT)]
        self.arena = self.sb("arena", [128, ARENA // 4], F32)
        self.arenaT = T("arena")
        self.hid = [self.view("hid%d" % i, 8192 + 4096 * i, [128, 4, 512], BF16) for i in range(2)]
        self.hid_i = 0
        self.sq = Ring([self.view("sq0", 0, [128, 8, 512], BF16)])
        self.f32a = Ring([self.view("f32a%d" % i, 16384 + 2048 * i, [128, 512], F32) for i in range(4)])
        self.rstd = Ring([self.view("rstd%d" % i, 24576 + 2048 * i, [128, 512], F32) for i in range(2)])
        self.modT = self.buf("modT", [128, DEPTH, 9, 8, 2], F32)
        self.gmod = self.buf("gmod", [128, DEPTH, 3, 8, 2], F32)
        self.gate = self.buf("gate", [128, DEPTH, 3, 8, 2], F32)
        self.cT = self.buf("cT", [128, 8, 2], F32)
        self.scT = self.buf("scT", [128, 8, 2], BF16)
        self.adab = self.buf("adab", [128, DEPTH, 72], F32)
        self.normg = self.buf("normg", [128, DEPTH, 3, 8], F32)
        self.small = self.buf("small", [128, SM_W], F32)
        self.consts = self.buf("consts", [128, 4, 128], BF16)
        self.derived = self.buf("derived", [128, 128], F32)
        self.ones_bf = self.buf("ones_bf", [128, 128], BF16)
        self.epsT = self.buf("epsT", [128, 1], F32)

        banks = [st.enter_context(nc.psum_tensor("bank%d" % i, [128, 512], F32)) for i in range(8)]
        self.gb = [Buf(banks[i], "gb%d" % i) for i in range(4)]
        self.ab = [Buf(banks[4 + i], "ab%d" % i) for i in range(4)]
        self.gbi = 0
        self.abi = 0
        self.ab_n = 4

        plan = []
        for l in L:
            first, cols = piece_index["ada%d" % l]
            plan += [(first + i, c) for i, c in enumerate(cols)]
        per_seq = []
        for l in L:
            for nm in ("ffn%d_0" % l, "mix%d" % l, "ffn%d_1" % l):
                if nm.startswith("mix") and not self.do_mixer:
                    continue
                if nm.startswith("ffn") and not self.do_ffn:
                    continue
                first, cols = piece_index[nm]
                per_seq += [(first + i, c) for i, c in enumerate(cols)]
        for _ in range(self.nseq):
            plan += per_seq
        self.w_plan = plan
        self.w_next_load = 0
        self.w_next_use = 0
        self.w_free = NSLOT
        self.piece_index = piece_index

        self.load(self.cT.ap[:], self.c_in[:, :, :], "ld_small", [self.cT.T])
        self.load(self.adab.ap[:], self.adab_in[:, :, :], "ld_small", [self.adab.T])
        self.load(self.normg.ap[:], self.normg_in[:, :, :, :], "ld_small", [self.normg.T])
        self.load(self.small.ap[:], self.small_in[:, :], "ld_small", [self.small.T])
        self.memset(self.ones_bf.ap[:], 1.0, [self.ones_bf.T])
        self.memset(self.epsT.ap[:], EPS, [self.epsT.T])
        self.w_prefetch()
        self.setup_extra()
        self.ada_phase()

        for s in range(self.nseq):
            for c in range(8):
                self.s.dma("sp", [lambda e, c=c, s=s: e.dma_start(out=self.xT[:, c, :], in_=self.x_in[s, :, c, :])],
                           "ld_x", (), self.xT_T[c])
            for l in L:
                if self.do_ffn:
                    self.barrier()
                    self.norm(l, 0, s)
                    if "h0" in self.debug and s == 0 and l == L[0]:
                        self.dump("h0", self.hT[:, :, :], [t for r in self.hT_T for t in r], BF16)
                        self.dump("modT", self.modT.ap[:], [self.modT.T])
                        self.dump("gmod", self.gmod.ap[:], [self.gmod.T])
                        self.dump("gate", self.gate.ap[:], [self.gate.T])
                    self.ffn(l, 0, s)
                    if "x1" in self.debug and s == 0 and l == L[0]:
                        self.dump("x1", self.xT[:, :, :], [t for r in self.xT_T for t in r])
                if self.do_mixer:
                    self.barrier()
                    self.norm(l, 1, s)
                    self.barrier()
                    if l % 2 == 0:
                        self.even_mixer(l, s)
                    else:
                        self.odd_mixer(l, s)
                if self.do_ffn:
                    self.barrier()
                    self.norm(l, 2, s)
                    self.ffn(l, 1, s)
            for c in range(8):
                self.s.dma("sp", [lambda e, c=c, s=s: e.dma_start(out=self.x_out[s, :, c, :], in_=self.xT[:, c, :])],
                           "st_x", self.xT_T[c], ())
        self.s.final_wait("sp", ["st_x"])
        assert self.w_next_use == len(self.w_plan), (self.w_next_use, len(self.w_plan))
        self.emit()
        return nc

    def setup_extra(self):
        AX = mybir.AxisListType.X
        ctmp = self.view("ctmp", 0, [128, 4, 128], F32)
        self.load(ctmp.ap[:], self.consts_in[:, :, :], "ld_a", [ctmp.T])
        self.copy(self.consts.ap[:], ctmp.ap[:], [ctmp.T], [self.consts.T])
        self.ident = self.consts.ap[:, 0, :]
        self.blk64 = self.consts.ap[:, 1, :]
        self.mask_f = self.consts.ap[:, 2, :]
        self.mask_b = self.consts.ap[:, 3, :]
        sm = self.small.ap
        smT = self.small.T
        dv = self.derived.ap
        dT = self.derived.T
        t = self.view("setup_t", 4096, [128, 512], F32)
        for e in range(2):
            lam_init = 0.8 - 0.6 * math.exp(-0.3 * (2 * e))
            base = SM_DLAM + e * 256
            self.tt(t.ap[:, 0:64], sm[:, base:base + 64], sm[:, base + 64:base + 128], ALU.mult, [smT], [t.T])
            self.tt(t.ap[:, 64:128], sm[:, base + 128:base + 192], sm[:, base + 192:base + 256], ALU.mult, [smT], [t.T])
            self.s.op("dve", lambda e_: e_.tensor_reduce(out=t.ap[:, 128:130],
                                                         in_=t.ap[:, 0:128].rearrange("p (a b) -> p a b", b=64),
                                                         axis=AX, op=ALU.add), [t.T], [t.T])
            self.act(t.ap[:, 130:132], t.ap[:, 128:130], AF.Exp, [t.T], [t.T])
            self.tt(t.ap[:, 132:133], t.ap[:, 131:132], t.ap[:, 130:131], ALU.subtract, [t.T], [t.T])
            self.ts(dv[:, e:e + 1], t.ap[:, 132:133], -lam_init, None, ALU.add, None, [t.T], [dT])
            sb_ = SM_SUBLN + e * 128
            self.ts(sm[:, sb_:sb_ + 128], sm[:, sb_:sb_ + 128], 1.0 - lam_init, None, ALU.mult, None, [smT], [smT])
        self.act(dv[:, 2:18], sm[:, SM_SINK:SM_SINK + 16], AF.Exp, [smT], [dT])
        ex = t.ap[:, 256:320].rearrange("p (d l h) -> p d l h", d=2, l=4)
        self.act(t.ap[:, 256:320], sm[:, SM_CLB:SM_CLB + 64], AF.Exp, [smT], [t.T])
        ssum = t.ap[:, 320:336].rearrange("p (d h) -> p d h", d=2)
        self.s.op("dve", lambda e_: e_.tensor_reduce(out=ssum, in_=t.ap[:, 256:320].rearrange("p (d l h) -> p d h l", d=2, l=4),
                                                     axis=AX, op=ALU.add), [t.T], [t.T])
        rs = t.ap[:, 336:352].rearrange("p (d h) -> p d h", d=2)
        self.recip(t.ap[:, 336:352], t.ap[:, 320:336], [t.T], [t.T])
        e23 = t.ap[:, 352:368].rearrange("p (d h) -> p d h", d=2)
        self.tt(e23, ex[:, :, 2, :], ex[:, :, 3, :], ALU.add, [t.T], [t.T])
        self.tt(e23, e23, ex[:, :, 1, :], ALU.add, [t.T], [t.T])
        lbv = dv[:, 18:50].rearrange("p (d o h) -> p d o h", d=2, o=2)
        self.tt(lbv[:, :, 0, :], ex[:, :, 1, :], rs, ALU.mult, [t.T], [dT])
        self.tt(lbv[:, :, 1, :], e23, rs, ALU.mult, [t.T], [dT])
        self.ts(dv[:, 50:82], dv[:, 18:50], -1.0, 1.0, ALU.mult, ALU.add, [dT], [dT])
        self.ts(dv[:, 82:114], dv[:, 18:50], -1.0, None, ALU.add, None, [dT], [dT])
        self.barrier()

    def out_proj_chunk(self, l, s, ytq, wslot, wT):
        for tb in range(4):
            tsl = slice(tb * 512, (tb + 1) * 512)
            for c in range(8):
                by = self.abank()
                self.mm([(by.ap[:, :], wslot[:, c * 128:(c + 1) * 128], ytq.ap[:, tsl], True, True)], [wT, ytq.T], [by.T])
                self.stt(self.xT[:, c, tsl], by.ap[:], self.gate.ap[:, l, 1, c, s:s + 1], self.xT[:, c, tsl],
                         ALU.mult, ALU.add, [by.T, self.gate.T, self.xT_T[c][tb]], [self.xT_T[c][tb]])

    def qk_proj_norm(self, slot, sT, col0, dst, dst_cols, gcol, tmp, sqr, nchain=2):
        dsts = dst if isinstance(dst, list) else [(dst, slice(0, 128))]
        for t0_ in range(0, 4, nchain):
            tbs = list(range(t0_, min(4, t0_ + nchain)))
            banks, raws, sqs, b2s, rss = {}, {}, {}, {}, {}
            for tb in tbs:
                tsl = slice(tb * 512, (tb + 1) * 512)
                hTs = [self.hT_T[c][tb] for c in range(8)]
                banks[tb] = self.gbank()
                self.mmacc(banks[tb].ap[:, :], [(slot[:, c * 384 + col0:c * 384 + col0 + 128], self.hT[:, c, tsl]) for c in range(8)],
                           [sT] + hTs, [banks[tb].T])
            for tb in tbs:
                sqs[tb] = sqr.next()
                self.act(sqs[tb].ap[:], banks[tb].ap[:], AF.Square, [banks[tb].T], [sqs[tb].T])
            for tb in tbs:
                raws[tb] = tmp.next()
                self.copy(raws[tb].ap[:], banks[tb].ap[:], [banks[tb].T], [raws[tb].T], eng="act")
            for tb in tbs:
                b2s[tb] = self.abank()
                self.mm([(b2s[tb].ap[:, :], self.blk64, sqs[tb].ap[:], True, True)], [sqs[tb].T, self.consts.T], [b2s[tb].T])
            for tb in tbs:
                rss[tb] = tmp.next()
                self.act(rss[tb].ap[:], b2s[tb].ap[:], AF.Ln, [b2s[tb].T, self.epsT.T], [rss[tb].T],
                         scale=1.0 / 64, bias=self.epsT.ap[:, 0:1])
            for tb in tbs:
                self.act(rss[tb].ap[:], rss[tb].ap[:], AF.Exp, [rss[tb].T], [rss[tb].T], scale=-0.5)
            for tb in tbs:
                tsl = slice(tb * 512, (tb + 1) * 512)
                for (db, ps_) in dsts:
                    self.stt(db.ap[ps_, tsl], raws[tb].ap[ps_, :], gcol[ps_, :], rss[tb].ap[ps_, :],
                             ALU.mult, ALU.mult, [raws[tb].T, rss[tb].T, self.small.T], [db.T])

    def even_mixer(self, l, s):
        AX = mybir.AxisListType.X
        e = l // 2
        first = self.piece_index["mix%d" % l][0]
        sm = self.small.ap
        dv = self.derived.ap
        Q1p = self.view("Q1p", 0, [128, 2048], BF16)
        Q2p = self.view("Q2p", 4096, [128, 2048], BF16)
        KT = self.view("KT", 8192, [128, 2048], BF16)
        Va = self.view("Va", 12288, [128, 16, 130], BF16)
        strips = self.view("stripsA", 16512, [128, 4, 384], F32)
        PT = Ring([self.view("PT%d" % i, 22656 + 1024 * i, [128, 512], BF16) for i in range(3)])
        o1 = self.view("o1", 25728, [128, 4, 128], F32)
        o2 = self.view("o2", 27776, [128, 4, 128], F32)
        tmp = Ring([self.view("tmpA%d" % i, 29824 + 2048 * i, [128, 512], F32) for i in range(8)])
        ystage = self.view("ystage", 46208, [128, 16, 128], BF16)
        ytq = self.view("ytq", 50304, [128, 2048], BF16)
        stat = self.view("stat", 54400, [128, 32], F32)
        SQ = Ring([self.view("sqA%d" % i, 54528 + 1024 * i, [128, 512], BF16) for i in range(4)])
        self.memset(Q1p.ap[64:128, :], 0.0, [Q1p.T])
        self.memset(Q2p.ap[0:64, :], 0.0, [Q2p.T])
        self.s.dma("sp", [lambda e_: e_.dma_start(out=strips.ap[:, :, :], in_=self.strips_in[0:4, :, :].rearrange("h p c -> p h c"))],
                   "ld_a", (), [strips.T])
        self.memset(Va.ap[:, :, 128:130], 1.0, [Va.T])
        for h in range(4):
            slot, sT = self.w_acquire(first + 2 * h)
            self.qk_proj_norm(slot, sT, 0, [(Q1p, slice(0, 64)), (Q2p, slice(64, 128))], None,
                              sm[:, SM_QKG + e * 4 + 0:SM_QKG + e * 4 + 1], tmp, SQ, nchain=4)
            self.qk_proj_norm(slot, sT, 128, KT, None, sm[:, SM_QKG + e * 4 + 1:SM_QKG + e * 4 + 2], tmp, SQ, nchain=4)
            for tg in range(4):
                bank = self.gbank()
                items = []
                for tt_ in range(4):
                    tok = slice((tg * 4 + tt_) * 128, (tg * 4 + tt_ + 1) * 128)
                    for c in range(8):
                        items.append((bank.ap[:, tt_ * 128:(tt_ + 1) * 128], self.hT[:, c, tok],
                                      slot[:, c * 384 + 256:c * 384 + 384], c == 0, c == 7))
                self.mm(items, [sT] + [self.hT_T[c][tg] for c in range(8)], [bank.T])
                self.copy(Va.ap[:, tg * 4:(tg + 1) * 4, 0:128], bank.ap[:, :].rearrange("p (a b) -> p a b", b=128),
                          [bank.T], [Va.T], eng="act")
            self.w_release(1)
            clo = sm[:, SM_CLOHI + 2 * h:SM_CLOHI + 2 * h + 1]
            chi = sm[:, SM_CLOHI + 2 * h + 1:SM_CLOHI + 2 * h + 2]
            for qb in range(4):
                qsl = slice(qb * 512, (qb + 1) * 512)
                for sidx in range(2):
                    QP = Q1p if sidx == 0 else Q2p
                    osb = o1 if sidx == 0 else o2
                    Ob = [self.abank() for _ in range(4)]

                    def s_mm(kt):
                        b = self.gbank()
                        self.mm([(b.ap[:, :], KT.ap[:, kt * 128:(kt + 1) * 128], QP.ap[:, qsl], True, True)],
                                [KT.T, QP.T], [b.T])
                        return b

                    nxt = s_mm(0)
                    for kt in range(16):
                        sbk = nxt
                        if kt < 15:
                            nxt = s_mm(kt + 1)
                        pt = PT.next()
                        ee = kt - 4 * qb
                        ds = [ee - j for j in range(4)]
                        near = [j for j in range(4) if abs(ds[j]) <= 1]
                        if not near:
                            bias = clo if ds[0] < 0 else chi
                            self.act(pt.ap[:], sbk.ap[:], AF.Exp, [sbk.T, self.small.T], [pt.T], scale=SCALE, bias=bias)
                        else:
                            j0, j1 = near[0], near[-1]
                            c0 = (1 - ds[j0]) * 128
                            w_ = (j1 - j0 + 1) * 128
                            self.stt(sbk.ap[:, j0 * 128:j0 * 128 + w_], sbk.ap[:, j0 * 128:j0 * 128 + w_], SCALE,
                                     strips.ap[:, h, c0:c0 + w_], ALU.mult, ALU.add, [sbk.T, strips.T], [sbk.T])
                            self.act(pt.ap[:, j0 * 128:j0 * 128 + w_], sbk.ap[:, j0 * 128:j0 * 128 + w_], AF.Exp,
                                     [sbk.T], [pt.T])
                            if j0 > 0:
                                self.act(pt.ap[:, 0:j0 * 128], sbk.ap[:, 0:j0 * 128], AF.Exp, [sbk.T, self.small.T], [pt.T],
                                         scale=SCALE, bias=chi)
                            if j1 < 3:
                                self.act(pt.ap[:, (j1 + 1) * 128:512], sbk.ap[:, (j1 + 1) * 128:512], AF.Exp,
                                         [sbk.T, self.small.T], [pt.T], scale=SCALE, bias=clo)
                        self.mm([(Ob[j].ap[:, 0:129], pt.ap[:, j * 128:(j + 1) * 128], Va.ap[:, kt, 0:129], kt == 0, kt == 15)
                                 for j in range(4)], [pt.T, Va.T], [b.T for b in Ob])
                    for j in range(4):
                        self.recip(stat.ap[:, j:j + 1], Ob[j].ap[:, 128:129], [Ob[j].T], [stat.T])
                        self.ts(osb.ap[:, j, :], Ob[j].ap[:, 0:128], stat.ap[:, j:j + 1], None, ALU.mult, None,
                                [Ob[j].T, stat.T], [osb.T])
                o1f = o1.ap[:, :, :].rearrange("p a b -> p (a b)")
                o2f = o2.ap[:, :, :].rearrange("p a b -> p (a b)")
                self.stt(o1f, o2f, dv[:, e:e + 1], o1f, ALU.mult, ALU.add, [o1.T, o2.T, self.derived.T], [o1.T])
                self.tt(o2f, o1f, o1f, ALU.mult, [o1.T], [o2.T])
                self.s.op("dve", lambda e_: e_.tensor_reduce(out=stat.ap[:, 8:12], in_=o2.ap[:, :, :], axis=AX, op=ALU.add),
                          [o2.T], [stat.T])
                self.act(stat.ap[:, 12:16], stat.ap[:, 8:12], AF.Sqrt, [stat.T, self.epsT.T], [stat.T],
                         scale=1.0 / 128, bias=self.epsT.ap[:, 0:1])
                self.recip(stat.ap[:, 16:20], stat.ap[:, 12:16], [stat.T], [stat.T])
                for j in range(4):
                    self.stt(ystage.ap[:, qb * 4 + j, :], o1.ap[:, j, :], stat.ap[:, 16 + j:17 + j],
                             sm[:, SM_SUBLN + e * 128:SM_SUBLN + (e + 1) * 128], ALU.mult, ALU.mult,
                             [o1.T, stat.T, self.small.T], [ystage.T])
                tb_ = self.gbank()
                tbf = tb_.ap[:, :].bitcast(BF16)
                self.transposes([(tbf[:, j * 128:(j + 1) * 128], ystage.ap[:, qb * 4 + j, :], self.ident) for j in range(4)],
                                [ystage.T, self.consts.T], [tb_.T])
                self.copy(ytq.ap[:, qsl], tbf[:, 0:512], [tb_.T], [ytq.T], eng="act")
            wslot, wT = self.w_acquire(first + 2 * h + 1)
            self.out_proj_chunk(l, s, ytq, wslot, wT)
            self.w_release(1)
        self.barrier()
        self.even_mixer_b(l, s)
        self.barrier()

    def even_mixer_b(self, l, s):
        e = l // 2
        first = self.piece_index["mix%d" % l][0] + 8
        sm = self.small.ap
        dv = self.derived.ap
        Qp = [self.view("Qpb%d" % j, 4096 * j, [128, 2048], BF16) for j in range(4)]
        KTb = self.view("KTb", 16384, [128, 2048], BF16)
        Vb = self.view("Vb", 20480, [128, 16, 2, 66], BF16)
        strips = self.view("stripsB", 24832, [128, 8, 384], F32)
        maskB = self.view("maskB", 37120, [128, 384], F32)
        PT = Ring([self.view("PTb%d" % i, 38656 + 1024 * i, [128, 512], BF16) for i in range(3)])
        tmp = Ring([self.view("tmpB%d" % i, 41728 + 2048 * i, [128, 512], F32) for i in range(4)])
        ystage = self.view("ystageB", 49920, [128, 16, 128], BF16)
        ytq = self.view("ytqB", 54016, [128, 2048], BF16)
        stat = self.view("statB", 58112, [128, 32], F32)
        for j in range(4):
            zs = slice(64, 128) if j % 2 == 0 else slice(0, 64)
            self.memset(Qp[j].ap[zs, :], 0.0, [Qp[j].T])
        self.s.dma("sp", [lambda e_: e_.dma_start(out=strips.ap[:, :, :], in_=self.strips_in[4:12, :, :].rearrange("h p c -> p h c")),
                          lambda e_: e_.dma_start(out=maskB.ap[:, :], in_=self.strips_in[12, :, :])],
                   "ld_a", (), [strips.T, maskB.T])
        self.tt(strips.ap[:, :, :], strips.ap[:, :, :], maskB.ap[:, :].unsqueeze(1).to_broadcast([128, 8, 384]), ALU.add,
                [strips.T, maskB.T], [strips.T])
        self.memset(Vb.ap[:, :, :, 64:66], 1.0, [Vb.T])
        slot, sT = self.w_acquire(first)
        for tg in range(4):
            bank = self.gbank()
            items = []
            for tt_ in range(4):
                tok = slice((tg * 4 + tt_) * 128, (tg * 4 + tt_ + 1) * 128)
                for c in range(8):
                    items.append((bank.ap[:, tt_ * 128:(tt_ + 1) * 128], self.hT[:, c, tok],
                                  slot[:, c * 128:(c + 1) * 128], c == 0, c == 7))
            self.mm(items, [sT] + [self.hT_T[c][tg] for c in range(8)], [bank.T])
            self.copy(Vb.ap[:, tg * 4:(tg + 1) * 4, :, 0:64],
                      bank.ap[:, :].rearrange("p (a g d) -> p a g d", g=2, d=64), [bank.T], [Vb.T], eng="act")
        self.w_release(1)
        for g in range(2):
            slot, sT = self.w_acquire(first + 1 + 3 * g)
            for cb in range(2):
                self.qk_proj_norm(slot, sT, cb * 128, [(Qp[2 * cb], slice(0, 64)), (Qp[2 * cb + 1], slice(64, 128))], None,
                                  sm[:, SM_QKG + e * 4 + 2:SM_QKG + e * 4 + 3], tmp, PT)
            self.qk_proj_norm(slot, sT, 256, KTb, None, sm[:, SM_QKG + e * 4 + 3:SM_QKG + e * 4 + 4], tmp, PT)
            self.w_release(1)
            for cb in range(2):
                for hh in range(2):
                    hq = g * 4 + cb * 2 + hh
                    ph = slice(64 * hh, 64 * hh + 64)
                    Ob = {}
                    qpj = Qp[2 * cb + hh]

                    def s_mm(kt):
                        qt0 = max(kt - 1, 0)
                        W = (min(kt + 1, 15) - qt0 + 1) * 128
                        b_ = self.gbank()
                        self.mm([(b_.ap[:, 0:W], KTb.ap[:, kt * 128:(kt + 1) * 128], qpj.ap[:, qt0 * 128:qt0 * 128 + W],
                                  True, True)], [KTb.T, qpj.T], [b_.T])
                        return b_

                    nxt = [s_mm(0), s_mm(1)]
                    for kt in range(16):
                        qt0 = max(kt - 1, 0)
                        qt1 = min(kt + 1, 15)
                        W = (qt1 - qt0 + 1) * 128
                        off = 128 if kt == 0 else 0
                        sbk = nxt.pop(0)
                        if kt + 2 < 16:
                            nxt.append(s_mm(kt + 2))
                        self.stt(sbk.ap[:, 0:W], sbk.ap[:, 0:W], SCALE, strips.ap[:, hq, off:off + W], ALU.mult, ALU.add,
                                 [sbk.T, strips.T], [sbk.T])
                        pt = PT.next()
                        self.act(pt.ap[:, 0:W], sbk.ap[:, 0:W], AF.Exp, [sbk.T], [pt.T])
                        items = []
                        for qt in range(qt0, qt1 + 1):
                            if qt not in Ob:
                                Ob[qt] = self.abank()
                            items.append((Ob[qt].ap[:, 0:65], pt.ap[:, (qt - qt0) * 128:(qt - qt0 + 1) * 128],
                                          Vb.ap[:, kt, g, 0:65], kt == max(qt - 1, 0), kt == min(qt + 1, 15)))
                        self.mm(items, [pt.T, Vb.T], [Ob[qt].T for qt in range(qt0, qt1 + 1)])
                        done = [qt for qt in range(qt0, qt1 + 1) if kt == min(qt + 1, 15)]
                        for qt in done:
                            ob = Ob.pop(qt)
                            self.ts(stat.ap[:, 0:1], ob.ap[:, 64:65], dv[:, 2 + e * 8 + hq:3 + e * 8 + hq], None, ALU.add, None,
                                    [ob.T, self.derived.T], [stat.T])
                            self.recip(stat.ap[:, 1:2], stat.ap[:, 0:1], [stat.T], [stat.T])
                            self.ts(ystage.ap[:, qt, hh * 64:(hh + 1) * 64], ob.ap[:, 0:64], stat.ap[:, 1:2], None,
                                    ALU.mult, None, [ob.T, stat.T], [ystage.T])
                for qb in range(4):
                    tb_ = self.gbank()
                    tbf = tb_.ap[:, :].bitcast(BF16)
                    self.transposes([(tbf[:, j * 128:(j + 1) * 128], ystage.ap[:, qb * 4 + j, :], self.ident) for j in range(4)],
                                    [ystage.T, self.consts.T], [tb_.T])
                    self.copy(ytq.ap[:, qb * 512:(qb + 1) * 512], tbf[:, 0:512], [tb_.T], [ytq.T], eng="act")
                wslot, wT = self.w_acquire(first + 2 + 3 * g + cb)
                self.out_proj_chunk(l, s, ytq, wslot, wT)
                self.w_release(1)

    def ada_phase(self):
        self.act(self.scT.ap[:], self.cT.ap[:], AF.Silu, [self.cT.T], [self.scT.T])
        for l in self.layers:
            bank = self.abank()
            for pj in range(N_ADA_PIECES):
                slot, sT = self.w_acquire(self.piece_index["ada%d" % l][0] + pj)
                items = []
                for mcc in range(3):
                    mc = pj * 3 + mcc
                    for dc in range(8):
                        items.append((bank.ap[:, mc * 2:mc * 2 + 2],
                                      slot[:, dc * 384 + mcc * 128: dc * 384 + (mcc + 1) * 128],
                                      self.scT.ap[:, dc, :], dc == 0, dc == 7))
                self.mm(items, [sT, self.scT.T], [bank.T])
                self.w_release(1)
            self.tt(self.modT.ap[:, l, :, :, :].rearrange("p m c s -> p (m c) s"),
                    bank.ap[:, 0:144].rearrange("p (m s) -> p m s", s=2),
                    self.adab.ap[:, l, :].unsqueeze(2).to_broadcast([128, 72, 2]),
                    ALU.add, [bank.T, self.adab.T], [self.modT.T])
            for j in range(3):
                self.stt(self.gmod.ap[:, l, j, :, :], self.modT.ap[:, l, 3 * j + 1, :, :], 1.0,
                         self.normg.ap[:, l, j, :].unsqueeze(2).to_broadcast([128, 8, 2]),
                         ALU.add, ALU.mult, [self.modT.T, self.normg.T], [self.gmod.T])
                self.ts(self.gate.ap[:, l, j, :, :], self.modT.ap[:, l, 3 * j + 2, :, :],
                        (1.0 if j == 1 else 0.5), None, ALU.mult, None, [self.modT.T], [self.gate.T])

    def norm(self, l, j, s):
        for tb in range(4):
            tsl = slice(tb * 512, (tb + 1) * 512)
            xTs = [self.xT_T[c][tb] for c in range(8)]
            sq = self.sq.next()
            self.act(sq.ap[:], self.xT[:, :, tsl], AF.Square, xTs, [sq.T])
            bank = self.gbank()
            self.mmacc(bank.ap[:, :], [(self.ones_bf.ap[:], sq.ap[:, c, :]) for c in range(8)],
                       [sq.T, self.ones_bf.T], [bank.T])
            rstd = self.rstd.next()
            self.act(rstd.ap[:], bank.ap[:], AF.Ln, [bank.T, self.epsT.T], [rstd.T], scale=1.0 / D, bias=self.epsT.ap[:, 0:1])
            self.act(rstd.ap[:], rstd.ap[:], AF.Exp, [rstd.T], [rstd.T], scale=-0.5)
            for c in range(8):
                t1 = self.f32a.next()
                self.stt(t1.ap[:], self.xT[:, c, tsl], self.gmod.ap[:, l, j, c, s:s + 1], rstd.ap[:],
                         ALU.mult, ALU.mult, [self.xT_T[c][tb], self.gmod.T, rstd.T], [t1.T])
                self.act(self.hT[:, c, tsl], t1.ap[:], AF.Identity, [t1.T, self.modT.T], [self.hT_T[c][tb]],
                         bias=self.modT.ap[:, l, 3 * j, c, s:s + 1], scale=1.0)

    def ffn(self, l, j, s):
        first = self.piece_index["ffn%d_%d" % (l, j)][0]
        jm = 0 if j == 0 else 2
        for (f0, f1) in FGROUPS:
            slots = [self.w_acquire(first + f) for f in range(f0, f1)]
            ng = f1 - f0
            for tb in range(4):
                tsl = slice(tb * 512, (tb + 1) * 512)
                hTs = [self.hT_T[c][tb] for c in range(8)]
                hid = self.hid[self.hid_i % 2]
                self.hid_i += 1
                for fi in range(ng):
                    slot, sT = slots[fi]
                    ba = self.gbank()
                    bb = self.gbank()
                    self.mmacc(ba.ap[:, :], [(slot[:, c * 256:c * 256 + 128], self.hT[:, c, tsl]) for c in range(8)],
                               [sT] + hTs, [ba.T])
                    self.mmacc(bb.ap[:, :], [(slot[:, c * 256 + 128:c * 256 + 256], self.hT[:, c, tsl]) for c in range(8)],
                               [sT] + hTs, [bb.T])
                    sa = self.f32a.next()
                    self.act(sa.ap[:], ba.ap[:], AF.Silu, [ba.T], [sa.T])
                    self.tt(hid.ap[:, fi, :], sa.ap[:], bb.ap[:], ALU.mult, [sa.T, bb.T], [hid.T])
                for c in range(8):
                    by = self.abank()
                    self.mmacc(by.ap[:, :], [(slots[fi][0][:, 2048 + c * 128:2048 + (c + 1) * 128], hid.ap[:, fi, :])
                                             for fi in range(ng)],
                               [hid.T] + [sl[1] for sl in slots], [by.T])
                    self.stt(self.xT[:, c, tsl], by.ap[:], self.gate.ap[:, l, jm, c, s:s + 1], self.xT[:, c, tsl],
                             ALU.mult, ALU.add, [by.T, self.gate.T, self.xT_T[c][tb]], [self.xT_T[c][tb]])
            self.w_release(ng)

    def odd_mixer(self, l, s):
        o = l // 2
        first = self.piece_index["mix%d" % l][0]
        sm = self.small.ap
        dv = self.derived.ap
        qs = self.view("qs", 0, [128, 2048], F32)
        V = self.view("Vo", 8192, [128, 16, 128], BF16)
        gs = self.view("gs", 12288, [128, 2048], BF16)
        qtl = [self.view("qtl%d" % d, 16384 + 4096 * d, [128, 2048], BF16) for d in range(2)]
        ktl = [self.view("ktl%d" % d, 24576 + 4096 * d, [128, 2048], BF16) for d in range(2)]
        khat = [self.view("khat%d" % d, 32768 + 4096 * d, [128, 16, 128], BF16) for d in range(2)]
        sets = [[self.view("t%d_%d" % (d, j), 40960 + 8192 * d + 2048 * j, [128, 512], F32) for j in range(4)] for d in range(2)]
        khTs = [self.view("khT%d" % d, 58368 + 1024 * d, [128, 512], BF16) for d in range(2)]
        self.ab_n = 3
        smask = self.ab[3]
        self.memset(smask.ap[:, :], 1.0, [smask.T])
        self.memset(smask.ap[:, :].rearrange("p (c k) -> p c k", k=64)[:, :, 0:1], 0.0, [smask.T])
        st8 = self.view("st8", 58112, [128, 16], F32)
        Shat = [self.view("Shat%d" % d, 40960 + 8192 * d, [128, 32, 128], BF16) for d in range(2)]
        cs = self.view("cs", 57344, [128, 2, 2, 32], F32)
        yq = Ring([self.view("yq%d" % i, 58368 + 1024 * i, [128, 512], BF16) for i in range(2)])
        Am = Ring([self.view("Am%d" % i, 256 * i, [128, 128], BF16) for i in range(4)])
        Sst = [self.view("Sst%d" % d, 1024 + 512 * d, [128, 128], F32) for d in range(2)]
        Ost = self.view("Ost", 2048, [128, 512], F32)
        sqO = self.view("sqO", 4096, [128, 512], BF16)
        rsO = self.view("rsO", 5120, [128, 512], F32)


        def c3(buf):
            return buf.ap[:, :].rearrange("p (c k) -> p c k", k=64)

        for h in range(8):
            p1, p1T = self.w_acquire(first + 3 * h)
            p2, p2T = self.w_acquire(first + 3 * h + 1)
            for tb in range(4):
                tsl = slice(tb * 512, (tb + 1) * 512)
                hTs = [self.hT_T[c][tb] for c in range(8)]
                bq = self.gbank()
                self.mmacc(bq.ap[:, :], [(p1[:, c * 384:c * 384 + 128], self.hT[:, c, tsl]) for c in range(8)], [p1T] + hTs, [bq.T])
                self.act(qs.ap[:, tsl], bq.ap[:], AF.Silu, [bq.T], [qs.T])
                bg = self.gbank()
                self.mmacc(bg.ap[:, :], [(p2[:, c * 256 + 128:c * 256 + 256], self.hT[:, c, tsl]) for c in range(8)], [p2T] + hTs, [bg.T])
                self.act(gs.ap[:, tsl], bg.ap[:], AF.Silu, [bg.T], [gs.T])
            for tg in range(4):
                bank = self.gbank()
                items = []
                for tt_ in range(4):
                    tok = slice((tg * 4 + tt_) * 128, (tg * 4 + tt_ + 1) * 128)
                    for c in range(8):
                        items.append((bank.ap[:, tt_ * 128:(tt_ + 1) * 128], self.hT[:, c, tok],
                                      p2[:, c * 256:c * 256 + 128], c == 0, c == 7))
                self.mm(items, [p2T] + [self.hT_T[c][tg] for c in range(8)], [bank.T])
                self.copy(V.ap[:, tg * 4:(tg + 1) * 4, :], bank.ap[:, :].rearrange("p (a b) -> p a b", b=128),
                          [bank.T], [V.T], eng="act")
            for tb in range(4):
                tsl = slice(tb * 512, (tb + 1) * 512)
                csl = slice(tb * 8, (tb + 1) * 8)
                hTs = [self.hT_T[c][tb] for c in range(8)]
                bfs = []
                for d in range(2):
                    bf_ = self.gbank()
                    self.mmacc(bf_.ap[:, :], [(p1[:, c * 384 + 128 * (1 + d):c * 384 + 128 * (2 + d)], self.hT[:, c, tsl])
                                              for c in range(8)], [p1T] + hTs, [bf_.T])
                    bfs.append(bf_)
                cols = []
                for d in range(2):
                    ci = (d * 2 + o) * 8 + h
                    cols.append((dv[:, 18 + ci:19 + ci], dv[:, 50 + ci:51 + ci], dv[:, 82 + ci:83 + ci]))
                for d in range(2):
                    S1, KK, G, TM = sets[d]
                    self.act(S1.ap[:], bfs[d].ap[:], AF.Sigmoid, [bfs[d].T], [S1.T])
                for d in range(2):
                    S1, KK, G, TM = sets[d]
                    lbc, omlc, nomlc = cols[d]
                    self.ts(KK.ap[:], S1.ap[:], nomlc, omlc, ALU.mult, ALU.add, [S1.T, self.derived.T], [KK.T])
                for d in range(2):
                    S1, KK, G, TM = sets[d]
                    lbc, omlc, nomlc = cols[d]
                    self.act(S1.ap[:], S1.ap[:], AF.Ln, [S1.T, self.derived.T, KK.T], [S1.T], scale=omlc, bias=lbc)
                for d in range(2):
                    S1, KK, G, TM = sets[d]
                    self.s.op("dve", lambda e_, G=G, S1=S1: e_.tensor_tensor_scan(out=G.ap[:], data0=smask.ap[:], data1=S1.ap[:],
                                                                                  initial=0.0, op0=ALU.mult, op1=ALU.add),
                              [smask.T, S1.T], [G.T])
                S1, KK, G, TM = sets[0]
                self.act(cs.ap[:, 0, 0, csl], c3(G)[:, :, 31], AF.Exp, [G.T], [cs.T])
                self.act(cs.ap[:, 0, 1, csl], c3(G)[:, :, 63], AF.Exp, [G.T], [cs.T])
                self.tt(c3(TM), c3(G), c3(G)[:, :, 31:32].to_broadcast([128, 8, 64]), ALU.subtract, [G.T], [TM.T])
                S1, KK, G, TM = sets[1]
                self.act(cs.ap[:, 1, 1, csl], c3(G)[:, :, 63], AF.Exp, [G.T], [cs.T])
                self.copy(st8.ap[:, 0:8], c3(G)[:, :, 63], [G.T], [st8.T])
                self.tt(G.ap[:], G.ap[:], S1.ap[:], ALU.subtract, [G.T, S1.T], [G.T])
                self.tt(st8.ap[:, 0:8], st8.ap[:, 0:8], c3(G)[:, :, 32], ALU.subtract, [st8.T, G.T], [st8.T])
                self.act(cs.ap[:, 1, 0, csl], st8.ap[:, 0:8], AF.Exp, [st8.T], [cs.T])
                self.tt(c3(TM), c3(G)[:, :, 32:33].to_broadcast([128, 8, 64]), c3(G), ALU.subtract, [G.T], [TM.T])
                for d in range(2):
                    S1, KK, G, TM = sets[d]
                    self.act(S1.ap[:], TM.ap[:], AF.Exp, [TM.T], [S1.T])
                for d in range(2):
                    S1, KK, G, TM = sets[d]
                    self.tt(qtl[d].ap[:, tsl], qs.ap[:, tsl], S1.ap[:], ALU.mult, [qs.T, S1.T], [qtl[d].T], eng="pool")
                for d in range(2):
                    S1, KK, G, TM = sets[d]
                    self.act(S1.ap[:], TM.ap[:], AF.Exp, [TM.T], [S1.T], scale=-1.0)
                for d in range(2):
                    S1, KK, G, TM = sets[d]
                    self.tt(ktl[d].ap[:, tsl], KK.ap[:], S1.ap[:], ALU.mult, [KK.T, S1.T], [ktl[d].T], eng="pool")
                S1, KK, G, TM = sets[0]
                self.tt(c3(TM), c3(G)[:, :, 63:64].to_broadcast([128, 8, 64]), c3(G), ALU.subtract, [G.T], [TM.T])
                self.act(TM.ap[:], TM.ap[:], AF.Exp, [TM.T], [TM.T])
                S1, KK, G, TM = sets[1]
                self.act(TM.ap[:], G.ap[:], AF.Exp, [G.T], [TM.T])
                for d in range(2):
                    S1, KK, G, TM = sets[d]
                    self.tt(khTs[d].ap[:], KK.ap[:], TM.ap[:], ALU.mult, [KK.T, TM.T], [khTs[d].T], eng="pool")
                for d in range(2):
                    tb_ = self.gbank()
                    tbf = tb_.ap[:, :].bitcast(BF16)
                    self.transposes([(tbf[:, j * 128:(j + 1) * 128], khTs[d].ap[:, j * 128:(j + 1) * 128], self.ident)
                                     for j in range(4)], [khTs[d].T, self.consts.T], [tb_.T])
                    self.copy(khat[d].ap[:, tb * 4:(tb + 1) * 4, :], tbf[:, 0:512].rearrange("p (a b) -> p a b", b=128),
                              [tb_.T], [khat[d].T], eng="act")
            self.w_release(2)
            wslot, wT = self.w_acquire(first + 3 * h + 2)
            self.barrier()
            for i in range(32):
                for d in range(2):
                    c = i if d == 0 else 31 - i
                    St = Sst[d]
                    rows = slice((c % 2) * 64, (c % 2) * 64 + 64)
                    if i == 0:
                        self.memset(Shat[d].ap[:, c, :], 0.0, [Shat[d].T])
                    else:
                        self.ts(Shat[d].ap[:, c, :], St.ap[:], cs.ap[:, d, 0, c:c + 1], None, ALU.mult, None,
                                [St.T, cs.T], [Shat[d].T])
                    bs = self.gbank()
                    self.mm([(bs.ap[:, 0:128], khat[d].ap[rows, c // 2, :], V.ap[rows, c // 2, :], True, True)],
                            [khat[d].T, V.T], [bs.T])
                    if i == 0:
                        self.copy(St.ap[:], bs.ap[:, 0:128], [bs.T], [St.T])
                    else:
                        self.stt(St.ap[:], St.ap[:], cs.ap[:, d, 1, c:c + 1], bs.ap[:, 0:128], ALU.mult, ALU.add,
                                 [St.T, cs.T, bs.T], [St.T])
            if "odd" in self.debug and h == 0 and s == 0:
                self.barrier()
                for d in range(2):
                    self.dump("Shat%d" % d, Shat[d].ap[:], [Shat[d].T], BF16)
                self.barrier()
            for qb in range(4):
                qsl = slice(qb * 512, (qb + 1) * 512)
                bO = self.abank()
                for blk in range(4):
                    nb = qb * 4 + blk
                    tok = slice(nb * 128, (nb + 1) * 128)
                    ams = []
                    for d in range(2):
                        ba = self.gbank()
                        self.mm([(ba.ap[:, 0:128], ktl[d].ap[:, tok], qtl[d].ap[:, tok], True, True)], [ktl[d].T, qtl[d].T], [ba.T])
                        am = Am.next()
                        self.tt(am.ap[:], ba.ap[:, 0:128], self.mask_f if d == 0 else self.mask_b, ALU.mult,
                                [ba.T, self.consts.T], [am.T])
                        ams.append(am)
                    oc = bO.ap[:, blk * 128:(blk + 1) * 128]
                    items = [(oc, V.ap[:, nb, :], ams[0].ap[:], True, False),
                             (oc, V.ap[:, nb, :], ams[1].ap[:], False, False)]
                    for d in range(2):
                        for hf in range(2):
                            c = 2 * nb + hf
                            items.append((bO.ap[:, blk * 128 + hf * 64:blk * 128 + (hf + 1) * 64], Shat[d].ap[:, c, :],
                                          qtl[d].ap[:, nb * 128 + hf * 64:nb * 128 + (hf + 1) * 64], False, d == 1 and hf == 1))
                    self.mm(items, [V.T, ams[0].T, ams[1].T, Shat[0].T, Shat[1].T, qtl[0].T, qtl[1].T], [bO.T])
                self.act(Ost.ap[:], bO.ap[:], AF.Copy, [bO.T], [Ost.T])
                if "odd" in self.debug and h == 0 and s == 0 and qb == 0:
                    self.dump("Ost", Ost.ap[:], [Ost.T])
                    self.dump("Am", ams[0].ap[:], [ams[0].T], BF16)
                    self.dump("Amb", ams[1].ap[:], [ams[1].T], BF16)
                    self.barrier()
                self.act(sqO.ap[:], bO.ap[:], AF.Square, [bO.T], [sqO.T])
                b2 = self.gbank()
                self.mm([(b2.ap[:, :], self.ones_bf.ap[:], sqO.ap[:], True, True)], [sqO.T, self.ones_bf.T], [b2.T])
                self.act(rsO.ap[:], b2.ap[:], AF.Ln, [b2.T, self.epsT.T], [rsO.T], scale=1.0 / 128, bias=self.epsT.ap[:, 0:1])
                self.act(rsO.ap[:], rsO.ap[:], AF.Exp, [rsO.T], [rsO.T], scale=-0.5)
                self.tt(Ost.ap[:], Ost.ap[:], rsO.ap[:], ALU.mult, [Ost.T, rsO.T], [Ost.T])
                y = yq.next()
                self.stt(y.ap[:], Ost.ap[:], sm[:, SM_OUTG + o:SM_OUTG + o + 1], gs.ap[:, qsl], ALU.mult, ALU.mult,
                         [Ost.T, self.small.T, gs.T], [y.T])
                if "odd" in self.debug and s == 0 and qb in (0, 3):
                    self.dump("y_h%d_q%d" % (h, qb), y.ap[:], [y.T], BF16)
                    self.dump("Ost_h%d_q%d" % (h, qb), Ost.ap[:], [Ost.T])
                    self.barrier()
                for c in range(8):
                    by = self.abank()
                    self.mm([(by.ap[:, :], wslot[:, c * 128:(c + 1) * 128], y.ap[:], True, True)], [wT, y.T], [by.T])
                    self.stt(self.xT[:, c, qsl], by.ap[:], self.gate.ap[:, l, 1, c, s:s + 1], self.xT[:, c, qsl],
                             ALU.mult, ALU.add, [by.T, self.gate.T, self.xT_T[c][qb]], [self.xT_T[c][qb]])
            self.w_release(1)
            self.barrier()
        self.ab_n = 4

    def emit(self):
        nc = self.nc
        keys = sorted(self.s.val.keys())
        for k in keys:
            self.sems[k] = self.stack.enter_context(nc.semaphore(k))
        sems = self.sems
        q = self.s.q

        def run(e, ops):
            for waits, fn, inc in ops:
                for k, v in waits:
                    e.wait_ge(sems[k], v)
                if fn is None:
                    continue
                ins = fn(e)
                ins.then_inc(sems[inc[0]], inc[1])

        with nc.Block() as block:
            @block.tensor
            def _(e):
                run(e, q["pe"])

            @block.scalar
            def _(e):
                run(e, q["act"])

            @block.vector
            def _(e):
                run(e, q["dve"])

            @block.gpsimd
            def _(e):
                run(e, q["pool"])

            @block.sync
            def _(e):
                run(e, q["sp"])
        self.stack.close()


def make_stream(inputs, layers, do_mixer=True, do_ffn=True):
    pieces = []
    index = {}

    def add(name, plist, cols):
        index[name] = (len(pieces), cols)
        pieces.extend(plist)

    for l in layers:
        add("ada%d" % l, ada_pieces(inputs["ada_w"], l), [SLOT] * N_ADA_PIECES)
    for l in layers:
        if do_ffn:
            add("ffn%d_0" % l, ffn_pieces(inputs["ffn_up"], inputs["ffn_down"], l, 0), [SLOT] * NF)
        if do_mixer:
            if l % 2 == 0:
                add("mix%d" % l, even_pieces(inputs["even_w_in"], inputs["even_w_out"], l // 2),
                    EVEN_COLS)
            else:
                add("mix%d" % l, odd_pieces(inputs["odd_w_in"], inputs["odd_w_out"], l // 2),
                    ODD_COLS)
        if do_ffn:
            add("ffn%d_1" % l, ffn_pieces(inputs["ffn_up"], inputs["ffn_down"], l, 1), [SLOT] * NF)
    return np.stack(pieces, axis=0), index


def lay_x(xb):
    return np.ascontiguousarray(xb.reshape(S, 8, 128).transpose(2, 1, 0))


def unlay_x(xt):
    return np.ascontiguousarray(xt.transpose(2, 1, 0).reshape(S, D))


def common_maps(inputs, x_cur, batch_ids_per_core, wstream):
    adab = np.ascontiguousarray(inputs["ada_b"].reshape(DEPTH, 72, 128).transpose(2, 0, 1))
    normg = np.ascontiguousarray(inputs["norm_g"].reshape(DEPTH, 3, 8, 128).transpose(3, 0, 1, 2))
    small = small_inputs(inputs)
    strips = strips_input(inputs)
    consts = consts_input()
    maps = []
    for bids in batch_ids_per_core:
        xin = np.stack([lay_x(x_cur[b]) for b in bids], axis=0)
        cT = np.stack([inputs["c"][b].reshape(8, 128).T for b in bids], axis=2)
        if len(bids) == 1:
            cT = np.concatenate([cT, cT], axis=2)
        maps.append({"wstream": wstream, "x_in": xin, "c_in": np.ascontiguousarray(cT, dtype=np.float32),
                     "adab_in": adab, "normg_in": normg, "small_in": small, "strips_in": strips, "consts_in": consts})
    return maps


def run_layers(inputs, x_cur, layers, batch_ids_per_core, do_mixer=True, do_ffn=True, trace=False, debug=None):
    inputs = {k: np.asarray(v, dtype=np.float32) for k, v in inputs.items()}
    nseq = len(batch_ids_per_core[0])
    wstream, index = make_stream(inputs, layers, do_mixer, do_ffn)
    b = Builder(layers, nseq=nseq, do_mixer=do_mixer, do_ffn=do_ffn, debug=debug)
    nc = b.build(index, wstream.shape[0])
    maps = common_maps(inputs, x_cur, batch_ids_per_core, wstream)
    res = run_bass_kernel_spmd(nc, maps, core_ids=list(range(len(maps))), **({"trace": True} if trace else {}))
    out = np.array(x_cur, dtype=np.float32, copy=True)
    for ci, bids in enumerate(batch_ids_per_core):
        xo = res.results[ci]["x_out"]
        for si, bb in enumerate(bids):
            out[bb] = unlay_x(xo[si])
    return out, res


FUSED = True


def kernel(**inputs):
    inputs = {k: np.asarray(v, dtype=np.float32) for k, v in inputs.items()}
    x = inputs["x"]
    bids = [[2 * i, 2 * i + 1] for i in range(NCORES)]
    if FUSED:
        out, _ = run_layers(inputs, x, list(range(DEPTH)), bids)
    else:
        out = x
        for l in range(DEPTH):
            out, _ = run_layers(inputs, out, [l], bids)
    return out.astype(np.float32)
```

```python
import math
from contextlib import ExitStack
import numpy as np
import concourse.bass as bass
import concourse.mybir as mybir
from concourse.bass_utils import run_bass_kernel_spmd

F32 = mybir.dt.float32
BF16 = mybir.dt.bfloat16
AF = mybir.ActivationFunctionType
ALU = mybir.AluOpType

D = 1024
S = 2048
DEPTH = 4
NCORES = 8
NSEQ = 2
DFF = 2816
NF = 22
EPS = 1e-6
SLOT = 3072
NSLOT = 7
FGROUPS = [(0, 4), (4, 8), (8, 12), (12, 16), (16, 19), (19, 22)]
N_ADA_PIECES = 24
SCALE = 0.125
NEG = -30000.0
ARENA = 60416


def _kin(w):
    return w.reshape(8, 128, -1).transpose(1, 0, 2)


def _pad(a):
    a = np.ascontiguousarray(a, dtype=np.float32).reshape(128, -1)
    out = np.zeros((128, SLOT), np.float32)
    out[:, : a.shape[1]] = a
    return out


def t5_bucket_np(rel):
    half = 16
    max_exact = 8
    n = np.abs(rel)
    nf = np.maximum(n, 1).astype(np.float32)
    large = max_exact + (np.log(nf / max_exact) / math.log(128 / max_exact) * (half - max_exact)).astype(np.int32)
    large = np.minimum(large, half - 1)
    return np.where(rel > 0, half, 0) + np.where(n < max_exact, n, large)


def ada_pieces(ada_w, l):
    w = _kin(ada_w[l])
    return [_pad(w[:, :, j * 384:(j + 1) * 384]) for j in range(N_ADA_PIECES)]


def ffn_pieces(ffn_up, ffn_down, l, j):
    up = _kin(ffn_up[l, j]).reshape(128, 8, 2, NF, 128)
    out = []
    for i in range(NF):
        u = up[:, :, :, i, :].reshape(128, 2048)
        dn = ffn_down[l, j, i * 128:(i + 1) * 128, :]
        out.append(_pad(np.concatenate([u, dn], axis=1)))
    return out


def wout_piece(w_out, chunk):
    return _pad(w_out[chunk * 128:(chunk + 1) * 128, :])


EVEN_COLS = [SLOT, 1024] * 4 + [1024] + [SLOT, 1024, 1024] * 2
ODD_COLS = [SLOT, 2048, 1024] * 8


def even_pieces(even_w_in, even_w_out, e):
    w = _kin(even_w_in[e])
    out = []
    for h in range(4):
        q = w[:, :, h * 128:(h + 1) * 128]
        k = w[:, :, 512 + h * 128:512 + (h + 1) * 128]
        v = w[:, :, 1024 + h * 128:1024 + (h + 1) * 128]
        out.append(_pad(np.concatenate([q, k, v], axis=2)))
        out.append(wout_piece(even_w_out[e], h))
    out.append(_pad(w[:, :, 2176:2304]))
    for g in range(2):
        q = w[:, :, 1536 + g * 256:1536 + (g + 1) * 256]
        k = w[:, :, 2048 + g * 64:2048 + (g + 1) * 64]
        out.append(_pad(np.concatenate([q, k, k], axis=2)))
        out.append(wout_piece(even_w_out[e], 4 + 2 * g))
        out.append(wout_piece(even_w_out[e], 5 + 2 * g))
    return out


def odd_pieces(odd_w_in, odd_w_out, o):
    w = _kin(odd_w_in[o])
    out = []
    for h in range(8):
        sl = slice(h * 128, (h + 1) * 128)
        q, ff, fb, iv, g = (w[:, :, k * 1024:(k + 1) * 1024][:, :, sl] for k in range(5))
        out.append(_pad(np.concatenate([q, ff, fb], axis=2)))
        out.append(_pad(np.concatenate([iv, g], axis=2)))
        out.append(wout_piece(odd_w_out[o], h))
    return out


SM_QKG, SM_DLAM, SM_SUBLN, SM_SINK, SM_CLOHI, SM_CLB, SM_OUTG, SM_W = 0, 8, 520, 776, 792, 816, 880, 882


def small_inputs(inputs):
    sm = np.zeros((128, SM_W), np.float32)
    p = np.arange(128)
    sm[:, SM_QKG:SM_QKG + 8] = inputs["qk_norm_g"][:, :, p % 64].transpose(2, 0, 1).reshape(128, 8)
    sm[:, SM_DLAM:SM_DLAM + 512] = inputs["diff_lambda"].reshape(1, 512)
    sm[:, SM_SUBLN:SM_SUBLN + 256] = inputs["diff_subln_g"].reshape(1, 256)
    sm[:, SM_SINK:SM_SINK + 16] = inputs["sink_logit"].reshape(1, 16)
    sm[:, SM_CLOHI:SM_CLOHI + 24] = inputs["rel_bias"][[15, 31], :].T.reshape(1, 24)
    sm[:, SM_CLB:SM_CLB + 64] = inputs["c_lower_bound"].reshape(2, 4, 8, 128).transpose(3, 0, 1, 2).reshape(128, 64)
    sm[:, SM_OUTG:SM_OUTG + 2] = inputs["c_out_norm_g"].T
    return sm


def strips_input(inputs):
    k = np.arange(128)[:, None]
    q = np.arange(128)[None, :]
    out = np.zeros((13, 128, 384), np.float32)
    for j, d in enumerate((1, 0, -1)):
        idx = t5_bucket_np(k - q + 128 * d)
        out[:12, :, j * 128:(j + 1) * 128] = inputs["rel_bias"][idx].transpose(2, 0, 1)
    out[12, :, 0:128] = np.where(k <= q, 0.0, NEG)
    out[12, :, 256:384] = np.where(k >= q, 0.0, NEG)
    return out


def consts_input():
    c = np.zeros((128, 4, 128), np.float32)
    c[:, 0, :] = np.eye(128)
    pp = np.arange(128)
    c[:, 1, :] = (pp[:, None] // 64 == pp[None, :] // 64)
    same = (pp[:, None] // 64 == pp[None, :] // 64)
    c[:, 2, :] = same & (pp[:, None] <= pp[None, :])
    c[:, 3, :] = same & (pp[:, None] >= pp[None, :])
    return c


class T:
    __slots__ = ("name", "w", "r")

    def __init__(self, name):
        self.name = name
        self.w = None
        self.r = {}


class Sched:
    ENG = ("pe", "act", "dve", "pool", "sp")

    def __init__(self):
        self.q = {e: [] for e in self.ENG}
        self.val = {}
        self.seen = {e: {} for e in self.ENG}

    def _deps(self, eng, reads, writes):
        need = {}

        def add(k, v):
            if v > need.get(k, 0):
                need[k] = v

        for t in reads:
            if t.w is not None:
                add(*t.w)
        for t in writes:
            if t.w is not None:
                add(*t.w)
            for k, v in t.r.items():
                add(k, v)
        waits = []
        seen = self.seen[eng]
        for k, v in need.items():
            if eng == "pe" and k == "c_pe":
                continue
            if seen.get(k, 0) < v:
                waits.append((k, v))
                seen[k] = v
        return waits

    def _mark(self, ev, reads, writes):
        k, v = ev
        for t in reads:
            if t.r.get(k, 0) < v:
                t.r[k] = v
        for t in writes:
            t.w = ev
            t.r = {}

    def op(self, eng, fn, reads=(), writes=()):
        waits = self._deps(eng, reads, writes)
        k = "c_" + eng
        v = self.val.get(k, 0) + 1
        self.val[k] = v
        self.q[eng].append((waits, fn, (k, 1)))
        self._mark((k, v), reads, writes)

    def dma(self, queue, fns, semkey, reads=(), writes=()):
        waits = self._deps(queue, reads, writes)
        for i, fn in enumerate(fns):
            self.val[semkey] = self.val.get(semkey, 0) + 16
            self.q[queue].append((waits if i == 0 else [], fn, (semkey, 16)))
        self._mark((semkey, self.val[semkey]), reads, writes)

    def final_wait(self, eng, semkeys):
        waits = [(k, self.val[k]) for k in semkeys if self.val.get(k, 0) > 0]
        self.q[eng].append((waits, None, None))


class Buf:
    def __init__(self, ap, name):
        self.ap = ap
        self.T = T(name)


class Ring:
    def __init__(self, bufs):
        self.bufs = bufs
        self.i = 0

    def next(self):
        b = self.bufs[self.i % len(self.bufs)]
        self.i += 1
        return b


class Builder:
    def __init__(self, layers, nseq=NSEQ, do_mixer=True, do_ffn=True, debug=None):
        self.debug = debug or set()
        self.layers = list(layers)
        self.nseq = nseq
        self.do_mixer = do_mixer
        self.do_ffn = do_ffn
        self.s = Sched()
        self.sems = {}
        self.nc = bass.Bass("TRN2", target_bir_lowering=False)
        self.stack = ExitStack()

    def sb(self, name, shape, dt=F32):
        t = self.stack.enter_context(self.nc.sbuf_tensor(name, list(shape), dt))
        return t

    def view(self, name, off, shape, dt=F32):
        n = int(np.prod(shape[1:]))
        nbytes = n * (4 if dt == F32 else 2)
        assert off % 4 == 0 and off + nbytes <= ARENA, (name, off, nbytes)
        ap = self.arena[:, off // 4:(off + (nbytes + 3) // 4 * 4) // 4]
        if dt != F32:
            ap = ap.bitcast(dt)
            ap = ap[:, 0:n]
        if len(shape) == 3:
            ap = ap.rearrange("p (a b) -> p a b", b=shape[2])
        elif len(shape) == 4:
            ap = ap.rearrange("p (a b c) -> p a b c", b=shape[2], c=shape[3])
        return Buf(ap, name)

    def barrier(self):
        keys = [k for k in self.s.val if k in ("c_pe", "c_act", "c_dve") or k.startswith("ld_a") or (self.debug and k == "st_x")]
        for eng in ("pe", "act", "dve", "sp"):
            waits = []
            for k in keys:
                v = self.s.val[k]
                if eng == "pe" and k == "c_pe":
                    continue
                if self.s.seen[eng].get(k, 0) < v:
                    waits.append((k, v))
                    self.s.seen[eng][k] = v
            if waits:
                self.s.q[eng].append((waits, None, None))

    def buf(self, name, shape, dt=F32):
        t = self.sb(name, shape, dt)
        return Buf(t, name)

    def act(self, out, in_, func, reads, writes, **kw):
        self.s.op("act", lambda e: e.activation(out=out, in_=in_, func=func, **kw), reads, writes)

    def tt(self, out, in0, in1, op, reads, writes, eng="dve"):
        self.s.op(eng, lambda e: e.tensor_tensor(out=out, in0=in0, in1=in1, op=op), reads, writes)

    def ts(self, out, in0, s1, s2, op0, op1, reads, writes, eng="dve"):
        if op1 is None:
            self.s.op(eng, lambda e: e.tensor_scalar(out=out, in0=in0, scalar1=s1, scalar2=None, op0=op0), reads, writes)
        else:
            self.s.op(eng, lambda e: e.tensor_scalar(out=out, in0=in0, scalar1=s1, scalar2=s2, op0=op0, op1=op1), reads, writes)

    def stt(self, out, in0, scalar, in1, op0, op1, reads, writes):
        self.s.op("dve", lambda e: e.scalar_tensor_tensor(out=out, in0=in0, scalar=scalar, in1=in1, op0=op0, op1=op1), reads, writes)

    def copy(self, out, in_, reads, writes, eng="dve"):
        if eng == "act":
            self.s.op("act", lambda e: e.activation(out=out, in_=in_, func=AF.Copy), reads, writes)
        else:
            self.s.op(eng, lambda e: e.tensor_copy(out=out, in_=in_), reads, writes)

    def recip(self, out, in_, reads, writes):
        self.s.op("dve", lambda e: e.reciprocal(out=out, in_=in_), reads, writes)

    def memset(self, ap, val, writes, eng="dve"):
        self.s.op(eng, lambda e: e.memset(ap, val), (), writes)

    def mm(self, items, reads, writes):
        items = list(items)

        def fn(e):
            ins = None
            for (o, l, r, st, sp) in items:
                ins = e.matmul(o, lhsT=l, rhs=r, start=st, stop=sp)
            return ins

        self.s.op("pe", fn, reads, writes)

    def mmacc(self, out, pairs, reads, writes):
        n = len(pairs)
        self.mm([(out, l, r, i == 0, i == n - 1) for i, (l, r) in enumerate(pairs)], reads, writes)

    def transposes(self, items, reads, writes):
        items = list(items)

        def fn(e):
            ins = None
            for (o, i_, ident) in items:
                ins = e.transpose(o, i_, ident)
            return ins

        self.s.op("pe", fn, reads, writes)

    def dump(self, name, ap, reads, dt=F32):
        shape = list(ap.shape)
        d = self.nc.dram_tensor("dbg_" + name, shape, dt, kind="ExternalOutput").ap()
        self.s.dma("sp", [lambda e: e.dma_start(out=d, in_=ap)], "st_x", reads, ())

    def load(self, out, in_, semkey, writes, queue="sp"):
        self.s.dma(queue, [lambda e: e.dma_start(out=out, in_=in_)], semkey, (), writes)

    def gbank(self):
        b = self.gb[self.gbi % 4]
        self.gbi += 1
        return b

    def abank(self):
        b = self.ab[self.abi % self.ab_n]
        self.abi += 1
        return b

    def w_prefetch(self):
        while self.w_free > 0 and self.w_next_load < len(self.w_plan):
            k = self.w_next_load
            idx, ncols = self.w_plan[k]
            slot = k % NSLOT
            out = self.ring[:, slot, 0:ncols]
            in_ = self.wstream[idx, :, 0:ncols]
            self.s.dma("pool", [lambda e, o=out, i=in_: e.dma_start(out=o, in_=i)], "w%d" % slot, (), [self.slotT[slot]])
            self.w_next_load += 1
            self.w_free -= 1

    def w_acquire(self, expect_idx=None):
        k = self.w_next_use
        assert k < self.w_next_load, "weight stream underflow"
        if expect_idx is not None:
            assert self.w_plan[k][0] == expect_idx, (k, self.w_plan[k], expect_idx)
        self.w_next_use += 1
        slot = k % NSLOT
        return self.ring[:, slot, :], self.slotT[slot]

    def w_release(self, n=1):
        self.w_free += n
        self.w_prefetch()

    def build(self, piece_index, n_pieces):
        nc = self.nc
        st = self.stack
        L = self.layers
        NL = len(L)
        self.wstream = nc.dram_tensor("wstream", [n_pieces, 128, SLOT], F32, kind="ExternalInput").ap()
        self.x_in = nc.dram_tensor("x_in", [self.nseq, 128, 8, S], F32, kind="ExternalInput").ap()
        self.x_out = nc.dram_tensor("x_out", [self.nseq, 128, 8, S], F32, kind="ExternalOutput").ap()
        self.c_in = nc.dram_tensor("c_in", [128, 8, 2], F32, kind="ExternalInput").ap()
        self.adab_in = nc.dram_tensor("adab_in", [128, DEPTH, 72], F32, kind="ExternalInput").ap()
        self.normg_in = nc.dram_tensor("normg_in", [128, DEPTH, 3, 8], F32, kind="ExternalInput").ap()
        self.small_in = nc.dram_tensor("small_in", [128, SM_W], F32, kind="ExternalInput").ap()
        self.strips_in = nc.dram_tensor("strips_in", [13, 128, 384], F32, kind="ExternalInput").ap()
        self.consts_in = nc.dram_tensor("consts_in", [128, 4, 128], F32, kind="ExternalInput").ap()

        self.xT = self.sb("xT", [128, 8, S], F32)
        self.xT_T = [[T("x%d_%d" % (c, b)) for b in range(4)] for c in range(8)]
        self.hT = self.sb("hT", [128, 8, S], BF16)
        self.hT_T = [[T("h%d_%d" % (c, b)) for b in range(4)] for c in range(8)]
        self.ring = self.sb("ring", [128, NSLOT, SLOT], BF16)
        self.slotT = [T("slot%d" % i) for i in range(NSLOT)]
        self.arena = self.sb("arena", [128, ARENA // 4], F32)
        self.arenaT = T("arena")
        self.hid = [self.view("hid%d" % i, 8192 + 4096 * i, [128, 4, 512], BF16) for i in range(2)]
        self.hid_i = 0
        self.sq = Ring([self.view("sq0", 0, [128, 8, 512], BF16)] +
                       [self.view("sq%d" % (i + 1), 28672 + 8192 * i, [128, 8, 512], BF16) for i in range(2)])
        self.f32a = Ring([self.view("f32a%d" % i, 16384 + 2048 * i, [128, 512], F32) for i in range(4)])
        self.rstd = Ring([self.view("rstd%d" % i, 24576 + 2048 * i, [128, 512], F32) for i in range(2)])
        self.modT = self.buf("modT", [128, DEPTH, 9, 8, 2], F32)
        self.gmod = self.buf("gmod", [128, DEPTH, 3, 8, 2], F32)
        self.gate = self.buf("gate", [128, DEPTH, 3, 8, 2], F32)
        self.cT = self.buf("cT", [128, 8, 2], F32)
        self.scT = self.buf("scT", [128, 8, 2], BF16)
        self.adab = self.buf("adab", [128, DEPTH, 72], F32)
        self.normg = self.buf("normg", [128, DEPTH, 3, 8], F32)
        self.small = self.buf("small", [128, SM_W], F32)
        self.consts = self.buf("consts", [128, 4, 128], BF16)
        self.derived = self.buf("derived", [128, 128], F32)
        self.ones_bf = self.buf("ones_bf", [128, 128], BF16)
        self.epsT = self.buf("epsT", [128, 1], F32)

        banks = [st.enter_context(nc.psum_tensor("bank%d" % i, [128, 512], F32)) for i in range(8)]
        self.gb = [Buf(banks[i], "gb%d" % i) for i in range(4)]
        self.ab = [Buf(banks[4 + i], "ab%d" % i) for i in range(4)]
        self.gbi = 0
        self.abi = 0
        self.ab_n = 4

        plan = []
        self.ada_overlap = self.do_ffn and self.do_mixer
        ada_up_front = [L[0]] if self.ada_overlap else L
        for l in ada_up_front:
            first, cols = piece_index["ada%d" % l]
            plan += [(first + i, c) for i, c in enumerate(cols)]
        for sq_ in range(self.nseq):
            for li, l in enumerate(L):
                for nm in ("ffn%d_0" % l, "ada", "mix%d" % l, "ffn%d_1" % l):
                    if nm == "ada":
                        if self.ada_overlap and sq_ == 0 and li + 1 < len(L):
                            first, cols = piece_index["ada%d" % L[li + 1]]
                            plan += [(first + i, c) for i, c in enumerate(cols)]
                        continue
                    if nm.startswith("mix") and not self.do_mixer:
                        continue
                    if nm.startswith("ffn") and not self.do_ffn:
                        continue
                    first, cols = piece_index[nm]
                    plan += [(first + i, c) for i, c in enumerate(cols)]
        self.w_plan = plan
        self.w_next_load = 0
        self.w_next_use = 0
        self.w_free = NSLOT
        self.piece_index = piece_index

        self.load(self.cT.ap[:], self.c_in[:, :, :], "ld_small", [self.cT.T])
        self.load(self.adab.ap[:], self.adab_in[:, :, :], "ld_small", [self.adab.T])
        self.load(self.normg.ap[:], self.normg_in[:, :, :, :], "ld_small", [self.normg.T])
        self.load(self.small.ap[:], self.small_in[:, :], "ld_small", [self.small.T])
        self.memset(self.ones_bf.ap[:], 1.0, [self.ones_bf.T])
        self.memset(self.epsT.ap[:], EPS, [self.epsT.T])
        self.w_prefetch()
        self.setup_extra()
        self.act(self.scT.ap[:], self.cT.ap[:], AF.Silu, [self.cT.T], [self.scT.T])
        self.ada_phase(ada_up_front)

        for s in range(self.nseq):
            for c in range(8):
                self.s.dma("sp", [lambda e, c=c, s=s: e.dma_start(out=self.xT[:, c, :], in_=self.x_in[s, :, c, :])],
                           "ld_x", (), self.xT_T[c])
            for l in L:
                if self.do_ffn:
                    if l == L[0] or not self.do_mixer:
                        self.barrier()
                    self.norm(l, 0, s)
                    if "h0" in self.debug and s == 0 and l == L[0]:
                        self.dump("h0", self.hT[:, :, :], [t for r in self.hT_T for t in r], BF16)
                        self.dump("modT", self.modT.ap[:], [self.modT.T])
                        self.dump("gmod", self.gmod.ap[:], [self.gmod.T])
                        self.dump("gate", self.gate.ap[:], [self.gate.T])
                    self.ffn(l, 0, s)
                    if self.ada_overlap and s == 0 and L.index(l) + 1 < len(L):
                        self.ada_phase([L[L.index(l) + 1]])
                if self.do_mixer:
                    self.barrier()
                    self.norm(l, 1, s)
                    self.barrier()
                    if l % 2 == 0:
                        self.even_mixer(l, s)
                    else:
                        self.odd_mixer(l, s)
                if self.do_ffn:
                    self.barrier()
                    self.norm(l, 2, s)
                    self.ffn(l, 1, s)
            for c in range(8):
                self.s.dma("sp", [lambda e, c=c, s=s: e.dma_start(out=self.x_out[s, :, c, :], in_=self.xT[:, c, :])],
                           "st_x", self.xT_T[c], ())
        self.s.final_wait("sp", ["st_x"])
        assert self.w_next_use == len(self.w_plan), (self.w_next_use, len(self.w_plan))
        self.emit()
        return nc

    def setup_extra(self):
        AX = mybir.AxisListType.X
        ctmp = self.view("ctmp", 0, [128, 4, 128], F32)
        self.load(ctmp.ap[:], self.consts_in[:, :, :], "ld_a", [ctmp.T])
        self.copy(self.consts.ap[:], ctmp.ap[:], [ctmp.T], [self.consts.T])
        self.ident = self.consts.ap[:, 0, :]
        self.blk64 = self.consts.ap[:, 1, :]
        self.mask_f = self.consts.ap[:, 2, :]
        self.mask_b = self.consts.ap[:, 3, :]
        sm = self.small.ap
        smT = self.small.T
        dv = self.derived.ap
        dT = self.derived.T
        t = self.view("setup_t", 4096, [128, 512], F32)
        for e in range(2):
            lam_init = 0.8 - 0.6 * math.exp(-0.3 * (2 * e))
            base = SM_DLAM + e * 256
            self.tt(t.ap[:, 0:64], sm[:, base:base + 64], sm[:, base + 64:base + 128], ALU.mult, [smT], [t.T])
            self.tt(t.ap[:, 64:128], sm[:, base + 128:base + 192], sm[:, base + 192:base + 256], ALU.mult, [smT], [t.T])
            self.s.op("dve", lambda e_: e_.tensor_reduce(out=t.ap[:, 128:130],
                                                         in_=t.ap[:, 0:128].rearrange("p (a b) -> p a b", b=64),
                                                         axis=AX, op=ALU.add), [t.T], [t.T])
            self.act(t.ap[:, 130:132], t.ap[:, 128:130], AF.Exp, [t.T], [t.T])
            self.tt(t.ap[:, 132:133], t.ap[:, 131:132], t.ap[:, 130:131], ALU.subtract, [t.T], [t.T])
            self.ts(dv[:, e:e + 1], t.ap[:, 132:133], -lam_init, None, ALU.add, None, [t.T], [dT])
            sb_ = SM_SUBLN + e * 128
            self.ts(sm[:, sb_:sb_ + 128], sm[:, sb_:sb_ + 128], 1.0 - lam_init, None, ALU.mult, None, [smT], [smT])
        self.act(dv[:, 2:18], sm[:, SM_SINK:SM_SINK + 16], AF.Exp, [smT], [dT])
        ex = t.ap[:, 256:320].rearrange("p (d l h) -> p d l h", d=2, l=4)
        self.act(t.ap[:, 256:320], sm[:, SM_CLB:SM_CLB + 64], AF.Exp, [smT], [t.T])
        ssum = t.ap[:, 320:336].rearrange("p (d h) -> p d h", d=2)
        self.s.op("dve", lambda e_: e_.tensor_reduce(out=ssum, in_=t.ap[:, 256:320].rearrange("p (d l h) -> p d h l", d=2, l=4),
                                                     axis=AX, op=ALU.add), [t.T], [t.T])
        rs = t.ap[:, 336:352].rearrange("p (d h) -> p d h", d=2)
        self.recip(t.ap[:, 336:352], t.ap[:, 320:336], [t.T], [t.T])
        e23 = t.ap[:, 352:368].rearrange("p (d h) -> p d h", d=2)
        self.tt(e23, ex[:, :, 2, :], ex[:, :, 3, :], ALU.add, [t.T], [t.T])
        self.tt(e23, e23, ex[:, :, 1, :], ALU.add, [t.T], [t.T])
        lbv = dv[:, 18:50].rearrange("p (d o h) -> p d o h", d=2, o=2)
        self.tt(lbv[:, :, 0, :], ex[:, :, 1, :], rs, ALU.mult, [t.T], [dT])
        self.tt(lbv[:, :, 1, :], e23, rs, ALU.mult, [t.T], [dT])
        self.ts(dv[:, 50:82], dv[:, 18:50], -1.0, 1.0, ALU.mult, ALU.add, [dT], [dT])
        self.ts(dv[:, 82:114], dv[:, 18:50], -1.0, None, ALU.add, None, [dT], [dT])
        self.barrier()

    def out_proj_chunk(self, l, s, ytq, wslot, wT):
        for tb in range(4):
            tsl = slice(tb * 512, (tb + 1) * 512)
            for c in range(8):
                by = self.abank()
                self.mm([(by.ap[:, :], wslot[:, c * 128:(c + 1) * 128], ytq.ap[:, tsl], True, True)], [wT, ytq.T], [by.T])
                self.stt(self.xT[:, c, tsl], by.ap[:], self.gate.ap[:, l, 1, c, s:s + 1], self.xT[:, c, tsl],
                         ALU.mult, ALU.add, [by.T, self.gate.T, self.xT_T[c][tb]], [self.xT_T[c][tb]])

    def qk_proj_norm(self, slot, sT, col0, dst, dst_cols, gcol, tmp, sqr, nchain=2):
        dsts = dst if isinstance(dst, list) else [(dst, slice(0, 128))]
        for t0_ in range(0, 4, nchain):
            tbs = list(range(t0_, min(4, t0_ + nchain)))
            banks, raws, sqs, b2s, rss = {}, {}, {}, {}, {}
            for tb in tbs:
                tsl = slice(tb * 512, (tb + 1) * 512)
                hTs = [self.hT_T[c][tb] for c in range(8)]
                banks[tb] = self.gbank()
                self.mmacc(banks[tb].ap[:, :], [(slot[:, c * 384 + col0:c * 384 + col0 + 128], self.hT[:, c, tsl]) for c in range(8)],
                           [sT] + hTs, [banks[tb].T])
            for tb in tbs:
                sqs[tb] = sqr.next()
                self.act(sqs[tb].ap[:], banks[tb].ap[:], AF.Square, [banks[tb].T], [sqs[tb].T])
            for tb in tbs:
                raws[tb] = tmp.next()
                self.copy(raws[tb].ap[:], banks[tb].ap[:], [banks[tb].T], [raws[tb].T], eng="act")
            for tb in tbs:
                b2s[tb] = self.abank()
                self.mm([(b2s[tb].ap[:, :], self.blk64, sqs[tb].ap[:], True, True)], [sqs[tb].T, self.consts.T], [b2s[tb].T])
            for tb in tbs:
                rss[tb] = tmp.next()
                self.act(rss[tb].ap[:], b2s[tb].ap[:], AF.Ln, [b2s[tb].T, self.epsT.T], [rss[tb].T],
                         scale=1.0 / 64, bias=self.epsT.ap[:, 0:1])
            for tb in tbs:
                self.act(rss[tb].ap[:], rss[tb].ap[:], AF.Exp, [rss[tb].T], [rss[tb].T], scale=-0.5)
            for tb in tbs:
                tsl = slice(tb * 512, (tb + 1) * 512)
                for (db, ps_) in dsts:
                    self.stt(db.ap[ps_, tsl], raws[tb].ap[ps_, :], gcol[ps_, :], rss[tb].ap[ps_, :],
                             ALU.mult, ALU.mult, [raws[tb].T, rss[tb].T, self.small.T], [db.T])

    def even_mixer(self, l, s):
        AX = mybir.AxisListType.X
        e = l // 2
        first = self.piece_index["mix%d" % l][0]
        sm = self.small.ap
        dv = self.derived.ap
        Q1p = self.view("Q1p", 0, [128, 2048], BF16)
        Q2p = self.view("Q2p", 4096, [128, 2048], BF16)
        KT = self.view("KT", 8192, [128, 2048], BF16)
        Va = self.view("Va", 12288, [128, 16, 130], BF16)
        strips = self.view("stripsA", 16512, [128, 4, 384], F32)
        PT = Ring([self.view("PT%d" % i, 22656 + 1024 * i, [128, 512], BF16) for i in range(3)])
        o1 = self.view("o1", 25728, [128, 4, 129], F32)
        o2 = self.view("o2", 27792, [128, 4, 129], F32)
        tmp = Ring([self.view("tmpA%d" % i, 29888 + 2048 * i, [128, 512], F32) for i in range(8)])
        ystage = self.view("ystage", 46272, [128, 16, 128], BF16)
        ytq = self.view("ytq", 50368, [128, 2048], BF16)
        stat = self.view("stat", 54464, [128, 32], F32)
        SQ = Ring([self.view("sqA%d" % i, 54592 + 1024 * i, [128, 512], BF16) for i in range(4)])
        self.memset(Q1p.ap[64:128, :], 0.0, [Q1p.T])
        self.memset(Q2p.ap[0:64, :], 0.0, [Q2p.T])
        self.s.dma("sp", [lambda e_: e_.dma_start(out=strips.ap[:, :, :], in_=self.strips_in[0:4, :, :].rearrange("h p c -> p h c"))],
                   "ld_a", (), [strips.T])
        self.memset(Va.ap[:, :, 128:130], 1.0, [Va.T])
        for h in range(4):
            slot, sT = self.w_acquire(first + 2 * h)
            self.qk_proj_norm(slot, sT, 0, [(Q1p, slice(0, 64)), (Q2p, slice(64, 128))], None,
                              sm[:, SM_QKG + e * 4 + 0:SM_QKG + e * 4 + 1], tmp, SQ, nchain=4)
            self.qk_proj_norm(slot, sT, 128, KT, None, sm[:, SM_QKG + e * 4 + 1:SM_QKG + e * 4 + 2], tmp, SQ, nchain=4)
            for tg in range(4):
                bank = self.gbank()
                items = []
                for tt_ in range(4):
                    tok = slice((tg * 4 + tt_) * 128, (tg * 4 + tt_ + 1) * 128)
                    for c in range(8):
                        items.append((bank.ap[:, tt_ * 128:(tt_ + 1) * 128], self.hT[:, c, tok],
                                      slot[:, c * 384 + 256:c * 384 + 384], c == 0, c == 7))
                self.mm(items, [sT] + [self.hT_T[c][tg] for c in range(8)], [bank.T])
                self.copy(Va.ap[:, tg * 4:(tg + 1) * 4, 0:128], bank.ap[:, :].rearrange("p (a b) -> p a b", b=128),
                          [bank.T], [Va.T], eng="act")
            self.w_release(1)
            clo = sm[:, SM_CLOHI + 2 * h:SM_CLOHI + 2 * h + 1]
            chi = sm[:, SM_CLOHI + 2 * h + 1:SM_CLOHI + 2 * h + 2]
            for qb in range(4):
                qsl = slice(qb * 512, (qb + 1) * 512)
                for sidx in range(2):
                    QP = Q1p if sidx == 0 else Q2p
                    osb = o1 if sidx == 0 else o2
                    Ob = [self.abank() for _ in range(4)]

                    def s_mm(kt):
                        b = self.gbank()
                        self.mm([(b.ap[:, :], KT.ap[:, kt * 128:(kt + 1) * 128], QP.ap[:, qsl], True, True)],
                                [KT.T, QP.T], [b.T])
                        return b

                    nxt = s_mm(0)
                    for kt in range(16):
                        sbk = nxt
                        if kt < 15:
                            nxt = s_mm(kt + 1)
                        pt = PT.next()
                        ee = kt - 4 * qb
                        ds = [ee - j for j in range(4)]
                        near = [j for j in range(4) if abs(ds[j]) <= 1]
                        if not near:
                            bias = clo if ds[0] < 0 else chi
                            self.act(pt.ap[:], sbk.ap[:], AF.Exp, [sbk.T, self.small.T], [pt.T], scale=SCALE, bias=bias)
                        else:
                            j0, j1 = near[0], near[-1]
                            c0 = (1 - ds[j0]) * 128
                            w_ = (j1 - j0 + 1) * 128
                            self.stt(sbk.ap[:, j0 * 128:j0 * 128 + w_], sbk.ap[:, j0 * 128:j0 * 128 + w_], SCALE,
                                     strips.ap[:, h, c0:c0 + w_], ALU.mult, ALU.add, [sbk.T, strips.T], [sbk.T])
                            self.act(pt.ap[:, j0 * 128:j0 * 128 + w_], sbk.ap[:, j0 * 128:j0 * 128 + w_], AF.Exp,
                                     [sbk.T], [pt.T])
                            if j0 > 0:
                                self.act(pt.ap[:, 0:j0 * 128], sbk.ap[:, 0:j0 * 128], AF.Exp, [sbk.T, self.small.T], [pt.T],
                                         scale=SCALE, bias=chi)
                            if j1 < 3:
                                self.act(pt.ap[:, (j1 + 1) * 128:512], sbk.ap[:, (j1 + 1) * 128:512], AF.Exp,
                                         [sbk.T, self.small.T], [pt.T], scale=SCALE, bias=clo)
                        self.mm([(Ob[j].ap[:, 0:129], pt.ap[:, j * 128:(j + 1) * 128], Va.ap[:, kt, 0:129], kt == 0, kt == 15)
                                 for j in range(4)], [pt.T, Va.T], [b.T for b in Ob])
                    for j in range(4):
                        self.copy(osb.ap[:, j, :], Ob[j].ap[:, 0:129], [Ob[j].T], [osb.T])
                    so = 4 * sidx
                    self.recip(stat.ap[:, so:so + 4], osb.ap[:, :, 128], [osb.T], [stat.T])
                    self.tt(osb.ap[:, :, 0:128], osb.ap[:, :, 0:128],
                            stat.ap[:, so:so + 4].unsqueeze(2).to_broadcast([128, 4, 128]), ALU.mult, [osb.T, stat.T], [osb.T])
                o1f = o1.ap[:, :, 0:128]
                o2f = o2.ap[:, :, 0:128]
                self.stt(o1f, o2f, dv[:, e:e + 1], o1f, ALU.mult, ALU.add, [o1.T, o2.T, self.derived.T], [o1.T])
                self.tt(o2f, o1f, o1f, ALU.mult, [o1.T], [o2.T])
                self.s.op("dve", lambda e_: e_.tensor_reduce(out=stat.ap[:, 8:12], in_=o2.ap[:, :, 0:128], axis=AX, op=ALU.add),
                          [o2.T], [stat.T])
                self.act(stat.ap[:, 12:16], stat.ap[:, 8:12], AF.Sqrt, [stat.T, self.epsT.T], [stat.T],
                         scale=1.0 / 128, bias=self.epsT.ap[:, 0:1])
                self.recip(stat.ap[:, 16:20], stat.ap[:, 12:16], [stat.T], [stat.T])
                for j in range(4):
                    self.stt(ystage.ap[:, qb * 4 + j, :], o1.ap[:, j, 0:128], stat.ap[:, 16 + j:17 + j],
                             sm[:, SM_SUBLN + e * 128:SM_SUBLN + (e + 1) * 128], ALU.mult, ALU.mult,
                             [o1.T, stat.T, self.small.T], [ystage.T])
                tb_ = self.gbank()
                tbf = tb_.ap[:, :].bitcast(BF16)
                self.transposes([(tbf[:, j * 128:(j + 1) * 128], ystage.ap[:, qb * 4 + j, :], self.ident) for j in range(4)],
                                [ystage.T, self.consts.T], [tb_.T])
                self.copy(ytq.ap[:, qsl], tbf[:, 0:512], [tb_.T], [ytq.T], eng="act")
            wslot, wT = self.w_acquire(first + 2 * h + 1)
            self.out_proj_chunk(l, s, ytq, wslot, wT)
            self.w_release(1)
        self.barrier()
        self.even_mixer_b(l, s)
        self.barrier()

    def even_mixer_b(self, l, s):
        e = l // 2
        first = self.piece_index["mix%d" % l][0] + 8
        sm = self.small.ap
        dv = self.derived.ap
        Qp = [self.view("Qpb%d" % j, 4096 * j, [128, 2048], BF16) for j in range(4)]
        KTb = self.view("KTb", 16384, [128, 2048], BF16)
        Vb = self.view("Vb", 20480, [128, 16, 2, 66], BF16)
        strips = self.view("stripsB", 24832, [128, 8, 384], F32)
        maskB = self.view("maskB", 37120, [128, 384], F32)
        PT = Ring([self.view("PTb%d" % i, 38656 + 1024 * i, [128, 512], BF16) for i in range(3)])
        tmp = Ring([self.view("tmpB%d" % i, 41728 + 2048 * i, [128, 512], F32) for i in range(4)])
        ystage = self.view("ystageB", 49920, [128, 16, 128], BF16)
        ytq = self.view("ytqB", 54016, [128, 2048], BF16)
        stat = self.view("statB", 58112, [128, 32], F32)
        ostage = self.view("ostageB", 41728, [128, 16, 65], F32)
        for j in range(4):
            zs = slice(64, 128) if j % 2 == 0 else slice(0, 64)
            self.memset(Qp[j].ap[zs, :], 0.0, [Qp[j].T])
        self.s.dma("sp", [lambda e_: e_.dma_start(out=strips.ap[:, :, :], in_=self.strips_in[4:12, :, :].rearrange("h p c -> p h c")),
                          lambda e_: e_.dma_start(out=maskB.ap[:, :], in_=self.strips_in[12, :, :])],
                   "ld_a", (), [strips.T, maskB.T])
        self.tt(strips.ap[:, :, :], strips.ap[:, :, :], maskB.ap[:, :].unsqueeze(1).to_broadcast([128, 8, 384]), ALU.add,
                [strips.T, maskB.T], [strips.T])
        self.memset(Vb.ap[:, :, :, 64:66], 1.0, [Vb.T])
        slot, sT = self.w_acquire(first)
        for tg in range(4):
            bank = self.gbank()
            items = []
            for tt_ in range(4):
                tok = slice((tg * 4 + tt_) * 128, (tg * 4 + tt_ + 1) * 128)
                for c in range(8):
                    items.append((bank.ap[:, tt_ * 128:(tt_ + 1) * 128], self.hT[:, c, tok],
                                  slot[:, c * 128:(c + 1) * 128], c == 0, c == 7))
            self.mm(items, [sT] + [self.hT_T[c][tg] for c in range(8)], [bank.T])
            self.copy(Vb.ap[:, tg * 4:(tg + 1) * 4, :, 0:64],
                      bank.ap[:, :].rearrange("p (a g d) -> p a g d", g=2, d=64), [bank.T], [Vb.T], eng="act")
        self.w_release(1)
        for g in range(2):
            self.barrier()
            slot, sT = self.w_acquire(first + 1 + 3 * g)
            for cb in range(2):
                self.qk_proj_norm(slot, sT, cb * 128, [(Qp[2 * cb], slice(0, 64)), (Qp[2 * cb + 1], slice(64, 128))], None,
                                  sm[:, SM_QKG + e * 4 + 2:SM_QKG + e * 4 + 3], tmp, PT)
            self.qk_proj_norm(slot, sT, 256, KTb, None, sm[:, SM_QKG + e * 4 + 3:SM_QKG + e * 4 + 4], tmp, PT)
            self.w_release(1)
            self.barrier()
            for cb in range(2):
                for hh in range(2):
                    hq = g * 4 + cb * 2 + hh
                    ph = slice(64 * hh, 64 * hh + 64)
                    Ob = {}
                    qpj = Qp[2 * cb + hh]

                    def s_mm(kt):
                        qt0 = max(kt - 1, 0)
                        W = (min(kt + 1, 15) - qt0 + 1) * 128
                        b_ = self.gbank()
                        self.mm([(b_.ap[:, 0:W], KTb.ap[:, kt * 128:(kt + 1) * 128], qpj.ap[:, qt0 * 128:qt0 * 128 + W],
                                  True, True)], [KTb.T, qpj.T], [b_.T])
                        return b_

                    nxt = [s_mm(0), s_mm(1)]
                    pts = {}

                    def bias_exp(kt, sbk):
                        qt0 = max(kt - 1, 0)
                        W = (min(kt + 1, 15) - qt0 + 1) * 128
                        off = 128 if kt == 0 else 0
                        self.stt(sbk.ap[:, 0:W], sbk.ap[:, 0:W], SCALE, strips.ap[:, hq, off:off + W], ALU.mult, ALU.add,
                                 [sbk.T, strips.T], [sbk.T])
                        pt = PT.next()
                        self.act(pt.ap[:, 0:W], sbk.ap[:, 0:W], AF.Exp, [sbk.T], [pt.T])
                        pts[kt] = pt

                    bias_exp(0, nxt.pop(0))
                    for kt in range(16):
                        qt0 = max(kt - 1, 0)
                        qt1 = min(kt + 1, 15)
                        if kt + 2 < 16:
                            nxt.append(s_mm(kt + 2))
                        if kt + 1 < 16:
                            bias_exp(kt + 1, nxt.pop(0))
                        pt = pts.pop(kt)
                        items = []
                        for qt in range(qt0, qt1 + 1):
                            if qt not in Ob:
                                Ob[qt] = self.abank()
                            items.append((Ob[qt].ap[:, 0:65], pt.ap[:, (qt - qt0) * 128:(qt - qt0 + 1) * 128],
                                          Vb.ap[:, kt, g, 0:65], kt == max(qt - 1, 0), kt == min(qt + 1, 15)))
                        self.mm(items, [pt.T, Vb.T], [Ob[qt].T for qt in range(qt0, qt1 + 1)])
                        done = [qt for qt in range(qt0, qt1 + 1) if kt == min(qt + 1, 15)]
                        for qt in done:
                            ob = Ob.pop(qt)
                            self.copy(ostage.ap[:, qt, :], ob.ap[:, 0:65], [ob.T], [ostage.T])
                    self.ts(stat.ap[:, 0:16], ostage.ap[:, :, 64], dv[:, 2 + e * 8 + hq:3 + e * 8 + hq], None, ALU.add, None,
                            [ostage.T, self.derived.T], [stat.T])
                    self.recip(stat.ap[:, 16:32], stat.ap[:, 0:16], [stat.T], [stat.T])
                    self.tt(ystage.ap[:, :, hh * 64:(hh + 1) * 64], ostage.ap[:, :, 0:64],
                            stat.ap[:, 16:32].unsqueeze(2).to_broadcast([128, 16, 64]), ALU.mult, [ostage.T, stat.T], [ystage.T])
                for qb in range(4):
                    tb_ = self.gbank()
                    tbf = tb_.ap[:, :].bitcast(BF16)
                    self.transposes([(tbf[:, j * 128:(j + 1) * 128], ystage.ap[:, qb * 4 + j, :], self.ident) for j in range(4)],
                                    [ystage.T, self.consts.T], [tb_.T])
                    self.copy(ytq.ap[:, qb * 512:(qb + 1) * 512], tbf[:, 0:512], [tb_.T], [ytq.T], eng="act")
                wslot, wT = self.w_acquire(first + 2 + 3 * g + cb)
                self.out_proj_chunk(l, s, ytq, wslot, wT)
                self.w_release(1)

    def ada_phase(self, layers):
        for l in layers:
            bank = self.abank()
            for pj in range(N_ADA_PIECES):
                slot, sT = self.w_acquire(self.piece_index["ada%d" % l][0] + pj)
                items = []
                for mcc in range(3):
                    mc = pj * 3 + mcc
                    for dc in range(8):
                        items.append((bank.ap[:, mc * 2:mc * 2 + 2],
                                      slot[:, dc * 384 + mcc * 128: dc * 384 + (mcc + 1) * 128],
                                      self.scT.ap[:, dc, :], dc == 0, dc == 7))
                self.mm(items, [sT, self.scT.T], [bank.T])
                self.w_release(1)
            self.tt(self.modT.ap[:, l, :, :, :].rearrange("p m c s -> p (m c) s"),
                    bank.ap[:, 0:144].rearrange("p (m s) -> p m s", s=2),
                    self.adab.ap[:, l, :].unsqueeze(2).to_broadcast([128, 72, 2]),
                    ALU.add, [bank.T, self.adab.T], [self.modT.T])
            for j in range(3):
                self.stt(self.gmod.ap[:, l, j, :, :], self.modT.ap[:, l, 3 * j + 1, :, :], 1.0,
                         self.normg.ap[:, l, j, :].unsqueeze(2).to_broadcast([128, 8, 2]),
                         ALU.add, ALU.mult, [self.modT.T, self.normg.T], [self.gmod.T])
                self.ts(self.gate.ap[:, l, j, :, :], self.modT.ap[:, l, 3 * j + 2, :, :],
                        (1.0 if j == 1 else 0.5), None, ALU.mult, None, [self.modT.T], [self.gate.T])

    def norm(self, l, j, s):
        sqs, banks, rstds = {}, {}, {}

        def square(tb):
            tsl = slice(tb * 512, (tb + 1) * 512)
            xTs = [self.xT_T[c][tb] for c in range(8)]
            sq = self.sq.next()
            self.tt(sq.ap[:], self.xT[:, :, tsl], self.xT[:, :, tsl], ALU.mult, xTs, [sq.T], eng="pool")
            sqs[tb] = sq

        def sumsq(tb):
            sq = sqs.pop(tb)
            bank = self.gbank()
            self.mmacc(bank.ap[:, :], [(self.ones_bf.ap[:], sq.ap[:, c, :]) for c in range(8)],
                       [sq.T, self.ones_bf.T], [bank.T])
            banks[tb] = bank

        def lnexp(tb):
            bank = banks.pop(tb)
            rstd = self.rstd.next()
            self.act(rstd.ap[:], bank.ap[:], AF.Ln, [bank.T, self.epsT.T], [rstd.T], scale=1.0 / D, bias=self.epsT.ap[:, 0:1])
            self.act(rstd.ap[:], rstd.ap[:], AF.Exp, [rstd.T], [rstd.T], scale=-0.5)
            rstds[tb] = rstd

        def apply(tb):
            tsl = slice(tb * 512, (tb + 1) * 512)
            rstd = rstds.pop(tb)
            for c in range(8):
                t1 = self.f32a.next()
                self.stt(t1.ap[:], self.xT[:, c, tsl], self.gmod.ap[:, l, j, c, s:s + 1], rstd.ap[:],
                         ALU.mult, ALU.mult, [self.xT_T[c][tb], self.gmod.T, rstd.T], [t1.T])
                self.act(self.hT[:, c, tsl], t1.ap[:], AF.Identity, [t1.T, self.modT.T], [self.hT_T[c][tb]],
                         bias=self.modT.ap[:, l, 3 * j, c, s:s + 1], scale=1.0)

        square(0)
        square(1)
        sumsq(0)
        lnexp(0)
        for tb in range(4):
            if tb + 2 < 4:
                square(tb + 2)
            if tb + 1 < 4:
                sumsq(tb + 1)
                lnexp(tb + 1)
            apply(tb)

    def ffn(self, l, j, s):
        first = self.piece_index["ffn%d_%d" % (l, j)][0]
        jm = 0 if j == 0 else 2
        for (f0, f1) in FGROUPS:
            slots = [self.w_acquire(first + f) for f in range(f0, f1)]
            ng = f1 - f0
            for tb in range(4):
                tsl = slice(tb * 512, (tb + 1) * 512)
                hTs = [self.hT_T[c][tb] for c in range(8)]
                hid = self.hid[self.hid_i % 2]
                self.hid_i += 1
                for fi in range(ng):
                    slot, sT = slots[fi]
                    ba = self.gbank()
                    bb = self.gbank()
                    self.mmacc(ba.ap[:, :], [(slot[:, c * 256:c * 256 + 128], self.hT[:, c, tsl]) for c in range(8)],
                               [sT] + hTs, [ba.T])
                    self.mmacc(bb.ap[:, :], [(slot[:, c * 256 + 128:c * 256 + 256], self.hT[:, c, tsl]) for c in range(8)],
                               [sT] + hTs, [bb.T])
                    sa = self.f32a.next()
                    self.act(sa.ap[:], ba.ap[:], AF.Silu, [ba.T], [sa.T])
                    self.tt(hid.ap[:, fi, :], sa.ap[:], bb.ap[:], ALU.mult, [sa.T, bb.T], [hid.T])
                for c in range(8):
                    by = self.abank()
                    self.mmacc(by.ap[:, :], [(slots[fi][0][:, 2048 + c * 128:2048 + (c + 1) * 128], hid.ap[:, fi, :])
                                             for fi in range(ng)],
                               [hid.T] + [sl[1] for sl in slots], [by.T])
                    self.stt(self.xT[:, c, tsl], by.ap[:], self.gate.ap[:, l, jm, c, s:s + 1], self.xT[:, c, tsl],
                             ALU.mult, ALU.add, [by.T, self.gate.T, self.xT_T[c][tb]], [self.xT_T[c][tb]])
            self.w_release(ng)

    def odd_mixer(self, l, s):
        o = l // 2
        first = self.piece_index["mix%d" % l][0]
        sm = self.small.ap
        dv = self.derived.ap
        qs = self.view("qs", 0, [128, 2048], F32)
        V = self.view("Vo", 8192, [128, 16, 128], BF16)
        gs = self.view("gs", 12288, [128, 2048], BF16)
        qtl = [self.view("qtl%d" % d, 16384 + 4096 * d, [128, 2048], BF16) for d in range(2)]
        ktl = [self.view("ktl%d" % d, 24576 + 4096 * d, [128, 2048], BF16) for d in range(2)]
        khat = [self.view("khat%d" % d, 32768 + 4096 * d, [128, 16, 128], BF16) for d in range(2)]
        sets = [[self.view("t%d_%d" % (d, j), 40960 + 8192 * d + 2048 * j, [128, 512], F32) for j in range(4)] for d in range(2)]
        khTs = [self.view("khT%d" % d, 58368 + 1024 * d, [128, 512], BF16) for d in range(2)]
        self.ab_n = 3
        smask = self.ab[3]
        self.memset(smask.ap[:, :], 1.0, [smask.T])
        self.memset(smask.ap[:, :].rearrange("p (c k) -> p c k", k=64)[:, :, 0:1], 0.0, [smask.T])
        st8 = self.view("st8", 58112, [128, 16], F32)
        Shat = [self.view("Shat%d" % d, 40960 + 8192 * d, [128, 32, 128], BF16) for d in range(2)]
        cs = self.view("cs", 57344, [128, 2, 2, 32], F32)
        yq = Ring([self.view("yq%d" % i, 58368 + 1024 * i, [128, 512], BF16) for i in range(2)])
        Am = Ring([self.view("Am%d" % i, 256 * i, [128, 128], BF16) for i in range(4)])
        Sst = [[self.view("Sst%d_%d" % (d, k), 7168 + 512 * k + 0 * d, [128, 128], F32) if d == 0 else
                self.view("Sst%d_%d" % (d, k), 1024 + 512 * k, [128, 128], F32) for k in range(2)] for d in range(2)]
        Ost = self.view("Ost", 2048, [128, 512], F32)
        sqO = self.view("sqO", 4096, [128, 512], BF16)
        rsO = self.view("rsO", 5120, [128, 512], F32)


        def c3(buf):
            return buf.ap[:, :].rearrange("p (c k) -> p c k", k=64)

        for h in range(8):
            p1, p1T = self.w_acquire(first + 3 * h)
            p2, p2T = self.w_acquire(first + 3 * h + 1)
            for tb in range(4):
                tsl = slice(tb * 512, (tb + 1) * 512)
                hTs = [self.hT_T[c][tb] for c in range(8)]
                bq = self.gbank()
                self.mmacc(bq.ap[:, :], [(p1[:, c * 384:c * 384 + 128], self.hT[:, c, tsl]) for c in range(8)], [p1T] + hTs, [bq.T])
                self.act(qs.ap[:, tsl], bq.ap[:], AF.Silu, [bq.T], [qs.T])
                bg = self.gbank()
                self.mmacc(bg.ap[:, :], [(p2[:, c * 256 + 128:c * 256 + 256], self.hT[:, c, tsl]) for c in range(8)], [p2T] + hTs, [bg.T])
                self.act(gs.ap[:, tsl], bg.ap[:], AF.Silu, [bg.T], [gs.T])
            for tg in range(4):
                bank = self.gbank()
                items = []
                for tt_ in range(4):
                    tok = slice((tg * 4 + tt_) * 128, (tg * 4 + tt_ + 1) * 128)
                    for c in range(8):
                        items.append((bank.ap[:, tt_ * 128:(tt_ + 1) * 128], self.hT[:, c, tok],
                                      p2[:, c * 256:c * 256 + 128], c == 0, c == 7))
                self.mm(items, [p2T] + [self.hT_T[c][tg] for c in range(8)], [bank.T])
                self.copy(V.ap[:, tg * 4:(tg + 1) * 4, :], bank.ap[:, :].rearrange("p (a b) -> p a b", b=128),
                          [bank.T], [V.T], eng="act")
            for tb in range(4):
                tsl = slice(tb * 512, (tb + 1) * 512)
                csl = slice(tb * 8, (tb + 1) * 8)
                hTs = [self.hT_T[c][tb] for c in range(8)]
                bfs = []
                for d in range(2):
                    bf_ = self.gbank()
                    self.mmacc(bf_.ap[:, :], [(p1[:, c * 384 + 128 * (1 + d):c * 384 + 128 * (2 + d)], self.hT[:, c, tsl])
                                              for c in range(8)], [p1T] + hTs, [bf_.T])
                    bfs.append(bf_)
                cols = []
                for d in range(2):
                    ci = (d * 2 + o) * 8 + h
                    cols.append((dv[:, 18 + ci:19 + ci], dv[:, 50 + ci:51 + ci], dv[:, 82 + ci:83 + ci]))
                for d in range(2):
                    S1, KK, G, TM = sets[d]
                    self.act(S1.ap[:], bfs[d].ap[:], AF.Sigmoid, [bfs[d].T], [S1.T])
                for d in range(2):
                    S1, KK, G, TM = sets[d]
                    lbc, omlc, nomlc = cols[d]
                    self.ts(KK.ap[:], S1.ap[:], nomlc, omlc, ALU.mult, ALU.add, [S1.T, self.derived.T], [KK.T])
                for d in range(2):
                    S1, KK, G, TM = sets[d]
                    lbc, omlc, nomlc = cols[d]
                    self.act(S1.ap[:], S1.ap[:], AF.Ln, [S1.T, self.derived.T, KK.T], [S1.T], scale=omlc, bias=lbc)
                for d in range(2):
                    S1, KK, G, TM = sets[d]
                    self.s.op("dve", lambda e_, G=G, S1=S1: e_.tensor_tensor_scan(out=G.ap[:], data0=smask.ap[:], data1=S1.ap[:],
                                                                                  initial=0.0, op0=ALU.mult, op1=ALU.add),
                              [smask.T, S1.T], [G.T])
                S1, KK, G, TM = sets[0]
                self.act(cs.ap[:, 0, 0, csl], c3(G)[:, :, 31], AF.Exp, [G.T], [cs.T])
                self.act(cs.ap[:, 0, 1, csl], c3(G)[:, :, 63], AF.Exp, [G.T], [cs.T])
                self.tt(c3(TM), c3(G), c3(G)[:, :, 31:32].to_broadcast([128, 8, 64]), ALU.subtract, [G.T], [TM.T])
                S1, KK, G, TM = sets[1]
                self.act(cs.ap[:, 1, 1, csl], c3(G)[:, :, 63], AF.Exp, [G.T], [cs.T])
                self.copy(st8.ap[:, 0:8], c3(G)[:, :, 63], [G.T], [st8.T])
                self.tt(G.ap[:], G.ap[:], S1.ap[:], ALU.subtract, [G.T, S1.T], [G.T])
                self.tt(st8.ap[:, 0:8], st8.ap[:, 0:8], c3(G)[:, :, 32], ALU.subtract, [st8.T, G.T], [st8.T])
                self.act(cs.ap[:, 1, 0, csl], st8.ap[:, 0:8], AF.Exp, [st8.T], [cs.T])
                self.tt(c3(TM), c3(G)[:, :, 32:33].to_broadcast([128, 8, 64]), c3(G), ALU.subtract, [G.T], [TM.T])
                for d in range(2):
                    S1, KK, G, TM = sets[d]
                    self.act(S1.ap[:], TM.ap[:], AF.Exp, [TM.T], [S1.T])
                for d in range(2):
                    S1, KK, G, TM = sets[d]
                    self.tt(qtl[d].ap[:, tsl], qs.ap[:, tsl], S1.ap[:], ALU.mult, [qs.T, S1.T], [qtl[d].T], eng="pool")
                for d in range(2):
                    S1, KK, G, TM = sets[d]
                    self.act(S1.ap[:], TM.ap[:], AF.Exp, [TM.T], [S1.T], scale=-1.0)
                for d in range(2):
                    S1, KK, G, TM = sets[d]
                    self.tt(ktl[d].ap[:, tsl], KK.ap[:], S1.ap[:], ALU.mult, [KK.T, S1.T], [ktl[d].T], eng="pool")
                S1, KK, G, TM = sets[0]
                self.tt(c3(TM), c3(G)[:, :, 63:64].to_broadcast([128, 8, 64]), c3(G), ALU.subtract, [G.T], [TM.T])
                self.act(TM.ap[:], TM.ap[:], AF.Exp, [TM.T], [TM.T])
                S1, KK, G, TM = sets[1]
                self.act(TM.ap[:], G.ap[:], AF.Exp, [G.T], [TM.T])
                for d in range(2):
                    S1, KK, G, TM = sets[d]
                    self.tt(khTs[d].ap[:], KK.ap[:], TM.ap[:], ALU.mult, [KK.T, TM.T], [khTs[d].T], eng="pool")
                for d in range(2):
                    tb_ = self.gbank()
                    tbf = tb_.ap[:, :].bitcast(BF16)
                    self.transposes([(tbf[:, j * 128:(j + 1) * 128], khTs[d].ap[:, j * 128:(j + 1) * 128], self.ident)
                                     for j in range(4)], [khTs[d].T, self.consts.T], [tb_.T])
                    self.copy(khat[d].ap[:, tb * 4:(tb + 1) * 4, :], tbf[:, 0:512].rearrange("p (a b) -> p a b", b=128),
                              [tb_.T], [khat[d].T], eng="act")
            self.w_release(2)
            wslot, wT = self.w_acquire(first + 3 * h + 2)
            self.barrier()
            for i in range(32):
                for d in range(2):
                    c = i if d == 0 else 31 - i
                    Sprev = Sst[d][(i + 1) % 2]
                    Snew = Sst[d][i % 2]
                    rows = slice((c % 2) * 64, (c % 2) * 64 + 64)
                    if i == 0:
                        self.memset(Shat[d].ap[:, c, :], 0.0, [Shat[d].T], eng="pool")
                    else:
                        self.act(Shat[d].ap[:, c, :], Sprev.ap[:], AF.Copy, [Sprev.T, cs.T], [Shat[d].T],
                                 scale=cs.ap[:, d, 0, c:c + 1])
                    bs = self.gbank()
                    self.mm([(bs.ap[:, 0:128], khat[d].ap[rows, c // 2, :], V.ap[rows, c // 2, :], True, True)],
                            [khat[d].T, V.T], [bs.T])
                    if i == 0:
                        self.copy(Snew.ap[:], bs.ap[:, 0:128], [bs.T], [Snew.T])
                    else:
                        self.stt(Snew.ap[:], Sprev.ap[:], cs.ap[:, d, 1, c:c + 1], bs.ap[:, 0:128], ALU.mult, ALU.add,
                                 [Sprev.T, cs.T, bs.T], [Snew.T])
            if "odd" in self.debug and h == 0 and s == 0:
                self.barrier()
                for d in range(2):
                    self.dump("Shat%d" % d, Shat[d].ap[:], [Shat[d].T], BF16)
                self.barrier()
            def d_blocks(qb):
                bO = self.abank()
                for blk in range(4):
                    nb = qb * 4 + blk
                    tok = slice(nb * 128, (nb + 1) * 128)
                    ams = []
                    for d in range(2):
                        ba = self.gbank()
                        self.mm([(ba.ap[:, 0:128], ktl[d].ap[:, tok], qtl[d].ap[:, tok], True, True)], [ktl[d].T, qtl[d].T], [ba.T])
                        am = Am.next()
                        self.tt(am.ap[:], ba.ap[:, 0:128], self.mask_f if d == 0 else self.mask_b, ALU.mult,
                                [ba.T, self.consts.T], [am.T])
                        ams.append(am)
                    oc = bO.ap[:, blk * 128:(blk + 1) * 128]
                    items = [(oc, V.ap[:, nb, :], ams[0].ap[:], True, False),
                             (oc, V.ap[:, nb, :], ams[1].ap[:], False, False)]
                    for d in range(2):
                        for hf in range(2):
                            c = 2 * nb + hf
                            items.append((bO.ap[:, blk * 128 + hf * 64:blk * 128 + (hf + 1) * 64], Shat[d].ap[:, c, :],
                                          qtl[d].ap[:, nb * 128 + hf * 64:nb * 128 + (hf + 1) * 64], False, d == 1 and hf == 1))
                    self.mm(items, [V.T, ams[0].T, ams[1].T, Shat[0].T, Shat[1].T, qtl[0].T, qtl[1].T], [bO.T])
                return bO

            def d_finish(qb, bO):
                qsl = slice(qb * 512, (qb + 1) * 512)
                self.act(Ost.ap[:], bO.ap[:], AF.Copy, [bO.T], [Ost.T])
                self.act(sqO.ap[:], bO.ap[:], AF.Square, [bO.T], [sqO.T])
                b2 = self.gbank()
                self.mm([(b2.ap[:, :], self.ones_bf.ap[:], sqO.ap[:], True, True)], [sqO.T, self.ones_bf.T], [b2.T])
                self.act(rsO.ap[:], b2.ap[:], AF.Ln, [b2.T, self.epsT.T], [rsO.T], scale=1.0 / 128, bias=self.epsT.ap[:, 0:1])
                self.act(rsO.ap[:], rsO.ap[:], AF.Exp, [rsO.T], [rsO.T], scale=-0.5)
                self.tt(Ost.ap[:], Ost.ap[:], rsO.ap[:], ALU.mult, [Ost.T, rsO.T], [Ost.T])
                y = yq.next()
                self.stt(y.ap[:], Ost.ap[:], sm[:, SM_OUTG + o:SM_OUTG + o + 1], gs.ap[:, qsl], ALU.mult, ALU.mult,
                         [Ost.T, self.small.T, gs.T], [y.T])
                for c in range(8):
                    by = self.gbank()
                    self.mm([(by.ap[:, :], wslot[:, c * 128:(c + 1) * 128], y.ap[:], True, True)], [wT, y.T], [by.T])
                    self.stt(self.xT[:, c, qsl], by.ap[:], self.gate.ap[:, l, 1, c, s:s + 1], self.xT[:, c, qsl],
                             ALU.mult, ALU.add, [by.T, self.gate.T, self.xT_T[c][qb]], [self.xT_T[c][qb]])

            prev = None
            for qb in range(4):
                bO = d_blocks(qb)
                if prev is not None:
                    d_finish(*prev)
                prev = (qb, bO)
            d_finish(*prev)
            self.w_release(1)
            self.barrier()
        self.ab_n = 4

    def emit(self):
        nc = self.nc
        keys = sorted(self.s.val.keys())
        for k in keys:
            self.sems[k] = self.stack.enter_context(nc.semaphore(k))
        sems = self.sems
        q = self.s.q

        def run(e, ops):
            for waits, fn, inc in ops:
                for k, v in waits:
                    e.wait_ge(sems[k], v)
                if fn is None:
                    continue
                ins = fn(e)
                ins.then_inc(sems[inc[0]], inc[1])

        with nc.Block() as block:
            @block.tensor
            def _(e):
                run(e, q["pe"])

            @block.scalar
            def _(e):
                run(e, q["act"])

            @block.vector
            def _(e):
                run(e, q["dve"])

            @block.gpsimd
            def _(e):
                run(e, q["pool"])

            @block.sync
            def _(e):
                run(e, q["sp"])
        self.stack.close()


def make_stream(inputs, layers, do_mixer=True, do_ffn=True):
    pieces = []
    index = {}

    def add(name, plist, cols):
        index[name] = (len(pieces), cols)
        pieces.extend(plist)

    for l in layers:
        add("ada%d" % l, ada_pieces(inputs["ada_w"], l), [SLOT] * N_ADA_PIECES)
    for l in layers:
        if do_ffn:
            add("ffn%d_0" % l, ffn_pieces(inputs["ffn_up"], inputs["ffn_down"], l, 0), [SLOT] * NF)
        if do_mixer:
            if l % 2 == 0:
                add("mix%d" % l, even_pieces(inputs["even_w_in"], inputs["even_w_out"], l // 2),
                    EVEN_COLS)
            else:
                add("mix%d" % l, odd_pieces(inputs["odd_w_in"], inputs["odd_w_out"], l // 2),
                    ODD_COLS)
        if do_ffn:
            add("ffn%d_1" % l, ffn_pieces(inputs["ffn_up"], inputs["ffn_down"], l, 1), [SLOT] * NF)
    return np.stack(pieces, axis=0), index


def lay_x(xb):
    return np.ascontiguousarray(xb.reshape(S, 8, 128).transpose(2, 1, 0))


def unlay_x(xt):
    return np.ascontiguousarray(xt.transpose(2, 1, 0).reshape(S, D))


def common_maps(inputs, x_cur, batch_ids_per_core, wstream):
    adab = np.ascontiguousarray(inputs["ada_b"].reshape(DEPTH, 72, 128).transpose(2, 0, 1))
    normg = np.ascontiguousarray(inputs["norm_g"].reshape(DEPTH, 3, 8, 128).transpose(3, 0, 1, 2))
    small = small_inputs(inputs)
    strips = strips_input(inputs)
    consts = consts_input()
    maps = []
    for bids in batch_ids_per_core:
        xin = np.stack([lay_x(x_cur[b]) for b in bids], axis=0)
        cT = np.stack([inputs["c"][b].reshape(8, 128).T for b in bids], axis=2)
        if len(bids) == 1:
            cT = np.concatenate([cT, cT], axis=2)
        maps.append({"wstream": wstream, "x_in": xin, "c_in": np.ascontiguousarray(cT, dtype=np.float32),
                     "adab_in": adab, "normg_in": normg, "small_in": small, "strips_in": strips, "consts_in": consts})
    return maps


def run_layers(inputs, x_cur, layers, batch_ids_per_core, do_mixer=True, do_ffn=True, trace=False, debug=None):
    inputs = {k: np.asarray(v, dtype=np.float32) for k, v in inputs.items()}
    nseq = len(batch_ids_per_core[0])
    wstream, index = make_stream(inputs, layers, do_mixer, do_ffn)
    b = Builder(layers, nseq=nseq, do_mixer=do_mixer, do_ffn=do_ffn, debug=debug)
    nc = b.build(index, wstream.shape[0])
    maps = common_maps(inputs, x_cur, batch_ids_per_core, wstream)
    res = run_bass_kernel_spmd(nc, maps, core_ids=list(range(len(maps))), **({"trace": True} if trace else {}))
    out = np.array(x_cur, dtype=np.float32, copy=True)
    for ci, bids in enumerate(batch_ids_per_core):
        xo = res.results[ci]["x_out"]
        for si, bb in enumerate(bids):
            out[bb] = unlay_x(xo[si])
    return out, res


FUSED = True


def kernel(**inputs):
    inputs = {k: np.asarray(v, dtype=np.float32) for k, v in inputs.items()}
    x = inputs["x"]
    bids = [[2 * i, 2 * i + 1] for i in range(NCORES)]
    if FUSED:
        out, _ = run_layers(inputs, x, list(range(DEPTH)), bids)
    else:
        out = x
        for l in range(DEPTH):
            out, _ = run_layers(inputs, out, [l], bids)
    return out.astype(np.float32)
```

```python
import math
from contextlib import ExitStack
import numpy as np
import concourse.bass as bass
import concourse.mybir as mybir
from concourse.bass_utils import run_bass_kernel_spmd

F32 = mybir.dt.float32
BF16 = mybir.dt.bfloat16
AF = mybir.ActivationFunctionType
ALU = mybir.AluOpType

D = 1024
S = 2048
DEPTH = 4
NCORES = 8
NSEQ = 2
DFF = 2816
NF = 22
EPS = 1e-6
SLOT = 3072
NSLOT = 7
FGROUPS = [(0, 4), (4, 8), (8, 12), (12, 16), (16, 19), (19, 22)]
N_ADA_PIECES = 24
SCALE = 0.125
NEG = -30000.0
ARENA = 60416
WARM_A = 0


def _kin(w):
    return w.reshape(8, 128, -1).transpose(1, 0, 2)


def _pad(a):
    a = np.ascontiguousarray(a, dtype=np.float32).reshape(128, -1)
    out = np.zeros((128, SLOT), np.float32)
    out[:, : a.shape[1]] = a
    return out


def t5_bucket_np(rel):
    half = 16
    max_exact = 8
    n = np.abs(rel)
    nf = np.maximum(n, 1).astype(np.float32)
    large = max_exact + (np.log(nf / max_exact) / math.log(128 / max_exact) * (half - max_exact)).astype(np.int32)
    large = np.minimum(large, half - 1)
    return np.where(rel > 0, half, 0) + np.where(n < max_exact, n, large)


def ada_pieces(ada_w, l):
    w = _kin(ada_w[l])
    return [_pad(w[:, :, j * 384:(j + 1) * 384]) for j in range(N_ADA_PIECES)]


def ffn_pieces(ffn_up, ffn_down, l, j):
    up = _kin(ffn_up[l, j]).reshape(128, 8, 2, NF, 128)
    out = []
    for i in range(NF):
        u = up[:, :, :, i, :].reshape(128, 2048)
        dn = ffn_down[l, j, i * 128:(i + 1) * 128, :]
        out.append(_pad(np.concatenate([u, dn], axis=1)))
    return out


def wout_piece(w_out, chunk):
    return _pad(w_out[chunk * 128:(chunk + 1) * 128, :])


EVEN_COLS = [SLOT, 1024] * 4 + [1024] + [SLOT, 1024, 1024] * 2
ODD_COLS = [SLOT, 2048, 1024] * 8


def even_pieces(even_w_in, even_w_out, e):
    w = _kin(even_w_in[e])
    out = []
    for h in range(4):
        q = w[:, :, h * 128:(h + 1) * 128]
        k = w[:, :, 512 + h * 128:512 + (h + 1) * 128]
        v = w[:, :, 1024 + h * 128:1024 + (h + 1) * 128]
        out.append(_pad(np.concatenate([q, k, v], axis=2)))
        out.append(wout_piece(even_w_out[e], h))
    out.append(_pad(w[:, :, 2176:2304]))
    for g in range(2):
        q = w[:, :, 1536 + g * 256:1536 + (g + 1) * 256]
        k = w[:, :, 2048 + g * 64:2048 + (g + 1) * 64]
        out.append(_pad(np.concatenate([q, k, k], axis=2)))
        out.append(wout_piece(even_w_out[e], 4 + 2 * g))
        out.append(wout_piece(even_w_out[e], 5 + 2 * g))
    return out


def odd_pieces(odd_w_in, odd_w_out, o):
    w = _kin(odd_w_in[o])
    out = []
    for h in range(8):
        sl = slice(h * 128, (h + 1) * 128)
        q, ff, fb, iv, g = (w[:, :, k * 1024:(k + 1) * 1024][:, :, sl] for k in range(5))
        out.append(_pad(np.concatenate([q, ff, fb], axis=2)))
        out.append(_pad(np.concatenate([iv, g], axis=2)))
        out.append(wout_piece(odd_w_out[o], h))
    return out


SM_QKG, SM_DLAM, SM_SUBLN, SM_SINK, SM_CLOHI, SM_CLB, SM_OUTG, SM_W = 0, 8, 520, 776, 792, 816, 880, 882


def small_inputs(inputs):
    sm = np.zeros((128, SM_W), np.float32)
    p = np.arange(128)
    sm[:, SM_QKG:SM_QKG + 8] = inputs["qk_norm_g"][:, :, p % 64].transpose(2, 0, 1).reshape(128, 8)
    sm[:, SM_DLAM:SM_DLAM + 512] = inputs["diff_lambda"].reshape(1, 512)
    sm[:, SM_SUBLN:SM_SUBLN + 256] = inputs["diff_subln_g"].reshape(1, 256)
    sm[:, SM_SINK:SM_SINK + 16] = inputs["sink_logit"].reshape(1, 16)
    sm[:, SM_CLOHI:SM_CLOHI + 24] = inputs["rel_bias"][[15, 31], :].T.reshape(1, 24)
    sm[:, SM_CLB:SM_CLB + 64] = inputs["c_lower_bound"].reshape(2, 4, 8, 128).transpose(3, 0, 1, 2).reshape(128, 64)
    sm[:, SM_OUTG:SM_OUTG + 2] = inputs["c_out_norm_g"].T
    return sm


def strips_input(inputs):
    k = np.arange(128)[:, None]
    q = np.arange(128)[None, :]
    out = np.zeros((13, 128, 384), np.float32)
    for j, d in enumerate((1, 0, -1)):
        idx = t5_bucket_np(k - q + 128 * d)
        out[:12, :, j * 128:(j + 1) * 128] = inputs["rel_bias"][idx].transpose(2, 0, 1)
    out[12, :, 0:128] = np.where(k <= q, 0.0, NEG)
    out[12, :, 256:384] = np.where(k >= q, 0.0, NEG)
    return out


def consts_input():
    c = np.zeros((128, 4, 128), np.float32)
    c[:, 0, :] = np.eye(128)
    pp = np.arange(128)
    c[:, 1, :] = (pp[:, None] // 64 == pp[None, :] // 64)
    same = (pp[:, None] // 64 == pp[None, :] // 64)
    c[:, 2, :] = same & (pp[:, None] <= pp[None, :])
    c[:, 3, :] = same & (pp[:, None] >= pp[None, :])
    return c


class T:
    __slots__ = ("name", "w", "r")

    def __init__(self, name):
        self.name = name
        self.w = None
        self.r = {}


class Sched:
    ENG = ("pe", "act", "dve", "pool", "sp")

    def __init__(self):
        self.q = {e: [] for e in self.ENG}
        self.val = {}
        self.seen = {e: {} for e in self.ENG}

    def _deps(self, eng, reads, writes):
        need = {}

        def add(k, v):
            if v > need.get(k, 0):
                need[k] = v

        for t in reads:
            if t.w is not None:
                add(*t.w)
        for t in writes:
            if t.w is not None:
                add(*t.w)
            for k, v in t.r.items():
                add(k, v)
        waits = []
        seen = self.seen[eng]
        for k, v in need.items():
            if eng == "pe" and k == "c_pe":
                continue
            if seen.get(k, 0) < v:
                waits.append((k, v))
                seen[k] = v
        return waits

    def _mark(self, ev, reads, writes):
        k, v = ev
        for t in reads:
            if t.r.get(k, 0) < v:
                t.r[k] = v
        for t in writes:
            t.w = ev
            t.r = {}

    def op(self, eng, fn, reads=(), writes=()):
        waits = self._deps(eng, reads, writes)
        k = "c_" + eng
        v = self.val.get(k, 0) + 1
        self.val[k] = v
        self.q[eng].append((waits, fn, (k, 1)))
        self._mark((k, v), reads, writes)

    def dma(self, queue, fns, semkey, reads=(), writes=()):
        waits = self._deps(queue, reads, writes)
        for i, fn in enumerate(fns):
            self.val[semkey] = self.val.get(semkey, 0) + 16
            self.q[queue].append((waits if i == 0 else [], fn, (semkey, 16)))
        self._mark((semkey, self.val[semkey]), reads, writes)

    def final_wait(self, eng, semkeys):
        waits = [(k, self.val[k]) for k in semkeys if self.val.get(k, 0) > 0]
        self.q[eng].append((waits, None, None))


class Buf:
    def __init__(self, ap, name):
        self.ap = ap
        self.T = T(name)


class Ring:
    def __init__(self, bufs):
        self.bufs = bufs
        self.i = 0

    def next(self):
        b = self.bufs[self.i % len(self.bufs)]
        self.i += 1
        return b


class Builder:
    def __init__(self, layers, nseq=NSEQ, do_mixer=True, do_ffn=True, debug=None):
        self.debug = debug or set()
        self.layers = list(layers)
        self.nseq = nseq
        self.do_mixer = do_mixer
        self.do_ffn = do_ffn
        self.s = Sched()
        self.sems = {}
        self.nc = bass.Bass("TRN2", target_bir_lowering=False)
        self.stack = ExitStack()

    def sb(self, name, shape, dt=F32):
        t = self.stack.enter_context(self.nc.sbuf_tensor(name, list(shape), dt))
        return t

    def view(self, name, off, shape, dt=F32):
        n = int(np.prod(shape[1:]))
        nbytes = n * (4 if dt == F32 else 2)
        assert off % 4 == 0 and off + nbytes <= ARENA, (name, off, nbytes)
        ap = self.arena[:, off // 4:(off + (nbytes + 3) // 4 * 4) // 4]
        if dt != F32:
            ap = ap.bitcast(dt)
            ap = ap[:, 0:n]
        if len(shape) == 3:
            ap = ap.rearrange("p (a b) -> p a b", b=shape[2])
        elif len(shape) == 4:
            ap = ap.rearrange("p (a b c) -> p a b c", b=shape[2], c=shape[3])
        return Buf(ap, name)

    def barrier(self):
        keys = [k for k in self.s.val if k in ("c_pe", "c_act", "c_dve", "c_pool") or k.startswith("ld_a") or (self.debug and k == "st_x")]
        for eng in ("pe", "act", "dve", "pool", "sp"):
            waits = []
            for k in keys:
                v = self.s.val[k]
                if eng == "pe" and k == "c_pe":
                    continue
                if self.s.seen[eng].get(k, 0) < v:
                    waits.append((k, v))
                    self.s.seen[eng][k] = v
            if waits:
                self.s.q[eng].append((waits, None, None))

    def buf(self, name, shape, dt=F32):
        t = self.sb(name, shape, dt)
        return Buf(t, name)

    def act(self, out, in_, func, reads, writes, **kw):
        self.s.op("act", lambda e: e.activation(out=out, in_=in_, func=func, **kw), reads, writes)

    def tt(self, out, in0, in1, op, reads, writes, eng="dve"):
        self.s.op(eng, lambda e: e.tensor_tensor(out=out, in0=in0, in1=in1, op=op), reads, writes)

    def ts(self, out, in0, s1, s2, op0, op1, reads, writes, eng="dve"):
        if op1 is None:
            self.s.op(eng, lambda e: e.tensor_scalar(out=out, in0=in0, scalar1=s1, scalar2=None, op0=op0), reads, writes)
        else:
            self.s.op(eng, lambda e: e.tensor_scalar(out=out, in0=in0, scalar1=s1, scalar2=s2, op0=op0, op1=op1), reads, writes)

    def stt(self, out, in0, scalar, in1, op0, op1, reads, writes):
        self.s.op("dve", lambda e: e.scalar_tensor_tensor(out=out, in0=in0, scalar=scalar, in1=in1, op0=op0, op1=op1), reads, writes)

    def copy(self, out, in_, reads, writes, eng="dve"):
        if eng == "act":
            self.s.op("act", lambda e: e.activation(out=out, in_=in_, func=AF.Copy), reads, writes)
        else:
            self.s.op(eng, lambda e: e.tensor_copy(out=out, in_=in_), reads, writes)

    def recip(self, out, in_, reads, writes):
        self.s.op("dve", lambda e: e.reciprocal(out=out, in_=in_), reads, writes)

    def memset(self, ap, val, writes, eng="dve"):
        self.s.op(eng, lambda e: e.memset(ap, val), (), writes)

    def mm(self, items, reads, writes):
        items = list(items)

        def fn(e):
            ins = None
            for (o, l, r, st, sp) in items:
                ins = e.matmul(o, lhsT=l, rhs=r, start=st, stop=sp)
            return ins

        self.s.op("pe", fn, reads, writes)

    def mmacc(self, out, pairs, reads, writes):
        n = len(pairs)
        self.mm([(out, l, r, i == 0, i == n - 1) for i, (l, r) in enumerate(pairs)], reads, writes)

    def transposes(self, items, reads, writes):
        items = list(items)

        def fn(e):
            ins = None
            for (o, i_, ident) in items:
                ins = e.transpose(o, i_, ident)
            return ins

        self.s.op("pe", fn, reads, writes)

    def dump(self, name, ap, reads, dt=F32):
        shape = list(ap.shape)
        d = self.nc.dram_tensor("dbg_" + name, shape, dt, kind="ExternalOutput").ap()
        self.s.dma("sp", [lambda e: e.dma_start(out=d, in_=ap)], "st_x", reads, ())

    def load(self, out, in_, semkey, writes, queue="sp"):
        self.s.dma(queue, [lambda e: e.dma_start(out=out, in_=in_)], semkey, (), writes)

    def gbank(self):
        b = self.gb[self.gbi % self.gb_n]
        self.gbi += 1
        return b

    def abank(self):
        b = self.ab[self.abi % self.ab_n]
        self.abi += 1
        return b

    def w_prefetch(self):
        while self.w_free > 0 and self.w_next_load < len(self.w_plan):
            k = self.w_next_load
            idx, ncols = self.w_plan[k]
            slot = k % NSLOT
            out = self.ring[:, slot, 0:ncols]
            in_ = self.wstream[idx, :, 0:ncols]
            self.s.dma("pool", [lambda e, o=out, i=in_: e.dma_start(out=o, in_=i)], "w%d" % slot, (), [self.slotT[slot]])
            self.w_next_load += 1
            self.w_free -= 1

    def w_acquire(self, expect_idx=None):
        k = self.w_next_use
        assert k < self.w_next_load, "weight stream underflow"
        if expect_idx is not None:
            assert self.w_plan[k][0] == expect_idx, (k, self.w_plan[k], expect_idx)
        self.w_next_use += 1
        slot = k % NSLOT
        return self.ring[:, slot, :], self.slotT[slot]

    def w_release(self, n=1):
        self.w_free += n
        self.w_prefetch()

    def build(self, piece_index, n_pieces):
        nc = self.nc
        st = self.stack
        L = self.layers
        NL = len(L)
        self.wstream = nc.dram_tensor("wstream", [n_pieces, 128, SLOT], F32, kind="ExternalInput").ap()
        self.x_in = nc.dram_tensor("x_in", [self.nseq, 128, 8, S], F32, kind="ExternalInput").ap()
        self.x_out = nc.dram_tensor("x_out", [self.nseq, 128, 8, S], F32, kind="ExternalOutput").ap()
        self.c_in = nc.dram_tensor("c_in", [128, 8, 2], F32, kind="ExternalInput").ap()
        self.adab_in = nc.dram_tensor("adab_in", [128, DEPTH, 72], F32, kind="ExternalInput").ap()
        self.normg_in = nc.dram_tensor("normg_in", [128, DEPTH, 3, 8], F32, kind="ExternalInput").ap()
        self.small_in = nc.dram_tensor("small_in", [128, SM_W], F32, kind="ExternalInput").ap()
        self.strips_in = nc.dram_tensor("strips_in", [13, 128, 384], F32, kind="ExternalInput").ap()
        self.consts_in = nc.dram_tensor("consts_in", [128, 4, 128], F32, kind="ExternalInput").ap()

        self.xT = self.sb("xT", [128, 8, S], F32)
        self.xT_T = [[T("x%d_%d" % (c, b)) for b in range(4)] for c in range(8)]
        self.hT = self.sb("hT", [128, 8, S], BF16)
        self.hT_T = [[T("h%d_%d" % (c, b)) for b in range(4)] for c in range(8)]
        self.ring = self.sb("ring", [128, NSLOT, SLOT], BF16)
        self.slotT = [T("slot%d" % i) for i in range(NSLOT)]
        self.arena = self.sb("arena", [128, ARENA // 4], F32)
        self.arenaT = T("arena")
        self.hid = [self.view("hid%d" % i, 8192 + 4096 * i, [128, 4, 512], BF16) for i in range(2)]
        self.hid_i = 0
        self.sq = Ring([self.view("sq0", 0, [128, 8, 512], BF16)] +
                       [self.view("sq%d" % (i + 1), 28672 + 8192 * i, [128, 8, 512], BF16) for i in range(2)])
        self.f32a = Ring([self.view("f32a%d" % i, 16384 + 2048 * i, [128, 512], F32) for i in range(4)])
        self.rstd = Ring([self.view("rstd%d" % i, 24576 + 2048 * i, [128, 512], F32) for i in range(2)])
        self.modT = self.buf("modT", [128, DEPTH, 9, 8, 2], F32)
        self.gmod = self.buf("gmod", [128, DEPTH, 3, 8, 2], F32)
        self.gate = self.buf("gate", [128, DEPTH, 3, 8, 2], F32)
        self.cT = self.buf("cT", [128, 8, 2], F32)
        self.scT = self.buf("scT", [128, 8, 2], BF16)
        self.adab = self.buf("adab", [128, DEPTH, 72], F32)
        self.normg = self.buf("normg", [128, DEPTH, 3, 8], F32)
        self.small = self.buf("small", [128, SM_W], F32)
        self.consts = self.buf("consts", [128, 4, 128], BF16)
        self.derived = self.buf("derived", [128, 128], F32)
        self.ones_bf = self.buf("ones_bf", [128, 128], BF16)
        self.epsT = self.buf("epsT", [128, 1], F32)

        banks = [st.enter_context(nc.psum_tensor("bank%d" % i, [128, 512], F32)) for i in range(8)]
        self.gb = [Buf(banks[i], "gb%d" % i) for i in range(4)]
        self.ab = [Buf(banks[4 + i], "ab%d" % i) for i in range(4)]
        self.gbi = 0
        self.abi = 0
        self.ab_n = 4
        self.gb_n = 4

        plan = []
        self.ada_overlap = self.do_ffn and self.do_mixer
        ada_up_front = [L[0]] if self.ada_overlap else L
        for l in ada_up_front:
            first, cols = piece_index["ada%d" % l]
            plan += [(first + i, c) for i, c in enumerate(cols)]
        for sq_ in range(self.nseq):
            for li, l in enumerate(L):
                for nm in ("ffn%d_0" % l, "ada", "mix%d" % l, "ffn%d_1" % l):
                    if nm == "ada":
                        if self.ada_overlap and sq_ == 0 and li + 1 < len(L):
                            first, cols = piece_index["ada%d" % L[li + 1]]
                            plan += [(first + i, c) for i, c in enumerate(cols)]
                        continue
                    if nm.startswith("mix") and not self.do_mixer:
                        continue
                    if nm.startswith("ffn") and not self.do_ffn:
                        continue
                    first, cols = piece_index[nm]
                    plan += [(first + i, c) for i, c in enumerate(cols)]
        self.w_plan = plan
        self.w_next_load = 0
        self.w_next_use = 0
        self.w_free = NSLOT
        self.piece_index = piece_index

        self.load(self.cT.ap[:], self.c_in[:, :, :], "ld_small", [self.cT.T])
        self.load(self.adab.ap[:], self.adab_in[:, :, :], "ld_small", [self.adab.T])
        self.load(self.normg.ap[:], self.normg_in[:, :, :, :], "ld_small", [self.normg.T])
        self.load(self.small.ap[:], self.small_in[:, :], "ld_small", [self.small.T])
        self.memset(self.ones_bf.ap[:], 1.0, [self.ones_bf.T])
        self.memset(self.epsT.ap[:], EPS, [self.epsT.T])
        self.w_prefetch()
        self.setup_extra()
        self.act(self.scT.ap[:], self.cT.ap[:], AF.Silu, [self.cT.T], [self.scT.T])
        self.ada_phase(ada_up_front)

        for s in range(self.nseq):
            for c in range(8):
                self.s.dma("sp", [lambda e, c=c, s=s: e.dma_start(out=self.xT[:, c, :], in_=self.x_in[s, :, c, :])],
                           "ld_x", (), self.xT_T[c])
            for l in L:
                if self.do_ffn:
                    if l == L[0] or not self.do_mixer:
                        self.barrier()
                    self.norm(l, 0, s)
                    if "h0" in self.debug and s == 0 and l == L[0]:
                        self.dump("h0", self.hT[:, :, :], [t for r in self.hT_T for t in r], BF16)
                        self.dump("modT", self.modT.ap[:], [self.modT.T])
                        self.dump("gmod", self.gmod.ap[:], [self.gmod.T])
                        self.dump("gate", self.gate.ap[:], [self.gate.T])
                    self.ffn(l, 0, s)
                    if self.ada_overlap and s == 0 and L.index(l) + 1 < len(L):
                        self.ada_phase([L[L.index(l) + 1]])
                if self.do_mixer:
                    self.barrier()
                    self.norm(l, 1, s)
                    self.barrier()
                    if l % 2 == 0:
                        self.even_mixer(l, s)
                    else:
                        self.odd_mixer(l, s)
                if self.do_ffn:
                    self.barrier()
                    self.norm(l, 2, s)
                    self.ffn(l, 1, s)
            for c in range(8):
                self.s.dma("sp", [lambda e, c=c, s=s: e.dma_start(out=self.x_out[s, :, c, :], in_=self.xT[:, c, :])],
                           "st_x", self.xT_T[c], ())
        self.s.final_wait("sp", ["st_x"])
        assert self.w_next_use == len(self.w_plan), (self.w_next_use, len(self.w_plan))
        self.emit()
        return nc

    def setup_extra(self):
        AX = mybir.AxisListType.X
        ctmp = self.view("ctmp", 0, [128, 4, 128], F32)
        self.load(ctmp.ap[:], self.consts_in[:, :, :], "ld_a", [ctmp.T])
        self.copy(self.consts.ap[:], ctmp.ap[:], [ctmp.T], [self.consts.T])
        self.ident = self.consts.ap[:, 0, :]
        self.blk64 = self.consts.ap[:, 1, :]
        self.mask_f = self.consts.ap[:, 2, :]
        self.mask_b = self.consts.ap[:, 3, :]
        sm = self.small.ap
        smT = self.small.T
        dv = self.derived.ap
        dT = self.derived.T
        t = self.view("setup_t", 4096, [128, 512], F32)
        for e in range(2):
            lam_init = 0.8 - 0.6 * math.exp(-0.3 * (2 * e))
            base = SM_DLAM + e * 256
            self.tt(t.ap[:, 0:64], sm[:, base:base + 64], sm[:, base + 64:base + 128], ALU.mult, [smT], [t.T])
            self.tt(t.ap[:, 64:128], sm[:, base + 128:base + 192], sm[:, base + 192:base + 256], ALU.mult, [smT], [t.T])
            self.s.op("dve", lambda e_: e_.tensor_reduce(out=t.ap[:, 128:130],
                                                         in_=t.ap[:, 0:128].rearrange("p (a b) -> p a b", b=64),
                                                         axis=AX, op=ALU.add), [t.T], [t.T])
            self.act(t.ap[:, 130:132], t.ap[:, 128:130], AF.Exp, [t.T], [t.T])
            self.tt(t.ap[:, 132:133], t.ap[:, 131:132], t.ap[:, 130:131], ALU.subtract, [t.T], [t.T])
            self.ts(dv[:, e:e + 1], t.ap[:, 132:133], -lam_init, None, ALU.add, None, [t.T], [dT])
            sb_ = SM_SUBLN + e * 128
            self.ts(sm[:, sb_:sb_ + 128], sm[:, sb_:sb_ + 128], 1.0 - lam_init, None, ALU.mult, None, [smT], [smT])
        self.act(dv[:, 2:18], sm[:, SM_SINK:SM_SINK + 16], AF.Exp, [smT], [dT])
        ex = t.ap[:, 256:320].rearrange("p (d l h) -> p d l h", d=2, l=4)
        self.act(t.ap[:, 256:320], sm[:, SM_CLB:SM_CLB + 64], AF.Exp, [smT], [t.T])
        ssum = t.ap[:, 320:336].rearrange("p (d h) -> p d h", d=2)
        self.s.op("dve", lambda e_: e_.tensor_reduce(out=ssum, in_=t.ap[:, 256:320].rearrange("p (d l h) -> p d h l", d=2, l=4),
                                                     axis=AX, op=ALU.add), [t.T], [t.T])
        rs = t.ap[:, 336:352].rearrange("p (d h) -> p d h", d=2)
        self.recip(t.ap[:, 336:352], t.ap[:, 320:336], [t.T], [t.T])
        e23 = t.ap[:, 352:368].rearrange("p (d h) -> p d h", d=2)
        self.tt(e23, ex[:, :, 2, :], ex[:, :, 3, :], ALU.add, [t.T], [t.T])
        self.tt(e23, e23, ex[:, :, 1, :], ALU.add, [t.T], [t.T])
        lbv = dv[:, 18:50].rearrange("p (d o h) -> p d o h", d=2, o=2)
        self.tt(lbv[:, :, 0, :], ex[:, :, 1, :], rs, ALU.mult, [t.T], [dT])
        self.tt(lbv[:, :, 1, :], e23, rs, ALU.mult, [t.T], [dT])
        self.ts(dv[:, 50:82], dv[:, 18:50], -1.0, 1.0, ALU.mult, ALU.add, [dT], [dT])
        self.ts(dv[:, 82:114], dv[:, 18:50], -1.0, None, ALU.add, None, [dT], [dT])
        self.barrier()

    def out_proj_chunk(self, l, s, ytq, wslot, wT):
        for tb in range(4):
            tsl = slice(tb * 512, (tb + 1) * 512)
            for c in range(8):
                by = self.abank()
                self.mm([(by.ap[:, :], wslot[:, c * 128:(c + 1) * 128], ytq.ap[:, tsl], True, True)], [wT, ytq.T], [by.T])
                self.stt(self.xT[:, c, tsl], by.ap[:], self.gate.ap[:, l, 1, c, s:s + 1], self.xT[:, c, tsl],
                         ALU.mult, ALU.add, [by.T, self.gate.T, self.xT_T[c][tb]], [self.xT_T[c][tb]])

    def qk_proj_norm(self, slot, sT, col0, dst, dst_cols, gcol, tmp, sqr, nchain=2):
        dsts = dst if isinstance(dst, list) else [(dst, slice(0, 128))]
        for t0_ in range(0, 4, nchain):
            tbs = list(range(t0_, min(4, t0_ + nchain)))
            banks, raws, sqs, b2s, rss = {}, {}, {}, {}, {}
            for tb in tbs:
                tsl = slice(tb * 512, (tb + 1) * 512)
                hTs = [self.hT_T[c][tb] for c in range(8)]
                banks[tb] = self.gbank()
                self.mmacc(banks[tb].ap[:, :], [(slot[:, c * 384 + col0:c * 384 + col0 + 128], self.hT[:, c, tsl]) for c in range(8)],
                           [sT] + hTs, [banks[tb].T])
            for tb in tbs:
                sqs[tb] = sqr.next()
                self.act(sqs[tb].ap[:], banks[tb].ap[:], AF.Square, [banks[tb].T], [sqs[tb].T])
            for tb in tbs:
                raws[tb] = tmp.next()
                self.copy(raws[tb].ap[:], banks[tb].ap[:], [banks[tb].T], [raws[tb].T], eng="act")
            for tb in tbs:
                b2s[tb] = self.abank()
                self.mm([(b2s[tb].ap[:, :], self.blk64, sqs[tb].ap[:], True, True)], [sqs[tb].T, self.consts.T], [b2s[tb].T])
            for tb in tbs:
                rss[tb] = tmp.next()
                self.act(rss[tb].ap[:], b2s[tb].ap[:], AF.Ln, [b2s[tb].T, self.epsT.T], [rss[tb].T],
                         scale=1.0 / 64, bias=self.epsT.ap[:, 0:1])
            for tb in tbs:
                self.act(rss[tb].ap[:], rss[tb].ap[:], AF.Exp, [rss[tb].T], [rss[tb].T], scale=-0.5)
            for tb in tbs:
                tsl = slice(tb * 512, (tb + 1) * 512)
                for (db, ps_) in dsts:
                    self.stt(db.ap[ps_, tsl], raws[tb].ap[ps_, :], gcol[ps_, :], rss[tb].ap[ps_, :],
                             ALU.mult, ALU.mult, [raws[tb].T, rss[tb].T, self.small.T], [db.T])

    def even_mixer(self, l, s):
        AX = mybir.AxisListType.X
        e = l // 2
        first = self.piece_index["mix%d" % l][0]
        sm = self.small.ap
        dv = self.derived.ap
        Q1p = self.view("Q1p", 0, [128, 2048], BF16)
        Q2p = self.view("Q2p", 4096, [128, 2048], BF16)
        KT = self.view("KT", 8192, [128, 2048], BF16)
        Va = self.view("Va", 12288, [128, 16, 130], BF16)
        strips = self.view("stripsA", 16512, [128, 4, 384], F32)
        PT = Ring([self.view("PT%d" % i, 22656 + 1024 * i, [128, 512], BF16) for i in range(3)])
        o1 = self.view("o1", 25728, [128, 4, 129], F32)
        o2 = self.view("o2", 27792, [128, 4, 129], F32)
        tmp = Ring([self.view("tmpA%d" % i, 29888 + 2048 * i, [128, 512], F32) for i in range(4)])
        ext = self.view("extA", 38080, [128, 9 * 128], F32)
        ystage = self.view("ystage", 46272, [128, 16, 128], BF16)
        ytq = self.view("ytq", 50368, [128, 2048], BF16)
        stat = self.view("stat", 54464, [128, 32], F32)
        SQ = Ring([self.view("sqA%d" % i, 54592 + 1024 * i, [128, 512], BF16) for i in range(4)])
        self.memset(Q1p.ap[64:128, :], 0.0, [Q1p.T])
        self.memset(Q2p.ap[0:64, :], 0.0, [Q2p.T])
        self.s.dma("sp", [lambda e_: e_.dma_start(out=strips.ap[:, :, :], in_=self.strips_in[0:4, :, :].rearrange("h p c -> p h c"))],
                   "ld_a", (), [strips.T])
        self.memset(Va.ap[:, :, 128:130], 1.0, [Va.T])
        for h in range(4):
            self.gb_n = 4
            slot, sT = self.w_acquire(first + 2 * h)
            self.qk_proj_norm(slot, sT, 0, [(Q1p, slice(0, 64)), (Q2p, slice(64, 128))], None,
                              sm[:, SM_QKG + e * 4 + 0:SM_QKG + e * 4 + 1], tmp, SQ, nchain=2)
            self.qk_proj_norm(slot, sT, 128, KT, None, sm[:, SM_QKG + e * 4 + 1:SM_QKG + e * 4 + 2], tmp, SQ, nchain=2)
            for tg in range(4):
                bank = self.gbank()
                items = []
                for tt_ in range(4):
                    tok = slice((tg * 4 + tt_) * 128, (tg * 4 + tt_ + 1) * 128)
                    for c in range(8):
                        items.append((bank.ap[:, tt_ * 128:(tt_ + 1) * 128], self.hT[:, c, tok],
                                      slot[:, c * 384 + 256:c * 384 + 384], c == 0, c == 7))
                self.mm(items, [sT] + [self.hT_T[c][tg] for c in range(8)], [bank.T])
                self.copy(Va.ap[:, tg * 4:(tg + 1) * 4, 0:128], bank.ap[:, :].rearrange("p (a b) -> p a b", b=128),
                          [bank.T], [Va.T], eng="act")
            self.w_release(1)
            if WARM_A:
                self.barrier()
                self.gb_n = 3
            clo = sm[:, SM_CLOHI + 2 * h:SM_CLOHI + 2 * h + 1]
            chi = sm[:, SM_CLOHI + 2 * h + 1:SM_CLOHI + 2 * h + 2]
            self.ts(ext.ap[:, 0:384], strips.ap[:, h, :], 0.0, chi, ALU.mult, ALU.add, [strips.T, self.small.T], [ext.T])
            self.copy(ext.ap[:, 384:768], strips.ap[:, h, :], [strips.T], [ext.T])
            self.ts(ext.ap[:, 768:1152], strips.ap[:, h, :], 0.0, clo, ALU.mult, ALU.add, [strips.T, self.small.T], [ext.T])
            for qb in range(4):
                qsl = slice(qb * 512, (qb + 1) * 512)
                for sidx in range(2):
                    QP = Q1p if sidx == 0 else Q2p
                    osb = o1 if sidx == 0 else o2
                    Ob = [self.abank() for _ in range(4)]

                    def s_mm(kt):
                        b = self.gbank()
                        self.mm([(b.ap[:, :], KT.ap[:, kt * 128:(kt + 1) * 128], QP.ap[:, qsl], True, True)],
                                [KT.T, QP.T], [b.T])
                        return b

                    nxt = s_mm(0)
                    for kt in range(16):
                        sbk = nxt
                        if kt < 15:
                            nxt = s_mm(kt + 1)
                        pt = PT.next()
                        ee = kt - 4 * qb
                        ds = [ee - j for j in range(4)]
                        near = [j for j in range(4) if abs(ds[j]) <= 1]
                        if not near:
                            bias = clo if ds[0] < 0 else chi
                            self.act(pt.ap[:], sbk.ap[:], AF.Exp, [sbk.T, self.small.T], [pt.T], scale=SCALE, bias=bias)
                        else:
                            c0 = (4 - ee) * 128
                            self.stt(sbk.ap[:, :], sbk.ap[:, :], SCALE, ext.ap[:, c0:c0 + 512], ALU.mult, ALU.add,
                                     [sbk.T, ext.T], [sbk.T])
                            self.act(pt.ap[:], sbk.ap[:], AF.Exp, [sbk.T], [pt.T])
                        self.mm([(Ob[j].ap[:, 0:129], pt.ap[:, j * 128:(j + 1) * 128], Va.ap[:, kt, 0:129], kt == 0, kt == 15)
                                 for j in range(4)], [pt.T, Va.T], [b.T for b in Ob])
                        if WARM_A:
                            self.mm([(self.gb[3].ap[:, :], KT.ap[:, 0:128], QP.ap[:, 0:512], True, True)] * WARM_A,
                                    [KT.T, QP.T], [])
                    for j in range(4):
                        self.copy(osb.ap[:, j, :], Ob[j].ap[:, 0:129], [Ob[j].T], [osb.T])
                    so = 4 * sidx
                    self.recip(stat.ap[:, so:so + 4], osb.ap[:, :, 128], [osb.T], [stat.T])
                    self.tt(osb.ap[:, :, 0:128], osb.ap[:, :, 0:128],
                            stat.ap[:, so:so + 4].unsqueeze(2).to_broadcast([128, 4, 128]), ALU.mult, [osb.T, stat.T], [osb.T])
                o1f = o1.ap[:, :, 0:128]
                o2f = o2.ap[:, :, 0:128]
                self.stt(o1f, o2f, dv[:, e:e + 1], o1f, ALU.mult, ALU.add, [o1.T, o2.T, self.derived.T], [o1.T])
                self.tt(o2f, o1f, o1f, ALU.mult, [o1.T], [o2.T])
                self.s.op("dve", lambda e_: e_.tensor_reduce(out=stat.ap[:, 8:12], in_=o2.ap[:, :, 0:128], axis=AX, op=ALU.add),
                          [o2.T], [stat.T])
                self.act(stat.ap[:, 12:16], stat.ap[:, 8:12], AF.Sqrt, [stat.T, self.epsT.T], [stat.T],
                         scale=1.0 / 128, bias=self.epsT.ap[:, 0:1])
                self.recip(stat.ap[:, 16:20], stat.ap[:, 12:16], [stat.T], [stat.T])
                for j in range(4):
                    self.stt(ystage.ap[:, qb * 4 + j, :], o1.ap[:, j, 0:128], stat.ap[:, 16 + j:17 + j],
                             sm[:, SM_SUBLN + e * 128:SM_SUBLN + (e + 1) * 128], ALU.mult, ALU.mult,
                             [o1.T, stat.T, self.small.T], [ystage.T])
                tb_ = self.gbank()
                tbf = tb_.ap[:, :].bitcast(BF16)
                self.transposes([(tbf[:, j * 128:(j + 1) * 128], ystage.ap[:, qb * 4 + j, :], self.ident) for j in range(4)],
                                [ystage.T, self.consts.T], [tb_.T])
                self.copy(ytq.ap[:, qsl], tbf[:, 0:512], [tb_.T], [ytq.T], eng="act")
            wslot, wT = self.w_acquire(first + 2 * h + 1)
            self.out_proj_chunk(l, s, ytq, wslot, wT)
            self.w_release(1)
        self.gb_n = 4
        self.barrier()
        self.even_mixer_b(l, s)
        self.barrier()

    def even_mixer_b(self, l, s):
        e = l // 2
        first = self.piece_index["mix%d" % l][0] + 8
        sm = self.small.ap
        dv = self.derived.ap
        Qp = [self.view("Qpb%d" % j, 4096 * j, [128, 2048], BF16) for j in range(4)]
        KTb = self.view("KTb", 16384, [128, 2048], BF16)
        Vb = self.view("Vb", 20480, [128, 16, 2, 66], BF16)
        strips = self.view("stripsB", 24832, [128, 8, 384], F32)
        maskB = self.view("maskB", 37120, [128, 384], F32)
        PT = Ring([self.view("PTb%d" % i, 38656 + 1024 * i, [128, 512], BF16) for i in range(3)])
        tmp = Ring([self.view("tmpB%d" % i, 41728 + 2048 * i, [128, 512], F32) for i in range(4)])
        ystage = self.view("ystageB", 49920, [128, 16, 128], BF16)
        ytq = self.view("ytqB", 54016, [128, 2048], BF16)
        stat = self.view("statB", 58112, [128, 32], F32)
        ostage = self.view("ostageB", 41728, [128, 16, 65], F32)
        for j in range(4):
            zs = slice(64, 128) if j % 2 == 0 else slice(0, 64)
            self.memset(Qp[j].ap[zs, :], 0.0, [Qp[j].T])
        self.s.dma("sp", [lambda e_: e_.dma_start(out=strips.ap[:, :, :], in_=self.strips_in[4:12, :, :].rearrange("h p c -> p h c")),
                          lambda e_: e_.dma_start(out=maskB.ap[:, :], in_=self.strips_in[12, :, :])],
                   "ld_a", (), [strips.T, maskB.T])
        self.tt(strips.ap[:, :, :], strips.ap[:, :, :], maskB.ap[:, :].unsqueeze(1).to_broadcast([128, 8, 384]), ALU.add,
                [strips.T, maskB.T], [strips.T])
        self.memset(Vb.ap[:, :, :, 64:66], 1.0, [Vb.T])
        slot, sT = self.w_acquire(first)
        for tg in range(4):
            bank = self.gbank()
            items = []
            for tt_ in range(4):
                tok = slice((tg * 4 + tt_) * 128, (tg * 4 + tt_ + 1) * 128)
                for c in range(8):
                    items.append((bank.ap[:, tt_ * 128:(tt_ + 1) * 128], self.hT[:, c, tok],
                                  slot[:, c * 128:(c + 1) * 128], c == 0, c == 7))
            self.mm(items, [sT] + [self.hT_T[c][tg] for c in range(8)], [bank.T])
            self.copy(Vb.ap[:, tg * 4:(tg + 1) * 4, :, 0:64],
                      bank.ap[:, :].rearrange("p (a g d) -> p a g d", g=2, d=64), [bank.T], [Vb.T], eng="act")
        self.w_release(1)
        for g in range(2):
            self.barrier()
            slot, sT = self.w_acquire(first + 1 + 3 * g)
            for cb in range(2):
                self.qk_proj_norm(slot, sT, cb * 128, [(Qp[2 * cb], slice(0, 64)), (Qp[2 * cb + 1], slice(64, 128))], None,
                                  sm[:, SM_QKG + e * 4 + 2:SM_QKG + e * 4 + 3], tmp, PT)
            self.qk_proj_norm(slot, sT, 256, KTb, None, sm[:, SM_QKG + e * 4 + 3:SM_QKG + e * 4 + 4], tmp, PT)
            self.w_release(1)
            self.barrier()
            for cb in range(2):
                for hh in range(2):
                    hq = g * 4 + cb * 2 + hh
                    ph = slice(64 * hh, 64 * hh + 64)
                    Ob = {}
                    qpj = Qp[2 * cb + hh]

                    def s_mm(kt):
                        qt0 = max(kt - 1, 0)
                        W = (min(kt + 1, 15) - qt0 + 1) * 128
                        b_ = self.gbank()
                        self.mm([(b_.ap[:, 0:W], KTb.ap[:, kt * 128:(kt + 1) * 128], qpj.ap[:, qt0 * 128:qt0 * 128 + W],
                                  True, True)], [KTb.T, qpj.T], [b_.T])
                        return b_

                    nxt = [s_mm(0), s_mm(1)]
                    pts = {}

                    def bias_exp(kt, sbk):
                        qt0 = max(kt - 1, 0)
                        W = (min(kt + 1, 15) - qt0 + 1) * 128
                        off = 128 if kt == 0 else 0
                        self.stt(sbk.ap[:, 0:W], sbk.ap[:, 0:W], SCALE, strips.ap[:, hq, off:off + W], ALU.mult, ALU.add,
                                 [sbk.T, strips.T], [sbk.T])
                        pt = PT.next()
                        self.act(pt.ap[:, 0:W], sbk.ap[:, 0:W], AF.Exp, [sbk.T], [pt.T])
                        pts[kt] = pt

                    bias_exp(0, nxt.pop(0))
                    for kt in range(16):
                        qt0 = max(kt - 1, 0)
                        qt1 = min(kt + 1, 15)
                        if kt + 2 < 16:
                            nxt.append(s_mm(kt + 2))
                        if kt + 1 < 16:
                            bias_exp(kt + 1, nxt.pop(0))
                        pt = pts.pop(kt)
                        items = []
                        for qt in range(qt0, qt1 + 1):
                            if qt not in Ob:
                                Ob[qt] = self.abank()
                            items.append((Ob[qt].ap[:, 0:65], pt.ap[:, (qt - qt0) * 128:(qt - qt0 + 1) * 128],
                                          Vb.ap[:, kt, g, 0:65], kt == max(qt - 1, 0), kt == min(qt + 1, 15)))
                        self.mm(items, [pt.T, Vb.T], [Ob[qt].T for qt in range(qt0, qt1 + 1)])
                        done = [qt for qt in range(qt0, qt1 + 1) if kt == min(qt + 1, 15)]
                        for qt in done:
                            ob = Ob.pop(qt)
                            self.copy(ostage.ap[:, qt, :], ob.ap[:, 0:65], [ob.T], [ostage.T])
                    self.ts(stat.ap[:, 0:16], ostage.ap[:, :, 64], dv[:, 2 + e * 8 + hq:3 + e * 8 + hq], None, ALU.add, None,
                            [ostage.T, self.derived.T], [stat.T])
                    self.recip(stat.ap[:, 16:32], stat.ap[:, 0:16], [stat.T], [stat.T])
                    self.tt(ystage.ap[:, :, hh * 64:(hh + 1) * 64], ostage.ap[:, :, 0:64],
                            stat.ap[:, 16:32].unsqueeze(2).to_broadcast([128, 16, 64]), ALU.mult, [ostage.T, stat.T], [ystage.T])
                for qb in range(4):
                    tb_ = self.gbank()
                    tbf = tb_.ap[:, :].bitcast(BF16)
                    self.transposes([(tbf[:, j * 128:(j + 1) * 128], ystage.ap[:, qb * 4 + j, :], self.ident) for j in range(4)],
                                    [ystage.T, self.consts.T], [tb_.T])
                    self.copy(ytq.ap[:, qb * 512:(qb + 1) * 512], tbf[:, 0:512], [tb_.T], [ytq.T], eng="act")
                wslot, wT = self.w_acquire(first + 2 + 3 * g + cb)
                self.out_proj_chunk(l, s, ytq, wslot, wT)
                self.w_release(1)

    def ada_phase(self, layers):
        for l in layers:
            bank = self.abank()
            for pj in range(N_ADA_PIECES):
                slot, sT = self.w_acquire(self.piece_index["ada%d" % l][0] + pj)
                items = []
                for mcc in range(3):
                    mc = pj * 3 + mcc
                    for dc in range(8):
                        items.append((bank.ap[:, mc * 2:mc * 2 + 2],
                                      slot[:, dc * 384 + mcc * 128: dc * 384 + (mcc + 1) * 128],
                                      self.scT.ap[:, dc, :], dc == 0, dc == 7))
                self.mm(items, [sT, self.scT.T], [bank.T])
                self.w_release(1)
            self.tt(self.modT.ap[:, l, :, :, :].rearrange("p m c s -> p (m c) s"),
                    bank.ap[:, 0:144].rearrange("p (m s) -> p m s", s=2),
                    self.adab.ap[:, l, :].unsqueeze(2).to_broadcast([128, 72, 2]),
                    ALU.add, [bank.T, self.adab.T], [self.modT.T])
            for j in range(3):
                self.stt(self.gmod.ap[:, l, j, :, :], self.modT.ap[:, l, 3 * j + 1, :, :], 1.0,
                         self.normg.ap[:, l, j, :].unsqueeze(2).to_broadcast([128, 8, 2]),
                         ALU.add, ALU.mult, [self.modT.T, self.normg.T], [self.gmod.T])
                self.ts(self.gate.ap[:, l, j, :, :], self.modT.ap[:, l, 3 * j + 2, :, :],
                        (1.0 if j == 1 else 0.5), None, ALU.mult, None, [self.modT.T], [self.gate.T])

    def norm(self, l, j, s):
        sqs, banks, rstds = {}, {}, {}

        def square(tb):
            tsl = slice(tb * 512, (tb + 1) * 512)
            xTs = [self.xT_T[c][tb] for c in range(8)]
            sq = self.sq.next()
            self.tt(sq.ap[:], self.xT[:, :, tsl], self.xT[:, :, tsl], ALU.mult, xTs, [sq.T], eng="pool")
            sqs[tb] = sq

        def sumsq(tb):
            sq = sqs.pop(tb)
            bank = self.gbank()
            self.mmacc(bank.ap[:, :], [(self.ones_bf.ap[:], sq.ap[:, c, :]) for c in range(8)],
                       [sq.T, self.ones_bf.T], [bank.T])
            banks[tb] = bank

        def lnexp(tb):
            bank = banks.pop(tb)
            rstd = self.rstd.next()
            self.act(rstd.ap[:], bank.ap[:], AF.Ln, [bank.T, self.epsT.T], [rstd.T], scale=1.0 / D, bias=self.epsT.ap[:, 0:1])
            self.act(rstd.ap[:], rstd.ap[:], AF.Exp, [rstd.T], [rstd.T], scale=-0.5)
            rstds[tb] = rstd

        def apply(tb):
            tsl = slice(tb * 512, (tb + 1) * 512)
            rstd = rstds.pop(tb)
            for c in range(8):
                t1 = self.f32a.next()
                self.stt(t1.ap[:], self.xT[:, c, tsl], self.gmod.ap[:, l, j, c, s:s + 1], rstd.ap[:],
                         ALU.mult, ALU.mult, [self.xT_T[c][tb], self.gmod.T, rstd.T], [t1.T])
                self.act(self.hT[:, c, tsl], t1.ap[:], AF.Identity, [t1.T, self.modT.T], [self.hT_T[c][tb]],
                         bias=self.modT.ap[:, l, 3 * j, c, s:s + 1], scale=1.0)

        square(0)
        square(1)
        sumsq(0)
        lnexp(0)
        for tb in range(4):
            if tb + 2 < 4:
                square(tb + 2)
            if tb + 1 < 4:
                sumsq(tb + 1)
                lnexp(tb + 1)
            apply(tb)

    def ffn(self, l, j, s):
        first = self.piece_index["ffn%d_%d" % (l, j)][0]
        jm = 0 if j == 0 else 2
        for (f0, f1) in FGROUPS:
            slots = [self.w_acquire(first + f) for f in range(f0, f1)]
            ng = f1 - f0
            for tb in range(4):
                tsl = slice(tb * 512, (tb + 1) * 512)
                hTs = [self.hT_T[c][tb] for c in range(8)]
                hid = self.hid[self.hid_i % 2]
                self.hid_i += 1
                for fi in range(ng):
                    slot, sT = slots[fi]
                    ba = self.gbank()
                    bb = self.gbank()
                    self.mmacc(ba.ap[:, :], [(slot[:, c * 256:c * 256 + 128], self.hT[:, c, tsl]) for c in range(8)],
                               [sT] + hTs, [ba.T])
                    self.mmacc(bb.ap[:, :], [(slot[:, c * 256 + 128:c * 256 + 256], self.hT[:, c, tsl]) for c in range(8)],
                               [sT] + hTs, [bb.T])
                    sa = self.f32a.next()
                    self.act(sa.ap[:], ba.ap[:], AF.Silu, [ba.T], [sa.T])
                    self.tt(hid.ap[:, fi, :], sa.ap[:], bb.ap[:], ALU.mult, [sa.T, bb.T], [hid.T])
                for c in range(8):
                    by = self.abank()
                    self.mmacc(by.ap[:, :], [(slots[fi][0][:, 2048 + c * 128:2048 + (c + 1) * 128], hid.ap[:, fi, :])
                                             for fi in range(ng)],
                               [hid.T] + [sl[1] for sl in slots], [by.T])
                    self.stt(self.xT[:, c, tsl], by.ap[:], self.gate.ap[:, l, jm, c, s:s + 1], self.xT[:, c, tsl],
                             ALU.mult, ALU.add, [by.T, self.gate.T, self.xT_T[c][tb]], [self.xT_T[c][tb]])
            self.w_release(ng)

    def odd_mixer(self, l, s):
        o = l // 2
        first = self.piece_index["mix%d" % l][0]
        sm = self.small.ap
        dv = self.derived.ap
        qs = self.view("qs", 0, [128, 2048], F32)
        V = self.view("Vo", 8192, [128, 16, 128], BF16)
        gs = self.view("gs", 12288, [128, 2048], BF16)
        qtl = [self.view("qtl%d" % d, 16384 + 4096 * d, [128, 2048], BF16) for d in range(2)]
        ktl = [self.view("ktl%d" % d, 24576 + 4096 * d, [128, 2048], BF16) for d in range(2)]
        khat = [self.view("khat%d" % d, 32768 + 4096 * d, [128, 16, 128], BF16) for d in range(2)]
        sets = [[self.view("t%d_%d" % (d, j), 40960 + 8192 * d + 2048 * j, [128, 512], F32) for j in range(4)] for d in range(2)]
        khTs = [self.view("khT%d" % d, 58368 + 1024 * d, [128, 512], BF16) for d in range(2)]
        self.ab_n = 3
        smask = self.ab[3]
        self.memset(smask.ap[:, :], 1.0, [smask.T])
        self.memset(smask.ap[:, :].rearrange("p (c k) -> p c k", k=64)[:, :, 0:1], 0.0, [smask.T])
        st8 = self.view("st8", 58112, [128, 16], F32)
        Shat = [self.view("Shat%d" % d, 40960 + 8192 * d, [128, 32, 128], BF16) for d in range(2)]
        cs = self.view("cs", 57344, [128, 2, 2, 32], F32)
        yq = Ring([self.view("yq%d" % i, 58368 + 1024 * i, [128, 512], BF16) for i in range(2)])
        Am = Ring([self.view("Am%d" % i, 256 * i, [128, 128], BF16) for i in range(4)])
        Sst = [[self.view("Sst%d_%d" % (d, k), 7168 + 512 * k + 0 * d, [128, 128], F32) if d == 0 else
                self.view("Sst%d_%d" % (d, k), 1024 + 512 * k, [128, 128], F32) for k in range(2)] for d in range(2)]
        Ost = self.view("Ost", 2048, [128, 512], F32)
        sqO = self.view("sqO", 4096, [128, 512], BF16)
        rsO = self.view("rsO", 5120, [128, 512], F32)


        def c3(buf):
            return buf.ap[:, :].rearrange("p (c k) -> p c k", k=64)

        for h in range(8):
            p1, p1T = self.w_acquire(first + 3 * h)
            p2, p2T = self.w_acquire(first + 3 * h + 1)
            for tb in range(4):
                tsl = slice(tb * 512, (tb + 1) * 512)
                hTs = [self.hT_T[c][tb] for c in range(8)]
                bq = self.gbank()
                self.mmacc(bq.ap[:, :], [(p1[:, c * 384:c * 384 + 128], self.hT[:, c, tsl]) for c in range(8)], [p1T] + hTs, [bq.T])
                self.act(qs.ap[:, tsl], bq.ap[:], AF.Silu, [bq.T], [qs.T])
                bg = self.gbank()
                self.mmacc(bg.ap[:, :], [(p2[:, c * 256 + 128:c * 256 + 256], self.hT[:, c, tsl]) for c in range(8)], [p2T] + hTs, [bg.T])
                self.act(gs.ap[:, tsl], bg.ap[:], AF.Silu, [bg.T], [gs.T])
            for tg in range(4):
                bank = self.gbank()
                items = []
                for tt_ in range(4):
                    tok = slice((tg * 4 + tt_) * 128, (tg * 4 + tt_ + 1) * 128)
                    for c in range(8):
                        items.append((bank.ap[:, tt_ * 128:(tt_ + 1) * 128], self.hT[:, c, tok],
                                      p2[:, c * 256:c * 256 + 128], c == 0, c == 7))
                self.mm(items, [p2T] + [self.hT_T[c][tg] for c in range(8)], [bank.T])
                self.copy(V.ap[:, tg * 4:(tg + 1) * 4, :], bank.ap[:, :].rearrange("p (a b) -> p a b", b=128),
                          [bank.T], [V.T], eng="act")
            for tb in range(4):
                tsl = slice(tb * 512, (tb + 1) * 512)
                csl = slice(tb * 8, (tb + 1) * 8)
                hTs = [self.hT_T[c][tb] for c in range(8)]
                bfs = []
                for d in range(2):
                    bf_ = self.gbank()
                    self.mmacc(bf_.ap[:, :], [(p1[:, c * 384 + 128 * (1 + d):c * 384 + 128 * (2 + d)], self.hT[:, c, tsl])
                                              for c in range(8)], [p1T] + hTs, [bf_.T])
                    bfs.append(bf_)
                cols = []
                for d in range(2):
                    ci = (d * 2 + o) * 8 + h
                    cols.append((dv[:, 18 + ci:19 + ci], dv[:, 50 + ci:51 + ci], dv[:, 82 + ci:83 + ci]))
                for d in range(2):
                    S1, KK, G, TM = sets[d]
                    self.act(S1.ap[:], bfs[d].ap[:], AF.Sigmoid, [bfs[d].T], [S1.T])
                for d in range(2):
                    S1, KK, G, TM = sets[d]
                    lbc, omlc, nomlc = cols[d]
                    self.ts(KK.ap[:], S1.ap[:], nomlc, omlc, ALU.mult, ALU.add, [S1.T, self.derived.T], [KK.T])
                for d in range(2):
                    S1, KK, G, TM = sets[d]
                    lbc, omlc, nomlc = cols[d]
                    self.act(S1.ap[:], S1.ap[:], AF.Ln, [S1.T, self.derived.T, KK.T], [S1.T], scale=omlc, bias=lbc)
                for d in range(2):
                    S1, KK, G, TM = sets[d]
                    self.s.op("dve", lambda e_, G=G, S1=S1: e_.tensor_tensor_scan(out=G.ap[:], data0=smask.ap[:], data1=S1.ap[:],
                                                                                  initial=0.0, op0=ALU.mult, op1=ALU.add),
                              [smask.T, S1.T], [G.T])
                S1, KK, G, TM = sets[0]
                self.act(cs.ap[:, 0, 0, csl], c3(G)[:, :, 31], AF.Exp, [G.T], [cs.T])
                self.act(cs.ap[:, 0, 1, csl], c3(G)[:, :, 63], AF.Exp, [G.T], [cs.T])
                self.tt(c3(TM), c3(G), c3(G)[:, :, 31:32].to_broadcast([128, 8, 64]), ALU.subtract, [G.T], [TM.T])
                S1, KK, G, TM = sets[1]
                self.act(cs.ap[:, 1, 1, csl], c3(G)[:, :, 63], AF.Exp, [G.T], [cs.T])
                self.copy(st8.ap[:, 0:8], c3(G)[:, :, 63], [G.T], [st8.T])
                self.tt(G.ap[:], G.ap[:], S1.ap[:], ALU.subtract, [G.T, S1.T], [G.T])
                self.tt(st8.ap[:, 0:8], st8.ap[:, 0:8], c3(G)[:, :, 32], ALU.subtract, [st8.T, G.T], [st8.T])
                self.act(cs.ap[:, 1, 0, csl], st8.ap[:, 0:8], AF.Exp, [st8.T], [cs.T])
                self.tt(c3(TM), c3(G)[:, :, 32:33].to_broadcast([128, 8, 64]), c3(G), ALU.subtract, [G.T], [TM.T])
                for d in range(2):
                    S1, KK, G, TM = sets[d]
                    self.act(S1.ap[:], TM.ap[:], AF.Exp, [TM.T], [S1.T])
                for d in range(2):
                    S1, KK, G, TM = sets[d]
                    self.tt(qtl[d].ap[:, tsl], qs.ap[:, tsl], S1.ap[:], ALU.mult, [qs.T, S1.T], [qtl[d].T], eng="pool")
                for d in range(2):
                    S1, KK, G, TM = sets[d]
                    self.act(S1.ap[:], TM.ap[:], AF.Exp, [TM.T], [S1.T], scale=-1.0)
                for d in range(2):
                    S1, KK, G, TM = sets[d]
                    self.tt(ktl[d].ap[:, tsl], KK.ap[:], S1.ap[:], ALU.mult, [KK.T, S1.T], [ktl[d].T], eng="pool")
                S1, KK, G, TM = sets[0]
                self.tt(c3(TM), c3(G)[:, :, 63:64].to_broadcast([128, 8, 64]), c3(G), ALU.subtract, [G.T], [TM.T])
                self.act(TM.ap[:], TM.ap[:], AF.Exp, [TM.T], [TM.T])
                S1, KK, G, TM = sets[1]
                self.act(TM.ap[:], G.ap[:], AF.Exp, [G.T], [TM.T])
                for d in range(2):
                    S1, KK, G, TM = sets[d]
                    self.tt(khTs[d].ap[:], KK.ap[:], TM.ap[:], ALU.mult, [KK.T, TM.T], [khTs[d].T])
                for d in range(2):
                    tb_ = self.gbank()
                    tbf = tb_.ap[:, :].bitcast(BF16)
                    self.transposes([(tbf[:, j * 128:(j + 1) * 128], khTs[d].ap[:, j * 128:(j + 1) * 128], self.ident)
                                     for j in range(4)], [khTs[d].T, self.consts.T], [tb_.T])
                    self.copy(khat[d].ap[:, tb * 4:(tb + 1) * 4, :], tbf[:, 0:512].rearrange("p (a b) -> p a b", b=128),
                              [tb_.T], [khat[d].T], eng="act")
            self.w_release(2)
            wslot, wT = self.w_acquire(first + 3 * h + 2)
            self.barrier()
            for i in range(32):
                for d in range(2):
                    c = i if d == 0 else 31 - i
                    Sprev = Sst[d][(i + 1) % 2]
                    Snew = Sst[d][i % 2]
                    rows = slice((c % 2) * 64, (c % 2) * 64 + 64)
                    if i == 0:
                        self.memset(Shat[d].ap[:, c, :], 0.0, [Shat[d].T], eng="pool")
                    else:
                        self.act(Shat[d].ap[:, c, :], Sprev.ap[:], AF.Copy, [Sprev.T, cs.T], [Shat[d].T],
                                 scale=cs.ap[:, d, 0, c:c + 1])
                    bs = self.gbank()
                    self.mm([(bs.ap[:, 0:128], khat[d].ap[rows, c // 2, :], V.ap[rows, c // 2, :], True, True)],
                            [khat[d].T, V.T], [bs.T])
                    if i == 0:
                        self.copy(Snew.ap[:], bs.ap[:, 0:128], [bs.T], [Snew.T])
                    else:
                        self.stt(Snew.ap[:], Sprev.ap[:], cs.ap[:, d, 1, c:c + 1], bs.ap[:, 0:128], ALU.mult, ALU.add,
                                 [Sprev.T, cs.T, bs.T], [Snew.T])
            if "odd" in self.debug and h == 0 and s == 0:
                self.barrier()
                for d in range(2):
                    self.dump("Shat%d" % d, Shat[d].ap[:], [Shat[d].T], BF16)
                self.barrier()
            def d_blocks(qb):
                bO = self.abank()
                for blk in range(4):
                    nb = qb * 4 + blk
                    tok = slice(nb * 128, (nb + 1) * 128)
                    ams = []
                    for d in range(2):
                        ba = self.gbank()
                        self.mm([(ba.ap[:, 0:128], ktl[d].ap[:, tok], qtl[d].ap[:, tok], True, True)], [ktl[d].T, qtl[d].T], [ba.T])
                        am = Am.next()
                        self.tt(am.ap[:], ba.ap[:, 0:128], self.mask_f if d == 0 else self.mask_b, ALU.mult,
                                [ba.T, self.consts.T], [am.T])
                        ams.append(am)
                    oc = bO.ap[:, blk * 128:(blk + 1) * 128]
                    items = [(oc, V.ap[:, nb, :], ams[0].ap[:], True, False),
                             (oc, V.ap[:, nb, :], ams[1].ap[:], False, False)]
                    for d in range(2):
                        for hf in range(2):
                            c = 2 * nb + hf
                            items.append((bO.ap[:, blk * 128 + hf * 64:blk * 128 + (hf + 1) * 64], Shat[d].ap[:, c, :],
                                          qtl[d].ap[:, nb * 128 + hf * 64:nb * 128 + (hf + 1) * 64], False, d == 1 and hf == 1))
                    self.mm(items, [V.T, ams[0].T, ams[1].T, Shat[0].T, Shat[1].T, qtl[0].T, qtl[1].T], [bO.T])
                return bO

            def d_finish(qb, bO):
                qsl = slice(qb * 512, (qb + 1) * 512)
                self.act(Ost.ap[:], bO.ap[:], AF.Copy, [bO.T], [Ost.T])
                self.act(sqO.ap[:], bO.ap[:], AF.Square, [bO.T], [sqO.T])
                b2 = self.gbank()
                self.mm([(b2.ap[:, :], self.ones_bf.ap[:], sqO.ap[:], True, True)], [sqO.T, self.ones_bf.T], [b2.T])
                self.act(rsO.ap[:], b2.ap[:], AF.Ln, [b2.T, self.epsT.T], [rsO.T], scale=1.0 / 128, bias=self.epsT.ap[:, 0:1])
                self.act(rsO.ap[:], rsO.ap[:], AF.Exp, [rsO.T], [rsO.T], scale=-0.5)
                self.tt(Ost.ap[:], Ost.ap[:], rsO.ap[:], ALU.mult, [Ost.T, rsO.T], [Ost.T])
                y = yq.next()
                self.stt(y.ap[:], Ost.ap[:], sm[:, SM_OUTG + o:SM_OUTG + o + 1], gs.ap[:, qsl], ALU.mult, ALU.mult,
                         [Ost.T, self.small.T, gs.T], [y.T])
                for c in range(8):
                    by = self.gbank()
                    self.mm([(by.ap[:, :], wslot[:, c * 128:(c + 1) * 128], y.ap[:], True, True)], [wT, y.T], [by.T])
                    self.stt(self.xT[:, c, qsl], by.ap[:], self.gate.ap[:, l, 1, c, s:s + 1], self.xT[:, c, qsl],
                             ALU.mult, ALU.add, [by.T, self.gate.T, self.xT_T[c][qb]], [self.xT_T[c][qb]])

            prev = None
            for qb in range(4):
                bO = d_blocks(qb)
                if prev is not None:
                    d_finish(*prev)
                prev = (qb, bO)
            d_finish(*prev)
            self.w_release(1)
            self.barrier()
        self.ab_n = 4

    def emit(self):
        nc = self.nc
        keys = sorted(self.s.val.keys())
        for k in keys:
            self.sems[k] = self.stack.enter_context(nc.semaphore(k))
        sems = self.sems
        q = self.s.q

        def run(e, ops):
            for waits, fn, inc in ops:
                for k, v in waits:
                    e.wait_ge(sems[k], v)
                if fn is None:
                    continue
                ins = fn(e)
                ins.then_inc(sems[inc[0]], inc[1])

        with nc.Block() as block:
            @block.tensor
            def _(e):
                run(e, q["pe"])

            @block.scalar
            def _(e):
                run(e, q["act"])

            @block.vector
            def _(e):
                run(e, q["dve"])

            @block.gpsimd
            def _(e):
                run(e, q["pool"])

            @block.sync
            def _(e):
                run(e, q["sp"])
        self.stack.close()


def make_stream(inputs, layers, do_mixer=True, do_ffn=True):
    pieces = []
    index = {}

    def add(name, plist, cols):
        index[name] = (len(pieces), cols)
        pieces.extend(plist)

    for l in layers:
        add("ada%d" % l, ada_pieces(inputs["ada_w"], l), [SLOT] * N_ADA_PIECES)
    for l in layers:
        if do_ffn:
            add("ffn%d_0" % l, ffn_pieces(inputs["ffn_up"], inputs["ffn_down"], l, 0), [SLOT] * NF)
        if do_mixer:
            if l % 2 == 0:
                add("mix%d" % l, even_pieces(inputs["even_w_in"], inputs["even_w_out"], l // 2),
                    EVEN_COLS)
            else:
                add("mix%d" % l, odd_pieces(inputs["odd_w_in"], inputs["odd_w_out"], l // 2),
                    ODD_COLS)
        if do_ffn:
            add("ffn%d_1" % l, ffn_pieces(inputs["ffn_up"], inputs["ffn_down"], l, 1), [SLOT] * NF)
    return np.stack(pieces, axis=0), index


def lay_x(xb):
    return np.ascontiguousarray(xb.reshape(S, 8, 128).transpose(2, 1, 0))


def unlay_x(xt):
    return np.ascontiguousarray(xt.transpose(2, 1, 0).reshape(S, D))


def common_maps(inputs, x_cur, batch_ids_per_core, wstream):
    adab = np.ascontiguousarray(inputs["ada_b"].reshape(DEPTH, 72, 128).transpose(2, 0, 1))
    normg = np.ascontiguousarray(inputs["norm_g"].reshape(DEPTH, 3, 8, 128).transpose(3, 0, 1, 2))
    small = small_inputs(inputs)
    strips = strips_input(inputs)
    consts = consts_input()
    maps = []
    for bids in batch_ids_per_core:
        xin = np.stack([lay_x(x_cur[b]) for b in bids], axis=0)
        cT = np.stack([inputs["c"][b].reshape(8, 128).T for b in bids], axis=2)
        if len(bids) == 1:
            cT = np.concatenate([cT, cT], axis=2)
        maps.append({"wstream": wstream, "x_in": xin, "c_in": np.ascontiguousarray(cT, dtype=np.float32),
                     "adab_in": adab, "normg_in": normg, "small_in": small, "strips_in": strips, "consts_in": consts})
    return maps


def run_layers(inputs, x_cur, layers, batch_ids_per_core, do_mixer=True, do_ffn=True, trace=False, debug=None):
    inputs = {k: np.asarray(v, dtype=np.float32) for k, v in inputs.items()}
    nseq = len(batch_ids_per_core[0])
    wstream, index = make_stream(inputs, layers, do_mixer, do_ffn)
    b = Builder(layers, nseq=nseq, do_mixer=do_mixer, do_ffn=do_ffn, debug=debug)
    nc = b.build(index, wstream.shape[0])
    maps = common_maps(inputs, x_cur, batch_ids_per_core, wstream)
    res = run_bass_kernel_spmd(nc, maps, core_ids=list(range(len(maps))), **({"trace": True} if trace else {}))
    out = np.array(x_cur, dtype=np.float32, copy=True)
    for ci, bids in enumerate(batch_ids_per_core):
        xo = res.results[ci]["x_out"]
        for si, bb in enumerate(bids):
            out[bb] = unlay_x(xo[si])
    return out, res


FUSED = True


def kernel(**inputs):
    inputs = {k: np.asarray(v, dtype=np.float32) for k, v in inputs.items()}
    x = inputs["x"]
    bids = [[2 * i, 2 * i + 1] for i in range(NCORES)]
    if FUSED:
        out, _ = run_layers(inputs, x, list(range(DEPTH)), bids)
    else:
        out = x
        for l in range(DEPTH):
            out, _ = run_layers(inputs, out, [l], bids)
    return out.astype(np.float32)
```

```python
import math
from contextlib import ExitStack
import numpy as np
import concourse.bass as bass
import concourse.mybir as mybir
from concourse.bass_utils import run_bass_kernel_spmd

F32 = mybir.dt.float32
BF16 = mybir.dt.bfloat16
AF = mybir.ActivationFunctionType
ALU = mybir.AluOpType

D = 1024
S = 2048
DEPTH = 4
NCORES = 8
NSEQ = 2
DFF = 2816
NF = 22
EPS = 1e-6
SLOT = 3072
NSLOT = 7
FGROUPS = [(0, 4), (4, 8), (8, 12), (12, 16), (16, 19), (19, 22)]
N_ADA_PIECES = 24
SCALE = 0.125
NEG = -30000.0
ARENA = 60416
WARM_A = 0


def _kin(w):
    return w.reshape(8, 128, -1).transpose(1, 0, 2)


def _pad(a):
    a = np.ascontiguousarray(a, dtype=np.float32).reshape(128, -1)
    out = np.zeros((128, SLOT), np.float32)
    out[:, : a.shape[1]] = a
    return out


def t5_bucket_np(rel):
    half = 16
    max_exact = 8
    n = np.abs(rel)
    nf = np.maximum(n, 1).astype(np.float32)
    large = max_exact + (np.log(nf / max_exact) / math.log(128 / max_exact) * (half - max_exact)).astype(np.int32)
    large = np.minimum(large, half - 1)
    return np.where(rel > 0, half, 0) + np.where(n < max_exact, n, large)


def ada_pieces(ada_w, l):
    w = _kin(ada_w[l])
    return [_pad(w[:, :, j * 384:(j + 1) * 384]) for j in range(N_ADA_PIECES)]


def ffn_pieces(ffn_up, ffn_down, l, j):
    up = _kin(ffn_up[l, j]).reshape(128, 8, 2, NF, 128)
    out = []
    for i in range(NF):
        u = up[:, :, :, i, :].reshape(128, 2048)
        dn = ffn_down[l, j, i * 128:(i + 1) * 128, :]
        out.append(_pad(np.concatenate([u, dn], axis=1)))
    return out


def wout_piece(w_out, chunk):
    return _pad(w_out[chunk * 128:(chunk + 1) * 128, :])


EVEN_COLS = [SLOT, 1024] * 4 + [1024] + [SLOT, 1024, 1024] * 2
ODD_COLS = [SLOT, 2048, 1024] * 8


def even_pieces(even_w_in, even_w_out, e):
    w = _kin(even_w_in[e])
    out = []
    for h in range(4):
        q = w[:, :, h * 128:(h + 1) * 128]
        k = w[:, :, 512 + h * 128:512 + (h + 1) * 128]
        v = w[:, :, 1024 + h * 128:1024 + (h + 1) * 128]
        out.append(_pad(np.concatenate([q, k, v], axis=2)))
        out.append(wout_piece(even_w_out[e], h))
    out.append(_pad(w[:, :, 2176:2304]))
    for g in range(2):
        q = w[:, :, 1536 + g * 256:1536 + (g + 1) * 256]
        k = w[:, :, 2048 + g * 64:2048 + (g + 1) * 64]
        out.append(_pad(np.concatenate([q, k, k], axis=2)))
        out.append(wout_piece(even_w_out[e], 4 + 2 * g))
        out.append(wout_piece(even_w_out[e], 5 + 2 * g))
    return out


def odd_pieces(odd_w_in, odd_w_out, o):
    w = _kin(odd_w_in[o])
    out = []
    for h in range(8):
        sl = slice(h * 128, (h + 1) * 128)
        q, ff, fb, iv, g = (w[:, :, k * 1024:(k + 1) * 1024][:, :, sl] for k in range(5))
        out.append(_pad(np.concatenate([q, ff, fb], axis=2)))
        out.append(_pad(np.concatenate([iv, g], axis=2)))
        out.append(wout_piece(odd_w_out[o], h))
    return out


SM_QKG, SM_DLAM, SM_SUBLN, SM_SINK, SM_CLOHI, SM_CLB, SM_OUTG, SM_W = 0, 8, 520, 776, 792, 816, 880, 882


def small_inputs(inputs):
    sm = np.zeros((128, SM_W), np.float32)
    p = np.arange(128)
    sm[:, SM_QKG:SM_QKG + 8] = inputs["qk_norm_g"][:, :, p % 64].transpose(2, 0, 1).reshape(128, 8)
    sm[:, SM_DLAM:SM_DLAM + 512] = inputs["diff_lambda"].reshape(1, 512)
    sm[:, SM_SUBLN:SM_SUBLN + 256] = inputs["diff_subln_g"].reshape(1, 256)
    sm[:, SM_SINK:SM_SINK + 16] = inputs["sink_logit"].reshape(1, 16)
    sm[:, SM_CLOHI:SM_CLOHI + 24] = inputs["rel_bias"][[15, 31], :].T.reshape(1, 24)
    sm[:, SM_CLB:SM_CLB + 64] = inputs["c_lower_bound"].reshape(2, 4, 8, 128).transpose(3, 0, 1, 2).reshape(128, 64)
    sm[:, SM_OUTG:SM_OUTG + 2] = inputs["c_out_norm_g"].T
    return sm


def strips_input(inputs):
    k = np.arange(128)[:, None]
    q = np.arange(128)[None, :]
    out = np.zeros((13, 128, 384), np.float32)
    for j, d in enumerate((1, 0, -1)):
        idx = t5_bucket_np(k - q + 128 * d)
        out[:12, :, j * 128:(j + 1) * 128] = inputs["rel_bias"][idx].transpose(2, 0, 1)
    out[12, :, 0:128] = np.where(k <= q, 0.0, NEG)
    out[12, :, 256:384] = np.where(k >= q, 0.0, NEG)
    return out


def consts_input():
    c = np.zeros((128, 4, 128), np.float32)
    c[:, 0, :] = np.eye(128)
    pp = np.arange(128)
    c[:, 1, :] = (pp[:, None] // 64 == pp[None, :] // 64)
    same = (pp[:, None] // 64 == pp[None, :] // 64)
    c[:, 2, :] = same & (pp[:, None] <= pp[None, :])
    c[:, 3, :] = same & (pp[:, None] >= pp[None, :])
    return c


class T:
    __slots__ = ("name", "w", "r")

    def __init__(self, name):
        self.name = name
        self.w = None
        self.r = {}


class Sched:
    ENG = ("pe", "act", "dve", "pool", "sp")

    def __init__(self):
        self.q = {e: [] for e in self.ENG}
        self.val = {}
        self.seen = {e: {} for e in self.ENG}

    def _deps(self, eng, reads, writes):
        need = {}

        def add(k, v):
            if v > need.get(k, 0):
                need[k] = v

        for t in reads:
            if t.w is not None:
                add(*t.w)
        for t in writes:
            if t.w is not None:
                add(*t.w)
            for k, v in t.r.items():
                add(k, v)
        waits = []
        seen = self.seen[eng]
        for k, v in need.items():
            if eng == "pe" and k == "c_pe":
                continue
            if seen.get(k, 0) < v:
                waits.append((k, v))
                seen[k] = v
        return waits

    def _mark(self, ev, reads, writes):
        k, v = ev
        for t in reads:
            if t.r.get(k, 0) < v:
                t.r[k] = v
        for t in writes:
            t.w = ev
            t.r = {}

    def op(self, eng, fn, reads=(), writes=()):
        waits = self._deps(eng, reads, writes)
        k = "c_" + eng
        v = self.val.get(k, 0) + 1
        self.val[k] = v
        self.q[eng].append((waits, fn, (k, 1)))
        self._mark((k, v), reads, writes)

    def dma(self, queue, fns, semkey, reads=(), writes=()):
        waits = self._deps(queue, reads, writes)
        for i, fn in enumerate(fns):
            self.val[semkey] = self.val.get(semkey, 0) + 16
            self.q[queue].append((waits if i == 0 else [], fn, (semkey, 16)))
        self._mark((semkey, self.val[semkey]), reads, writes)

    def final_wait(self, eng, semkeys):
        waits = [(k, self.val[k]) for k in semkeys if self.val.get(k, 0) > 0]
        self.q[eng].append((waits, None, None))


class Buf:
    def __init__(self, ap, name):
        self.ap = ap
        self.T = T(name)


class Ring:
    def __init__(self, bufs):
        self.bufs = bufs
        self.i = 0

    def next(self):
        b = self.bufs[self.i % len(self.bufs)]
        self.i += 1
        return b


class Builder:
    def __init__(self, layers, nseq=NSEQ, do_mixer=True, do_ffn=True, debug=None):
        self.debug = debug or set()
        self.layers = list(layers)
        self.nseq = nseq
        self.do_mixer = do_mixer
        self.do_ffn = do_ffn
        self.s = Sched()
        self.sems = {}
        self.nc = bass.Bass("TRN2", target_bir_lowering=False)
        self.stack = ExitStack()

    def sb(self, name, shape, dt=F32):
        t = self.stack.enter_context(self.nc.sbuf_tensor(name, list(shape), dt))
        return t

    def view(self, name, off, shape, dt=F32):
        n = int(np.prod(shape[1:]))
        nbytes = n * (4 if dt == F32 else 2)
        assert off % 4 == 0 and off + nbytes <= ARENA, (name, off, nbytes)
        ap = self.arena[:, off // 4:(off + (nbytes + 3) // 4 * 4) // 4]
        if dt != F32:
            ap = ap.bitcast(dt)
            ap = ap[:, 0:n]
        if len(shape) == 3:
            ap = ap.rearrange("p (a b) -> p a b", b=shape[2])
        elif len(shape) == 4:
            ap = ap.rearrange("p (a b c) -> p a b c", b=shape[2], c=shape[3])
        return Buf(ap, name)

    def barrier(self):
        keys = [k for k in self.s.val if k in ("c_pe", "c_act", "c_dve", "c_pool") or k.startswith("ld_a") or (self.debug and k == "st_x")]
        for eng in ("pe", "act", "dve", "pool", "sp"):
            waits = []
            for k in keys:
                v = self.s.val[k]
                if eng == "pe" and k == "c_pe":
                    continue
                if self.s.seen[eng].get(k, 0) < v:
                    waits.append((k, v))
                    self.s.seen[eng][k] = v
            if waits:
                self.s.q[eng].append((waits, None, None))

    def buf(self, name, shape, dt=F32):
        t = self.sb(name, shape, dt)
        return Buf(t, name)

    def act(self, out, in_, func, reads, writes, **kw):
        self.s.op("act", lambda e: e.activation(out=out, in_=in_, func=func, **kw), reads, writes)

    def tt(self, out, in0, in1, op, reads, writes, eng="dve"):
        self.s.op(eng, lambda e: e.tensor_tensor(out=out, in0=in0, in1=in1, op=op), reads, writes)

    def ts(self, out, in0, s1, s2, op0, op1, reads, writes, eng="dve"):
        if op1 is None:
            self.s.op(eng, lambda e: e.tensor_scalar(out=out, in0=in0, scalar1=s1, scalar2=None, op0=op0), reads, writes)
        else:
            self.s.op(eng, lambda e: e.tensor_scalar(out=out, in0=in0, scalar1=s1, scalar2=s2, op0=op0, op1=op1), reads, writes)

    def stt(self, out, in0, scalar, in1, op0, op1, reads, writes):
        self.s.op("dve", lambda e: e.scalar_tensor_tensor(out=out, in0=in0, scalar=scalar, in1=in1, op0=op0, op1=op1), reads, writes)

    def copy(self, out, in_, reads, writes, eng="dve"):
        if eng == "act":
            self.s.op("act", lambda e: e.activation(out=out, in_=in_, func=AF.Copy), reads, writes)
        else:
            self.s.op(eng, lambda e: e.tensor_copy(out=out, in_=in_), reads, writes)

    def recip(self, out, in_, reads, writes):
        self.s.op("dve", lambda e: e.reciprocal(out=out, in_=in_), reads, writes)

    def memset(self, ap, val, writes, eng="dve"):
        self.s.op(eng, lambda e: e.memset(ap, val), (), writes)

    def mm(self, items, reads, writes):
        items = list(items)

        def fn(e):
            ins = None
            for (o, l, r, st, sp) in items:
                ins = e.matmul(o, lhsT=l, rhs=r, start=st, stop=sp)
            return ins

        self.s.op("pe", fn, reads, writes)

    def mmacc(self, out, pairs, reads, writes):
        n = len(pairs)
        self.mm([(out, l, r, i == 0, i == n - 1) for i, (l, r) in enumerate(pairs)], reads, writes)

    def transposes(self, items, reads, writes):
        items = list(items)

        def fn(e):
            ins = None
            for (o, i_, ident) in items:
                ins = e.transpose(o, i_, ident)
            return ins

        self.s.op("pe", fn, reads, writes)

    def dump(self, name, ap, reads, dt=F32):
        shape = list(ap.shape)
        d = self.nc.dram_tensor("dbg_" + name, shape, dt, kind="ExternalOutput").ap()
        self.s.dma("sp", [lambda e: e.dma_start(out=d, in_=ap)], "st_x", reads, ())

    def load(self, out, in_, semkey, writes, queue="sp"):
        self.s.dma(queue, [lambda e: e.dma_start(out=out, in_=in_)], semkey, (), writes)

    def gbank(self):
        b = self.gb[self.gbi % self.gb_n]
        self.gbi += 1
        return b

    def abank(self):
        b = self.ab[self.abi % self.ab_n]
        self.abi += 1
        return b

    def w_prefetch(self):
        while self.w_free > 0 and self.w_next_load < len(self.w_plan):
            k = self.w_next_load
            idx, ncols = self.w_plan[k]
            slot = k % NSLOT
            out = self.ring[:, slot, 0:ncols]
            in_ = self.wstream[idx, :, 0:ncols]
            self.s.dma("pool", [lambda e, o=out, i=in_: e.dma_start(out=o, in_=i)], "w%d" % slot, (), [self.slotT[slot]])
            self.w_next_load += 1
            self.w_free -= 1

    def w_acquire(self, expect_idx=None):
        k = self.w_next_use
        assert k < self.w_next_load, "weight stream underflow"
        if expect_idx is not None:
            assert self.w_plan[k][0] == expect_idx, (k, self.w_plan[k], expect_idx)
        self.w_next_use += 1
        slot = k % NSLOT
        return self.ring[:, slot, :], self.slotT[slot]

    def w_release(self, n=1):
        self.w_free += n
        self.w_prefetch()

    def build(self, piece_index, n_pieces):
        nc = self.nc
        st = self.stack
        L = self.layers
        NL = len(L)
        self.wstream = nc.dram_tensor("wstream", [n_pieces, 128, SLOT], F32, kind="ExternalInput").ap()
        self.x_in = nc.dram_tensor("x_in", [self.nseq, 128, 8, S], F32, kind="ExternalInput").ap()
        self.x_out = nc.dram_tensor("x_out", [self.nseq, 128, 8, S], F32, kind="ExternalOutput").ap()
        self.c_in = nc.dram_tensor("c_in", [128, 8, 2], F32, kind="ExternalInput").ap()
        self.adab_in = nc.dram_tensor("adab_in", [128, DEPTH, 72], F32, kind="ExternalInput").ap()
        self.normg_in = nc.dram_tensor("normg_in", [128, DEPTH, 3, 8], F32, kind="ExternalInput").ap()
        self.small_in = nc.dram_tensor("small_in", [128, SM_W], F32, kind="ExternalInput").ap()
        self.strips_in = nc.dram_tensor("strips_in", [13, 128, 384], F32, kind="ExternalInput").ap()
        self.consts_in = nc.dram_tensor("consts_in", [128, 4, 128], F32, kind="ExternalInput").ap()

        self.xT = self.sb("xT", [128, 8, S], F32)
        self.xT_T = [[T("x%d_%d" % (c, b)) for b in range(4)] for c in range(8)]
        self.hT = self.sb("hT", [128, 8, S], BF16)
        self.hT_T = [[T("h%d_%d" % (c, b)) for b in range(4)] for c in range(8)]
        self.ring = self.sb("ring", [128, NSLOT, SLOT], BF16)
        self.slotT = [T("slot%d" % i) for i in range(NSLOT)]
        self.arena = self.sb("arena", [128, ARENA // 4], F32)
        self.arenaT = T("arena")
        self.hid = [self.view("hid%d" % i, 8192 + 4096 * i, [128, 4, 512], BF16) for i in range(2)]
        self.hid_i = 0
        self.sq = Ring([self.view("sq0", 0, [128, 8, 512], BF16)] +
                       [self.view("sq%d" % (i + 1), 28672 + 8192 * i, [128, 8, 512], BF16) for i in range(2)])
        self.f32a = Ring([self.view("f32a%d" % i, 16384 + 2048 * i, [128, 512], F32) for i in range(4)])
        self.rstd = Ring([self.view("rstd%d" % i, 24576 + 2048 * i, [128, 512], F32) for i in range(2)])
        self.modT = self.buf("modT", [128, DEPTH, 9, 8, 2], F32)
        self.gmod = self.buf("gmod", [128, DEPTH, 3, 8, 2], F32)
        self.gate = self.buf("gate", [128, DEPTH, 3, 8, 2], F32)
        self.cT = self.buf("cT", [128, 8, 2], F32)
        self.scT = self.buf("scT", [128, 8, 2], BF16)
        self.adab = self.buf("adab", [128, DEPTH, 72], F32)
        self.normg = self.buf("normg", [128, DEPTH, 3, 8], F32)
        self.small = self.buf("small", [128, SM_W], F32)
        self.consts = self.buf("consts", [128, 4, 128], BF16)
        self.derived = self.buf("derived", [128, 128], F32)
        self.ones_bf = self.buf("ones_bf", [128, 128], BF16)
        self.epsT = self.buf("epsT", [128, 1], F32)

        banks = [st.enter_context(nc.psum_tensor("bank%d" % i, [128, 512], F32)) for i in range(8)]
        self.gb = [Buf(banks[i], "gb%d" % i) for i in range(4)]
        self.ab = [Buf(banks[4 + i], "ab%d" % i) for i in range(4)]
        self.gbi = 0
        self.abi = 0
        self.ab_n = 4
        self.gb_n = 4

        plan = []
        self.ada_overlap = self.do_ffn and self.do_mixer
        ada_up_front = [L[0]] if self.ada_overlap else L
        for l in ada_up_front:
            first, cols = piece_index["ada%d" % l]
            plan += [(first + i, c) for i, c in enumerate(cols)]
        for sq_ in range(self.nseq):
            for li, l in enumerate(L):
                for nm in ("ffn%d_0" % l, "ada", "mix%d" % l, "ffn%d_1" % l):
                    if nm == "ada":
                        if self.ada_overlap and sq_ == 0 and li + 1 < len(L):
                            first, cols = piece_index["ada%d" % L[li + 1]]
                            plan += [(first + i, c) for i, c in enumerate(cols)]
                        continue
                    if nm.startswith("mix") and not self.do_mixer:
                        continue
                    if nm.startswith("ffn") and not self.do_ffn:
                        continue
                    first, cols = piece_index[nm]
                    plan += [(first + i, c) for i, c in enumerate(cols)]
        self.w_plan = plan
        self.w_next_load = 0
        self.w_next_use = 0
        self.w_free = NSLOT
        self.piece_index = piece_index

        self.load(self.cT.ap[:], self.c_in[:, :, :], "ld_c", [self.cT.T])
        self.load(self.adab.ap[:], self.adab_in[:, :, :], "ld_adab", [self.adab.T])
        self.load(self.normg.ap[:], self.normg_in[:, :, :, :], "ld_normg", [self.normg.T])
        self.load(self.small.ap[:], self.small_in[:, :], "ld_small", [self.small.T])
        self.memset(self.ones_bf.ap[:], 1.0, [self.ones_bf.T])
        self.memset(self.epsT.ap[:], EPS, [self.epsT.T])
        self.w_prefetch()
        self.setup_extra()
        self.act(self.scT.ap[:], self.cT.ap[:], AF.Silu, [self.cT.T], [self.scT.T])
        self.ada_phase(ada_up_front)

        for s in range(self.nseq):
            allx = [t for r_ in self.xT_T for t in r_]
            self.s.dma("sp", [(lambda e, c=c, s=s: e.dma_start(out=self.xT[:, c, :], in_=self.x_in[s, :, c, :])) for c in range(8)],
                       "ld_x", (), allx)
            for l in L:
                if self.do_ffn:
                    if l == L[0] or not self.do_mixer:
                        self.barrier()
                    self.norm(l, 0, s)
                    if "h0" in self.debug and s == 0 and l == L[0]:
                        self.dump("h0", self.hT[:, :, :], [t for r in self.hT_T for t in r], BF16)
                        self.dump("modT", self.modT.ap[:], [self.modT.T])
                        self.dump("gmod", self.gmod.ap[:], [self.gmod.T])
                        self.dump("gate", self.gate.ap[:], [self.gate.T])
                    self.ffn(l, 0, s)
                    if self.ada_overlap and s == 0 and L.index(l) + 1 < len(L):
                        self.ada_phase([L[L.index(l) + 1]])
                if self.do_mixer:
                    self.barrier()
                    self.norm(l, 1, s)
                    self.barrier()
                    if l % 2 == 0:
                        self.even_mixer(l, s)
                    else:
                        self.odd_mixer(l, s)
                if self.do_ffn:
                    self.barrier()
                    self.norm(l, 2, s)
                    self.ffn(l, 1, s)
            self.s.dma("sp", [(lambda e, c=c, s=s: e.dma_start(out=self.x_out[s, :, c, :], in_=self.xT[:, c, :])) for c in range(8)],
                       "st_x", allx, ())
        self.s.final_wait("sp", ["st_x"])
        assert self.w_next_use == len(self.w_plan), (self.w_next_use, len(self.w_plan))
        self.emit()
        return nc

    def setup_extra(self):
        AX = mybir.AxisListType.X
        ctmp = self.view("ctmp", 0, [128, 4, 128], F32)
        self.load(ctmp.ap[:], self.consts_in[:, :, :], "ld_a", [ctmp.T])
        self.copy(self.consts.ap[:], ctmp.ap[:], [ctmp.T], [self.consts.T])
        self.ident = self.consts.ap[:, 0, :]
        self.blk64 = self.consts.ap[:, 1, :]
        self.mask_f = self.consts.ap[:, 2, :]
        self.mask_b = self.consts.ap[:, 3, :]
        sm = self.small.ap
        smT = self.small.T
        dv = self.derived.ap
        dT = self.derived.T
        t = self.view("setup_t", 4096, [128, 512], F32)
        for e in range(2):
            lam_init = 0.8 - 0.6 * math.exp(-0.3 * (2 * e))
            base = SM_DLAM + e * 256
            self.tt(t.ap[:, 0:64], sm[:, base:base + 64], sm[:, base + 64:base + 128], ALU.mult, [smT], [t.T])
            self.tt(t.ap[:, 64:128], sm[:, base + 128:base + 192], sm[:, base + 192:base + 256], ALU.mult, [smT], [t.T])
            self.s.op("dve", lambda e_: e_.tensor_reduce(out=t.ap[:, 128:130],
                                                         in_=t.ap[:, 0:128].rearrange("p (a b) -> p a b", b=64),
                                                         axis=AX, op=ALU.add), [t.T], [t.T])
            self.act(t.ap[:, 130:132], t.ap[:, 128:130], AF.Exp, [t.T], [t.T])
            self.tt(t.ap[:, 132:133], t.ap[:, 131:132], t.ap[:, 130:131], ALU.subtract, [t.T], [t.T])
            self.ts(dv[:, e:e + 1], t.ap[:, 132:133], -lam_init, None, ALU.add, None, [t.T], [dT])
            sb_ = SM_SUBLN + e * 128
            self.ts(sm[:, sb_:sb_ + 128], sm[:, sb_:sb_ + 128], 1.0 - lam_init, None, ALU.mult, None, [smT], [smT])
        self.act(dv[:, 2:18], sm[:, SM_SINK:SM_SINK + 16], AF.Exp, [smT], [dT])
        ex = t.ap[:, 256:320].rearrange("p (d l h) -> p d l h", d=2, l=4)
        self.act(t.ap[:, 256:320], sm[:, SM_CLB:SM_CLB + 64], AF.Exp, [smT], [t.T])
        ssum = t.ap[:, 320:336].rearrange("p (d h) -> p d h", d=2)
        self.s.op("dve", lambda e_: e_.tensor_reduce(out=ssum, in_=t.ap[:, 256:320].rearrange("p (d l h) -> p d h l", d=2, l=4),
                                                     axis=AX, op=ALU.add), [t.T], [t.T])
        rs = t.ap[:, 336:352].rearrange("p (d h) -> p d h", d=2)
        self.recip(t.ap[:, 336:352], t.ap[:, 320:336], [t.T], [t.T])
        e23 = t.ap[:, 352:368].rearrange("p (d h) -> p d h", d=2)
        self.tt(e23, ex[:, :, 2, :], ex[:, :, 3, :], ALU.add, [t.T], [t.T])
        self.tt(e23, e23, ex[:, :, 1, :], ALU.add, [t.T], [t.T])
        lbv = dv[:, 18:50].rearrange("p (d o h) -> p d o h", d=2, o=2)
        self.tt(lbv[:, :, 0, :], ex[:, :, 1, :], rs, ALU.mult, [t.T], [dT])
        self.tt(lbv[:, :, 1, :], e23, rs, ALU.mult, [t.T], [dT])
        self.ts(dv[:, 50:82], dv[:, 18:50], -1.0, 1.0, ALU.mult, ALU.add, [dT], [dT])
        self.ts(dv[:, 82:114], dv[:, 18:50], -1.0, None, ALU.add, None, [dT], [dT])
        self.barrier()

    def out_proj_chunk(self, l, s, ytq, wslot, wT):
        for tb in range(4):
            tsl = slice(tb * 512, (tb + 1) * 512)
            for c in range(8):
                by = self.abank()
                self.mm([(by.ap[:, :], wslot[:, c * 128:(c + 1) * 128], ytq.ap[:, tsl], True, True)], [wT, ytq.T], [by.T])
                self.stt(self.xT[:, c, tsl], by.ap[:], self.gate.ap[:, l, 1, c, s:s + 1], self.xT[:, c, tsl],
                         ALU.mult, ALU.add, [by.T, self.gate.T, self.xT_T[c][tb]], [self.xT_T[c][tb]])

    def qk_proj_norm(self, slot, sT, col0, dst, dst_cols, gcol, tmp, sqr, nchain=2):
        dsts = dst if isinstance(dst, list) else [(dst, slice(0, 128))]
        for t0_ in range(0, 4, nchain):
            tbs = list(range(t0_, min(4, t0_ + nchain)))
            banks, raws, sqs, b2s, rss = {}, {}, {}, {}, {}
            for tb in tbs:
                tsl = slice(tb * 512, (tb + 1) * 512)
                hTs = [self.hT_T[c][tb] for c in range(8)]
                banks[tb] = self.gbank()
                self.mmacc(banks[tb].ap[:, :], [(slot[:, c * 384 + col0:c * 384 + col0 + 128], self.hT[:, c, tsl]) for c in range(8)],
                           [sT] + hTs, [banks[tb].T])
            for tb in tbs:
                sqs[tb] = sqr.next()
                self.act(sqs[tb].ap[:], banks[tb].ap[:], AF.Square, [banks[tb].T], [sqs[tb].T])
            for tb in tbs:
                raws[tb] = tmp.next()
                self.copy(raws[tb].ap[:], banks[tb].ap[:], [banks[tb].T], [raws[tb].T], eng="act")
            for tb in tbs:
                b2s[tb] = self.abank()
                self.mm([(b2s[tb].ap[:, :], self.blk64, sqs[tb].ap[:], True, True)], [sqs[tb].T, self.consts.T], [b2s[tb].T])
            for tb in tbs:
                rss[tb] = tmp.next()
                self.act(rss[tb].ap[:], b2s[tb].ap[:], AF.Ln, [b2s[tb].T, self.epsT.T], [rss[tb].T],
                         scale=1.0 / 64, bias=self.epsT.ap[:, 0:1])
            for tb in tbs:
                self.act(rss[tb].ap[:], rss[tb].ap[:], AF.Exp, [rss[tb].T], [rss[tb].T], scale=-0.5)
            for tb in tbs:
                tsl = slice(tb * 512, (tb + 1) * 512)
                for (db, ps_) in dsts:
                    self.stt(db.ap[ps_, tsl], raws[tb].ap[ps_, :], gcol[ps_, :], rss[tb].ap[ps_, :],
                             ALU.mult, ALU.mult, [raws[tb].T, rss[tb].T, self.small.T], [db.T])

    def even_mixer(self, l, s):
        AX = mybir.AxisListType.X
        e = l // 2
        first = self.piece_index["mix%d" % l][0]
        sm = self.small.ap
        dv = self.derived.ap
        Q1p = self.view("Q1p", 0, [128, 2048], BF16)
        Q2p = self.view("Q2p", 4096, [128, 2048], BF16)
        KT = self.view("KT", 8192, [128, 2048], BF16)
        Va = self.view("Va", 12288, [128, 16, 130], BF16)
        strips = self.view("stripsA", 16512, [128, 4, 384], F32)
        PT = Ring([self.view("PT%d" % i, 22656 + 1024 * i, [128, 512], BF16) for i in range(3)])
        o1 = self.view("o1", 25728, [128, 4, 129], F32)
        o2 = self.view("o2", 27792, [128, 4, 129], F32)
        tmp = Ring([self.view("tmpA%d" % i, 29888 + 2048 * i, [128, 512], F32) for i in range(4)])
        ext = self.view("extA", 38080, [128, 9 * 128], F32)
        ystage = self.view("ystage", 46272, [128, 16, 128], BF16)
        ytq = self.view("ytq", 50368, [128, 2048], BF16)
        stat = self.view("stat", 54464, [128, 32], F32)
        SQ = Ring([self.view("sqA%d" % i, 54592 + 1024 * i, [128, 512], BF16) for i in range(4)])
        self.memset(Q1p.ap[64:128, :], 0.0, [Q1p.T])
        self.memset(Q2p.ap[0:64, :], 0.0, [Q2p.T])
        self.s.dma("sp", [lambda e_: e_.dma_start(out=strips.ap[:, :, :], in_=self.strips_in[0:4, :, :].rearrange("h p c -> p h c"))],
                   "ld_a", (), [strips.T])
        self.memset(Va.ap[:, :, 128:130], 1.0, [Va.T])
        for h in range(4):
            self.gb_n = 4
            slot, sT = self.w_acquire(first + 2 * h)
            self.qk_proj_norm(slot, sT, 0, [(Q1p, slice(0, 64)), (Q2p, slice(64, 128))], None,
                              sm[:, SM_QKG + e * 4 + 0:SM_QKG + e * 4 + 1], tmp, SQ, nchain=2)
            self.qk_proj_norm(slot, sT, 128, KT, None, sm[:, SM_QKG + e * 4 + 1:SM_QKG + e * 4 + 2], tmp, SQ, nchain=2)
            for tg in range(4):
                bank = self.gbank()
                items = []
                for tt_ in range(4):
                    tok = slice((tg * 4 + tt_) * 128, (tg * 4 + tt_ + 1) * 128)
                    for c in range(8):
                        items.append((bank.ap[:, tt_ * 128:(tt_ + 1) * 128], self.hT[:, c, tok],
                                      slot[:, c * 384 + 256:c * 384 + 384], c == 0, c == 7))
                self.mm(items, [sT] + [self.hT_T[c][tg] for c in range(8)], [bank.T])
                self.copy(Va.ap[:, tg * 4:(tg + 1) * 4, 0:128], bank.ap[:, :].rearrange("p (a b) -> p a b", b=128),
                          [bank.T], [Va.T], eng="act")
            self.w_release(1)
            if WARM_A:
                self.barrier()
                self.gb_n = 3
            clo = sm[:, SM_CLOHI + 2 * h:SM_CLOHI + 2 * h + 1]
            chi = sm[:, SM_CLOHI + 2 * h + 1:SM_CLOHI + 2 * h + 2]
            self.ts(ext.ap[:, 0:384], strips.ap[:, h, :], 0.0, chi, ALU.mult, ALU.add, [strips.T, self.small.T], [ext.T])
            self.copy(ext.ap[:, 384:768], strips.ap[:, h, :], [strips.T], [ext.T])
            self.ts(ext.ap[:, 768:1152], strips.ap[:, h, :], 0.0, clo, ALU.mult, ALU.add, [strips.T, self.small.T], [ext.T])
            for qb in range(4):
                qsl = slice(qb * 512, (qb + 1) * 512)
                for sidx in range(2):
                    QP = Q1p if sidx == 0 else Q2p
                    osb = o1 if sidx == 0 else o2
                    Ob = [self.abank() for _ in range(4)]

                    def s_mm(kt):
                        b = self.gbank()
                        self.mm([(b.ap[:, :], KT.ap[:, kt * 128:(kt + 1) * 128], QP.ap[:, qsl], True, True)],
                                [KT.T, QP.T], [b.T])
                        return b

                    nxt = s_mm(0)
                    for kt in range(16):
                        sbk = nxt
                        if kt < 15:
                            nxt = s_mm(kt + 1)
                        pt = PT.next()
                        ee = kt - 4 * qb
                        ds = [ee - j for j in range(4)]
                        near = [j for j in range(4) if abs(ds[j]) <= 1]
                        if not near:
                            bias = clo if ds[0] < 0 else chi
                            self.act(pt.ap[:], sbk.ap[:], AF.Exp, [sbk.T, self.small.T], [pt.T], scale=SCALE, bias=bias)
                        else:
                            c0 = (4 - ee) * 128
                            self.stt(sbk.ap[:, :], sbk.ap[:, :], SCALE, ext.ap[:, c0:c0 + 512], ALU.mult, ALU.add,
                                     [sbk.T, ext.T], [sbk.T])
                            self.act(pt.ap[:], sbk.ap[:], AF.Exp, [sbk.T], [pt.T])
                        self.mm([(Ob[j].ap[:, 0:129], pt.ap[:, j * 128:(j + 1) * 128], Va.ap[:, kt, 0:129], kt == 0, kt == 15)
                                 for j in range(4)], [pt.T, Va.T], [b.T for b in Ob])
                        if WARM_A:
                            self.mm([(self.gb[3].ap[:, :], KT.ap[:, 0:128], QP.ap[:, 0:512], True, True)] * WARM_A,
                                    [KT.T, QP.T], [])
                    for j in range(4):
                        self.copy(osb.ap[:, j, :], Ob[j].ap[:, 0:129], [Ob[j].T], [osb.T])
                    so = 4 * sidx
                    self.recip(stat.ap[:, so:so + 4], osb.ap[:, :, 128], [osb.T], [stat.T])
                    self.tt(osb.ap[:, :, 0:128], osb.ap[:, :, 0:128],
                            stat.ap[:, so:so + 4].unsqueeze(2).to_broadcast([128, 4, 128]), ALU.mult, [osb.T, stat.T], [osb.T])
                o1f = o1.ap[:, :, 0:128]
                o2f = o2.ap[:, :, 0:128]
                self.stt(o1f, o2f, dv[:, e:e + 1], o1f, ALU.mult, ALU.add, [o1.T, o2.T, self.derived.T], [o1.T])
                self.tt(o2f, o1f, o1f, ALU.mult, [o1.T], [o2.T])
                self.s.op("dve", lambda e_: e_.tensor_reduce(out=stat.ap[:, 8:12], in_=o2.ap[:, :, 0:128], axis=AX, op=ALU.add),
                          [o2.T], [stat.T])
                self.act(stat.ap[:, 12:16], stat.ap[:, 8:12], AF.Sqrt, [stat.T, self.epsT.T], [stat.T],
                         scale=1.0 / 128, bias=self.epsT.ap[:, 0:1])
                self.recip(stat.ap[:, 16:20], stat.ap[:, 12:16], [stat.T], [stat.T])
                for j in range(4):
                    self.stt(ystage.ap[:, qb * 4 + j, :], o1.ap[:, j, 0:128], stat.ap[:, 16 + j:17 + j],
                             sm[:, SM_SUBLN + e * 128:SM_SUBLN + (e + 1) * 128], ALU.mult, ALU.mult,
                             [o1.T, stat.T, self.small.T], [ystage.T])
                tb_ = self.gbank()
                tbf = tb_.ap[:, :].bitcast(BF16)
                self.transposes([(tbf[:, j * 128:(j + 1) * 128], ystage.ap[:, qb * 4 + j, :], self.ident) for j in range(4)],
                                [ystage.T, self.consts.T], [tb_.T])
                self.copy(ytq.ap[:, qsl], tbf[:, 0:512], [tb_.T], [ytq.T], eng="act")
            wslot, wT = self.w_acquire(first + 2 * h + 1)
            self.out_proj_chunk(l, s, ytq, wslot, wT)
            self.w_release(1)
        self.gb_n = 4
        self.barrier()
        self.even_mixer_b(l, s)
        self.barrier()

    def even_mixer_b(self, l, s):
        e = l // 2
        first = self.piece_index["mix%d" % l][0] + 8
        sm = self.small.ap
        dv = self.derived.ap
        Qp = [self.view("Qpb%d" % j, 4096 * j, [128, 2048], BF16) for j in range(4)]
        KTb = self.view("KTb", 16384, [128, 2048], BF16)
        Vb = self.view("Vb", 20480, [128, 16, 2, 66], BF16)
        strips = self.view("stripsB", 24832, [128, 8, 384], F32)
        maskB = self.view("maskB", 37120, [128, 384], F32)
        PT = Ring([self.view("PTb%d" % i, 38656 + 1024 * i, [128, 512], BF16) for i in range(3)])
        tmp = Ring([self.view("tmpB%d" % i, 41728 + 2048 * i, [128, 512], F32) for i in range(4)])
        ystage = self.view("ystageB", 49920, [128, 16, 128], BF16)
        ytq = self.view("ytqB", 54016, [128, 2048], BF16)
        stat = self.view("statB", 58112, [128, 32], F32)
        ostage = self.view("ostageB", 41728, [128, 16, 65], F32)
        for j in range(4):
            zs = slice(64, 128) if j % 2 == 0 else slice(0, 64)
            self.memset(Qp[j].ap[zs, :], 0.0, [Qp[j].T])
        self.s.dma("sp", [lambda e_: e_.dma_start(out=strips.ap[:, :, :], in_=self.strips_in[4:12, :, :].rearrange("h p c -> p h c")),
                          lambda e_: e_.dma_start(out=maskB.ap[:, :], in_=self.strips_in[12, :, :])],
                   "ld_a", (), [strips.T, maskB.T])
        self.tt(strips.ap[:, :, :], strips.ap[:, :, :], maskB.ap[:, :].unsqueeze(1).to_broadcast([128, 8, 384]), ALU.add,
                [strips.T, maskB.T], [strips.T])
        self.memset(Vb.ap[:, :, :, 64:66], 1.0, [Vb.T])
        slot, sT = self.w_acquire(first)
        for tg in range(4):
            bank = self.gbank()
            items = []
            for tt_ in range(4):
                tok = slice((tg * 4 + tt_) * 128, (tg * 4 + tt_ + 1) * 128)
                for c in range(8):
                    items.append((bank.ap[:, tt_ * 128:(tt_ + 1) * 128], self.hT[:, c, tok],
                                  slot[:, c * 128:(c + 1) * 128], c == 0, c == 7))
            self.mm(items, [sT] + [self.hT_T[c][tg] for c in range(8)], [bank.T])
            self.copy(Vb.ap[:, tg * 4:(tg + 1) * 4, :, 0:64],
                      bank.ap[:, :].rearrange("p (a g d) -> p a g d", g=2, d=64), [bank.T], [Vb.T], eng="act")
        self.w_release(1)
        for g in range(2):
            self.barrier()
            slot, sT = self.w_acquire(first + 1 + 3 * g)
            for cb in range(2):
                self.qk_proj_norm(slot, sT, cb * 128, [(Qp[2 * cb], slice(0, 64)), (Qp[2 * cb + 1], slice(64, 128))], None,
                                  sm[:, SM_QKG + e * 4 + 2:SM_QKG + e * 4 + 3], tmp, PT)
            self.qk_proj_norm(slot, sT, 256, KTb, None, sm[:, SM_QKG + e * 4 + 3:SM_QKG + e * 4 + 4], tmp, PT)
            self.w_release(1)
            self.barrier()
            for cb in range(2):
                for hh in range(2):
                    hq = g * 4 + cb * 2 + hh
                    ph = slice(64 * hh, 64 * hh + 64)
                    Ob = {}
                    qpj = Qp[2 * cb + hh]

                    def s_mm(kt):
                        qt0 = max(kt - 1, 0)
                        W = (min(kt + 1, 15) - qt0 + 1) * 128
                        b_ = self.gbank()
                        self.mm([(b_.ap[:, 0:W], KTb.ap[:, kt * 128:(kt + 1) * 128], qpj.ap[:, qt0 * 128:qt0 * 128 + W],
                                  True, True)], [KTb.T, qpj.T], [b_.T])
                        return b_

                    nxt = [s_mm(0), s_mm(1)]
                    pts = {}

                    def bias_exp(kt, sbk):
                        qt0 = max(kt - 1, 0)
                        W = (min(kt + 1, 15) - qt0 + 1) * 128
                        off = 128 if kt == 0 else 0
                        self.stt(sbk.ap[:, 0:W], sbk.ap[:, 0:W], SCALE, strips.ap[:, hq, off:off + W], ALU.mult, ALU.add,
                                 [sbk.T, strips.T], [sbk.T])
                        pt = PT.next()
                        self.act(pt.ap[:, 0:W], sbk.ap[:, 0:W], AF.Exp, [sbk.T], [pt.T])
                        pts[kt] = pt

                    bias_exp(0, nxt.pop(0))
                    for kt in range(16):
                        qt0 = max(kt - 1, 0)
                        qt1 = min(kt + 1, 15)
                        if kt + 2 < 16:
                            nxt.append(s_mm(kt + 2))
                        if kt + 1 < 16:
                            bias_exp(kt + 1, nxt.pop(0))
                        pt = pts.pop(kt)
                        items = []
                        for qt in range(qt0, qt1 + 1):
                            if qt not in Ob:
                                Ob[qt] = self.abank()
                            items.append((Ob[qt].ap[:, 0:65], pt.ap[:, (qt - qt0) * 128:(qt - qt0 + 1) * 128],
                                          Vb.ap[:, kt, g, 0:65], kt == max(qt - 1, 0), kt == min(qt + 1, 15)))
                        self.mm(items, [pt.T, Vb.T], [Ob[qt].T for qt in range(qt0, qt1 + 1)])
                        done = [qt for qt in range(qt0, qt1 + 1) if kt == min(qt + 1, 15)]
                        for qt in done:
                            ob = Ob.pop(qt)
                            self.copy(ostage.ap[:, qt, :], ob.ap[:, 0:65], [ob.T], [ostage.T])
                    self.ts(stat.ap[:, 0:16], ostage.ap[:, :, 64], dv[:, 2 + e * 8 + hq:3 + e * 8 + hq], None, ALU.add, None,
                            [ostage.T, self.derived.T], [stat.T])
                    self.recip(stat.ap[:, 16:32], stat.ap[:, 0:16], [stat.T], [stat.T])
                    self.tt(ystage.ap[:, :, hh * 64:(hh + 1) * 64], ostage.ap[:, :, 0:64],
                            stat.ap[:, 16:32].unsqueeze(2).to_broadcast([128, 16, 64]), ALU.mult, [ostage.T, stat.T], [ystage.T])
                for qb in range(4):
                    tb_ = self.gbank()
                    tbf = tb_.ap[:, :].bitcast(BF16)
                    self.transposes([(tbf[:, j * 128:(j + 1) * 128], ystage.ap[:, qb * 4 + j, :], self.ident) for j in range(4)],
                                    [ystage.T, self.consts.T], [tb_.T])
                    self.copy(ytq.ap[:, qb * 512:(qb + 1) * 512], tbf[:, 0:512], [tb_.T], [ytq.T], eng="act")
                wslot, wT = self.w_acquire(first + 2 + 3 * g + cb)
                self.out_proj_chunk(l, s, ytq, wslot, wT)
                self.w_release(1)

    def ada_phase(self, layers):
        for l in layers:
            bank = self.abank()
            for pj in range(N_ADA_PIECES):
                slot, sT = self.w_acquire(self.piece_index["ada%d" % l][0] + pj)
                items = []
                for mcc in range(3):
                    mc = pj * 3 + mcc
                    for dc in range(8):
                        items.append((bank.ap[:, mc * 2:mc * 2 + 2],
                                      slot[:, dc * 384 + mcc * 128: dc * 384 + (mcc + 1) * 128],
                                      self.scT.ap[:, dc, :], dc == 0, dc == 7))
                self.mm(items, [sT, self.scT.T], [bank.T])
                self.w_release(1)
            self.tt(self.modT.ap[:, l, :, :, :].rearrange("p m c s -> p (m c) s"),
                    bank.ap[:, 0:144].rearrange("p (m s) -> p m s", s=2),
                    self.adab.ap[:, l, :].unsqueeze(2).to_broadcast([128, 72, 2]),
                    ALU.add, [bank.T, self.adab.T], [self.modT.T])
            for j in range(3):
                self.stt(self.gmod.ap[:, l, j, :, :], self.modT.ap[:, l, 3 * j + 1, :, :], 1.0,
                         self.normg.ap[:, l, j, :].unsqueeze(2).to_broadcast([128, 8, 2]),
                         ALU.add, ALU.mult, [self.modT.T, self.normg.T], [self.gmod.T])
                self.ts(self.gate.ap[:, l, j, :, :], self.modT.ap[:, l, 3 * j + 2, :, :],
                        (1.0 if j == 1 else 0.5), None, ALU.mult, None, [self.modT.T], [self.gate.T])

    def norm(self, l, j, s):
        sqs, banks, rstds = {}, {}, {}

        def square(tb):
            tsl = slice(tb * 512, (tb + 1) * 512)
            xTs = [self.xT_T[c][tb] for c in range(8)]
            sq = self.sq.next()
            self.tt(sq.ap[:], self.xT[:, :, tsl], self.xT[:, :, tsl], ALU.mult, xTs, [sq.T], eng="pool")
            sqs[tb] = sq

        def sumsq(tb):
            sq = sqs.pop(tb)
            bank = self.gbank()
            self.mmacc(bank.ap[:, :], [(self.ones_bf.ap[:], sq.ap[:, c, :]) for c in range(8)],
                       [sq.T, self.ones_bf.T], [bank.T])
            banks[tb] = bank

        def lnexp(tb):
            bank = banks.pop(tb)
            rstd = self.rstd.next()
            self.act(rstd.ap[:], bank.ap[:], AF.Ln, [bank.T, self.epsT.T], [rstd.T], scale=1.0 / D, bias=self.epsT.ap[:, 0:1])
            self.act(rstd.ap[:], rstd.ap[:], AF.Exp, [rstd.T], [rstd.T], scale=-0.5)
            rstds[tb] = rstd

        def apply(tb):
            tsl = slice(tb * 512, (tb + 1) * 512)
            rstd = rstds.pop(tb)
            for c in range(8):
                t1 = self.f32a.next()
                self.stt(t1.ap[:], self.xT[:, c, tsl], self.gmod.ap[:, l, j, c, s:s + 1], rstd.ap[:],
                         ALU.mult, ALU.mult, [self.xT_T[c][tb], self.gmod.T, rstd.T], [t1.T])
                self.act(self.hT[:, c, tsl], t1.ap[:], AF.Identity, [t1.T, self.modT.T], [self.hT_T[c][tb]],
                         bias=self.modT.ap[:, l, 3 * j, c, s:s + 1], scale=1.0)

        square(0)
        square(1)
        sumsq(0)
        lnexp(0)
        for tb in range(4):
            if tb + 2 < 4:
                square(tb + 2)
            if tb + 1 < 4:
                sumsq(tb + 1)
                lnexp(tb + 1)
            apply(tb)

    def ffn(self, l, j, s):
        first = self.piece_index["ffn%d_%d" % (l, j)][0]
        jm = 0 if j == 0 else 2
        for (f0, f1) in FGROUPS:
            slots = [self.w_acquire(first + f) for f in range(f0, f1)]
            ng = f1 - f0
            for tb in range(4):
                tsl = slice(tb * 512, (tb + 1) * 512)
                hTs = [self.hT_T[c][tb] for c in range(8)]
                hid = self.hid[self.hid_i % 2]
                self.hid_i += 1
                for fi in range(ng):
                    slot, sT = slots[fi]
                    ba = self.gbank()
                    bb = self.gbank()
                    self.mmacc(ba.ap[:, :], [(slot[:, c * 256:c * 256 + 128], self.hT[:, c, tsl]) for c in range(8)],
                               [sT] + hTs, [ba.T])
                    self.mmacc(bb.ap[:, :], [(slot[:, c * 256 + 128:c * 256 + 256], self.hT[:, c, tsl]) for c in range(8)],
                               [sT] + hTs, [bb.T])
                    sa = self.f32a.next()
                    self.act(sa.ap[:], ba.ap[:], AF.Silu, [ba.T], [sa.T])
                    self.tt(hid.ap[:, fi, :], sa.ap[:], bb.ap[:], ALU.mult, [sa.T, bb.T], [hid.T])
                for c in range(8):
                    by = self.abank()
                    self.mmacc(by.ap[:, :], [(slots[fi][0][:, 2048 + c * 128:2048 + (c + 1) * 128], hid.ap[:, fi, :])
                                             for fi in range(ng)],
                               [hid.T] + [sl[1] for sl in slots], [by.T])
                    self.stt(self.xT[:, c, tsl], by.ap[:], self.gate.ap[:, l, jm, c, s:s + 1], self.xT[:, c, tsl],
                             ALU.mult, ALU.add, [by.T, self.gate.T, self.xT_T[c][tb]], [self.xT_T[c][tb]])
            self.w_release(ng)

    def odd_mixer(self, l, s):
        o = l // 2
        first = self.piece_index["mix%d" % l][0]
        sm = self.small.ap
        dv = self.derived.ap
        qs = self.view("qs", 0, [128, 2048], F32)
        V = self.view("Vo", 8192, [128, 16, 128], BF16)
        gs = self.view("gs", 12288, [128, 2048], BF16)
        qtl = [self.view("qtl%d" % d, 16384 + 4096 * d, [128, 2048], BF16) for d in range(2)]
        ktl = [self.view("ktl%d" % d, 24576 + 4096 * d, [128, 2048], BF16) for d in range(2)]
        khat = [self.view("khat%d" % d, 32768 + 4096 * d, [128, 16, 128], BF16) for d in range(2)]
        sets = [[self.view("t%d_%d" % (d, j), 40960 + 8192 * d + 2048 * j, [128, 512], F32) for j in range(4)] for d in range(2)]
        khTs = [self.view("khT%d" % d, 58368 + 1024 * d, [128, 512], BF16) for d in range(2)]
        self.ab_n = 3
        smask = self.ab[3]
        self.memset(smask.ap[:, :], 1.0, [smask.T])
        self.memset(smask.ap[:, :].rearrange("p (c k) -> p c k", k=64)[:, :, 0:1], 0.0, [smask.T])
        st8 = self.view("st8", 58112, [128, 16], F32)
        Shat = [self.view("Shat%d" % d, 40960 + 8192 * d, [128, 32, 128], BF16) for d in range(2)]
        cs = self.view("cs", 57344, [128, 2, 2, 32], F32)
        yq = Ring([self.view("yq%d" % i, 58368 + 1024 * i, [128, 512], BF16) for i in range(2)])
        Am = Ring([self.view("Am%d" % i, 256 * i, [128, 128], BF16) for i in range(4)])
        Sst = [[self.view("Sst%d_%d" % (d, k), 7168 + 512 * k + 0 * d, [128, 128], F32) if d == 0 else
                self.view("Sst%d_%d" % (d, k), 1024 + 512 * k, [128, 128], F32) for k in range(2)] for d in range(2)]
        Ost = self.view("Ost", 2048, [128, 512], F32)
        sqO = self.view("sqO", 4096, [128, 512], BF16)
        rsO = self.view("rsO", 5120, [128, 512], F32)


        def c3(buf):
            return buf.ap[:, :].rearrange("p (c k) -> p c k", k=64)

        for h in range(8):
            p1, p1T = self.w_acquire(first + 3 * h)
            p2, p2T = self.w_acquire(first + 3 * h + 1)
            for tb in range(4):
                tsl = slice(tb * 512, (tb + 1) * 512)
                hTs = [self.hT_T[c][tb] for c in range(8)]
                bq = self.gbank()
                self.mmacc(bq.ap[:, :], [(p1[:, c * 384:c * 384 + 128], self.hT[:, c, tsl]) for c in range(8)], [p1T] + hTs, [bq.T])
                self.act(qs.ap[:, tsl], bq.ap[:], AF.Silu, [bq.T], [qs.T])
                bg = self.gbank()
                self.mmacc(bg.ap[:, :], [(p2[:, c * 256 + 128:c * 256 + 256], self.hT[:, c, tsl]) for c in range(8)], [p2T] + hTs, [bg.T])
                self.act(gs.ap[:, tsl], bg.ap[:], AF.Silu, [bg.T], [gs.T])
            for tg in range(4):
                bank = self.gbank()
                items = []
                for tt_ in range(4):
                    tok = slice((tg * 4 + tt_) * 128, (tg * 4 + tt_ + 1) * 128)
                    for c in range(8):
                        items.append((bank.ap[:, tt_ * 128:(tt_ + 1) * 128], self.hT[:, c, tok],
                                      p2[:, c * 256:c * 256 + 128], c == 0, c == 7))
                self.mm(items, [p2T] + [self.hT_T[c][tg] for c in range(8)], [bank.T])
                self.copy(V.ap[:, tg * 4:(tg + 1) * 4, :], bank.ap[:, :].rearrange("p (a b) -> p a b", b=128),
                          [bank.T], [V.T], eng="act")
            for tb in range(4):
                tsl = slice(tb * 512, (tb + 1) * 512)
                csl = slice(tb * 8, (tb + 1) * 8)
                hTs = [self.hT_T[c][tb] for c in range(8)]
                bfs = []
                for d in range(2):
                    bf_ = self.gbank()
                    self.mmacc(bf_.ap[:, :], [(p1[:, c * 384 + 128 * (1 + d):c * 384 + 128 * (2 + d)], self.hT[:, c, tsl])
                                              for c in range(8)], [p1T] + hTs, [bf_.T])
                    bfs.append(bf_)
                cols = []
                for d in range(2):
                    ci = (d * 2 + o) * 8 + h
                    cols.append((dv[:, 18 + ci:19 + ci], dv[:, 50 + ci:51 + ci], dv[:, 82 + ci:83 + ci]))
                for d in range(2):
                    S1, KK, G, TM = sets[d]
                    self.act(S1.ap[:], bfs[d].ap[:], AF.Sigmoid, [bfs[d].T], [S1.T])
                for d in range(2):
                    S1, KK, G, TM = sets[d]
                    lbc, omlc, nomlc = cols[d]
                    self.ts(KK.ap[:], S1.ap[:], nomlc, omlc, ALU.mult, ALU.add, [S1.T, self.derived.T], [KK.T])
                for d in range(2):
                    S1, KK, G, TM = sets[d]
                    lbc, omlc, nomlc = cols[d]
                    self.act(S1.ap[:], S1.ap[:], AF.Ln, [S1.T, self.derived.T, KK.T], [S1.T], scale=omlc, bias=lbc)
                for d in range(2):
                    S1, KK, G, TM = sets[d]
                    self.s.op("dve", lambda e_, G=G, S1=S1: e_.tensor_tensor_scan(out=G.ap[:], data0=smask.ap[:], data1=S1.ap[:],
                                                                                  initial=0.0, op0=ALU.mult, op1=ALU.add),
                              [smask.T, S1.T], [G.T])
                S1, KK, G, TM = sets[0]
                self.act(cs.ap[:, 0, 0, csl], c3(G)[:, :, 31], AF.Exp, [G.T], [cs.T])
                self.act(cs.ap[:, 0, 1, csl], c3(G)[:, :, 63], AF.Exp, [G.T], [cs.T])
                self.tt(c3(TM), c3(G), c3(G)[:, :, 31:32].to_broadcast([128, 8, 64]), ALU.subtract, [G.T], [TM.T])
                S1, KK, G, TM = sets[1]
                self.act(cs.ap[:, 1, 1, csl], c3(G)[:, :, 63], AF.Exp, [G.T], [cs.T])
                self.copy(st8.ap[:, 0:8], c3(G)[:, :, 63], [G.T], [st8.T])
                self.tt(G.ap[:], G.ap[:], S1.ap[:], ALU.subtract, [G.T, S1.T], [G.T])
                self.tt(st8.ap[:, 0:8], st8.ap[:, 0:8], c3(G)[:, :, 32], ALU.subtract, [st8.T, G.T], [st8.T])
                self.act(cs.ap[:, 1, 0, csl], st8.ap[:, 0:8], AF.Exp, [st8.T], [cs.T])
                self.tt(c3(TM), c3(G)[:, :, 32:33].to_broadcast([128, 8, 64]), c3(G), ALU.subtract, [G.T], [TM.T])
                for d in range(2):
                    S1, KK, G, TM = sets[d]
                    self.act(S1.ap[:], TM.ap[:], AF.Exp, [TM.T], [S1.T])
                for d in range(2):
                    S1, KK, G, TM = sets[d]
                    self.tt(qtl[d].ap[:, tsl], qs.ap[:, tsl], S1.ap[:], ALU.mult, [qs.T, S1.T], [qtl[d].T], eng="pool")
                for d in range(2):
                    S1, KK, G, TM = sets[d]
                    self.act(S1.ap[:], TM.ap[:], AF.Exp, [TM.T], [S1.T], scale=-1.0)
                for d in range(2):
                    S1, KK, G, TM = sets[d]
                    self.tt(ktl[d].ap[:, tsl], KK.ap[:], S1.ap[:], ALU.mult, [KK.T, S1.T], [ktl[d].T], eng="pool")
                S1, KK, G, TM = sets[0]
                self.tt(c3(TM), c3(G)[:, :, 63:64].to_broadcast([128, 8, 64]), c3(G), ALU.subtract, [G.T], [TM.T])
                self.act(TM.ap[:], TM.ap[:], AF.Exp, [TM.T], [TM.T])
                S1, KK, G, TM = sets[1]
                self.act(TM.ap[:], G.ap[:], AF.Exp, [G.T], [TM.T])
                for d in range(2):
                    S1, KK, G, TM = sets[d]
                    self.tt(khTs[d].ap[:], KK.ap[:], TM.ap[:], ALU.mult, [KK.T, TM.T], [khTs[d].T])
                for d in range(2):
                    tb_ = self.gbank()
                    tbf = tb_.ap[:, :].bitcast(BF16)
                    self.transposes([(tbf[:, j * 128:(j + 1) * 128], khTs[d].ap[:, j * 128:(j + 1) * 128], self.ident)
                                     for j in range(4)], [khTs[d].T, self.consts.T], [tb_.T])
                    self.copy(khat[d].ap[:, tb * 4:(tb + 1) * 4, :], tbf[:, 0:512].rearrange("p (a b) -> p a b", b=128),
                              [tb_.T], [khat[d].T], eng="act")
            self.w_release(2)
            wslot, wT = self.w_acquire(first + 3 * h + 2)
            self.barrier()
            for i in range(32):
                for d in range(2):
                    c = i if d == 0 else 31 - i
                    Sprev = Sst[d][(i + 1) % 2]
                    Snew = Sst[d][i % 2]
                    rows = slice((c % 2) * 64, (c % 2) * 64 + 64)
                    if i == 0:
                        self.memset(Shat[d].ap[:, c, :], 0.0, [Shat[d].T], eng="pool")
                    else:
                        self.act(Shat[d].ap[:, c, :], Sprev.ap[:], AF.Copy, [Sprev.T, cs.T], [Shat[d].T],
                                 scale=cs.ap[:, d, 0, c:c + 1])
                    bs = self.gbank()
                    self.mm([(bs.ap[:, 0:128], khat[d].ap[rows, c // 2, :], V.ap[rows, c // 2, :], True, True)],
                            [khat[d].T, V.T], [bs.T])
                    if i == 0:
                        self.copy(Snew.ap[:], bs.ap[:, 0:128], [bs.T], [Snew.T])
                    else:
                        self.stt(Snew.ap[:], Sprev.ap[:], cs.ap[:, d, 1, c:c + 1], bs.ap[:, 0:128], ALU.mult, ALU.add,
                                 [Sprev.T, cs.T, bs.T], [Snew.T])
            if "odd" in self.debug and h == 0 and s == 0:
                self.barrier()
                for d in range(2):
                    self.dump("Shat%d" % d, Shat[d].ap[:], [Shat[d].T], BF16)
                self.barrier()
            def d_blocks(qb):
                bO = self.abank()
                for blk in range(4):
                    nb = qb * 4 + blk
                    tok = slice(nb * 128, (nb + 1) * 128)
                    ams = []
                    for d in range(2):
                        ba = self.gbank()
                        self.mm([(ba.ap[:, 0:128], ktl[d].ap[:, tok], qtl[d].ap[:, tok], True, True)], [ktl[d].T, qtl[d].T], [ba.T])
                        am = Am.next()
                        self.tt(am.ap[:], ba.ap[:, 0:128], self.mask_f if d == 0 else self.mask_b, ALU.mult,
                                [ba.T, self.consts.T], [am.T])
                        ams.append(am)
                    oc = bO.ap[:, blk * 128:(blk + 1) * 128]
                    items = [(oc, V.ap[:, nb, :], ams[0].ap[:], True, False),
                             (oc, V.ap[:, nb, :], ams[1].ap[:], False, False)]
                    for d in range(2):
                        for hf in range(2):
                            c = 2 * nb + hf
                            items.append((bO.ap[:, blk * 128 + hf * 64:blk * 128 + (hf + 1) * 64], Shat[d].ap[:, c, :],
                                          qtl[d].ap[:, nb * 128 + hf * 64:nb * 128 + (hf + 1) * 64], False, d == 1 and hf == 1))
                    self.mm(items, [V.T, ams[0].T, ams[1].T, Shat[0].T, Shat[1].T, qtl[0].T, qtl[1].T], [bO.T])
                return bO

            def d_finish(qb, bO):
                qsl = slice(qb * 512, (qb + 1) * 512)
                self.act(Ost.ap[:], bO.ap[:], AF.Copy, [bO.T], [Ost.T])
                self.act(sqO.ap[:], bO.ap[:], AF.Square, [bO.T], [sqO.T])
                b2 = self.gbank()
                self.mm([(b2.ap[:, :], self.ones_bf.ap[:], sqO.ap[:], True, True)], [sqO.T, self.ones_bf.T], [b2.T])
                self.act(rsO.ap[:], b2.ap[:], AF.Ln, [b2.T, self.epsT.T], [rsO.T], scale=1.0 / 128, bias=self.epsT.ap[:, 0:1])
                self.act(rsO.ap[:], rsO.ap[:], AF.Exp, [rsO.T], [rsO.T], scale=-0.5)
                self.tt(Ost.ap[:], Ost.ap[:], rsO.ap[:], ALU.mult, [Ost.T, rsO.T], [Ost.T])
                y = yq.next()
                self.stt(y.ap[:], Ost.ap[:], sm[:, SM_OUTG + o:SM_OUTG + o + 1], gs.ap[:, qsl], ALU.mult, ALU.mult,
                         [Ost.T, self.small.T, gs.T], [y.T])
                for c in range(8):
                    by = self.gbank()
                    self.mm([(by.ap[:, :], wslot[:, c * 128:(c + 1) * 128], y.ap[:], True, True)], [wT, y.T], [by.T])
                    self.stt(self.xT[:, c, qsl], by.ap[:], self.gate.ap[:, l, 1, c, s:s + 1], self.xT[:, c, qsl],
                             ALU.mult, ALU.add, [by.T, self.gate.T, self.xT_T[c][qb]], [self.xT_T[c][qb]])

            prev = None
            for qb in range(4):
                bO = d_blocks(qb)
                if prev is not None:
                    d_finish(*prev)
                prev = (qb, bO)
            d_finish(*prev)
            self.w_release(1)
            self.barrier()
        self.ab_n = 4

    def emit(self):
        nc = self.nc
        keys = sorted(self.s.val.keys())
        for k in keys:
            self.sems[k] = self.stack.enter_context(nc.semaphore(k))
        sems = self.sems
        q = self.s.q

        def run(e, ops):
            for waits, fn, inc in ops:
                for k, v in waits:
                    e.wait_ge(sems[k], v)
                if fn is None:
                    continue
                ins = fn(e)
                ins.then_inc(sems[inc[0]], inc[1])

        with nc.Block() as block:
            @block.tensor
            def _(e):
                run(e, q["pe"])

            @block.scalar
            def _(e):
                run(e, q["act"])

            @block.vector
            def _(e):
                run(e, q["dve"])

            @block.gpsimd
            def _(e):
                run(e, q["pool"])

            @block.sync
            def _(e):
                run(e, q["sp"])
        self.stack.close()


def make_stream(inputs, layers, do_mixer=True, do_ffn=True):
    pieces = []
    index = {}

    def add(name, plist, cols):
        index[name] = (len(pieces), cols)
        pieces.extend(plist)

    for l in layers:
        add("ada%d" % l, ada_pieces(inputs["ada_w"], l), [SLOT] * N_ADA_PIECES)
    for l in layers:
        if do_ffn:
            add("ffn%d_0" % l, ffn_pieces(inputs["ffn_up"], inputs["ffn_down"], l, 0), [SLOT] * NF)
        if do_mixer:
            if l % 2 == 0:
                add("mix%d" % l, even_pieces(inputs["even_w_in"], inputs["even_w_out"], l // 2),
                    EVEN_COLS)
            else:
                add("mix%d" % l, odd_pieces(inputs["odd_w_in"], inputs["odd_w_out"], l // 2),
                    ODD_COLS)
        if do_ffn:
            add("ffn%d_1" % l, ffn_pieces(inputs["ffn_up"], inputs["ffn_down"], l, 1), [SLOT] * NF)
    return np.stack(pieces, axis=0), index


def lay_x(xb):
    return np.ascontiguousarray(xb.reshape(S, 8, 128).transpose(2, 1, 0))


def unlay_x(xt):
    return np.ascontiguousarray(xt.transpose(2, 1, 0).reshape(S, D))


def common_maps(inputs, x_cur, batch_ids_per_core, wstream):
    adab = np.ascontiguousarray(inputs["ada_b"].reshape(DEPTH, 72, 128).transpose(2, 0, 1))
    normg = np.ascontiguousarray(inputs["norm_g"].reshape(DEPTH, 3, 8, 128).transpose(3, 0, 1, 2))
    small = small_inputs(inputs)
    strips = strips_input(inputs)
    consts = consts_input()
    maps = []
    for bids in batch_ids_per_core:
        xin = np.stack([lay_x(x_cur[b]) for b in bids], axis=0)
        cT = np.stack([inputs["c"][b].reshape(8, 128).T for b in bids], axis=2)
        if len(bids) == 1:
            cT = np.concatenate([cT, cT], axis=2)
        maps.append({"wstream": wstream, "x_in": xin, "c_in": np.ascontiguousarray(cT, dtype=np.float32),
                     "adab_in": adab, "normg_in": normg, "small_in": small, "strips_in": strips, "consts_in": consts})
    return maps


def run_layers(inputs, x_cur, layers, batch_ids_per_core, do_mixer=True, do_ffn=True, trace=False, debug=None):
    inputs = {k: np.asarray(v, dtype=np.float32) for k, v in inputs.items()}
    nseq = len(batch_ids_per_core[0])
    wstream, index = make_stream(inputs, layers, do_mixer, do_ffn)
    b = Builder(layers, nseq=nseq, do_mixer=do_mixer, do_ffn=do_ffn, debug=debug)
    nc = b.build(index, wstream.shape[0])
    maps = common_maps(inputs, x_cur, batch_ids_per_core, wstream)
    res = run_bass_kernel_spmd(nc, maps, core_ids=list(range(len(maps))), **({"trace": True} if trace else {}))
    out = np.array(x_cur, dtype=np.float32, copy=True)
    for ci, bids in enumerate(batch_ids_per_core):
        xo = res.results[ci]["x_out"]
        for si, bb in enumerate(bids):
            out[bb] = unlay_x(xo[si])
    return out, res


FUSED = True


def kernel(**inputs):
    inputs = {k: np.asarray(v, dtype=np.float32) for k, v in inputs.items()}
    x = inputs["x"]
    bids = [[2 * i, 2 * i + 1] for i in range(NCORES)]
    if FUSED:
        out, _ = run_layers(inputs, x, list(range(DEPTH)), bids)
    else:
        out = x
        for l in range(DEPTH):
            out, _ = run_layers(inputs, out, [l], bids)
    return out.astype(np.float32)
```
